# Optimizing a Trainium2 kernel written in Bass

```python
import math
import jax, jax.numpy as jnp
from jax import lax
import numpy as np

D_MODEL = 2048
BATCH = 4
SEQ = 8192
DEPTH = 1

D_SSM = 1024
SSM_GROUP = 16
N_SSM_GROUPS = D_SSM // SSM_GROUP
SSM_STATE = 64
DT_MIN = 0.001
DT_MAX = 0.1
N_Q_HEADS = 16
N_KV_HEADS = 4
HEAD_DIM = 64
Q_PER_KV = N_Q_HEADS // N_KV_HEADS
D_ATTN = N_Q_HEADS * HEAD_DIM
D_KV = N_KV_HEADS * HEAD_DIM
WINDOW = 128
BLOCK = 128
N_BUCKETS = 32
MAX_DISTANCE = 128
N_BRANCHES = 2
D_IN = D_SSM + D_SSM + D_ATTN + D_KV + D_KV + D_ATTN + N_BRANCHES * D_MODEL
DEEPNORM_ALPHA = (2.0 * DEPTH) ** 0.25
DEEPNORM_BETA = (8.0 * DEPTH) ** -0.25
LN_EPS = 1e-5
NEG_INF = -1e30

kernel_name = "hybrid_s5_swa_sink_gated_deepnorm"


def _split_columns(proj):
    sizes = (D_SSM, D_SSM, D_ATTN, D_KV, D_KV, D_ATTN, N_BRANCHES * D_MODEL)
    points = []
    acc = 0
    for s in sizes[:-1]:
        acc += s
        points.append(acc)
    return jnp.split(proj, points, axis=-1)


def _layer_norm(x, gain, bias):
    xf = x.astype(jnp.float32)
    mu = jnp.mean(xf, axis=-1, keepdims=True)
    var = jnp.mean(jnp.square(xf - mu), axis=-1, keepdims=True)
    y = (xf - mu) * lax.rsqrt(var + LN_EPS) * gain.astype(jnp.float32) + bias.astype(jnp.float32)
    return y.astype(x.dtype)


def _t5_causal_bucket(dist):
    max_exact = N_BUCKETS // 2
    is_small = dist < max_exact
    d = jnp.maximum(dist, 1).astype(jnp.float32)
    large = max_exact + (jnp.log(d / max_exact) / math.log(MAX_DISTANCE / max_exact)
                         * (N_BUCKETS - max_exact)).astype(jnp.int32)
    large = jnp.minimum(large, N_BUCKETS - 1)
    return jnp.where(is_small, dist, large)


def _band_bias_and_mask(rel_bias_table, n_blocks):
    i = jnp.arange(BLOCK)[:, None]
    j = jnp.arange(2 * BLOCK)[None, :]
    dist = BLOCK + i - j
    band_ok = (dist >= 0) & (dist < WINDOW)
    bucket = _t5_causal_bucket(jnp.clip(dist, 0, None))
    bias = rel_bias_table.astype(jnp.float32)[bucket]
    bias = jnp.transpose(bias, (2, 0, 1)).reshape(N_KV_HEADS, Q_PER_KV, BLOCK, 2 * BLOCK)
    n = jnp.arange(n_blocks)[:, None, None]
    key_abs = n * BLOCK - BLOCK + j[None]
    mask = band_ok[None] & (key_abs >= 0)
    return bias, mask[None, :, None, None]


def _sliding_window_gqa(q, k, v, sinks, rel_bias_table):
    b, s, _ = q.shape
    nb = s // BLOCK
    q = q.reshape(b, nb, BLOCK, N_KV_HEADS, Q_PER_KV, HEAD_DIM)

    def band(t):
        t = t.reshape(b, s, N_KV_HEADS, HEAD_DIM)
        t = jnp.pad(t, ((0, 0), (BLOCK, 0), (0, 0), (0, 0))).reshape(b, nb + 1, BLOCK, N_KV_HEADS, HEAD_DIM)
        return jnp.concatenate([t[:, :-1], t[:, 1:]], axis=2)

    kb, vb = band(k), band(v)
    bias, mask = _band_bias_and_mask(rel_bias_table, nb)
    logits = jnp.einsum("bnqkgd,bnskd->bnkgqs", q, kb).astype(jnp.float32) * (HEAD_DIM ** -0.5)
    logits = jnp.where(mask, logits + bias, NEG_INF)
    sink = sinks.astype(jnp.float32).reshape(N_KV_HEADS, Q_PER_KV)[None, None, :, :, None, None]
    m = jnp.maximum(jnp.max(logits, axis=-1, keepdims=True), sink)
    p = jnp.exp(logits - m)
    p = p / (jnp.sum(p, axis=-1, keepdims=True) + jnp.exp(sink - m))
    out = jnp.einsum("bnkgqs,bnskd->bnqkgd", p.astype(vb.dtype), vb)
    return out.reshape(b, s, D_ATTN)


def _s5_scan_op(e1, e2):
    a1, b1 = e1
    a2, b2 = e2
    return a1 * a2, a2 * b1 + b2


def _s5_ssm(u, lam_re, lam_im, b_re, b_im, c_re, c_im, d_skip, log_step):
    b, s, _ = u.shape
    f32 = jnp.float32
    step = jnp.exp(log_step.astype(f32))[:, None]
    lam = lax.complex(lam_re.astype(f32), lam_im.astype(f32))
    lam_bar = jnp.exp(lam * step)
    b_cplx = lax.complex(b_re.astype(f32), b_im.astype(f32))
    b_bar = ((lam_bar - 1.0) / lam)[..., None] * b_cplx
    ug = u.astype(f32).reshape(b, s, N_SSM_GROUPS, SSM_GROUP)
    bu = lax.complex(jnp.einsum("bsgh,gph->sbgp", ug, jnp.real(b_bar)),
                     jnp.einsum("bsgh,gph->sbgp", ug, jnp.imag(b_bar)))
    a = jnp.broadcast_to(lam_bar[None, None], (s, 1, N_SSM_GROUPS, SSM_STATE))
    _, states = lax.associative_scan(_s5_scan_op, (a, bu), axis=0)
    y = (jnp.einsum("sbgp,ghp->bsgh", jnp.real(states), c_re.astype(f32))
         - jnp.einsum("sbgp,ghp->bsgh", jnp.imag(states), c_im.astype(f32)))
    y = y + d_skip.astype(f32).reshape(N_SSM_GROUPS, SSM_GROUP) * ug
    return y.reshape(b, s, D_SSM)


def setup_inputs(seed: int = 0) -> dict:
    key = jax.random.key(seed)
    ks = jax.random.split(key, 20)
    f32 = jnp.float32
    x = jax.random.normal(ks[0], (BATCH, SEQ, D_MODEL), f32)
    w_in = jax.random.normal(ks[1], (DEPTH, D_MODEL, D_IN), f32) * D_MODEL ** -0.5
    n_idx = jnp.arange(SSM_STATE, dtype=f32)
    ssm_lambda_re = -0.5 + 0.01 * jax.random.normal(ks[2], (DEPTH, N_SSM_GROUPS, SSM_STATE), f32)
    ssm_lambda_im = math.pi * n_idx + 0.01 * jax.random.normal(ks[3], (DEPTH, N_SSM_GROUPS, SSM_STATE), f32)
    ssm_b_re = jax.random.normal(ks[4], (DEPTH, N_SSM_GROUPS, SSM_STATE, SSM_GROUP), f32) * (2.0 * SSM_GROUP) ** -0.5
    ssm_b_im = jax.random.normal(ks[5], (DEPTH, N_SSM_GROUPS, SSM_STATE, SSM_GROUP), f32) * (2.0 * SSM_GROUP) ** -0.5
    ssm_c_re = jax.random.normal(ks[6], (DEPTH, N_SSM_GROUPS, SSM_GROUP, SSM_STATE), f32) * SSM_STATE ** -0.5
    ssm_c_im = jax.random.normal(ks[7], (DEPTH, N_SSM_GROUPS, SSM_GROUP, SSM_STATE), f32) * SSM_STATE ** -0.5
    ssm_d = jax.random.normal(ks[8], (DEPTH, D_SSM), f32)
    ssm_log_step = jax.random.uniform(ks[9], (DEPTH, N_SSM_GROUPS), f32,
                                      minval=math.log(DT_MIN), maxval=math.log(DT_MAX))
    w_glu = jax.random.normal(ks[10], (DEPTH, D_SSM, 2 * D_SSM), f32) * D_SSM ** -0.5
    attn_sinks = jax.random.normal(ks[11], (DEPTH, N_Q_HEADS), f32)
    rel_bias_table = 0.5 * jax.random.normal(ks[12], (N_BUCKETS, N_Q_HEADS), f32)
    w_branch_ssm = jax.random.normal(ks[13], (DEPTH, D_SSM, D_MODEL), f32) * D_SSM ** -0.5 * DEEPNORM_BETA
    w_branch_attn = jax.random.normal(ks[14], (DEPTH, D_ATTN, D_MODEL), f32) * D_ATTN ** -0.5 * DEEPNORM_BETA
    w_out = jax.random.normal(ks[15], (DEPTH, D_MODEL, D_MODEL), f32) * D_MODEL ** -0.5 * DEEPNORM_BETA
    ln_gain = 1.0 + 0.02 * jax.random.normal(ks[16], (DEPTH, D_MODEL), f32)
    ln_bias = 0.02 * jax.random.normal(ks[17], (DEPTH, D_MODEL), f32)
    return {"x": x, "w_in": w_in, "ssm_lambda_re": ssm_lambda_re, "ssm_lambda_im": ssm_lambda_im,
            "ssm_b_re": ssm_b_re, "ssm_b_im": ssm_b_im, "ssm_c_re": ssm_c_re, "ssm_c_im": ssm_c_im,
            "ssm_d": ssm_d, "ssm_log_step": ssm_log_step, "w_glu": w_glu, "attn_sinks": attn_sinks,
            "rel_bias_table": rel_bias_table, "w_branch_ssm": w_branch_ssm, "w_branch_attn": w_branch_attn,
            "w_out": w_out, "ln_gain": ln_gain, "ln_bias": ln_bias}


def reference(x, w_in, ssm_lambda_re, ssm_lambda_im, ssm_b_re, ssm_b_im, ssm_c_re, ssm_c_im,
              ssm_d, ssm_log_step, w_glu, attn_sinks, rel_bias_table, w_branch_ssm, w_branch_attn,
              w_out, ln_gain, ln_bias):
    for layer in range(DEPTH):
        proj = jnp.einsum("bsd,de->bse", x, w_in[layer])
        u_ssm, z_ssm, q, k, v, z_attn, gate_logits = _split_columns(proj)

        y_ssm = _s5_ssm(u_ssm, ssm_lambda_re[layer], ssm_lambda_im[layer], ssm_b_re[layer], ssm_b_im[layer],
                        ssm_c_re[layer], ssm_c_im[layer], ssm_d[layer], ssm_log_step[layer])
        glu_in = jax.nn.gelu(y_ssm, approximate=False)
        glu_a, glu_b = jnp.split(jnp.einsum("bsc,ce->bse", glu_in, w_glu[layer].astype(jnp.float32)), 2, axis=-1)
        h_ssm = (glu_a * jax.nn.sigmoid(glu_b)).astype(x.dtype) * jax.nn.silu(z_ssm)

        h_attn = _sliding_window_gqa(q, k, v, attn_sinks[layer], rel_bias_table) * jax.nn.silu(z_attn)

        gates = jax.nn.sigmoid(gate_logits.astype(jnp.float32)).astype(x.dtype)
        gate_ssm, gate_attn = jnp.split(gates, 2, axis=-1)
        merged = (gate_ssm * jnp.einsum("bsc,cd->bsd", h_ssm, w_branch_ssm[layer])
                  + gate_attn * jnp.einsum("bsc,cd->bsd", h_attn, w_branch_attn[layer]))
        out = jnp.einsum("bsd,de->bse", merged, w_out[layer])

        x = _layer_norm(DEEPNORM_ALPHA * x + out.astype(x.dtype), ln_gain[layer], ln_bias[layer])
    return x
```

```python
import math
import numpy as np
from contextlib import ExitStack
import concourse.bass as bass
import concourse.mybir as mybir
from concourse.bass_utils import run_bass_kernel_spmd

F32 = mybir.dt.float32
BF16 = mybir.dt.bfloat16
I32 = mybir.dt.int32
ALU = mybir.AluOpType
AF = mybir.ActivationFunctionType
PI = float(np.pi)
TWO_PI = float(2 * np.pi)

NCORES = 8
TOK = 4096
SPAN = 512
NB = SPAN // 8
NSPAN = TOK // SPAN
D = 2048
DIN = 8704
ALPHA = 2.0 ** 0.25
LN_EPS = 1e-5
NEG = -30000.0
C_U, C_ZS, C_Q, C_K, C_V, C_ZA, C_G = 0, 1024, 2048, 3072, 3328, 3584, 4608

ENGS = ["pe", "act", "dve", "pool", "sp"]
NDS = 12
NBG = 6


class Prog:
    def __init__(self, nc, st):
        self.nc = nc
        self.q = {e: [] for e in ENGS}
        self.sem = {e: st.enter_context(nc.semaphore("c_" + e)) for e in ENGS if e != "sp"}
        self.cnt = {e: 0 for e in ENGS}
        self.dsem = [st.enter_context(nc.semaphore(f"dq{i}")) for i in range(NDS + NBG)]
        self.dcnt = [0] * (NDS + NBG)
        self.dnext = 0
        self.bnext = 0
        self.seen = {e: {} for e in ENGS}
        self.lastw = {}
        self.readers = {}

    def _need(self, eng, tok, waits, raw=False):
        kind, src, val = tok
        if kind == "e" and src == eng and (eng == "pe" or not raw):
            return
        key = (kind, src)
        if self.seen[eng].get(key, 0) >= val:
            return
        self.seen[eng][key] = val
        waits.append(tok)

    def _deps(self, eng, r, w):
        waits = []
        for b in r:
            t = self.lastw.get(b)
            if t:
                self._need(eng, t, waits, raw=True)
        for b in w:
            t = self.lastw.get(b)
            if t:
                self._need(eng, t, waits)
            for t in self.readers.get(b, ()):
                self._need(eng, t, waits)
        return waits

    def _commit(self, tok, r, w):
        for b in r:
            self.readers.setdefault(b, []).append(tok)
        for b in w:
            self.lastw[b] = tok
            self.readers[b] = []

    def op(self, eng, fn, r=(), w=()):
        waits = self._deps(eng, r, w)
        self.cnt[eng] += 1
        tok = ("e", eng, self.cnt[eng])
        self.q[eng].append((waits, fn, tok))
        self._commit(tok, r, w)

    def group(self, eng, fns, r=(), w=()):
        waits = self._deps(eng, r, w)
        self.cnt[eng] += 1
        tok = ("e", eng, self.cnt[eng])
        for i, fn in enumerate(fns):
            self.q[eng].append((waits if i == 0 else [], fn, tok if i == len(fns) - 1 else None))
        self._commit(tok, r, w)

    def dma(self, eng, out, in_, r=(), w=(), bg=False):
        waits = self._deps(eng, r, w)
        if bg:
            s = NDS + self.bnext
            self.bnext = (self.bnext + 1) % NBG
        else:
            s = self.dnext
            self.dnext = (self.dnext + 1) % NDS
        if self.dcnt[s]:
            self._need(eng, ("d", s, self.dcnt[s]), waits)
        self.dcnt[s] += 16
        tok = ("d", s, self.dcnt[s])
        self.q[eng].append((waits, lambda e, o=out, i=in_: e.dma_start(out=o, in_=i), tok))
        self._commit(tok, r, w)

    def barrier(self):
        for e in ENGS:
            waits = []
            for f in ENGS:
                if f != "sp" and self.cnt[f]:
                    self._need(e, ("e", f, self.cnt[f]), waits)
            for s in range(NDS):
                if self.dcnt[s]:
                    self._need(e, ("d", s, self.dcnt[s]), waits)
            if waits:
                self.q[e].append((waits, None, None))
        keep = {k: v for k, v in self.lastw.items() if isinstance(k, tuple) and v[0] == "d" and v[1] >= NDS}
        self.lastw.clear()
        self.lastw.update(keep)
        self.readers.clear()

    def emit(self, eng, e):
        for waits, fn, tok in self.q[eng]:
            for kind, src, val in waits:
                e.wait_ge(self.sem[src] if kind == "e" else self.dsem[src], val)
            if fn is None:
                continue
            ins = fn(e)
            if tok is not None:
                if tok[0] == "e":
                    ins.then_inc(self.sem[eng], 1)
                else:
                    ins.then_inc(self.dsem[tok[1]], 16)


def build_program(debug=None):
    dbg = debug or {}
    nc = bass.Bass("TRN2", target_bir_lowering=False)
    dr = lambda n, s, k="ExternalInput", d=F32: nc.dram_tensor(n, list(s), d, kind=k).ap()
    xo = dr("xo", [TOK, D])
    xp = dr("xp", [TOK, D])
    w_in = dr("w_in", [D, DIN])
    w_glu = dr("w_glu", [1024, 2048])
    w_bs = dr("w_bs", [1024, 2048])
    w_ba = dr("w_ba", [1024, 2048])
    w_out = dr("w_out", [D, D])
    lamr_d = dr("lamr", [128, 32]); lami_d = dr("lami", [128, 32]); lstep_d = dr("lstep", [128, 32])
    br_d = dr("br", [128, 512]); bi_d = dr("bi", [128, 512]); cr_d = dr("cr", [128, 512]); ci_d = dr("ci", [128, 512])
    dI_d = dr("dI", [16, 1024])
    sk_d = dr("sk", [128, 8])
    line_d = dr("line", [16, 384])
    hneg_d = dr("hneg", [128, 1])
    lng_d = dr("lng", [128, D]); lnb_d = dr("lnb", [128, D])
    idf_d = dr("idf", [128, 128]); anti_d = dr("anti", [128, 128])
    msc_d = dr("msc", [128, 65])
    y_out = dr("y", [TOK, D], k="ExternalOutput")
    s_in = dr("s_in", [34, 128, 4096], k="Internal", d=BF16)
    s_glu = dr("s_glu", [8, 128, 2048], k="Internal", d=BF16)
    s_bs = dr("s_bs", [8, 128, 2048], k="Internal", d=BF16)
    s_ba = dr("s_ba", [8, 128, 2048], k="Internal", d=BF16)
    s_out = dr("s_out", [8, 128, 4096], k="Internal", d=BF16)
    s_k2 = dr("s_k2", [2, 128, 4096], k="Internal", d=BF16)
    s_w1 = dr("s_w1", [2, 128, 4096], k="Internal", d=BF16)
    s_wc = dr("s_wc", [2, 128, 4096], k="Internal", d=BF16)
    s_tm = dr("s_tm", [2, 128, 4096], k="Internal", d=BF16)

    with ExitStack() as st:
        P = Prog(nc, st)
        used = {}
        dumped = {}

        def dump(name, t, key, parts=128):
            if name not in dbg.get("dumps", ()) or name in dumped:
                return
            shape = [parts, t.shape[-1]]
            dst = nc.dram_tensor("dbg_" + name, shape, t.dtype, kind="ExternalOutput").ap()
            P.dma("sp", dst, t[0:parts, :], r=[key])
            dumped[name] = True

        memo = {}
        own_scope = ExitStack()
        pre_scope = ExitStack()

        def sb(n, s, d=F32, c=st):
            if c is own_scope or c is pre_scope:
                if n in memo:
                    return memo[n]
            k = used.get(n, 0)
            used[n] = k + 1
            t = c.enter_context(nc.sbuf_tensor(n if k == 0 else f"{n}_{k}", list(s), d))
            if c is own_scope or c is pre_scope:
                memo[n] = t
            return t
        identb = sb("identb", [128, 128], BF16)
        identf = sb("identf", [128, 128])
        antif = sb("antif", [128, 128])
        ones64 = sb("ones64", [128, 64], BF16)
        biasT = sb("biasT", [128, 16 * 2 * 128])
        CA = sb("CA", [128, 64])
        CBn = sb("CBn", [128, 32]); CBp = sb("CBp", [128, 32])
        carry = sb("carry", [128, 64])
        esk = sb("esk", [128, 8])
        ar = sb("ar", [128, 32]); ai = sb("ai", [128, 32])
        hneg = sb("hneg_s", [128, 1])
        wstate = {"i": 0}
        wbuf = []
        psbig = st.enter_context(nc.psum_tensor("psbig", [128, 6 * 512], F32))
        psf = [psbig[:, 512 * i:512 * i + 512] for i in range(6)]
        psb = [st.enter_context(nc.psum_tensor(f"psb{i}", [128, 1024], BF16)) for i in range(2)]
        pstate = {"f": 0, "b": 0}
        psbf = [psb[i][:].bitcast(F32) for i in range(2)]

        def nps():
            i = pstate["f"]; pstate["f"] = (i + 1) % 6
            return psf[i], f"psf{i}"

        def nps2():
            i = pstate["f"]
            if i % 2:
                i = (i + 1) % 6
            pstate["f"] = (i + 2) % 6
            return psbig[:, 512 * i:512 * i + 1024], (f"psf{i}", f"psf{i + 1}")

        def npb():
            i = pstate["b"]; pstate["b"] = (i + 1) % 2
            return psb[i], f"psb{i}"

        def nwb():
            i = wstate["i"]; wstate["i"] = (i + 1) % 4
            return wbuf[i], f"wbuf{i}"

        evs = {"i": 0}

        def evac_eng():
            evs["i"] ^= 1
            return "dve" if evs["i"] else "act"

        def copy(eng, out, in_, r, w):
            if eng == "act":
                P.op("act", lambda e, o=out, i=in_: e.copy(o, i), r=r, w=w)
            else:
                P.op(eng, lambda e, o=out, i=in_: e.tensor_copy(o, i), r=r, w=w)

        def tt(eng, out, a, b, op, r, w):
            P.op(eng, lambda e, o=out, x=a, y=b, p=op: e.tensor_tensor(o, x, y, p), r=r, w=w)

        def ts(eng, out, a, s1, s2, op0, op1, r, w):
            if op1 is None:
                P.op(eng, lambda e, o=out, x=a: e.tensor_scalar(o, x, s1, None, op0), r=r, w=w)
            else:
                P.op(eng, lambda e, o=out, x=a: e.tensor_scalar(o, x, s1, s2, op0, op1), r=r, w=w)

        def taylor_exp(q, x, deg, keyq, keyx):
            ts("dve", q, x, 1.0 / deg, 1.0, ALU.mult, ALU.add, [keyx], [keyq])
            for k in range(deg - 1, 0, -1):
                tt("dve", q, q, x, ALU.mult, [keyq, keyx], [keyq])
                ts("dve", q, q, 1.0 / k, 1.0, ALU.mult, ALU.add, [keyq], [keyq])

        C1 = 6.28125
        C2 = TWO_PI - C1

        def sin_reduced(A, T_, K_):
            a, t_, k_ = A.name, T_.name, K_.name
            ts("dve", T_[:], A[:], 1.0 / TWO_PI, None, ALU.mult, None, [a], [t_])
            copy("dve", K_[:], T_[:], [t_], [k_])
            copy("dve", T_[:], K_[:], [k_], [t_])
            P.op("dve", lambda e: e.scalar_tensor_tensor(A[:], T_[:], -C1, A[:], ALU.mult, ALU.add), r=[t_, a], w=[a])
            P.op("dve", lambda e: e.scalar_tensor_tensor(A[:], T_[:], -C2, A[:], ALU.mult, ALU.add), r=[t_, a], w=[a])
            ts("dve", T_[:], A[:], PI, -TWO_PI, ALU.is_gt, ALU.mult, [a], [t_])
            tt("dve", A[:], A[:], T_[:], ALU.add, [t_, a], [a])
            ts("dve", T_[:], A[:], -PI, TWO_PI, ALU.is_lt, ALU.mult, [a], [t_])
            tt("dve", A[:], A[:], T_[:], ALU.add, [t_, a], [a])
            ts("dve", A[:], A[:], PI, -PI, ALU.min, ALU.max, [a], [a])
            act(A[:], A[:], AF.Sin, [a], [a])

        def act(out, in_, func, r, w, scale=1.0):
            P.op("act", lambda e, o=out, i=in_, f=func, s=scale: e.activation(o, i, f, scale=s), r=r, w=w)

        def cast_w(scr, sname, idx, src, c0, ncols, col_off=0):
            nk = src.shape[0] // 128
            dst = scr[idx].rearrange("p (k c) -> p k c", c=256)[:, 0:nk, col_off:col_off + ncols]
            P.dma("pool", dst, src.rearrange("(k p) c -> p k c", p=128)[:, :, c0:c0 + ncols],
                  w=[(sname, idx, col_off)], bg=True)

        cast_jobs = []

        def cast_all(first):
            if first:
                for i in list(range(4)) + [13]:
                    cast_w(s_in, "s_in", i, w_in, 256 * i, 256)
                for kh in range(4):
                    for dup in range(2):
                        cast_w(s_k2, "s_k2", kh // 2, w_in, C_K + 64 * kh, 64, 128 * (kh % 2) + 64 * dup)
                return
            J = cast_jobs.append
            for i in list(range(8, 12)) + list(range(14, 18)):
                J((s_in, "s_in", i, w_in, 256 * i, 256, 0))
            for i in range(8):
                J((s_in, "s_in", 26 + i, w_in, 256 * (26 + i), 256, 0))
                J((s_ba, "s_ba", i, w_ba, 256 * i, 256, 0))
            for i in range(4, 8):
                J((s_in, "s_in", i, w_in, 256 * i, 256, 0))
            for i in range(8):
                J((s_glu, "s_glu", i, w_glu, 128 * i, 128, 0)); J((s_glu, "s_glu", i, w_glu, 1024 + 128 * i, 128, 128))
            for i in range(8):
                J((s_in, "s_in", 18 + i, w_in, 256 * (18 + i), 256, 0))
                J((s_bs, "s_bs", i, w_bs, 256 * i, 256, 0))
            for i in range(8):
                J((s_out, "s_out", i, w_out, 256 * i, 256, 0))

        def issue_casts(n):
            for _ in range(min(n, len(cast_jobs))):
                cast_w(*cast_jobs.pop(0))

        def load_c(scr, sname, idx, nk=16, cw=256, offs=(0, 128)):
            dst, key = nwb()
            P.dma("sp", dst[:, 0:nk * 256], scr[idx][:, 0:nk * 256], r=[(sname, idx, o_) for o_ in offs], w=[key])
            return dst[:].rearrange("p (k c) -> p k c", c=cw), key

        with ExitStack() as s0:
            sb0 = lambda n, s, d=F32: sb(n, s, d, s0)
            lamr = sb0("lamr_s", [128, 32]); lami = sb0("lami_s", [128, 32]); stp = sb0("stp", [128, 32])
            W1re = sb0("W1re", [128, 32 * 128], BF16)
            W1im = sb0("W1im", [128, 32 * 128], BF16)
            Wc = sb0("Wc", [128, 32 * 2 * 128], BF16)
            Tm = sb0("Tm", [128, 64 * 128], BF16)
            brs = sb0("brs", [128, 512]); bis = sb0("bis", [128, 512]); crs = sb0("crs", [128, 512]); cis = sb0("cis", [128, 512])
            dIs = sb0("dIs", [16, 1024]); sks = sb0("sks", [128, 8])
            for t_, d_ in ((lamr, lamr_d), (lami, lami_d), (stp, lstep_d), (brs, br_d), (bis, bi_d), (crs, cr_d),
                           (cis, ci_d), (dIs, dI_d), (sks, sk_d), (hneg, hneg_d), (identf, idf_d), (antif, anti_d)):
                P.dma("sp", t_[:], d_, w=[t_.name])
            copy("dve", identb[:], identf[:], [identf.name], [identb.name])
            VH = [sb0(f"VH{i}", [128, 256]) for i in range(4)]

            def bias_head(h):
                vh = VH[h % 4]
                for kt_ in range(2):
                    P.dma("sp", vh[:, 128 * kt_:128 * kt_ + 128],
                          bass.AP(line_d.tensor, 384 * h + 129 - 128 * kt_, [[1, 128], [1, 128]]), w=[vh.name])
                ps, pk = nps()
                P.group("pe", [lambda e, o=ps[:, 128 * k_:128 * k_ + 128], r_=vh[:, 128 * k_:128 * k_ + 128]:
                               e.matmul(o, antif[:], r_, start=True, stop=True) for k_ in range(2)], r=[vh.name, antif.name], w=[pk])
                copy("act", biasT[:, 256 * h:256 * h + 256], ps[:, 0:256], [pk], ["biasT"])
            P.op("dve", lambda e: e.memset(ones64[:], 1.0), w=["ones64"])
            P.op("dve", lambda e: e.memset(carry[:], 0.0), w=["carry"])
            act(esk[:], sks[:], AF.Exp, [sks.name], ["esk"])
            ts("dve", ar[:], stp[:], 0.125, None, ALU.mult, None, [stp.name], ["ar"])
            taylor_exp(stp[:], ar[:], 10, stp.name, "ar")
            for _ in range(3):
                tt("dve", stp[:], stp[:], stp[:], ALU.mult, [stp.name], [stp.name])
            tt("dve", ar[:], lamr[:], stp[:], ALU.mult, [lamr.name, stp.name], ["ar"])
            tt("dve", ai[:], lami[:], stp[:], ALU.mult, [lami.name, stp.name], ["ai"])
            MAR = sb0("MAR", [128, 9 * 32]); ANG = sb0("ANG", [128, 2 * 9 * 32]); TMPA = sb0("TMPA", [128, 576])
            KI = sb0("KI", [128, 576], I32)
            for m in range(9):
                ts("dve", MAR[:, 32 * m:32 * m + 32], ar[:], float(m), None, ALU.mult, None, ["ar"], ["MAR"])
                ts("dve", ANG[:, 32 * m:32 * m + 32], ai[:], float(m), None, ALU.mult, None, ["ai"], ["ANG"])
                ts("dve", ANG[:, 288 + 32 * m:288 + 32 * m + 32], ai[:], float(m), PI / 2, ALU.mult, ALU.add, ["ai"], ["ANG"])
            sin_reduced(ANG, TMPA, KI)
            MAG = sb0("MAG", [128, 9 * 32])
            taylor_exp(MAG[:], MAR[:], 8, "MAG", "MAR")
            PR = sb0("PR", [128, 288]); PIm = sb0("PIm", [128, 288])
            tt("dve", PR[:], MAG[:], ANG[:, 288:576], ALU.mult, ["MAG", "ANG"], ["PR"])
            tt("dve", PIm[:], MAG[:], ANG[:, 0:288], ALU.mult, ["MAG", "ANG"], ["PIm"])
            copy("dve", CA[:, 0:32], PR[:, 256:288], ["PR"], ["CA"])
            copy("dve", CA[:, 32:64], PR[:, 256:288], ["PR"], ["CA"])
            copy("dve", CBp[:], PIm[:, 256:288], ["PIm"], ["CBp"])
            ts("dve", CBn[:], PIm[:, 256:288], -1.0, None, ALU.mult, None, ["PIm"], ["CBn"])
            nr = sb0("nr", [128, 32]); den = sb0("den", [128, 32]); t1 = sb0("t1", [128, 32]); t2 = sb0("t2", [128, 32])
            kr = sb0("kr", [128, 32]); ki_ = sb0("ki_", [128, 32])
            ts("dve", nr[:], PR[:, 32:64], -1.0, None, ALU.add, None, ["PR"], ["nr"])
            tt("dve", den[:], lamr[:], lamr[:], ALU.mult, [lamr.name], ["den"])
            tt("dve", t1[:], lami[:], lami[:], ALU.mult, [lami.name], ["t1"])
            tt("dve", den[:], den[:], t1[:], ALU.add, ["t1"], ["den"])
            P.op("dve", lambda e: e.reciprocal(den[:], den[:]), r=["den"], w=["den"])
            tt("dve", t1[:], nr[:], lamr[:], ALU.mult, ["nr"], ["t1"])
            tt("dve", t2[:], PIm[:, 32:64], lami[:], ALU.mult, ["PIm"], ["t2"])
            tt("dve", t1[:], t1[:], t2[:], ALU.add, ["t2"], ["t1"])
            tt("dve", kr[:], t1[:], den[:], ALU.mult, ["t1", "den"], ["kr"])
            tt("dve", t1[:], PIm[:, 32:64], lamr[:], ALU.mult, ["PIm"], ["t1"])
            tt("dve", t2[:], nr[:], lami[:], ALU.mult, ["nr"], ["t2"])
            tt("dve", t1[:], t1[:], t2[:], ALU.subtract, ["t2"], ["t1"])
            tt("dve", ki_[:], t1[:], den[:], ALU.mult, ["t1", "den"], ["ki_"])
            OMR = sb0("OMR", [128, 256]); OMI = sb0("OMI", [128, 256]); T8 = sb0("T8", [128, 256])
            pr3 = PR[:, 0:256].rearrange("p (m q) -> p m q", q=32); pi3 = PIm[:, 0:256].rearrange("p (m q) -> p m q", q=32)
            krb = kr[:].unsqueeze(1).to_broadcast([128, 8, 32]); kib = ki_[:].unsqueeze(1).to_broadcast([128, 8, 32])
            omr3 = OMR[:].rearrange("p (m q) -> p m q", q=32); omi3 = OMI[:].rearrange("p (m q) -> p m q", q=32)
            t83 = T8[:].rearrange("p (m q) -> p m q", q=32)
            tt("dve", omr3, pr3, krb, ALU.mult, ["PR", "kr"], ["OMR"])
            tt("dve", t83, pi3, kib, ALU.mult, ["PIm", "ki_"], ["T8"])
            tt("dve", OMR[:], OMR[:], T8[:], ALU.subtract, ["T8"], ["OMR"])
            tt("dve", omi3, pi3, krb, ALU.mult, ["PIm", "kr"], ["OMI"])
            tt("dve", t83, pr3, kib, ALU.mult, ["PR", "ki_"], ["T8"])
            tt("dve", OMI[:], OMI[:], T8[:], ALU.add, ["T8"], ["OMI"])
            TB = sb0("TB", [128, 512]); A0r = sb0("A0r", [128, 512]); A0i = sb0("A0i", [128, 512])
            sA = ExitStack()
            AR = sb("AR", [128, 4096], F32, sA); AI = sb("AI", [128, 4096], F32, sA)
            ar4 = AR[:].rearrange("p (q j h) -> p q j h", j=8, h=16); ai4 = AI[:].rearrange("p (q j h) -> p q j h", j=8, h=16)
            br3 = brs[:].rearrange("p (q h) -> p q h", h=16); bi3 = bis[:].rearrange("p (q h) -> p q h", h=16)
            tb3 = TB[:].rearrange("p (q h) -> p q h", h=16)
            for j in range(8):
                m = 7 - j
                orb = OMR[:, 32 * m:32 * m + 32].unsqueeze(2).to_broadcast([128, 32, 16])
                oib = OMI[:, 32 * m:32 * m + 32].unsqueeze(2).to_broadcast([128, 32, 16])
                tt("dve", ar4[:, :, j, :], br3, orb, ALU.mult, [brs.name, "OMR"], ["AR"])
                tt("dve", tb3, bi3, oib, ALU.mult, [bis.name, "OMI"], ["TB"])
                tt("dve", ar4[:, :, j, :], ar4[:, :, j, :], tb3, ALU.subtract, ["TB"], ["AR"])
                tt("dve", ai4[:, :, j, :], bi3, orb, ALU.mult, [bis.name, "OMR"], ["AI"])
                tt("dve", tb3, br3, oib, ALU.mult, [brs.name, "OMI"], ["TB"])
                tt("dve", ai4[:, :, j, :], ai4[:, :, j, :], tb3, ALU.add, ["TB"], ["AI"])
            for src, dst, nm in ((AR, W1re, "W1re"), (AI, W1im, "W1im")):
                for q0 in range(0, 32, 4):
                    ps, pk = nps()
                    P.group("pe", [lambda e, o=ps[:, 128 * i:128 * i + 128], s_=src[:, 128 * (q0 + i):128 * (q0 + i) + 128]:
                                   e.transpose(o, s_, identf[:]) for i in range(4)], r=[src.name, identf.name], w=[pk])
                    copy(evac_eng(), dst[:, 128 * q0:128 * q0 + 512], ps[:, :], [pk], [nm])
            copy("dve", A0r[:].rearrange("p (q h) -> p q h", h=16), ar4[:, :, 7, :], ["AR"], ["A0r"])
            copy("dve", A0i[:].rearrange("p (q h) -> p q h", h=16), ai4[:, :, 7, :], ["AI"], ["A0i"])
            P.barrier()
            sA.close()
            ER = sb0("ER", [128, 32 * 9 * 16]); NEI = sb0("NEI", [128, 32 * 9 * 16])
            er4 = ER[:].rearrange("p (q k h) -> p q k h", k=9, h=16); ne4 = NEI[:].rearrange("p (q k h) -> p q k h", k=9, h=16)
            cr3 = crs[:].rearrange("p (q h) -> p q h", h=16); ci3 = cis[:].rearrange("p (q h) -> p q h", h=16)
            for k in range(9):
                prb = PR[:, 32 * k:32 * k + 32].unsqueeze(2).to_broadcast([128, 32, 16])
                pib = PIm[:, 32 * k:32 * k + 32].unsqueeze(2).to_broadcast([128, 32, 16])
                tt("dve", er4[:, :, k, :], cr3, prb, ALU.mult, [crs.name, "PR"], ["ER"])
                tt("dve", tb3, ci3, pib, ALU.mult, [cis.name, "PIm"], ["TB"])
                tt("dve", er4[:, :, k, :], er4[:, :, k, :], tb3, ALU.subtract, ["TB"], ["ER"])
                tt("dve", ne4[:, :, k, :], cr3, pib, ALU.mult, [crs.name, "PIm"], ["NEI"])
                tt("dve", tb3, ci3, prb, ALU.mult, [cis.name, "PR"], ["TB"])
                P.op("dve", lambda e, o=ne4[:, :, k, :]: e.scalar_tensor_tensor(o, o, -1.0, tb3, ALU.mult, ALU.subtract), r=["TB"], w=["NEI"])
            P.op("pool", lambda e: e.memset(Tm[:], 0.0), w=["Tm"])
            cast_all(True)
            cast_all(False)
            wc4 = Wc[:].rearrange("p (q r c) -> p q r c", r=2, c=128)
            for ri, src in ((0, ER), (1, NEI)):
                s4 = src[:].rearrange("p (q c) -> p q c", c=144)
                copy("dve", wc4[:, :, ri, :], s4[:, :, 16:144], [src.name], ["Wc"])
            WBF = [sb0(f"WBF{i}", [128, 2 * 2 * 128]) for i in range(4)]
            KTb = sb0("KTb", [16, 64 * 128], BF16)
            for wb in WBF:
                P.op("dve", lambda e, t_=wb: e.memset(t_[:], 0.0), w=[wb.name])
            dI3 = dIs[:].rearrange("p (g h) -> p g h", h=16)
            kt4 = KTb[:].rearrange("p (g k h) -> p g k h", k=8, h=16)
            for q in range(32):
                if q % 2 == 0:
                    bias_head(q // 2)
                wb = WBF[q % 4]
                w4 = wb[:].rearrange("p (r g c) -> p r g c", r=2, g=2)
                for g2 in range(2):
                    rows = slice(64 * g2, 64 * g2 + 64)
                    copy("act", w4[rows, 0, g2, :], ER[rows, 144 * q:144 * q + 128], ["ER"], [wb.name])
                    copy("act", w4[rows, 1, g2, :], NEI[rows, 144 * q:144 * q + 128], ["NEI"], [wb.name])
                ps, pk = nps()
                P.group("pe", [
                    lambda e, o=ps[0:16, 0:256], l=A0r[:, 16 * q:16 * q + 16], r_=wb[:, 0:256]: e.matmul(o, l, r_, start=True, stop=False),
                    lambda e, o=ps[0:16, 0:256], l=A0i[:, 16 * q:16 * q + 16], r_=wb[:, 256:512]: e.matmul(o, l, r_, start=False, stop=True),
                ], r=["A0r", "A0i", wb.name], w=[pk])
                copy("dve", KTb[:, 256 * q:256 * q + 256], ps[0:16, 0:256], [pk], ["KTb"])
                tt("dve", kt4[:, 2 * q:2 * q + 2, 0, :], ps[0:16, 0:256].rearrange("p (g k h) -> p g k h", g=2, h=16)[:, :, 0, :],
                   dI3[:, 2 * q:2 * q + 2, :], ALU.add, [pk, dIs.name], ["KTb"])
            tm3 = Tm[:].rearrange("p (g c) -> p g c", c=128)
            kt3 = KTb[:].rearrange("p (g c) -> p g c", c=128)
            for jp in range(8):
                P.dma("sp", tm3[16 * jp:16 * jp + 16, :, 16 * jp:128], kt3[:, :, 0:(8 - jp) * 16], r=["KTb", "Tm"], w=[("Tm", jp)])
            for i, (t_, nm_) in enumerate(((W1re, "W1re"), (W1im, "W1im"))):
                P.dma("sp", s_w1[i], t_[:], r=[nm_])
            for i in range(2):
                P.dma("sp", s_wc[i], Wc[:, 4096 * i:4096 * i + 4096], r=["Wc"])
                P.dma("sp", s_tm[i], Tm[:, 4096 * i:4096 * i + 4096], r=["Tm"] + [("Tm", jp) for jp in range(8)])
            P.barrier()
            for nm_, t_ in (("PR", PR), ("PIm", PIm), ("kr", kr), ("ki_", ki_), ("W1re", W1re), ("W1im", W1im), ("Wc", Wc), ("Tm", Tm),
                            ("biasT", biasT), ("esk", esk), ("CA", CA), ("CBn", CBn), ("ER", ER), ("NEI", NEI)):
                dump(nm_, t_, "none")
            dump("KTb", KTb, "none", parts=16)
            P.barrier()

        xT = sb("xT", [128, 16 * SPAN], BF16)
        gyT = sb("gyT", [128, 8 * SPAN], BF16)
        kT = sb("kT", [128, 4 * 640], BF16)
        vext = sb("vext", [128, 5 * 4 * 64], BF16)
        wbuf.extend(sb(f"wbuf{i}", [128, 16 * 256], BF16) for i in range(4))
        xb_t = [sb(f"xb{i}", [128, D], BF16) for i in range(2)]
        xb = [(t_[:, :], t_.name) for t_ in xb_t]
        xpre = {"n": 0}
        P.op("dve", lambda e: e.memset(kT[:], 0.0), w=["kT"])
        P.op("dve", lambda e: e.memset(vext[:], 0.0), w=["vext"])
        xT3 = xT[:].rearrange("p (k t) -> p k t", t=SPAN)
        gy3 = gyT[:].rearrange("p (c t) -> p c t", t=SPAN)
        kT3 = kT[:].rearrange("p (h t) -> p h t", t=640)
        vx4 = vext[:].rearrange("p (t h d) -> p t h d", h=4, d=64)
        bT4 = biasT[:].rearrange("p (h k q) -> p h k q", k=2, q=128)

        def build_xT(xsrc, row0, bufs, ev=None):
            for t4 in range(SPAN // 128):
                b_, bk_ = bufs[t4 % len(bufs)]
                if t4 >= xpre["n"]:
                    P.dma("pool", b_, xsrc[row0 + 128 * t4:row0 + 128 * t4 + 128, :], w=[bk_])
                for half in range(2):
                    ps, pk = npb()
                    P.group("pe", [lambda e, o=ps[:, 128 * i:128 * i + 128], s2=b_[:, 128 * (8 * half + i):128 * (8 * half + i) + 128]:
                                   e.transpose(o, s2, identb[:]) for i in range(8)], r=[bk_], w=[pk])
                    copy(ev or evac_eng(), xT3[:, 8 * half:8 * half + 8, 128 * t4:128 * t4 + 128],
                         ps[:, :].rearrange("p (k t) -> p k t", t=128), [pk], ["xT"])
            xpre["n"] = 0

        def prefetch_x(xsrc, row0, bufs, t_lo, t_hi):
            for t4 in range(t_lo, t_hi):
                P.dma("pool", bufs[t4][0], xsrc[row0 + 128 * t4:row0 + 128 * t4 + 128, :], w=[bufs[t4][1]])
            xpre["n"] = t_hi

        def ssm_state(sc, own, pfx=None):
            ubm = sb("ubm", [128, 8 * 1024], BF16, sc)
            U = sb("U", [128, 64 * NB], BF16, sc)
            X = sb("X", [128, 2 * 32 * NB], F32, sc)
            ub4 = ubm[:].rearrange("p (g j h) -> p g j h", j=8, h=16)
            U3 = U[:].rearrange("p (g b) -> p g b", b=NB)
            X4 = X[:].rearrange("p (r q b) -> p r q b", r=2, b=NB)
            xTj = xT[:].rearrange("p (k b j) -> p k j b", j=8, b=NB)
            for cb in range(4):
                wv, wk = load_c(s_in, "s_in", cb)
                for j in range(8):
                    ps, pk = nps()
                    P.group("pe", [lambda e, o=ps[0:NB, 0:256], l=xTj[:, kt, j, :], r_=wv[:, kt, :], kt=kt:
                                   e.matmul(o, l, r_, start=(kt == 0), stop=(kt == 15)) for kt in range(16)],
                            r=["xT", wk], w=[pk])
                    copy("act" if (pfx is not None and cb == 0) else evac_eng(),
                         ub4[0:NB, 16 * cb:16 * cb + 16, j, :], ps[0:NB, 0:256].rearrange("p (g h) -> p g h", h=16), [pk], ["ubm"])
                if own and cb in (0, 2) and late.get("pending"):
                    late["pending"].pop(0)()
            for g0 in range(0, 64, 16):
                ps, pk = npb()
                P.group("pe", [lambda e, o=ps[:, NB * i:NB * i + NB], s_=ubm[0:NB, 128 * (g0 + i):128 * (g0 + i) + 128]:
                               e.transpose(o, s_, identb[0:NB, 0:NB]) for i in range(16)], r=["ubm"], w=[pk])
                copy(evac_eng(), U[:, NB * g0:NB * g0 + NB * 16], ps[:, 0:NB * 16], [pk], ["U"])
            w1r, w1rk = load_c(s_w1, "s_w1", 0, 16, 128)
            w1i, w1ik = load_c(s_w1, "s_w1", 1, 16, 128)
            w13 = {0: w1r, 1: w1i}
            for q0 in range(0, 32, 4):
                ps, pk = nps()
                ps4 = ps[:, :].rearrange("p (q r b) -> p q r b", r=2, b=NB)
                fns = []
                for i in range(4):
                    q = q0 + i
                    for g2 in range(2):
                        for ri in range(2):
                            fns.append(lambda e, o=ps4[64 * g2:64 * g2 + 64, i, ri, :], l=w13[ri][:, q, 64 * g2:64 * g2 + 64],
                                       r_=U3[:, 2 * q + g2, :]: e.matmul(o, l, r_, start=True, stop=True))
                P.group("pe", fns, r=["U", w1rk, w1ik], w=[pk])
                copy(evac_eng(), X4[:, :, q0:q0 + 4, :], ps4.rearrange("p q r b -> p r q b"), [pk], ["X"])
            tA = sb("tA", [128, 64], F32, sc); tB = sb("tB", [128, 64], F32, sc)
            if pfx is not None:
                t1 = sb("pt1", [128, 32 * NB], F32, sc); t2 = sb("pt2", [128, 32 * NB], F32, sc)
                red = sb("pred", [128, 64], F32, sc)
                xr, xi = X[:, 0:32 * NB], X[:, 32 * NB:64 * NB]
                for ri_, (a_, b_, op_) in enumerate(((pfx["PWr"], pfx["PWi"], ALU.subtract), (pfx["PWi"], pfx["PWr"], ALU.add))):
                    tt("dve", t1[:], xr, a_[:], ALU.mult, ["X"], ["pt1"])
                    tt("dve", t2[:], xi, b_[:], ALU.mult, ["X"], ["pt2"])
                    tt("dve", t1[:], t1[:], t2[:], op_, ["pt1", "pt2"], ["pt1"])
                    P.op("dve", lambda e, o=red[:, 32 * ri_:32 * ri_ + 32], i_=t1[:].rearrange("p (q b) -> p q b", b=NB):
                         e.reduce_sum(o, i_, mybir.AxisListType.X), r=["pt1"], w=["pred"])
                c3 = carry[:].rearrange("p (r q) -> p r q", r=2)
                ta3 = tA[:].rearrange("p (r q) -> p r q", r=2); tb3_ = tB[:].rearrange("p (r q) -> p r q", r=2)
                tt("dve", ta3, c3, pfx["CA64"][:].rearrange("p (r q) -> p r q", r=2), ALU.mult, ["carry"], ["tA"])
                tt("dve", tb3_[:, 0, :], c3[:, 1, :], pfx["CB64n"][:], ALU.mult, ["carry"], ["tB"])
                tt("dve", tb3_[:, 1, :], c3[:, 0, :], pfx["CB64p"][:], ALU.mult, ["carry"], ["tB"])
                tt("dve", tA[:], tA[:], tB[:], ALU.add, ["tA", "tB"], ["tA"])
                tt("dve", carry[:], tA[:], red[:], ALU.add, ["tA", "pred"], ["carry"])
                return None
            Sh = None
            if own:
                Sh = sb("Sh", [128, 2 * 2 * 32 * NB], BF16, sc)
                Sh5 = Sh[:].rearrange("p (g r q b) -> p g r q b", g=2, r=2, b=NB)
                P.op("dve", lambda e, o=Sh5[64:128, 0]: e.memset(o, 0.0), w=["Sh"])
                P.op("dve", lambda e, o=Sh5[0:64, 1]: e.memset(o, 0.0), w=["Sh"])
                for g2 in range(2):
                    rows = slice(64 * g2, 64 * g2 + 64)
                    copy("pool", Sh5[rows, g2, :, :, 0], carry[rows, :].rearrange("p (r q) -> p r q", r=2), ["carry"], ["Sh"])
            c3 = carry[:].rearrange("p (r q) -> p r q", r=2)
            ca3 = CA[:].rearrange("p (r q) -> p r q", r=2)
            ta3 = tA[:].rearrange("p (r q) -> p r q", r=2); tb3_ = tB[:].rearrange("p (r q) -> p r q", r=2)
            for b in range(NB):
                prev = c3 if b == 0 else X4[:, :, :, b - 1]
                tt("pool", ta3, prev, ca3, ALU.mult, ["X", "carry"], ["tA"])
                tt("pool", tb3_[:, 0, :], prev[:, 1, :], CBn[:], ALU.mult, ["X", "carry"], ["tB"])
                tt("pool", tb3_[:, 1, :], prev[:, 0, :], CBp[:], ALU.mult, ["X", "carry"], ["tB"])
                tt("pool", ta3, ta3, tb3_, ALU.add, ["tB"], ["tA"])
                tt("pool", X4[:, :, :, b], X4[:, :, :, b], ta3, ALU.add, ["tA"], ["X"])
            copy("pool", c3, X4[:, :, :, NB - 1], ["X", "Sh"], ["carry"])
            return ubm, U3, (Sh, X4), (U, X)

        def ssm_out(ubm, U3, Sh):
            Sh, X4 = Sh
            Sh5 = Sh[:].rearrange("p (g r q b) -> p g r q b", g=2, r=2, b=NB)
            for g2 in range(2):
                rows = slice(64 * g2, 64 * g2 + 64)
                copy("dve" if g2 else "act", Sh5[rows, g2, :, :, 1:NB], X4[rows, :, :, 0:NB - 1], ["X"], ["Sh"])
            gb4 = ubm[:].rearrange("p (j g h) -> p g j h", g=64, h=16)
            for g0 in range(0, 64, 4):
                if g0 % 32 == 0:
                    tmv, tmk = load_c(s_tm, "s_tm", g0 // 32, 16, 128)
                    wcv, wck = load_c(s_wc, "s_wc", g0 // 32, 16, 128)
                ps, pk = nps()
                fns = []
                for i in range(4):
                    g = g0 + i; q, g2 = g // 2, g % 2
                    o = ps[0:NB, 128 * i:128 * i + 128]
                    fns.append(lambda e, o=o, l=U3[:, g, :], r_=tmv[:, g % 32, :]: e.matmul(o, l, r_, start=True, stop=False))
                    fns.append(lambda e, o=o, l=Sh5[:, g2, 0, q, :], r_=wcv[:, 2 * (q % 16), :]: e.matmul(o, l, r_, start=False, stop=False))
                    fns.append(lambda e, o=o, l=Sh5[:, g2, 1, q, :], r_=wcv[:, 2 * (q % 16) + 1, :]: e.matmul(o, l, r_, start=False, stop=True))
                P.group("pe", fns, r=["U", "Sh", tmk, wck], w=[pk])
                act(gb4[0:NB, g0:g0 + 4, :, :], ps[0:NB, :].rearrange("p (g j h) -> p g j h", j=8, h=16), AF.Gelu, [pk, "U"], ["ubm"])
            for ct in range(8):
                ps, pk = npb()
                P.group("pe", [lambda e, o=ps[:, NB * j:NB * j + NB], s_=ubm[0:NB, 1024 * j + 128 * ct:1024 * j + 128 * ct + 128]:
                               e.transpose(o, s_, identb[0:NB, 0:NB]) for j in range(8)], r=["ubm"], w=[pk])
                copy(evac_eng(), gy3[:, ct, :].rearrange("p (b j) -> p j b", j=8),
                     ps[:, 0:8 * NB].rearrange("p (j b) -> p j b", b=NB), [pk], ["gyT"])

        def proj_fm_unused(col0, ncols_tiles, consume):
            for c2 in range(0, ncols_tiles, 2):
                n = min(2, ncols_tiles - c2)
                wv, wk = load_c(s_in, "s_in", (col0 + 128 * c2) // 256)
                for i in range(n):
                    ps, pk = nps()
                    P.group("pe", [lambda e, o=ps[:, :], l=wv[:, kt, 128 * i:128 * i + 128], r_=xT3[:, kt, :], kt=kt:
                                   e.matmul(o, l, r_, start=(kt == 0), stop=(kt == 15)) for kt in range(16)], r=["xT", wk], w=[pk])
                    consume(c2 + i, ps, pk)

        def kv_proj(tok0, ntok, dst_tile0, kcol0):
            for kh in range(4):
                if kh % 2 == 0:
                    wv, wk = load_c(s_k2, "s_k2", kh // 2, offs=(0, 64, 128, 192))
                ps, pk = nps()
                P.group("pe", [lambda e, o=ps[:, 0:ntok], l=wv[:, kt, 128 * (kh % 2):128 * (kh % 2) + 128], r_=xT3[:, kt, tok0:tok0 + ntok], kt=kt:
                               e.matmul(o, l, r_, start=(kt == 0), stop=(kt == 15)) for kt in range(16)], r=["xT", wk], w=[pk])
                copy(evac_eng(), kT3[:, kh, kcol0:kcol0 + ntok], ps[:, 0:ntok], [pk], ["kT"])
            wv, wk = load_c(s_in, "s_in", 13)
            for t4 in range(ntok // 128):
                ps, pk = nps()
                P.group("pe", [lambda e, o=ps[:, 0:256], l=xT3[:, kt, tok0 + 128 * t4:tok0 + 128 * t4 + 128], r_=wv[:, kt, :], kt=kt:
                               e.matmul(o, l, r_, start=(kt == 0), stop=(kt == 15)) for kt in range(16)], r=["xT", wk], w=[pk])
                copy(evac_eng(), vx4[:, dst_tile0 + t4, :, :], ps[:, 0:256].rearrange("p (h d) -> p h d", d=64), [pk], ["vext"])

        late = {}

        def attn_stage(sp_i, sa):
            ha3 = late["ha3"]; haT = late["haT"]
            if True:
                qT = sb("qT", [128, 2 * SPAN], BF16, sa); qT4 = qT[:].rearrange("p (b h q) -> p b h q", h=2, q=128)
                pT = sb("pT", [128, 4 * 2 * 512], BF16, sa)
                pT5 = pT[:].rearrange("p (b k h q) -> p b k h q", k=2, h=4, q=128)
                lg = [sb(f"lg{i}", [128, 512], F32, sa) for i in range(2)]
                dn = sb("dn", [128, 512], F32, sa); zt2 = sb("zt", [128, 2 * 512], BF16, sa)
                kv_proj(0, SPAN, 1, 128)
                for kh in range(4):
                    wv, wk = load_c(s_in, "s_in", 8 + kh)
                    wza, wzak = load_c(s_in, "s_in", 14 + kh)
                    for hp in range(2):
                        ps, pk = nps()
                        P.group("pe", [lambda e, o=ps[:, :], l=wv[:, kt, 128 * hp:128 * hp + 128], r_=xT3[:, kt, :], kt=kt:
                                       e.matmul(o, l, r_, start=(kt == 0), stop=(kt == 15)) for kt in range(16)], r=["xT", wk], w=[pk])
                        act(qT4[:, :, hp, :], ps[:, :].rearrange("p (b q) -> p b q", q=128), AF.Identity, [pk], ["qT"], scale=0.125)
                    for blk in range(4):
                        for kt_ in range(2):
                            l_ = lg[(2 * blk + kt_) % 2]
                            pp, pks = nps2()
                            for half in range(2):
                                rows = slice(64 * half, 64 * half + 64)
                                P.group("pe", [lambda e, o=pp[:, 512 * half:512 * half + 256],
                                               l=kT3[rows, kh, 128 * (blk + kt_):128 * (blk + kt_) + 128],
                                               r_=qT[rows, 256 * blk:256 * blk + 256]: e.matmul(o, l, r_, start=True, stop=True)],
                                        r=["kT", "qT"], w=[pks[half]])
                            tt("dve", l_[:].rearrange("p (hf hp q) -> p hf hp q", hf=2, q=128),
                               pp.rearrange("p (b hp q) -> p b hp q", b=2, q=128)[:, :, 0:2, :],
                               bT4[:, 4 * kh:4 * kh + 4, kt_, :].rearrange("p (hp hf) q -> p hf hp q", hf=2),
                               ALU.add, [pks[0], pks[1], "biasT"], [l_.name])
                            if sp_i == 0 and blk == 0 and kt_ == 0:
                                P.op("act", lambda e, o=pT5[:, blk, kt_, :, :], i_=l_[:].rearrange("p (h q) -> p h q", q=128):
                                     e.activation(o, i_, AF.Exp, bias=hneg[:, 0:1]), r=[l_.name, "hneg_s"], w=["pT"])
                            else:
                                act(pT5[:, blk, kt_, :, :], l_[:].rearrange("p (h q) -> p h q", q=128), AF.Exp, [l_.name], ["pT"])
                            ci = 2 * blk + kt_
                            hp = ci // 4
                            P.group("pe", [lambda e, o=psbf[hp], l=wza[:, kt, 128 * hp:128 * hp + 128], r_=xT3[:, kt, :], kt=kt:
                                           e.matmul(o, l, r_, start=(kt == 0), stop=(kt == 15)) for kt in range(4 * (ci % 4), 4 * (ci % 4) + 4)],
                                    r=["xT", wzak], w=[f"psb{hp}"])
                            if ci % 4 == 3:
                                act(zt2[:, 512 * hp:512 * hp + 512], psbf[hp], AF.Silu, [f"psb{hp}"], ["zt"])
                    for hp in range(2):
                        hpg = 2 * kh + hp
                        zt = zt2[:, 512 * hp:512 * hp + 512]
                        pv, pvk = nps(); rs, rsk = nps()
                        fns = []
                        for blk in range(4):
                            for half in range(2):
                                hq = 2 * half + hp
                                for kt_ in range(2):
                                    fns.append(lambda e, o=pv[64 * half:64 * half + 64, 128 * blk:128 * blk + 128], l=vx4[:, blk + kt_, kh, :],
                                               r_=pT5[:, blk, kt_, hq, :], kt_=kt_: e.matmul(o, l, r_, start=(kt_ == 0), stop=(kt_ == 1)))
                                for kt_ in range(2):
                                    fns.append(lambda e, o=rs[64 * half:64 * half + 64, 128 * blk:128 * blk + 128], r_=pT5[:, blk, kt_, hq, :], kt_=kt_:
                                               e.matmul(o, ones64[:], r_, start=(kt_ == 0), stop=(kt_ == 1)))
                        P.group("pe", fns, r=["pT", "vext", "ones64"], w=[pvk, rsk])
                        P.op("act", lambda e, r2=rs, h_=hpg: e.activation(dn[:], r2[:, :], AF.Ln, bias=esk[:, h_:h_ + 1]), r=[rsk, "esk"], w=["dn"])
                        P.op("act", lambda e: e.activation(dn[:], dn[:], AF.Exp, scale=-1.0), r=["dn"], w=["dn"])
                        tt("dve", dn[:], dn[:], pv[:, :], ALU.mult, [pvk], ["dn"])
                        tt("dve", ha3[:, hpg, :], dn[:], zt, ALU.mult, ["dn", "zt"], ["haT"])
                copy("dve", kT3[:, :, 0:128], kT3[:, :, 512:640], ["kT"], ["kT"])
                copy("dve", vx4[:, 0, :, :], vx4[:, 4, :, :], ["vext"], ["vext"])

        def merge_half(attn, sm):
            tg = "a" if attn else "s"
            ha3, hs3, mT3 = late["ha3"], late["hs3"], late["mT3"]
            g1 = sb("g1", [128, 512], F32, sm); m1 = late["ftmp"]
            h3, hn = (ha3, "haT") if attn else (hs3, "hsT")
            for d2 in range(8):
                wg3, wgk = load_c(s_in, "s_in", (26 if attn else 18) + d2)
                wb3, wbk = load_c(s_ba if attn else s_bs, "s_ba" if attn else "s_bs", d2, 8)
                for i in range(2):
                    dt = 2 * d2 + i
                    ps, pk = nps()
                    P.group("pe", [lambda e, o=ps[:, :], l=wg3[:, kt, 128 * i:128 * i + 128], r_=xT3[:, kt, :], kt=kt:
                                   e.matmul(o, l, r_, start=(kt == 0), stop=(kt == 15)) for kt in range(16)], r=["xT", wgk], w=[pk])
                    act(g1[:], ps[:, :], AF.Sigmoid, [pk], [g1.name])
                    ps, pk = nps()
                    P.group("pe", [lambda e, o=ps[:, :], l=wb3[:, ct, 128 * i:128 * i + 128], r_=h3[:, ct, :], ct=ct:
                                   e.matmul(o, l, r_, start=(ct == 0), stop=(ct == 7)) for ct in range(8)], r=[hn, wbk], w=[pk])
                    if attn:
                        tt("dve", mT3[:, dt, :], ps[:, :], g1[:], ALU.mult, [pk, g1.name], ["mT"])
                    else:
                        tt("dve", m1[:], ps[:, :], g1[:], ALU.mult, [pk, g1.name], ["ftmp"])
                        tt("dve", mT3[:, dt, :], mT3[:, dt, :], m1[:], ALU.add, ["ftmp"], ["mT"])

        def main_stage(sp_i, sc, ubm, U, X, Sh, pre_tail, defer):
            hs3, mT3, hsT, mT = late["hs3"], late["mT3"], late["hsT"], late["mT"]
            if True:
                sg = sc
                zs = sb("zs", [128, 512], BF16, sg); sgb = sb("sgb", [128, 512], F32, sg); ga_ = late["ftmp"]
                for et in range(8):
                    if et % 2 == 0:
                        wv2, wk2 = load_c(s_in, "s_in", 4 + et // 2)
                    ps, pk = nps()
                    P.group("pe", [lambda e, o=ps[:, :], l=wv2[:, kt, 128 * (et % 2):128 * (et % 2) + 128], r_=xT3[:, kt, :], kt=kt:
                                   e.matmul(o, l, r_, start=(kt == 0), stop=(kt == 15)) for kt in range(16)], r=["xT", wk2], w=[pk])
                    act(zs[:], ps[:, :], AF.Silu, [pk], ["zs"])
                    wg3, wgk = load_c(s_glu, "s_glu", et, 8)
                    pa, pak = nps(); pb, pbk = nps()
                    P.group("pe", [lambda e, o=pa[:, :], l=wg3[:, ct, 0:128], r_=gy3[:, ct, :], ct=ct:
                                   e.matmul(o, l, r_, start=(ct == 0), stop=(ct == 7)) for ct in range(8)], r=["gyT", wgk], w=[pak])
                    P.group("pe", [lambda e, o=pb[:, :], l=wg3[:, ct, 128:256], r_=gy3[:, ct, :], ct=ct:
                                   e.matmul(o, l, r_, start=(ct == 0), stop=(ct == 7)) for ct in range(8)], r=["gyT", wgk], w=[pbk])
                    act(sgb[:], pb[:, :], AF.Sigmoid, [pbk], ["sgb"])
                    tt("dve", ga_[:], pa[:, :], sgb[:], ALU.mult, [pak, "sgb"], ["ftmp"])
                    tt("dve", hs3[:, et, :], ga_[:], zs[:], ALU.mult, ["ftmp", "zs"], ["hsT"])
            merge_half(False, sc)
            if True:
                so = sc
                rrA = ubm[:].bitcast(F32).rearrange("p (t c) -> p t c", c=D)
                rrB = Sh[:].bitcast(F32).rearrange("p (t c) -> p t c", c=D)
                rts = [(rrA[:, 0, :], "ubm"), (rrA[:, 1, :], "ubm"), (rrB[:, 0, :], "Sh"), (rrB[:, 1, :], "Sh")]
                sq = U[:].bitcast(F32)
                gch = X[:, 0:D]; bch = X[:, D:2 * D]
                st = sb("lnst", [128, 8], F32, so)
                for t4 in range(4):
                    rt, rk = rts[t4]
                    P.dma("pool", rt, xo[SPAN * sp_i + 128 * t4:SPAN * sp_i + 128 * t4 + 128, :], w=[rk])
                P.dma("pool", gch, lng_d, w=["X"]); P.dma("pool", bch, lnb_d, w=["X"])

                def layer_norm(t4):
                    rt, rk = rts[t4]
                    P.op("dve", lambda e: e.reduce_sum(st[:, 0:1], rt, mybir.AxisListType.X), r=[rk], w=["lnst"])
                    tt("dve", sq, rt, rt, ALU.mult, [rk], ["U"])
                    P.op("dve", lambda e: e.reduce_sum(st[:, 1:2], sq, mybir.AxisListType.X), r=["U"], w=["lnst"])
                    ts("dve", st[:, 2:3], st[:, 0:1], 1.0 / D, None, ALU.mult, None, ["lnst"], ["lnst"])
                    tt("dve", st[:, 3:4], st[:, 2:3], st[:, 2:3], ALU.mult, ["lnst"], ["lnst"])
                    ts("dve", st[:, 4:5], st[:, 1:2], 1.0 / D, float(LN_EPS), ALU.mult, ALU.add, ["lnst"], ["lnst"])
                    tt("dve", st[:, 4:5], st[:, 4:5], st[:, 3:4], ALU.subtract, ["lnst"], ["lnst"])
                    P.op("act", lambda e: e.activation(st[:, 5:6], st[:, 4:5], AF.Sqrt), r=["lnst"], w=["lnst"])
                    P.op("dve", lambda e: e.reciprocal(st[:, 5:6], st[:, 5:6]), r=["lnst"], w=["lnst"])
                    P.op("dve", lambda e: e.scalar_tensor_tensor(st[:, 6:7], st[:, 2:3], -1.0, st[:, 5:6], ALU.mult, ALU.mult),
                         r=["lnst"], w=["lnst"])
                    P.op("act", lambda e: e.activation(rt, rt, AF.Identity, bias=st[:, 6:7], scale=st[:, 5:6]), r=["lnst"], w=[rk])
                    tt("dve", rt, rt, gch, ALU.mult, ["X"], [rk])
                    tt("dve", rt, rt, bch, ALU.add, ["X"], [rk])
                    P.dma("pool", y_out[SPAN * sp_i + 128 * t4:SPAN * sp_i + 128 * t4 + 128, :], rt, r=[rk])

                for th in range(2):
                    for cb in range(8):
                        wv, wk = load_c(s_out, "s_out", cb)
                        for t2 in range(2):
                            t4 = 2 * th + t2
                            rt, rk = rts[t4]
                            ps, pk = nps()
                            P.group("pe", [lambda e, o=ps[:, 0:256], l=mT3[:, dt, 128 * t4:128 * t4 + 128], r_=wv[:, dt, :], dt=dt:
                                           e.matmul(o, l, r_, start=(dt == 0), stop=(dt == 15)) for dt in range(16)], r=["mT", wk], w=[pk])
                            P.op("dve", lambda e, o=rt[:, 256 * cb:256 * cb + 256], p_=ps[:, 0:256]:
                                 e.scalar_tensor_tensor(o, o, float(ALPHA), p_, ALU.mult, ALU.add), r=[pk], w=[rk])
                        if th == 1 and cb == 1:
                            layer_norm(0)
                        if th == 1 and cb == 4:
                            layer_norm(1)
                pre_tail()
                if defer:
                    late["pending"] = [lambda: layer_norm(2), lambda: layer_norm(3)]
                else:
                    layer_norm(2)
                    layer_norm(3)

        n_pre = dbg.get("n_pre", NSPAN)
        n_own = dbg.get("n_own", NSPAN)
        spre = ExitStack()
        PWr = sb("PWr", [128, 32 * NB], F32, spre); PWi = sb("PWi", [128, 32 * NB], F32, spre)
        CA64 = sb("CA64", [128, 64], F32, spre); CB64n = sb("CB64n", [128, 32], F32, spre); CB64p = sb("CB64p", [128, 32], F32, spre)
        with ExitStack() as sw:
            NM = 65
            msc = sb("msc_s", [128, NM], F32, sw)
            ANGx = sb("ANGx", [128, 2 * NM * 32], F32, sw); TMPx = sb("TMPx", [128, 2 * NM * 32], F32, sw)
            KIx = sb("KIx", [128, 2 * NM * 32], I32, sw); MARx = sb("MARx", [128, NM * 32], F32, sw)
            P.dma("sp", msc[:], msc_d, w=["msc_s"])
            an4 = ANGx[:].rearrange("p (s m q) -> p s m q", s=2, q=32)
            mb = msc[:].unsqueeze(2).to_broadcast([128, NM, 32])
            tt("dve", an4[:, 0], mb, ai[:].unsqueeze(1).to_broadcast([128, NM, 32]), ALU.mult, ["msc_s", "ai"], ["ANGx"])
            ts("dve", an4[:, 1], an4[:, 0], PI / 2, None, ALU.add, None, ["ANGx"], ["ANGx"])
            sin_reduced(ANGx, TMPx, KIx)
            ma3 = MARx[:].rearrange("p (m q) -> p m q", q=32)
            tt("dve", ma3, mb, ar[:].unsqueeze(1).to_broadcast([128, NM, 32]), ALU.mult, ["msc_s", "ar"], ["MARx"])
            act(MARx[:], MARx[:], AF.Exp, ["MARx"], ["MARx"])
            tt("dve", PWr[:].rearrange("p (q m) -> p m q", m=NB), ma3[:, 0:NB, :], an4[:, 1, 0:NB, :], ALU.mult, ["MARx", "ANGx"], ["PWr"])
            tt("dve", PWi[:].rearrange("p (q m) -> p m q", m=NB), ma3[:, 0:NB, :], an4[:, 0, 0:NB, :], ALU.mult, ["MARx", "ANGx"], ["PWi"])
            tt("dve", CA64[:, 0:32], ma3[:, NB, :], an4[:, 1, NB, :], ALU.mult, ["MARx", "ANGx"], ["CA64"])
            copy("dve", CA64[:, 32:64], CA64[:, 0:32], ["CA64"], ["CA64"])
            tt("dve", CB64p[:], ma3[:, NB, :], an4[:, 0, NB, :], ALU.mult, ["MARx", "ANGx"], ["CB64p"])
            ts("dve", CB64n[:], CB64p[:], -1.0, None, ALU.mult, None, ["CB64p"], ["CB64n"])
            P.barrier()
        pfx = dict(PWr=PWr, PWi=PWi, CA64=CA64, CB64n=CB64n, CB64p=CB64p)
        xb4 = xb + [(t_[:, :], t_.name) for t_ in (sb(f"xb{i}", [128, D], BF16, pre_scope) for i in (2, 3))]
        n_cast = -(-len(cast_jobs) // max(1, n_pre))
        for sp_i in range(NSPAN - n_pre, NSPAN):
            sc = pre_scope
            build_xT(xp, SPAN * sp_i, xb4, ev="act")
            ssm_state(sc, own=False, pfx=pfx)
            if sp_i + 1 < NSPAN:
                prefetch_x(xp, SPAN * (sp_i + 1), xb4, 0, 4)
            elif n_own:
                prefetch_x(xo, 0, xb, 0, 2)
            issue_casts(n_cast)
            if sp_i == NSPAN - 1:
                kv_proj(SPAN - 128, 128, 0, 0)
        issue_casts(len(cast_jobs))
        P.barrier()
        pre_scope.close()
        memo.clear()
        spre.close()
        mT = sb("mT", [128, 16 * SPAN], BF16); haT = sb("haT", [128, 8 * SPAN], BF16); hsT = sb("hsT", [128, 8 * SPAN], BF16)
        late.update(mT=mT, haT=haT, hsT=hsT, mT3=mT[:].rearrange("p (c t) -> p c t", t=SPAN),
                    ha3=haT[:].rearrange("p (c t) -> p c t", t=SPAN), hs3=hsT[:].rearrange("p (c t) -> p c t", t=SPAN))
        xbuilt = {}
        for sp_i in range(n_own):
            if True:
                sc = own_scope
                if not xbuilt.get(sp_i):
                    build_xT(xo, SPAN * sp_i, xb)
                late["ftmp"] = sb("ftmp", [128, 512], F32, sc)
                ubm, U3, Sh, (U_, X_) = ssm_state(sc, own=True)
                if sp_i + 1 < n_own:
                    prefetch_x(xo, SPAN * (sp_i + 1), xb, 0, 2)
                attn_stage(sp_i, sc)
                pT_ = memo["pT"]
                xb_own = xb + [(pT_[:, 0:D], "pT"), (pT_[:, D:2 * D], "pT")]
                if sp_i + 1 < n_own:
                    prefetch_x(xo, SPAN * (sp_i + 1), xb_own, 2, 4)
                merge_half(True, sc)
                ssm_out(ubm, U3, Sh)

                def pre_tail(n=sp_i):
                    if n + 1 < n_own:
                        build_xT(xo, SPAN * (n + 1), xb_own)
                        xbuilt[n + 1] = True
                main_stage(sp_i, sc, ubm, U_, X_, Sh[0], pre_tail, sp_i + 1 < n_own)

        P.barrier()
        own_scope.close()
        with nc.Block() as block:
            @block.tensor
            def _(e):
                P.emit("pe", e)

            @block.scalar
            def _(e):
                P.emit("act", e)

            @block.vector
            def _(e):
                P.emit("dve", e)

            @block.gpsimd
            def _(e):
                P.emit("pool", e)

            @block.sync
            def _(e):
                P.emit("sp", e)
    return nc


def _t5_bucket(dist):
    max_exact = 16
    d = np.maximum(dist, 1).astype(np.float32)
    large = max_exact + (np.log(d / np.float32(max_exact)) / np.float32(math.log(128 / max_exact)) * np.float32(16)).astype(np.int32)
    large = np.minimum(large, 31)
    return np.where(dist < max_exact, dist, large)


def kernel(x, w_in, ssm_lambda_re, ssm_lambda_im, ssm_b_re, ssm_b_im, ssm_c_re, ssm_c_im, ssm_d, ssm_log_step,
           w_glu, attn_sinks, rel_bias_table, w_branch_ssm, w_branch_attn, w_out, ln_gain, ln_bias):
    f = lambda a: np.ascontiguousarray(np.asarray(a, dtype=np.float32))
    x = f(x)
    pl = lambda a: f(a.reshape(32, 2, 64).transpose(1, 2, 0).reshape(128, 32))
    plb = lambda a: f(a.reshape(32, 2, 64, 16).transpose(1, 2, 0, 3).reshape(128, 512))
    plc = lambda a: f(a.reshape(32, 2, 16, 64).transpose(1, 3, 0, 2).reshape(128, 512))
    ls = np.asarray(ssm_log_step[0], np.float32).reshape(32, 2)
    lstep = f(np.broadcast_to(ls.T[:, None, :], (2, 64, 32)).reshape(128, 32))
    dI = np.zeros((16, 64, 16), np.float32)
    dd = np.asarray(ssm_d[0], np.float32).reshape(64, 16)
    for h in range(16):
        dI[h, :, h] = dd[:, h]
    sk = np.zeros((128, 8), np.float32)
    sinks = np.asarray(attn_sinks[0], np.float32)
    for hp in range(8):
        sk[0:64, hp] = sinks[2 * hp]
        sk[64:128, hp] = sinks[2 * hp + 1]
    tab = np.asarray(rel_bias_table, np.float32)
    line = np.full((16, 384), NEG, np.float32)
    bk = _t5_bucket(np.arange(128))
    line[:, 128:256] = tab[bk, :].T
    common = {
        "w_in": f(w_in[0]), "w_glu": f(w_glu[0]), "w_bs": f(w_branch_ssm[0]), "w_ba": f(w_branch_attn[0]), "w_out": f(w_out[0]),
        "lamr": pl(np.asarray(ssm_lambda_re[0])), "lami": pl(np.asarray(ssm_lambda_im[0])), "lstep": lstep,
        "br": plb(np.asarray(ssm_b_re[0])), "bi": plb(np.asarray(ssm_b_im[0])),
        "cr": plc(np.asarray(ssm_c_re[0])), "ci": plc(np.asarray(ssm_c_im[0])),
        "dI": f(dI.reshape(16, 1024)), "sk": sk, "line": line,
        "lng": f(np.broadcast_to(np.asarray(ln_gain[0], np.float32)[None, :], (128, D))),
        "lnb": f(np.broadcast_to(np.asarray(ln_bias[0], np.float32)[None, :], (128, D))),
        "idf": np.eye(128, dtype=np.float32), "anti": f(np.eye(128, dtype=np.float32)[::-1]),
        "msc": f(np.broadcast_to(np.array([8.0 * (NB - 1 - b) for b in range(NB)] + [8.0 * NB], np.float32)[None, :], (128, 65))),
    }
    in_maps = []
    for c in range(NCORES):
        b, h = c // 2, c % 2
        m = dict(common)
        m["xo"] = f(x[b, h * TOK:(h + 1) * TOK])
        m["xp"] = f(x[b, 0:TOK]) if h == 1 else np.zeros((TOK, D), np.float32)
        m["hneg"] = np.full((128, 1), 0.0 if h == 1 else NEG, np.float32)
        in_maps.append(m)
    nc = build_program()
    res = run_bass_kernel_spmd(nc, in_maps, core_ids=list(range(NCORES)))
    out = np.zeros((4, 8192, D), np.float32)
    for c in range(NCORES):
        b, h = c // 2, c % 2
        out[b, h * TOK:(h + 1) * TOK] = np.asarray(res.results[c]["y"], np.float32)
    return out
```

```python
import math
import numpy as np
from contextlib import ExitStack
import concourse.bass as bass
import concourse.mybir as mybir
from concourse.bass_utils import run_bass_kernel_spmd

F32 = mybir.dt.float32
BF16 = mybir.dt.bfloat16
I32 = mybir.dt.int32
ALU = mybir.AluOpType
AF = mybir.ActivationFunctionType
PI = float(np.pi)
TWO_PI = float(2 * np.pi)

NCORES = 8
TOK = 4096
SPAN = 512
NB = SPAN // 8
NSPAN = TOK // SPAN
D = 2048
DIN = 8704
ALPHA = 2.0 ** 0.25
LN_EPS = 1e-5
NEG = -30000.0
C_U, C_ZS, C_Q, C_K, C_V, C_ZA, C_G = 0, 1024, 2048, 3072, 3328, 3584, 4608

ENGS = ["pe", "act", "dve", "pool", "sp"]
NDS = 12
NBG = 6


class Prog:
    def __init__(self, nc, st):
        self.nc = nc
        self.q = {e: [] for e in ENGS}
        self.sem = {e: st.enter_context(nc.semaphore("c_" + e)) for e in ENGS if e != "sp"}
        self.cnt = {e: 0 for e in ENGS}
        self.dsem = [st.enter_context(nc.semaphore(f"dq{i}")) for i in range(NDS + NBG)]
        self.dcnt = [0] * (NDS + NBG)
        self.dnext = 0
        self.bnext = 0
        self.seen = {e: {} for e in ENGS}
        self.lastw = {}
        self.readers = {}

    def _need(self, eng, tok, waits, raw=False):
        kind, src, val = tok
        if kind == "e" and src == eng and (eng == "pe" or not raw):
            return
        key = (kind, src)
        if self.seen[eng].get(key, 0) >= val:
            return
        self.seen[eng][key] = val
        waits.append(tok)

    def _deps(self, eng, r, w):
        waits = []
        for b in r:
            t = self.lastw.get(b)
            if t:
                self._need(eng, t, waits, raw=True)
        for b in w:
            t = self.lastw.get(b)
            if t:
                self._need(eng, t, waits)
            for t in self.readers.get(b, ()):
                self._need(eng, t, waits)
        return waits

    def _commit(self, tok, r, w):
        for b in r:
            self.readers.setdefault(b, []).append(tok)
        for b in w:
            self.lastw[b] = tok
            self.readers[b] = []

    def op(self, eng, fn, r=(), w=()):
        waits = self._deps(eng, r, w)
        self.cnt[eng] += 1
        tok = ("e", eng, self.cnt[eng])
        self.q[eng].append((waits, fn, tok))
        self._commit(tok, r, w)

    def group(self, eng, fns, r=(), w=()):
        waits = self._deps(eng, r, w)
        self.cnt[eng] += 1
        tok = ("e", eng, self.cnt[eng])
        for i, fn in enumerate(fns):
            self.q[eng].append((waits if i == 0 else [], fn, tok if i == len(fns) - 1 else None))
        self._commit(tok, r, w)

    def dma(self, eng, out, in_, r=(), w=(), bg=False):
        waits = self._deps(eng, r, w)
        if bg:
            s = NDS + self.bnext
            self.bnext = (self.bnext + 1) % NBG
        else:
            s = self.dnext
            self.dnext = (self.dnext + 1) % NDS
        if self.dcnt[s]:
            self._need(eng, ("d", s, self.dcnt[s]), waits)
        self.dcnt[s] += 16
        tok = ("d", s, self.dcnt[s])
        self.q[eng].append((waits, lambda e, o=out, i=in_: e.dma_start(out=o, in_=i), tok))
        self._commit(tok, r, w)

    def barrier(self):
        for e in ENGS:
            waits = []
            for f in ENGS:
                if f != "sp" and self.cnt[f]:
                    self._need(e, ("e", f, self.cnt[f]), waits)
            for s in range(NDS):
                if self.dcnt[s]:
                    self._need(e, ("d", s, self.dcnt[s]), waits)
            if waits:
                self.q[e].append((waits, None, None))
        keep = {k: v for k, v in self.lastw.items() if isinstance(k, tuple) and v[0] == "d" and v[1] >= NDS}
        self.lastw.clear()
        self.lastw.update(keep)
        self.readers.clear()

    def emit(self, eng, e):
        for waits, fn, tok in self.q[eng]:
            for kind, src, val in waits:
                e.wait_ge(self.sem[src] if kind == "e" else self.dsem[src], val)
            if fn is None:
                continue
            ins = fn(e)
            if tok is not None:
                if tok[0] == "e":
                    ins.then_inc(self.sem[eng], 1)
                else:
                    ins.then_inc(self.dsem[tok[1]], 16)


def build_program(debug=None):
    dbg = debug or {}
    nc = bass.Bass("TRN2", target_bir_lowering=False)
    dr = lambda n, s, k="ExternalInput", d=F32: nc.dram_tensor(n, list(s), d, kind=k).ap()
    xo = dr("xo", [TOK, D])
    xp = dr("xp", [TOK, D])
    w_in = dr("w_in", [D, DIN])
    w_glu = dr("w_glu", [1024, 2048])
    w_bs = dr("w_bs", [1024, 2048])
    w_ba = dr("w_ba", [1024, 2048])
    w_out = dr("w_out", [D, D])
    lamr_d = dr("lamr", [128, 32]); lami_d = dr("lami", [128, 32]); lstep_d = dr("lstep", [128, 32])
    br_d = dr("br", [128, 512]); bi_d = dr("bi", [128, 512]); cr_d = dr("cr", [128, 512]); ci_d = dr("ci", [128, 512])
    dI_d = dr("dI", [16, 1024])
    sk_d = dr("sk", [128, 8])
    line_d = dr("line", [16, 384])
    hneg_d = dr("hneg", [128, 1])
    lng_d = dr("lng", [128, D]); lnb_d = dr("lnb", [128, D])
    idf_d = dr("idf", [128, 128]); anti_d = dr("anti", [128, 128])
    msc_d = dr("msc", [128, 65])
    y_out = dr("y", [TOK, D], k="ExternalOutput")
    s_in = dr("s_in", [34, 128, 4096], k="Internal", d=BF16)
    s_glu = dr("s_glu", [8, 128, 2048], k="Internal", d=BF16)
    s_bs = dr("s_bs", [8, 128, 2048], k="Internal", d=BF16)
    s_ba = dr("s_ba", [8, 128, 2048], k="Internal", d=BF16)
    s_out = dr("s_out", [8, 128, 4096], k="Internal", d=BF16)
    s_k2 = dr("s_k2", [2, 128, 4096], k="Internal", d=BF16)
    s_w1 = dr("s_w1", [2, 128, 4096], k="Internal", d=BF16)
    s_wc = dr("s_wc", [2, 128, 4096], k="Internal", d=BF16)
    s_tm = dr("s_tm", [2, 128, 4096], k="Internal", d=BF16)

    with ExitStack() as st:
        P = Prog(nc, st)
        used = {}
        dumped = {}

        def dump(name, t, key, parts=128):
            if name not in dbg.get("dumps", ()) or name in dumped:
                return
            shape = [parts, t.shape[-1]]
            dst = nc.dram_tensor("dbg_" + name, shape, t.dtype, kind="ExternalOutput").ap()
            P.dma("sp", dst, t[0:parts, :], r=[key])
            dumped[name] = True

        memo = {}
        own_scope = ExitStack()
        pre_scope = ExitStack()

        def sb(n, s, d=F32, c=st):
            if c is own_scope or c is pre_scope:
                if n in memo:
                    return memo[n]
            k = used.get(n, 0)
            used[n] = k + 1
            t = c.enter_context(nc.sbuf_tensor(n if k == 0 else f"{n}_{k}", list(s), d))
            if c is own_scope or c is pre_scope:
                memo[n] = t
            return t
        identb = sb("identb", [128, 128], BF16)
        identf = sb("identf", [128, 128])
        antif = sb("antif", [128, 128])
        ones64 = sb("ones64", [128, 64], BF16)
        biasT = sb("biasT", [128, 16 * 2 * 128])
        CA = sb("CA", [128, 64])
        CBn = sb("CBn", [128, 32]); CBp = sb("CBp", [128, 32])
        carry = sb("carry", [128, 64])
        esk = sb("esk", [128, 8])
        ar = sb("ar", [128, 32]); ai = sb("ai", [128, 32])
        hneg = sb("hneg_s", [128, 1])
        wstate = {"i": 0}
        wbuf = []
        psbig = st.enter_context(nc.psum_tensor("psbig", [128, 6 * 512], F32))
        psf = [psbig[:, 512 * i:512 * i + 512] for i in range(6)]
        psb = [st.enter_context(nc.psum_tensor(f"psb{i}", [128, 1024], BF16)) for i in range(2)]
        pstate = {"f": 0, "b": 0}

        def nps():
            i = pstate["f"]; pstate["f"] = (i + 1) % 6
            return psf[i], f"psf{i}"

        def nps2():
            i = pstate["f"]
            if i % 2:
                i = (i + 1) % 6
            pstate["f"] = (i + 2) % 6
            return psbig[:, 512 * i:512 * i + 1024], (f"psf{i}", f"psf{i + 1}")

        def npb():
            i = pstate["b"]; pstate["b"] = (i + 1) % 2
            return psb[i], f"psb{i}"

        def nwb():
            i = wstate["i"]; wstate["i"] = (i + 1) % 4
            return wbuf[i], f"wbuf{i}"

        evs = {"i": 0}

        def evac_eng():
            evs["i"] ^= 1
            return "dve" if evs["i"] else "act"

        def copy(eng, out, in_, r, w):
            if eng == "act":
                P.op("act", lambda e, o=out, i=in_: e.copy(o, i), r=r, w=w)
            else:
                P.op(eng, lambda e, o=out, i=in_: e.tensor_copy(o, i), r=r, w=w)

        def tt(eng, out, a, b, op, r, w):
            P.op(eng, lambda e, o=out, x=a, y=b, p=op: e.tensor_tensor(o, x, y, p), r=r, w=w)

        def ts(eng, out, a, s1, s2, op0, op1, r, w):
            if op1 is None:
                P.op(eng, lambda e, o=out, x=a: e.tensor_scalar(o, x, s1, None, op0), r=r, w=w)
            else:
                P.op(eng, lambda e, o=out, x=a: e.tensor_scalar(o, x, s1, s2, op0, op1), r=r, w=w)

        def taylor_exp(q, x, deg, keyq, keyx):
            ts("dve", q, x, 1.0 / deg, 1.0, ALU.mult, ALU.add, [keyx], [keyq])
            for k in range(deg - 1, 0, -1):
                tt("dve", q, q, x, ALU.mult, [keyq, keyx], [keyq])
                ts("dve", q, q, 1.0 / k, 1.0, ALU.mult, ALU.add, [keyq], [keyq])

        C1 = 6.28125
        C2 = TWO_PI - C1

        def sin_reduced(A, T_, K_):
            a, t_, k_ = A.name, T_.name, K_.name
            ts("dve", T_[:], A[:], 1.0 / TWO_PI, None, ALU.mult, None, [a], [t_])
            copy("dve", K_[:], T_[:], [t_], [k_])
            copy("dve", T_[:], K_[:], [k_], [t_])
            P.op("dve", lambda e: e.scalar_tensor_tensor(A[:], T_[:], -C1, A[:], ALU.mult, ALU.add), r=[t_, a], w=[a])
            P.op("dve", lambda e: e.scalar_tensor_tensor(A[:], T_[:], -C2, A[:], ALU.mult, ALU.add), r=[t_, a], w=[a])
            ts("dve", T_[:], A[:], PI, -TWO_PI, ALU.is_gt, ALU.mult, [a], [t_])
            tt("dve", A[:], A[:], T_[:], ALU.add, [t_, a], [a])
            ts("dve", T_[:], A[:], -PI, TWO_PI, ALU.is_lt, ALU.mult, [a], [t_])
            tt("dve", A[:], A[:], T_[:], ALU.add, [t_, a], [a])
            ts("dve", A[:], A[:], PI, -PI, ALU.min, ALU.max, [a], [a])
            act(A[:], A[:], AF.Sin, [a], [a])

        def act(out, in_, func, r, w, scale=1.0):
            P.op("act", lambda e, o=out, i=in_, f=func, s=scale: e.activation(o, i, f, scale=s), r=r, w=w)

        def cast_w(scr, sname, idx, src, c0, ncols, col_off=0):
            nk = src.shape[0] // 128
            dst = scr[idx].rearrange("p (k c) -> p k c", c=256)[:, 0:nk, col_off:col_off + ncols]
            P.dma("pool", dst, src.rearrange("(k p) c -> p k c", p=128)[:, :, c0:c0 + ncols],
                  w=[(sname, idx, col_off)], bg=True)

        cast_jobs = []

        def cast_all(first):
            if first:
                for i in list(range(4)) + [13]:
                    cast_w(s_in, "s_in", i, w_in, 256 * i, 256)
                for kh in range(4):
                    for dup in range(2):
                        cast_w(s_k2, "s_k2", kh // 2, w_in, C_K + 64 * kh, 64, 128 * (kh % 2) + 64 * dup)
                return
            J = cast_jobs.append
            for i in list(range(8, 12)) + list(range(14, 18)):
                J((s_in, "s_in", i, w_in, 256 * i, 256, 0))
            for i in range(8):
                J((s_in, "s_in", 26 + i, w_in, 256 * (26 + i), 256, 0))
                J((s_ba, "s_ba", i, w_ba, 256 * i, 256, 0))
            for i in range(4, 8):
                J((s_in, "s_in", i, w_in, 256 * i, 256, 0))
            for i in range(8):
                J((s_glu, "s_glu", i, w_glu, 128 * i, 128, 0)); J((s_glu, "s_glu", i, w_glu, 1024 + 128 * i, 128, 128))
            for i in range(8):
                J((s_in, "s_in", 18 + i, w_in, 256 * (18 + i), 256, 0))
                J((s_bs, "s_bs", i, w_bs, 256 * i, 256, 0))
            for i in range(8):
                J((s_out, "s_out", i, w_out, 256 * i, 256, 0))

        def issue_casts(n):
            for _ in range(min(n, len(cast_jobs))):
                cast_w(*cast_jobs.pop(0))

        def load_c(scr, sname, idx, nk=16, cw=256, offs=(0, 128)):
            dst, key = nwb()
            P.dma("sp", dst[:, 0:nk * 256], scr[idx][:, 0:nk * 256], r=[(sname, idx, o_) for o_ in offs], w=[key])
            return dst[:].rearrange("p (k c) -> p k c", c=cw), key

        with ExitStack() as s0:
            sb0 = lambda n, s, d=F32: sb(n, s, d, s0)
            lamr = sb0("lamr_s", [128, 32]); lami = sb0("lami_s", [128, 32]); stp = sb0("stp", [128, 32])
            W1re = sb0("W1re", [128, 32 * 128], BF16)
            W1im = sb0("W1im", [128, 32 * 128], BF16)
            Wc = sb0("Wc", [128, 32 * 2 * 128], BF16)
            Tm = sb0("Tm", [128, 64 * 128], BF16)
            brs = sb0("brs", [128, 512]); bis = sb0("bis", [128, 512]); crs = sb0("crs", [128, 512]); cis = sb0("cis", [128, 512])
            dIs = sb0("dIs", [16, 1024]); sks = sb0("sks", [128, 8])
            for t_, d_ in ((lamr, lamr_d), (lami, lami_d), (stp, lstep_d), (brs, br_d), (bis, bi_d), (crs, cr_d),
                           (cis, ci_d), (dIs, dI_d), (sks, sk_d), (hneg, hneg_d), (identf, idf_d), (antif, anti_d)):
                P.dma("sp", t_[:], d_, w=[t_.name])
            copy("dve", identb[:], identf[:], [identf.name], [identb.name])
            VH = [sb0(f"VH{i}", [128, 256]) for i in range(4)]

            def bias_head(h):
                vh = VH[h % 4]
                for kt_ in range(2):
                    P.dma("sp", vh[:, 128 * kt_:128 * kt_ + 128],
                          bass.AP(line_d.tensor, 384 * h + 129 - 128 * kt_, [[1, 128], [1, 128]]), w=[vh.name])
                ps, pk = nps()
                P.group("pe", [lambda e, o=ps[:, 128 * k_:128 * k_ + 128], r_=vh[:, 128 * k_:128 * k_ + 128]:
                               e.matmul(o, antif[:], r_, start=True, stop=True) for k_ in range(2)], r=[vh.name, antif.name], w=[pk])
                copy("act", biasT[:, 256 * h:256 * h + 256], ps[:, 0:256], [pk], ["biasT"])
            P.op("dve", lambda e: e.memset(ones64[:], 1.0), w=["ones64"])
            P.op("dve", lambda e: e.memset(carry[:], 0.0), w=["carry"])
            act(esk[:], sks[:], AF.Exp, [sks.name], ["esk"])
            ts("dve", ar[:], stp[:], 0.125, None, ALU.mult, None, [stp.name], ["ar"])
            taylor_exp(stp[:], ar[:], 10, stp.name, "ar")
            for _ in range(3):
                tt("dve", stp[:], stp[:], stp[:], ALU.mult, [stp.name], [stp.name])
            tt("dve", ar[:], lamr[:], stp[:], ALU.mult, [lamr.name, stp.name], ["ar"])
            tt("dve", ai[:], lami[:], stp[:], ALU.mult, [lami.name, stp.name], ["ai"])
            MAR = sb0("MAR", [128, 9 * 32]); ANG = sb0("ANG", [128, 2 * 9 * 32]); TMPA = sb0("TMPA", [128, 576])
            KI = sb0("KI", [128, 576], I32)
            for m in range(9):
                ts("dve", MAR[:, 32 * m:32 * m + 32], ar[:], float(m), None, ALU.mult, None, ["ar"], ["MAR"])
                ts("dve", ANG[:, 32 * m:32 * m + 32], ai[:], float(m), None, ALU.mult, None, ["ai"], ["ANG"])
                ts("dve", ANG[:, 288 + 32 * m:288 + 32 * m + 32], ai[:], float(m), PI / 2, ALU.mult, ALU.add, ["ai"], ["ANG"])
            sin_reduced(ANG, TMPA, KI)
            MAG = sb0("MAG", [128, 9 * 32])
            taylor_exp(MAG[:], MAR[:], 8, "MAG", "MAR")
            PR = sb0("PR", [128, 288]); PIm = sb0("PIm", [128, 288])
            tt("dve", PR[:], MAG[:], ANG[:, 288:576], ALU.mult, ["MAG", "ANG"], ["PR"])
            tt("dve", PIm[:], MAG[:], ANG[:, 0:288], ALU.mult, ["MAG", "ANG"], ["PIm"])
            copy("dve", CA[:, 0:32], PR[:, 256:288], ["PR"], ["CA"])
            copy("dve", CA[:, 32:64], PR[:, 256:288], ["PR"], ["CA"])
            copy("dve", CBp[:], PIm[:, 256:288], ["PIm"], ["CBp"])
            ts("dve", CBn[:], PIm[:, 256:288], -1.0, None, ALU.mult, None, ["PIm"], ["CBn"])
            nr = sb0("nr", [128, 32]); den = sb0("den", [128, 32]); t1 = sb0("t1", [128, 32]); t2 = sb0("t2", [128, 32])
            kr = sb0("kr", [128, 32]); ki_ = sb0("ki_", [128, 32])
            ts("dve", nr[:], PR[:, 32:64], -1.0, None, ALU.add, None, ["PR"], ["nr"])
            tt("dve", den[:], lamr[:], lamr[:], ALU.mult, [lamr.name], ["den"])
            tt("dve", t1[:], lami[:], lami[:], ALU.mult, [lami.name], ["t1"])
            tt("dve", den[:], den[:], t1[:], ALU.add, ["t1"], ["den"])
            P.op("dve", lambda e: e.reciprocal(den[:], den[:]), r=["den"], w=["den"])
            tt("dve", t1[:], nr[:], lamr[:], ALU.mult, ["nr"], ["t1"])
            tt("dve", t2[:], PIm[:, 32:64], lami[:], ALU.mult, ["PIm"], ["t2"])
            tt("dve", t1[:], t1[:], t2[:], ALU.add, ["t2"], ["t1"])
            tt("dve", kr[:], t1[:], den[:], ALU.mult, ["t1", "den"], ["kr"])
            tt("dve", t1[:], PIm[:, 32:64], lamr[:], ALU.mult, ["PIm"], ["t1"])
            tt("dve", t2[:], nr[:], lami[:], ALU.mult, ["nr"], ["t2"])
            tt("dve", t1[:], t1[:], t2[:], ALU.subtract, ["t2"], ["t1"])
            tt("dve", ki_[:], t1[:], den[:], ALU.mult, ["t1", "den"], ["ki_"])
            OMR = sb0("OMR", [128, 256]); OMI = sb0("OMI", [128, 256]); T8 = sb0("T8", [128, 256])
            pr3 = PR[:, 0:256].rearrange("p (m q) -> p m q", q=32); pi3 = PIm[:, 0:256].rearrange("p (m q) -> p m q", q=32)
            krb = kr[:].unsqueeze(1).to_broadcast([128, 8, 32]); kib = ki_[:].unsqueeze(1).to_broadcast([128, 8, 32])
            omr3 = OMR[:].rearrange("p (m q) -> p m q", q=32); omi3 = OMI[:].rearrange("p (m q) -> p m q", q=32)
            t83 = T8[:].rearrange("p (m q) -> p m q", q=32)
            tt("dve", omr3, pr3, krb, ALU.mult, ["PR", "kr"], ["OMR"])
            tt("dve", t83, pi3, kib, ALU.mult, ["PIm", "ki_"], ["T8"])
            tt("dve", OMR[:], OMR[:], T8[:], ALU.subtract, ["T8"], ["OMR"])
            tt("dve", omi3, pi3, krb, ALU.mult, ["PIm", "kr"], ["OMI"])
            tt("dve", t83, pr3, kib, ALU.mult, ["PR", "ki_"], ["T8"])
            tt("dve", OMI[:], OMI[:], T8[:], ALU.add, ["T8"], ["OMI"])
            TB = sb0("TB", [128, 512]); A0r = sb0("A0r", [128, 512]); A0i = sb0("A0i", [128, 512])
            sA = ExitStack()
            AR = sb("AR", [128, 4096], F32, sA); AI = sb("AI", [128, 4096], F32, sA)
            ar4 = AR[:].rearrange("p (q j h) -> p q j h", j=8, h=16); ai4 = AI[:].rearrange("p (q j h) -> p q j h", j=8, h=16)
            br3 = brs[:].rearrange("p (q h) -> p q h", h=16); bi3 = bis[:].rearrange("p (q h) -> p q h", h=16)
            tb3 = TB[:].rearrange("p (q h) -> p q h", h=16)
            for j in range(8):
                m = 7 - j
                orb = OMR[:, 32 * m:32 * m + 32].unsqueeze(2).to_broadcast([128, 32, 16])
                oib = OMI[:, 32 * m:32 * m + 32].unsqueeze(2).to_broadcast([128, 32, 16])
                tt("dve", ar4[:, :, j, :], br3, orb, ALU.mult, [brs.name, "OMR"], ["AR"])
                tt("dve", tb3, bi3, oib, ALU.mult, [bis.name, "OMI"], ["TB"])
                tt("dve", ar4[:, :, j, :], ar4[:, :, j, :], tb3, ALU.subtract, ["TB"], ["AR"])
                tt("dve", ai4[:, :, j, :], bi3, orb, ALU.mult, [bis.name, "OMR"], ["AI"])
                tt("dve", tb3, br3, oib, ALU.mult, [brs.name, "OMI"], ["TB"])
                tt("dve", ai4[:, :, j, :], ai4[:, :, j, :], tb3, ALU.add, ["TB"], ["AI"])
            for src, dst, nm in ((AR, W1re, "W1re"), (AI, W1im, "W1im")):
                for q0 in range(0, 32, 4):
                    ps, pk = nps()
                    P.group("pe", [lambda e, o=ps[:, 128 * i:128 * i + 128], s_=src[:, 128 * (q0 + i):128 * (q0 + i) + 128]:
                                   e.transpose(o, s_, identf[:]) for i in range(4)], r=[src.name, identf.name], w=[pk])
                    copy(evac_eng(), dst[:, 128 * q0:128 * q0 + 512], ps[:, :], [pk], [nm])
            copy("dve", A0r[:].rearrange("p (q h) -> p q h", h=16), ar4[:, :, 7, :], ["AR"], ["A0r"])
            copy("dve", A0i[:].rearrange("p (q h) -> p q h", h=16), ai4[:, :, 7, :], ["AI"], ["A0i"])
            P.barrier()
            sA.close()
            ER = sb0("ER", [128, 32 * 9 * 16]); NEI = sb0("NEI", [128, 32 * 9 * 16])
            er4 = ER[:].rearrange("p (q k h) -> p q k h", k=9, h=16); ne4 = NEI[:].rearrange("p (q k h) -> p q k h", k=9, h=16)
            cr3 = crs[:].rearrange("p (q h) -> p q h", h=16); ci3 = cis[:].rearrange("p (q h) -> p q h", h=16)
            for k in range(9):
                prb = PR[:, 32 * k:32 * k + 32].unsqueeze(2).to_broadcast([128, 32, 16])
                pib = PIm[:, 32 * k:32 * k + 32].unsqueeze(2).to_broadcast([128, 32, 16])
                tt("dve", er4[:, :, k, :], cr3, prb, ALU.mult, [crs.name, "PR"], ["ER"])
                tt("dve", tb3, ci3, pib, ALU.mult, [cis.name, "PIm"], ["TB"])
                tt("dve", er4[:, :, k, :], er4[:, :, k, :], tb3, ALU.subtract, ["TB"], ["ER"])
                tt("dve", ne4[:, :, k, :], cr3, pib, ALU.mult, [crs.name, "PIm"], ["NEI"])
                tt("dve", tb3, ci3, prb, ALU.mult, [cis.name, "PR"], ["TB"])
                P.op("dve", lambda e, o=ne4[:, :, k, :]: e.scalar_tensor_tensor(o, o, -1.0, tb3, ALU.mult, ALU.subtract), r=["TB"], w=["NEI"])
            P.op("pool", lambda e: e.memset(Tm[:], 0.0), w=["Tm"])
            cast_all(True)
            cast_all(False)
            wc4 = Wc[:].rearrange("p (q r c) -> p q r c", r=2, c=128)
            for ri, src in ((0, ER), (1, NEI)):
                s4 = src[:].rearrange("p (q c) -> p q c", c=144)
                copy("dve", wc4[:, :, ri, :], s4[:, :, 16:144], [src.name], ["Wc"])
            WBF = [sb0(f"WBF{i}", [128, 2 * 2 * 128]) for i in range(4)]
            KTb = sb0("KTb", [16, 64 * 128], BF16)
            for wb in WBF:
                P.op("dve", lambda e, t_=wb: e.memset(t_[:], 0.0), w=[wb.name])
            dI3 = dIs[:].rearrange("p (g h) -> p g h", h=16)
            kt4 = KTb[:].rearrange("p (g k h) -> p g k h", k=8, h=16)
            for q in range(32):
                if q % 2 == 0:
                    bias_head(q // 2)
                wb = WBF[q % 4]
                w4 = wb[:].rearrange("p (r g c) -> p r g c", r=2, g=2)
                for g2 in range(2):
                    rows = slice(64 * g2, 64 * g2 + 64)
                    copy("act", w4[rows, 0, g2, :], ER[rows, 144 * q:144 * q + 128], ["ER"], [wb.name])
                    copy("act", w4[rows, 1, g2, :], NEI[rows, 144 * q:144 * q + 128], ["NEI"], [wb.name])
                ps, pk = nps()
                P.group("pe", [
                    lambda e, o=ps[0:16, 0:256], l=A0r[:, 16 * q:16 * q + 16], r_=wb[:, 0:256]: e.matmul(o, l, r_, start=True, stop=False),
                    lambda e, o=ps[0:16, 0:256], l=A0i[:, 16 * q:16 * q + 16], r_=wb[:, 256:512]: e.matmul(o, l, r_, start=False, stop=True),
                ], r=["A0r", "A0i", wb.name], w=[pk])
                copy("dve", KTb[:, 256 * q:256 * q + 256], ps[0:16, 0:256], [pk], ["KTb"])
                tt("dve", kt4[:, 2 * q:2 * q + 2, 0, :], ps[0:16, 0:256].rearrange("p (g k h) -> p g k h", g=2, h=16)[:, :, 0, :],
                   dI3[:, 2 * q:2 * q + 2, :], ALU.add, [pk, dIs.name], ["KTb"])
            tm3 = Tm[:].rearrange("p (g c) -> p g c", c=128)
            kt3 = KTb[:].rearrange("p (g c) -> p g c", c=128)
            for jp in range(8):
                P.dma("sp", tm3[16 * jp:16 * jp + 16, :, 16 * jp:128], kt3[:, :, 0:(8 - jp) * 16], r=["KTb", "Tm"], w=[("Tm", jp)])
            for i, (t_, nm_) in enumerate(((W1re, "W1re"), (W1im, "W1im"))):
                P.dma("sp", s_w1[i], t_[:], r=[nm_])
            for i in range(2):
                P.dma("sp", s_wc[i], Wc[:, 4096 * i:4096 * i + 4096], r=["Wc"])
                P.dma("sp", s_tm[i], Tm[:, 4096 * i:4096 * i + 4096], r=["Tm"] + [("Tm", jp) for jp in range(8)])
            P.barrier()
            for nm_, t_ in (("PR", PR), ("PIm", PIm), ("kr", kr), ("ki_", ki_), ("W1re", W1re), ("W1im", W1im), ("Wc", Wc), ("Tm", Tm),
                            ("biasT", biasT), ("esk", esk), ("CA", CA), ("CBn", CBn), ("ER", ER), ("NEI", NEI)):
                dump(nm_, t_, "none")
            dump("KTb", KTb, "none", parts=16)
            P.barrier()

        xT = sb("xT", [128, 16 * SPAN], BF16)
        gyT = sb("gyT", [128, 8 * SPAN], BF16)
        kT = sb("kT", [128, 4 * 640], BF16)
        vext = sb("vext", [128, 5 * 4 * 64], BF16)
        wbuf.extend(sb(f"wbuf{i}", [128, 16 * 256], BF16) for i in range(4))
        xb_t = [sb(f"xb{i}", [128, D], BF16) for i in range(2)]
        xb = [(t_[:, :], t_.name) for t_ in xb_t]
        xpre = {"n": 0}
        P.op("dve", lambda e: e.memset(kT[:], 0.0), w=["kT"])
        P.op("dve", lambda e: e.memset(vext[:], 0.0), w=["vext"])
        xT3 = xT[:].rearrange("p (k t) -> p k t", t=SPAN)
        gy3 = gyT[:].rearrange("p (c t) -> p c t", t=SPAN)
        kT3 = kT[:].rearrange("p (h t) -> p h t", t=640)
        vx4 = vext[:].rearrange("p (t h d) -> p t h d", h=4, d=64)
        bT4 = biasT[:].rearrange("p (h k q) -> p h k q", k=2, q=128)

        def build_xT(xsrc, row0, bufs, ev=None):
            for t4 in range(SPAN // 128):
                b_, bk_ = bufs[t4 % len(bufs)]
                if t4 >= xpre["n"]:
                    P.dma("pool", b_, xsrc[row0 + 128 * t4:row0 + 128 * t4 + 128, :], w=[bk_])
                for half in range(2):
                    ps, pk = npb()
                    P.group("pe", [lambda e, o=ps[:, 128 * i:128 * i + 128], s2=b_[:, 128 * (8 * half + i):128 * (8 * half + i) + 128]:
                                   e.transpose(o, s2, identb[:]) for i in range(8)], r=[bk_], w=[pk])
                    copy(ev or evac_eng(), xT3[:, 8 * half:8 * half + 8, 128 * t4:128 * t4 + 128],
                         ps[:, :].rearrange("p (k t) -> p k t", t=128), [pk], ["xT"])
            xpre["n"] = 0

        def prefetch_x(xsrc, row0, bufs, t_lo, t_hi):
            for t4 in range(t_lo, t_hi):
                P.dma("pool", bufs[t4][0], xsrc[row0 + 128 * t4:row0 + 128 * t4 + 128, :], w=[bufs[t4][1]])
            xpre["n"] = t_hi

        def ssm_state(sc, own, pfx=None):
            ubm = sb("ubm", [128, 8 * 1024], BF16, sc)
            U = sb("U", [128, 64 * NB], BF16, sc)
            X = sb("X", [128, 2 * 32 * NB], F32, sc)
            ub4 = ubm[:, 0:4096].rearrange("p (g j h) -> p g j h", j=4, h=16)
            U3 = U[:].rearrange("p (g b) -> p g b", b=NB)
            X4 = X[:].rearrange("p (r q b) -> p r q b", r=2, b=NB)
            xTj = xT[:].rearrange("p (k b j) -> p k j b", j=8, b=NB)
            for cb in range(4):
                wv, wk = load_c(s_in, "s_in", cb)
                for j4 in range(4):
                    ps, pk = nps()
                    fns = []
                    for kt in range(16):
                        for jh in range(2):
                            fns.append(lambda e, o=ps[64 * jh:64 * jh + 64, 0:256], l=xTj[:, kt, 4 * jh + j4, :], r_=wv[:, kt, :], kt=kt:
                                       e.matmul(o, l, r_, start=(kt == 0), stop=(kt == 15)))
                    P.group("pe", fns, r=["xT", wk], w=[pk])
                    copy("act" if (pfx is not None and cb == 0) else evac_eng(),
                         ub4[:, 16 * cb:16 * cb + 16, j4, :], ps[:, 0:256].rearrange("p (g h) -> p g h", h=16), [pk], ["ubm"])
                if own and cb in (0, 2) and late.get("pending"):
                    late["pending"].pop(0)()
            for g0 in range(0, 64, 16):
                ps, pk = npb()
                fns = []
                for i in range(16):
                    for jh in range(2):
                        rows = slice(64 * jh, 64 * jh + 64)
                        fns.append(lambda e, o=ps[rows, NB * i:NB * i + NB], s_=ubm[rows, 64 * (g0 + i):64 * (g0 + i) + 64],
                                   idn=identb[rows, 64 * jh:64 * jh + 64]: e.transpose(o, s_, idn))
                P.group("pe", fns, r=["ubm"], w=[pk])
                copy(evac_eng(), U[:, NB * g0:NB * g0 + NB * 16], ps[:, 0:NB * 16], [pk], ["U"])
            w1r, w1rk = load_c(s_w1, "s_w1", 0, 16, 128)
            w1i, w1ik = load_c(s_w1, "s_w1", 1, 16, 128)
            w13 = {0: w1r, 1: w1i}
            for q0 in range(0, 32, 4):
                ps, pk = nps()
                ps4 = ps[:, :].rearrange("p (q r b) -> p q r b", r=2, b=NB)
                fns = []
                for i in range(4):
                    q = q0 + i
                    for g2 in range(2):
                        for ri in range(2):
                            fns.append(lambda e, o=ps4[64 * g2:64 * g2 + 64, i, ri, :], l=w13[ri][:, q, 64 * g2:64 * g2 + 64],
                                       r_=U3[:, 2 * q + g2, :]: e.matmul(o, l, r_, start=True, stop=True))
                P.group("pe", fns, r=["U", w1rk, w1ik], w=[pk])
                copy(evac_eng(), X4[:, :, q0:q0 + 4, :], ps4.rearrange("p q r b -> p r q b"), [pk], ["X"])
            tA = sb("tA", [128, 64], F32, sc); tB = sb("tB", [128, 64], F32, sc)
            if pfx is not None:
                t1 = sb("pt1", [128, 32 * NB], F32, sc); t2 = sb("pt2", [128, 32 * NB], F32, sc)
                red = sb("pred", [128, 64], F32, sc)
                xr, xi = X[:, 0:32 * NB], X[:, 32 * NB:64 * NB]
                for ri_, (a_, b_, op_) in enumerate(((pfx["PWr"], pfx["PWi"], ALU.subtract), (pfx["PWi"], pfx["PWr"], ALU.add))):
                    tt("dve", t1[:], xr, a_[:], ALU.mult, ["X"], ["pt1"])
                    tt("dve", t2[:], xi, b_[:], ALU.mult, ["X"], ["pt2"])
                    tt("dve", t1[:], t1[:], t2[:], op_, ["pt1", "pt2"], ["pt1"])
                    P.op("dve", lambda e, o=red[:, 32 * ri_:32 * ri_ + 32], i_=t1[:].rearrange("p (q b) -> p q b", b=NB):
                         e.reduce_sum(o, i_, mybir.AxisListType.X), r=["pt1"], w=["pred"])
                c3 = carry[:].rearrange("p (r q) -> p r q", r=2)
                ta3 = tA[:].rearrange("p (r q) -> p r q", r=2); tb3_ = tB[:].rearrange("p (r q) -> p r q", r=2)
                tt("dve", ta3, c3, pfx["CA64"][:].rearrange("p (r q) -> p r q", r=2), ALU.mult, ["carry"], ["tA"])
                tt("dve", tb3_[:, 0, :], c3[:, 1, :], pfx["CB64n"][:], ALU.mult, ["carry"], ["tB"])
                tt("dve", tb3_[:, 1, :], c3[:, 0, :], pfx["CB64p"][:], ALU.mult, ["carry"], ["tB"])
                tt("dve", tA[:], tA[:], tB[:], ALU.add, ["tA", "tB"], ["tA"])
                tt("dve", carry[:], tA[:], red[:], ALU.add, ["tA", "pred"], ["carry"])
                return None
            Sh = None
            if own:
                Sh = sb("Sh", [128, 2 * 2 * 32 * NB], BF16, sc)
                Sh5 = Sh[:].rearrange("p (g r q b) -> p g r q b", g=2, r=2, b=NB)
                P.op("dve", lambda e, o=Sh5[64:128, 0]: e.memset(o, 0.0), w=["Sh"])
                P.op("dve", lambda e, o=Sh5[0:64, 1]: e.memset(o, 0.0), w=["Sh"])
                for g2 in range(2):
                    rows = slice(64 * g2, 64 * g2 + 64)
                    copy("pool", Sh5[rows, g2, :, :, 0], carry[rows, :].rearrange("p (r q) -> p r q", r=2), ["carry"], ["Sh"])
            c3 = carry[:].rearrange("p (r q) -> p r q", r=2)
            ca3 = CA[:].rearrange("p (r q) -> p r q", r=2)
            ta3 = tA[:].rearrange("p (r q) -> p r q", r=2); tb3_ = tB[:].rearrange("p (r q) -> p r q", r=2)
            for b in range(NB):
                prev = c3 if b == 0 else X4[:, :, :, b - 1]
                tt("pool", ta3, prev, ca3, ALU.mult, ["X", "carry"], ["tA"])
                tt("pool", tb3_[:, 0, :], prev[:, 1, :], CBn[:], ALU.mult, ["X", "carry"], ["tB"])
                tt("pool", tb3_[:, 1, :], prev[:, 0, :], CBp[:], ALU.mult, ["X", "carry"], ["tB"])
                tt("pool", ta3, ta3, tb3_, ALU.add, ["tB"], ["tA"])
                tt("pool", X4[:, :, :, b], X4[:, :, :, b], ta3, ALU.add, ["tA"], ["X"])
            copy("pool", c3, X4[:, :, :, NB - 1], ["X", "Sh"], ["carry"])
            return ubm, U3, (Sh, X4), (U, X)

        def ssm_out(ubm, U3, Sh):
            Sh, X4 = Sh
            Sh5 = Sh[:].rearrange("p (g r q b) -> p g r q b", g=2, r=2, b=NB)
            for g2 in range(2):
                rows = slice(64 * g2, 64 * g2 + 64)
                copy("dve" if g2 else "act", Sh5[rows, g2, :, :, 1:NB], X4[rows, :, :, 0:NB - 1], ["X"], ["Sh"])
            gb4 = ubm[:].rearrange("p (j g h) -> p g j h", g=64, h=16)
            for g0 in range(0, 64, 4):
                if g0 % 32 == 0:
                    tmv, tmk = load_c(s_tm, "s_tm", g0 // 32, 16, 128)
                    wcv, wck = load_c(s_wc, "s_wc", g0 // 32, 16, 128)
                ps, pk = nps()
                fns = []
                for i in range(4):
                    g = g0 + i; q, g2 = g // 2, g % 2
                    o = ps[0:NB, 128 * i:128 * i + 128]
                    fns.append(lambda e, o=o, l=U3[:, g, :], r_=tmv[:, g % 32, :]: e.matmul(o, l, r_, start=True, stop=False))
                    fns.append(lambda e, o=o, l=Sh5[:, g2, 0, q, :], r_=wcv[:, 2 * (q % 16), :]: e.matmul(o, l, r_, start=False, stop=False))
                    fns.append(lambda e, o=o, l=Sh5[:, g2, 1, q, :], r_=wcv[:, 2 * (q % 16) + 1, :]: e.matmul(o, l, r_, start=False, stop=True))
                P.group("pe", fns, r=["U", "Sh", tmk, wck], w=[pk])
                act(gb4[0:NB, g0:g0 + 4, :, :], ps[0:NB, :].rearrange("p (g j h) -> p g j h", j=8, h=16), AF.Gelu, [pk, "U"], ["ubm"])
            for ct in range(8):
                ps, pk = npb()
                P.group("pe", [lambda e, o=ps[:, NB * j:NB * j + NB], s_=ubm[0:NB, 1024 * j + 128 * ct:1024 * j + 128 * ct + 128]:
                               e.transpose(o, s_, identb[0:NB, 0:NB]) for j in range(8)], r=["ubm"], w=[pk])
                copy(evac_eng(), gy3[:, ct, :].rearrange("p (b j) -> p j b", j=8),
                     ps[:, 0:8 * NB].rearrange("p (j b) -> p j b", b=NB), [pk], ["gyT"])

        def proj_fm_unused(col0, ncols_tiles, consume):
            for c2 in range(0, ncols_tiles, 2):
                n = min(2, ncols_tiles - c2)
                wv, wk = load_c(s_in, "s_in", (col0 + 128 * c2) // 256)
                for i in range(n):
                    ps, pk = nps()
                    P.group("pe", [lambda e, o=ps[:, :], l=wv[:, kt, 128 * i:128 * i + 128], r_=xT3[:, kt, :], kt=kt:
                                   e.matmul(o, l, r_, start=(kt == 0), stop=(kt == 15)) for kt in range(16)], r=["xT", wk], w=[pk])
                    consume(c2 + i, ps, pk)

        def kv_proj(tok0, ntok, dst_tile0, kcol0):
            for kh in range(4):
                if kh % 2 == 0:
                    wv, wk = load_c(s_k2, "s_k2", kh // 2, offs=(0, 64, 128, 192))
                ps, pk = nps()
                P.group("pe", [lambda e, o=ps[:, 0:ntok], l=wv[:, kt, 128 * (kh % 2):128 * (kh % 2) + 128], r_=xT3[:, kt, tok0:tok0 + ntok], kt=kt:
                               e.matmul(o, l, r_, start=(kt == 0), stop=(kt == 15)) for kt in range(16)], r=["xT", wk], w=[pk])
                copy(evac_eng(), kT3[:, kh, kcol0:kcol0 + ntok], ps[:, 0:ntok], [pk], ["kT"])
            wv, wk = load_c(s_in, "s_in", 13)
            for t4 in range(ntok // 128):
                ps, pk = nps()
                P.group("pe", [lambda e, o=ps[:, 0:256], l=xT3[:, kt, tok0 + 128 * t4:tok0 + 128 * t4 + 128], r_=wv[:, kt, :], kt=kt:
                               e.matmul(o, l, r_, start=(kt == 0), stop=(kt == 15)) for kt in range(16)], r=["xT", wk], w=[pk])
                copy(evac_eng(), vx4[:, dst_tile0 + t4, :, :], ps[:, 0:256].rearrange("p (h d) -> p h d", d=64), [pk], ["vext"])

        late = {}

        def attn_stage(sp_i, sa):
            ha3 = late["ha3"]; haT = late["haT"]
            if True:
                qT = sb("qT", [128, 2 * SPAN], BF16, sa); qT4 = qT[:].rearrange("p (b h q) -> p b h q", h=2, q=128)
                pT = sb("pT", [128, 4 * 2 * 512], BF16, sa)
                pT5 = pT[:].rearrange("p (b k h q) -> p b k h q", k=2, h=4, q=128)
                lg = [sb(f"lg{i}", [128, 512], F32, sa) for i in range(2)]
                dn = sb("dn", [128, 512], F32, sa); zt2 = sb("zt", [128, 2 * 512], BF16, sa)
                kv_proj(0, SPAN, 1, 128)
                for kh in range(4):
                    wv, wk = load_c(s_in, "s_in", 8 + kh)
                    wza, wzak = load_c(s_in, "s_in", 14 + kh)
                    for hp in range(2):
                        ps, pk = nps()
                        P.group("pe", [lambda e, o=ps[:, :], l=wv[:, kt, 128 * hp:128 * hp + 128], r_=xT3[:, kt, :], kt=kt:
                                       e.matmul(o, l, r_, start=(kt == 0), stop=(kt == 15)) for kt in range(16)], r=["xT", wk], w=[pk])
                        act(qT4[:, :, hp, :], ps[:, :].rearrange("p (b q) -> p b q", q=128), AF.Identity, [pk], ["qT"], scale=0.125)
                    for blk in range(4):
                        for kt_ in range(2):
                            l_ = lg[(2 * blk + kt_) % 2]
                            pp, pks = nps2()
                            for half in range(2):
                                rows = slice(64 * half, 64 * half + 64)
                                P.group("pe", [lambda e, o=pp[:, 512 * half:512 * half + 256],
                                               l=kT3[rows, kh, 128 * (blk + kt_):128 * (blk + kt_) + 128],
                                               r_=qT[rows, 256 * blk:256 * blk + 256]: e.matmul(o, l, r_, start=True, stop=True)],
                                        r=["kT", "qT"], w=[pks[half]])
                            tt("dve", l_[:].rearrange("p (hf hp q) -> p hf hp q", hf=2, q=128),
                               pp.rearrange("p (b hp q) -> p b hp q", b=2, q=128)[:, :, 0:2, :],
                               bT4[:, 4 * kh:4 * kh + 4, kt_, :].rearrange("p (hp hf) q -> p hf hp q", hf=2),
                               ALU.add, [pks[0], pks[1], "biasT"], [l_.name])
                            if sp_i == 0 and blk == 0 and kt_ == 0:
                                P.op("act", lambda e, o=pT5[:, blk, kt_, :, :], i_=l_[:].rearrange("p (h q) -> p h q", q=128):
                                     e.activation(o, i_, AF.Exp, bias=hneg[:, 0:1]), r=[l_.name, "hneg_s"], w=["pT"])
                            else:
                                act(pT5[:, blk, kt_, :, :], l_[:].rearrange("p (h q) -> p h q", q=128), AF.Exp, [l_.name], ["pT"])
                    for hp in range(2):
                        ps, pk = nps()
                        P.group("pe", [lambda e, o=ps[:, :], l=wza[:, kt, 128 * hp:128 * hp + 128], r_=xT3[:, kt, :], kt=kt:
                                       e.matmul(o, l, r_, start=(kt == 0), stop=(kt == 15)) for kt in range(16)], r=["xT", wzak], w=[pk])
                        act(zt2[:, 512 * hp:512 * hp + 512], ps[:, :], AF.Silu, [pk], ["zt"])
                    for hp in range(2):
                        hpg = 2 * kh + hp
                        zt = zt2[:, 512 * hp:512 * hp + 512]
                        pv, pvk = nps(); rs, rsk = nps()
                        fns = []
                        for blk in range(4):
                            for half in range(2):
                                hq = 2 * half + hp
                                for kt_ in range(2):
                                    fns.append(lambda e, o=pv[64 * half:64 * half + 64, 128 * blk:128 * blk + 128], l=vx4[:, blk + kt_, kh, :],
                                               r_=pT5[:, blk, kt_, hq, :], kt_=kt_: e.matmul(o, l, r_, start=(kt_ == 0), stop=(kt_ == 1)))
                                for kt_ in range(2):
                                    fns.append(lambda e, o=rs[64 * half:64 * half + 64, 128 * blk:128 * blk + 128], r_=pT5[:, blk, kt_, hq, :], kt_=kt_:
                                               e.matmul(o, ones64[:], r_, start=(kt_ == 0), stop=(kt_ == 1)))
                        P.group("pe", fns, r=["pT", "vext", "ones64"], w=[pvk, rsk])
                        P.op("act", lambda e, r2=rs, h_=hpg: e.activation(dn[:], r2[:, :], AF.Ln, bias=esk[:, h_:h_ + 1]), r=[rsk, "esk"], w=["dn"])
                        P.op("act", lambda e: e.activation(dn[:], dn[:], AF.Exp, scale=-1.0), r=["dn"], w=["dn"])
                        tt("dve", dn[:], dn[:], pv[:, :], ALU.mult, [pvk], ["dn"])
                        tt("dve", ha3[:, hpg, :], dn[:], zt, ALU.mult, ["dn", "zt"], ["haT"])
                copy("dve", kT3[:, :, 0:128], kT3[:, :, 512:640], ["kT"], ["kT"])
                copy("dve", vx4[:, 0, :, :], vx4[:, 4, :, :], ["vext"], ["vext"])

        def merge_half(attn, sm):
            tg = "a" if attn else "s"
            ha3, hs3, mT3 = late["ha3"], late["hs3"], late["mT3"]
            g1 = sb("g1", [128, 512], F32, sm); m1 = late["ftmp"]
            h3, hn = (ha3, "haT") if attn else (hs3, "hsT")
            for d2 in range(8):
                wg3, wgk = load_c(s_in, "s_in", (26 if attn else 18) + d2)
                wb3, wbk = load_c(s_ba if attn else s_bs, "s_ba" if attn else "s_bs", d2, 8)
                for i in range(2):
                    dt = 2 * d2 + i
                    ps, pk = nps()
                    P.group("pe", [lambda e, o=ps[:, :], l=wg3[:, kt, 128 * i:128 * i + 128], r_=xT3[:, kt, :], kt=kt:
                                   e.matmul(o, l, r_, start=(kt == 0), stop=(kt == 15)) for kt in range(16)], r=["xT", wgk], w=[pk])
                    act(g1[:], ps[:, :], AF.Sigmoid, [pk], [g1.name])
                    ps, pk = nps()
                    P.group("pe", [lambda e, o=ps[:, :], l=wb3[:, ct, 128 * i:128 * i + 128], r_=h3[:, ct, :], ct=ct:
                                   e.matmul(o, l, r_, start=(ct == 0), stop=(ct == 7)) for ct in range(8)], r=[hn, wbk], w=[pk])
                    if attn:
                        tt("dve", mT3[:, dt, :], ps[:, :], g1[:], ALU.mult, [pk, g1.name], ["mT"])
                    else:
                        tt("dve", m1[:], ps[:, :], g1[:], ALU.mult, [pk, g1.name], ["ftmp"])
                        tt("dve", mT3[:, dt, :], mT3[:, dt, :], m1[:], ALU.add, ["ftmp"], ["mT"])

        def main_stage(sp_i, sc, ubm, U, X, Sh, pre_tail, defer):
            hs3, mT3, hsT, mT = late["hs3"], late["mT3"], late["hsT"], late["mT"]
            if True:
                sg = sc
                zs = sb("zs", [128, 512], BF16, sg); sgb = sb("sgb", [128, 512], F32, sg); ga_ = late["ftmp"]
                for et in range(8):
                    if et % 2 == 0:
                        wv2, wk2 = load_c(s_in, "s_in", 4 + et // 2)
                    ps, pk = nps()
                    P.group("pe", [lambda e, o=ps[:, :], l=wv2[:, kt, 128 * (et % 2):128 * (et % 2) + 128], r_=xT3[:, kt, :], kt=kt:
                                   e.matmul(o, l, r_, start=(kt == 0), stop=(kt == 15)) for kt in range(16)], r=["xT", wk2], w=[pk])
                    act(zs[:], ps[:, :], AF.Silu, [pk], ["zs"])
                    wg3, wgk = load_c(s_glu, "s_glu", et, 8)
                    pa, pak = nps(); pb, pbk = nps()
                    P.group("pe", [lambda e, o=pa[:, :], l=wg3[:, ct, 0:128], r_=gy3[:, ct, :], ct=ct:
                                   e.matmul(o, l, r_, start=(ct == 0), stop=(ct == 7)) for ct in range(8)], r=["gyT", wgk], w=[pak])
                    P.group("pe", [lambda e, o=pb[:, :], l=wg3[:, ct, 128:256], r_=gy3[:, ct, :], ct=ct:
                                   e.matmul(o, l, r_, start=(ct == 0), stop=(ct == 7)) for ct in range(8)], r=["gyT", wgk], w=[pbk])
                    act(sgb[:], pb[:, :], AF.Sigmoid, [pbk], ["sgb"])
                    tt("dve", ga_[:], pa[:, :], sgb[:], ALU.mult, [pak, "sgb"], ["ftmp"])
                    tt("dve", hs3[:, et, :], ga_[:], zs[:], ALU.mult, ["ftmp", "zs"], ["hsT"])
            merge_half(False, sc)
            if True:
                so = sc
                rrA = ubm[:].bitcast(F32).rearrange("p (t c) -> p t c", c=D)
                rrB = Sh[:].bitcast(F32).rearrange("p (t c) -> p t c", c=D)
                rts = [(rrA[:, 0, :], "ubm"), (rrA[:, 1, :], "ubm"), (rrB[:, 0, :], "Sh"), (rrB[:, 1, :], "Sh")]
                sq = U[:].bitcast(F32)
                gch = X[:, 0:D]; bch = X[:, D:2 * D]
                st = sb("lnst", [128, 8], F32, so)
                for t4 in range(4):
                    rt, rk = rts[t4]
                    P.dma("pool", rt, xo[SPAN * sp_i + 128 * t4:SPAN * sp_i + 128 * t4 + 128, :], w=[rk])
                P.dma("pool", gch, lng_d, w=["X"]); P.dma("pool", bch, lnb_d, w=["X"])

                def layer_norm(t4):
                    rt, rk = rts[t4]
                    P.op("dve", lambda e: e.reduce_sum(st[:, 0:1], rt, mybir.AxisListType.X), r=[rk], w=["lnst"])
                    tt("dve", sq, rt, rt, ALU.mult, [rk], ["U"])
                    P.op("dve", lambda e: e.reduce_sum(st[:, 1:2], sq, mybir.AxisListType.X), r=["U"], w=["lnst"])
                    ts("dve", st[:, 2:3], st[:, 0:1], 1.0 / D, None, ALU.mult, None, ["lnst"], ["lnst"])
                    tt("dve", st[:, 3:4], st[:, 2:3], st[:, 2:3], ALU.mult, ["lnst"], ["lnst"])
                    ts("dve", st[:, 4:5], st[:, 1:2], 1.0 / D, float(LN_EPS), ALU.mult, ALU.add, ["lnst"], ["lnst"])
                    tt("dve", st[:, 4:5], st[:, 4:5], st[:, 3:4], ALU.subtract, ["lnst"], ["lnst"])
                    P.op("act", lambda e: e.activation(st[:, 5:6], st[:, 4:5], AF.Sqrt), r=["lnst"], w=["lnst"])
                    P.op("dve", lambda e: e.reciprocal(st[:, 5:6], st[:, 5:6]), r=["lnst"], w=["lnst"])
                    P.op("dve", lambda e: e.scalar_tensor_tensor(st[:, 6:7], st[:, 2:3], -1.0, st[:, 5:6], ALU.mult, ALU.mult),
                         r=["lnst"], w=["lnst"])
                    P.op("act", lambda e: e.activation(rt, rt, AF.Identity, bias=st[:, 6:7], scale=st[:, 5:6]), r=["lnst"], w=[rk])
                    tt("dve", rt, rt, gch, ALU.mult, ["X"], [rk])
                    tt("dve", rt, rt, bch, ALU.add, ["X"], [rk])
                    P.dma("pool", y_out[SPAN * sp_i + 128 * t4:SPAN * sp_i + 128 * t4 + 128, :], rt, r=[rk])

                for th in range(2):
                    for cb in range(8):
                        wv, wk = load_c(s_out, "s_out", cb)
                        for t2 in range(2):
                            t4 = 2 * th + t2
                            rt, rk = rts[t4]
                            ps, pk = nps()
                            P.group("pe", [lambda e, o=ps[:, 0:256], l=mT3[:, dt, 128 * t4:128 * t4 + 128], r_=wv[:, dt, :], dt=dt:
                                           e.matmul(o, l, r_, start=(dt == 0), stop=(dt == 15)) for dt in range(16)], r=["mT", wk], w=[pk])
                            P.op("dve", lambda e, o=rt[:, 256 * cb:256 * cb + 256], p_=ps[:, 0:256]:
                                 e.scalar_tensor_tensor(o, o, float(ALPHA), p_, ALU.mult, ALU.add), r=[pk], w=[rk])
                        if th == 1 and cb == 1:
                            layer_norm(0)
                        if th == 1 and cb == 4:
                            layer_norm(1)
                pre_tail()
                if defer:
                    late["pending"] = [lambda: layer_norm(2), lambda: layer_norm(3)]
                else:
                    layer_norm(2)
                    layer_norm(3)

        n_pre = dbg.get("n_pre", NSPAN)
        n_own = dbg.get("n_own", NSPAN)
        spre = ExitStack()
        PWr = sb("PWr", [128, 32 * NB], F32, spre); PWi = sb("PWi", [128, 32 * NB], F32, spre)
        CA64 = sb("CA64", [128, 64], F32, spre); CB64n = sb("CB64n", [128, 32], F32, spre); CB64p = sb("CB64p", [128, 32], F32, spre)
        with ExitStack() as sw:
            NM = 65
            msc = sb("msc_s", [128, NM], F32, sw)
            ANGx = sb("ANGx", [128, 2 * NM * 32], F32, sw); TMPx = sb("TMPx", [128, 2 * NM * 32], F32, sw)
            KIx = sb("KIx", [128, 2 * NM * 32], I32, sw); MARx = sb("MARx", [128, NM * 32], F32, sw)
            P.dma("sp", msc[:], msc_d, w=["msc_s"])
            an4 = ANGx[:].rearrange("p (s m q) -> p s m q", s=2, q=32)
            mb = msc[:].unsqueeze(2).to_broadcast([128, NM, 32])
            tt("dve", an4[:, 0], mb, ai[:].unsqueeze(1).to_broadcast([128, NM, 32]), ALU.mult, ["msc_s", "ai"], ["ANGx"])
            ts("dve", an4[:, 1], an4[:, 0], PI / 2, None, ALU.add, None, ["ANGx"], ["ANGx"])
            sin_reduced(ANGx, TMPx, KIx)
            ma3 = MARx[:].rearrange("p (m q) -> p m q", q=32)
            tt("dve", ma3, mb, ar[:].unsqueeze(1).to_broadcast([128, NM, 32]), ALU.mult, ["msc_s", "ar"], ["MARx"])
            act(MARx[:], MARx[:], AF.Exp, ["MARx"], ["MARx"])
            tt("dve", PWr[:].rearrange("p (q m) -> p m q", m=NB), ma3[:, 0:NB, :], an4[:, 1, 0:NB, :], ALU.mult, ["MARx", "ANGx"], ["PWr"])
            tt("dve", PWi[:].rearrange("p (q m) -> p m q", m=NB), ma3[:, 0:NB, :], an4[:, 0, 0:NB, :], ALU.mult, ["MARx", "ANGx"], ["PWi"])
            tt("dve", CA64[:, 0:32], ma3[:, NB, :], an4[:, 1, NB, :], ALU.mult, ["MARx", "ANGx"], ["CA64"])
            copy("dve", CA64[:, 32:64], CA64[:, 0:32], ["CA64"], ["CA64"])
            tt("dve", CB64p[:], ma3[:, NB, :], an4[:, 0, NB, :], ALU.mult, ["MARx", "ANGx"], ["CB64p"])
            ts("dve", CB64n[:], CB64p[:], -1.0, None, ALU.mult, None, ["CB64p"], ["CB64n"])
            P.barrier()
        pfx = dict(PWr=PWr, PWi=PWi, CA64=CA64, CB64n=CB64n, CB64p=CB64p)
        xb4 = xb + [(t_[:, :], t_.name) for t_ in (sb(f"xb{i}", [128, D], BF16, pre_scope) for i in (2, 3))]
        n_cast = -(-len(cast_jobs) // max(1, n_pre))
        for sp_i in range(NSPAN - n_pre, NSPAN):
            sc = pre_scope
            build_xT(xp, SPAN * sp_i, xb4, ev="act")
            ssm_state(sc, own=False, pfx=pfx)
            if sp_i + 1 < NSPAN:
                prefetch_x(xp, SPAN * (sp_i + 1), xb4, 0, 4)
            elif n_own:
                prefetch_x(xo, 0, xb, 0, 2)
            issue_casts(n_cast)
            if sp_i == NSPAN - 1:
                kv_proj(SPAN - 128, 128, 0, 0)
        issue_casts(len(cast_jobs))
        P.barrier()
        pre_scope.close()
        memo.clear()
        spre.close()
        mT = sb("mT", [128, 16 * SPAN], BF16); haT = sb("haT", [128, 8 * SPAN], BF16); hsT = sb("hsT", [128, 8 * SPAN], BF16)
        late.update(mT=mT, haT=haT, hsT=hsT, mT3=mT[:].rearrange("p (c t) -> p c t", t=SPAN),
                    ha3=haT[:].rearrange("p (c t) -> p c t", t=SPAN), hs3=hsT[:].rearrange("p (c t) -> p c t", t=SPAN))
        xbuilt = {}
        for sp_i in range(n_own):
            if True:
                sc = own_scope
                if not xbuilt.get(sp_i):
                    build_xT(xo, SPAN * sp_i, xb)
                late["ftmp"] = sb("ftmp", [128, 512], F32, sc)
                ubm, U3, Sh, (U_, X_) = ssm_state(sc, own=True)
                if sp_i + 1 < n_own:
                    prefetch_x(xo, SPAN * (sp_i + 1), xb, 0, 2)
                attn_stage(sp_i, sc)
                pT_ = memo["pT"]
                xb_own = xb + [(pT_[:, 0:D], "pT"), (pT_[:, D:2 * D], "pT")]
                if sp_i + 1 < n_own:
                    prefetch_x(xo, SPAN * (sp_i + 1), xb_own, 2, 4)
                merge_half(True, sc)
                ssm_out(ubm, U3, Sh)

                def pre_tail(n=sp_i):
                    if n + 1 < n_own:
                        build_xT(xo, SPAN * (n + 1), xb_own)
                        xbuilt[n + 1] = True
                main_stage(sp_i, sc, ubm, U_, X_, Sh[0], pre_tail, sp_i + 1 < n_own)

        P.barrier()
        own_scope.close()
        with nc.Block() as block:
            @block.tensor
            def _(e):
                P.emit("pe", e)

            @block.scalar
            def _(e):
                P.emit("act", e)

            @block.vector
            def _(e):
                P.emit("dve", e)

            @block.gpsimd
            def _(e):
                P.emit("pool", e)

            @block.sync
            def _(e):
                P.emit("sp", e)
    return nc


def _t5_bucket(dist):
    max_exact = 16
    d = np.maximum(dist, 1).astype(np.float32)
    large = max_exact + (np.log(d / np.float32(max_exact)) / np.float32(math.log(128 / max_exact)) * np.float32(16)).astype(np.int32)
    large = np.minimum(large, 31)
    return np.where(dist < max_exact, dist, large)


def kernel(x, w_in, ssm_lambda_re, ssm_lambda_im, ssm_b_re, ssm_b_im, ssm_c_re, ssm_c_im, ssm_d, ssm_log_step,
           w_glu, attn_sinks, rel_bias_table, w_branch_ssm, w_branch_attn, w_out, ln_gain, ln_bias):
    f = lambda a: np.ascontiguousarray(np.asarray(a, dtype=np.float32))
    x = f(x)
    pl = lambda a: f(a.reshape(32, 2, 64).transpose(1, 2, 0).reshape(128, 32))
    plb = lambda a: f(a.reshape(32, 2, 64, 16).transpose(1, 2, 0, 3).reshape(128, 512))
    plc = lambda a: f(a.reshape(32, 2, 16, 64).transpose(1, 3, 0, 2).reshape(128, 512))
    ls = np.asarray(ssm_log_step[0], np.float32).reshape(32, 2)
    lstep = f(np.broadcast_to(ls.T[:, None, :], (2, 64, 32)).reshape(128, 32))
    dI = np.zeros((16, 64, 16), np.float32)
    dd = np.asarray(ssm_d[0], np.float32).reshape(64, 16)
    for h in range(16):
        dI[h, :, h] = dd[:, h]
    sk = np.zeros((128, 8), np.float32)
    sinks = np.asarray(attn_sinks[0], np.float32)
    for hp in range(8):
        sk[0:64, hp] = sinks[2 * hp]
        sk[64:128, hp] = sinks[2 * hp + 1]
    tab = np.asarray(rel_bias_table, np.float32)
    line = np.full((16, 384), NEG, np.float32)
    bk = _t5_bucket(np.arange(128))
    line[:, 128:256] = tab[bk, :].T
    common = {
        "w_in": f(w_in[0]), "w_glu": f(w_glu[0]), "w_bs": f(w_branch_ssm[0]), "w_ba": f(w_branch_attn[0]), "w_out": f(w_out[0]),
        "lamr": pl(np.asarray(ssm_lambda_re[0])), "lami": pl(np.asarray(ssm_lambda_im[0])), "lstep": lstep,
        "br": plb(np.asarray(ssm_b_re[0])), "bi": plb(np.asarray(ssm_b_im[0])),
        "cr": plc(np.asarray(ssm_c_re[0])), "ci": plc(np.asarray(ssm_c_im[0])),
        "dI": f(dI.reshape(16, 1024)), "sk": sk, "line": line,
        "lng": f(np.broadcast_to(np.asarray(ln_gain[0], np.float32)[None, :], (128, D))),
        "lnb": f(np.broadcast_to(np.asarray(ln_bias[0], np.float32)[None, :], (128, D))),
        "idf": np.eye(128, dtype=np.float32), "anti": f(np.eye(128, dtype=np.float32)[::-1]),
        "msc": f(np.broadcast_to(np.array([8.0 * (NB - 1 - b) for b in range(NB)] + [8.0 * NB], np.float32)[None, :], (128, 65))),
    }
    in_maps = []
    for c in range(NCORES):
        b, h = c // 2, c % 2
        m = dict(common)
        m["xo"] = f(x[b, h * TOK:(h + 1) * TOK])
        m["xp"] = f(x[b, 0:TOK]) if h == 1 else np.zeros((TOK, D), np.float32)
        m["hneg"] = np.full((128, 1), 0.0 if h == 1 else NEG, np.float32)
        in_maps.append(m)
    nc = build_program()
    res = run_bass_kernel_spmd(nc, in_maps, core_ids=list(range(NCORES)))
    out = np.zeros((4, 8192, D), np.float32)
    for c in range(NCORES):
        b, h = c // 2, c % 2
        out[b, h * TOK:(h + 1) * TOK] = np.asarray(res.results[c]["y"], np.float32)
    return out
```

```python
import math
import numpy as np
from contextlib import ExitStack
import concourse.bass as bass
import concourse.mybir as mybir
from concourse.bass_utils import run_bass_kernel_spmd

F32 = mybir.dt.float32
BF16 = mybir.dt.bfloat16
I32 = mybir.dt.int32
ALU = mybir.AluOpType
AF = mybir.ActivationFunctionType
PI = float(np.pi)
TWO_PI = float(2 * np.pi)

NCORES = 8
TOK = 4096
SPAN = 512
NB = SPAN // 8
NSPAN = TOK // SPAN
D = 2048
DIN = 8704
ALPHA = 2.0 ** 0.25
LN_EPS = 1e-5
NEG = -30000.0
C_U, C_ZS, C_Q, C_K, C_V, C_ZA, C_G = 0, 1024, 2048, 3072, 3328, 3584, 4608

ENGS = ["pe", "act", "dve", "pool", "sp"]
NDS = 12
NBG = 6


class Prog:
    def __init__(self, nc, st):
        self.nc = nc
        self.q = {e: [] for e in ENGS}
        self.sem = {e: st.enter_context(nc.semaphore("c_" + e)) for e in ENGS if e != "sp"}
        self.cnt = {e: 0 for e in ENGS}
        self.dsem = [st.enter_context(nc.semaphore(f"dq{i}")) for i in range(NDS + NBG)]
        self.dcnt = [0] * (NDS + NBG)
        self.dnext = 0
        self.bnext = 0
        self.seen = {e: {} for e in ENGS}
        self.lastw = {}
        self.readers = {}

    def _need(self, eng, tok, waits, raw=False):
        kind, src, val = tok
        if kind == "e" and src == eng and (eng == "pe" or not raw):
            return
        key = (kind, src)
        if self.seen[eng].get(key, 0) >= val:
            return
        self.seen[eng][key] = val
        waits.append(tok)

    def _deps(self, eng, r, w):
        waits = []
        for b in r:
            t = self.lastw.get(b)
            if t:
                self._need(eng, t, waits, raw=True)
        for b in w:
            t = self.lastw.get(b)
            if t:
                self._need(eng, t, waits)
            for t in self.readers.get(b, ()):
                self._need(eng, t, waits)
        return waits

    def _commit(self, tok, r, w):
        for b in r:
            self.readers.setdefault(b, []).append(tok)
        for b in w:
            self.lastw[b] = tok
            self.readers[b] = []

    def op(self, eng, fn, r=(), w=()):
        waits = self._deps(eng, r, w)
        self.cnt[eng] += 1
        tok = ("e", eng, self.cnt[eng])
        self.q[eng].append((waits, fn, tok))
        self._commit(tok, r, w)

    def group(self, eng, fns, r=(), w=()):
        waits = self._deps(eng, r, w)
        self.cnt[eng] += 1
        tok = ("e", eng, self.cnt[eng])
        for i, fn in enumerate(fns):
            self.q[eng].append((waits if i == 0 else [], fn, tok if i == len(fns) - 1 else None))
        self._commit(tok, r, w)

    def dma(self, eng, out, in_, r=(), w=(), bg=False):
        waits = self._deps(eng, r, w)
        if bg:
            s = NDS + self.bnext
            self.bnext = (self.bnext + 1) % NBG
        else:
            s = self.dnext
            self.dnext = (self.dnext + 1) % NDS
        if self.dcnt[s]:
            self._need(eng, ("d", s, self.dcnt[s]), waits)
        self.dcnt[s] += 16
        tok = ("d", s, self.dcnt[s])
        self.q[eng].append((waits, lambda e, o=out, i=in_: e.dma_start(out=o, in_=i), tok))
        self._commit(tok, r, w)

    def barrier(self):
        for e in ENGS:
            waits = []
            for f in ENGS:
                if f != "sp" and self.cnt[f]:
                    self._need(e, ("e", f, self.cnt[f]), waits)
            for s in range(NDS):
                if self.dcnt[s]:
                    self._need(e, ("d", s, self.dcnt[s]), waits)
            if waits:
                self.q[e].append((waits, None, None))
        keep = {k: v for k, v in self.lastw.items() if isinstance(k, tuple) and v[0] == "d" and v[1] >= NDS}
        self.lastw.clear()
        self.lastw.update(keep)
        self.readers.clear()

    def emit(self, eng, e):
        for waits, fn, tok in self.q[eng]:
            for kind, src, val in waits:
                e.wait_ge(self.sem[src] if kind == "e" else self.dsem[src], val)
            if fn is None:
                continue
            ins = fn(e)
            if tok is not None:
                if tok[0] == "e":
                    ins.then_inc(self.sem[eng], 1)
                else:
                    ins.then_inc(self.dsem[tok[1]], 16)


def build_program(debug=None):
    dbg = debug or {}
    nc = bass.Bass("TRN2", target_bir_lowering=False)
    dr = lambda n, s, k="ExternalInput", d=F32: nc.dram_tensor(n, list(s), d, kind=k).ap()
    xo = dr("xo", [TOK, D])
    xp = dr("xp", [TOK, D])
    w_in = dr("w_in", [D, DIN])
    w_glu = dr("w_glu", [1024, 2048])
    w_bs = dr("w_bs", [1024, 2048])
    w_ba = dr("w_ba", [1024, 2048])
    w_out = dr("w_out", [D, D])
    lamr_d = dr("lamr", [128, 32]); lami_d = dr("lami", [128, 32]); lstep_d = dr("lstep", [128, 32])
    br_d = dr("br", [128, 512]); bi_d = dr("bi", [128, 512]); cr_d = dr("cr", [128, 512]); ci_d = dr("ci", [128, 512])
    dI_d = dr("dI", [16, 1024])
    sk_d = dr("sk", [128, 8])
    line_d = dr("line", [16, 384])
    hneg_d = dr("hneg", [128, 1])
    lng_d = dr("lng", [128, D]); lnb_d = dr("lnb", [128, D])
    idf_d = dr("idf", [128, 128]); anti_d = dr("anti", [128, 128])
    msc_d = dr("msc", [128, 65])
    y_out = dr("y", [TOK, D], k="ExternalOutput")
    s_in = dr("s_in", [34, 128, 4096], k="Internal", d=BF16)
    s_glu = dr("s_glu", [8, 128, 2048], k="Internal", d=BF16)
    s_bs = dr("s_bs", [8, 128, 2048], k="Internal", d=BF16)
    s_ba = dr("s_ba", [8, 128, 2048], k="Internal", d=BF16)
    s_out = dr("s_out", [8, 128, 4096], k="Internal", d=BF16)
    s_k2 = dr("s_k2", [2, 128, 4096], k="Internal", d=BF16)
    s_w1 = dr("s_w1", [2, 128, 4096], k="Internal", d=BF16)
    s_wc = dr("s_wc", [2, 128, 4096], k="Internal", d=BF16)
    s_tm = dr("s_tm", [2, 128, 4096], k="Internal", d=BF16)

    with ExitStack() as st:
        P = Prog(nc, st)
        used = {}
        dumped = {}

        def dump(name, t, key, parts=128):
            if name not in dbg.get("dumps", ()) or name in dumped:
                return
            shape = [parts, t.shape[-1]]
            dst = nc.dram_tensor("dbg_" + name, shape, t.dtype, kind="ExternalOutput").ap()
            P.dma("sp", dst, t[0:parts, :], r=[key])
            dumped[name] = True

        memo = {}
        own_scope = ExitStack()
        pre_scope = ExitStack()

        def sb(n, s, d=F32, c=st):
            if c is own_scope or c is pre_scope:
                if n in memo:
                    return memo[n]
            k = used.get(n, 0)
            used[n] = k + 1
            t = c.enter_context(nc.sbuf_tensor(n if k == 0 else f"{n}_{k}", list(s), d))
            if c is own_scope or c is pre_scope:
                memo[n] = t
            return t
        identb = sb("identb", [128, 128], BF16)
        identf = sb("identf", [128, 128])
        antif = sb("antif", [128, 128])
        ones64 = sb("ones64", [128, 64], BF16)
        biasT = sb("biasT", [128, 16 * 2 * 128])
        CA = sb("CA", [128, 64])
        CBn = sb("CBn", [128, 32]); CBp = sb("CBp", [128, 32])
        carry = sb("carry", [128, 64])
        esk = sb("esk", [128, 8])
        ar = sb("ar", [128, 32]); ai = sb("ai", [128, 32])
        hneg = sb("hneg_s", [128, 1])
        wstate = {"i": 0}
        wbuf = []
        psbig = st.enter_context(nc.psum_tensor("psbig", [128, 6 * 512], F32))
        psf = [psbig[:, 512 * i:512 * i + 512] for i in range(6)]
        psb = [st.enter_context(nc.psum_tensor(f"psb{i}", [128, 1024], BF16)) for i in range(2)]
        pstate = {"f": 0, "b": 0}

        def nps():
            i = pstate["f"]; pstate["f"] = (i + 1) % 6
            return psf[i], f"psf{i}"

        def nps2():
            i = pstate["f"]
            if i % 2:
                i = (i + 1) % 6
            pstate["f"] = (i + 2) % 6
            return psbig[:, 512 * i:512 * i + 1024], (f"psf{i}", f"psf{i + 1}")

        def npb():
            i = pstate["b"]; pstate["b"] = (i + 1) % 2
            return psb[i], f"psb{i}"

        def nwb():
            i = wstate["i"]; wstate["i"] = (i + 1) % 4
            return wbuf[i], f"wbuf{i}"

        evs = {"i": 0}

        def evac_eng():
            evs["i"] ^= 1
            return "dve" if evs["i"] else "act"

        def copy(eng, out, in_, r, w):
            if eng == "act":
                P.op("act", lambda e, o=out, i=in_: e.copy(o, i), r=r, w=w)
            else:
                P.op(eng, lambda e, o=out, i=in_: e.tensor_copy(o, i), r=r, w=w)

        def tt(eng, out, a, b, op, r, w):
            P.op(eng, lambda e, o=out, x=a, y=b, p=op: e.tensor_tensor(o, x, y, p), r=r, w=w)

        def ts(eng, out, a, s1, s2, op0, op1, r, w):
            if op1 is None:
                P.op(eng, lambda e, o=out, x=a: e.tensor_scalar(o, x, s1, None, op0), r=r, w=w)
            else:
                P.op(eng, lambda e, o=out, x=a: e.tensor_scalar(o, x, s1, s2, op0, op1), r=r, w=w)

        def taylor_exp(q, x, deg, keyq, keyx):
            ts("dve", q, x, 1.0 / deg, 1.0, ALU.mult, ALU.add, [keyx], [keyq])
            for k in range(deg - 1, 0, -1):
                tt("dve", q, q, x, ALU.mult, [keyq, keyx], [keyq])
                ts("dve", q, q, 1.0 / k, 1.0, ALU.mult, ALU.add, [keyq], [keyq])

        C1 = 6.28125
        C2 = TWO_PI - C1

        def sin_reduced(A, T_, K_):
            a, t_, k_ = A.name, T_.name, K_.name
            ts("dve", T_[:], A[:], 1.0 / TWO_PI, None, ALU.mult, None, [a], [t_])
            copy("dve", K_[:], T_[:], [t_], [k_])
            copy("dve", T_[:], K_[:], [k_], [t_])
            P.op("dve", lambda e: e.scalar_tensor_tensor(A[:], T_[:], -C1, A[:], ALU.mult, ALU.add), r=[t_, a], w=[a])
            P.op("dve", lambda e: e.scalar_tensor_tensor(A[:], T_[:], -C2, A[:], ALU.mult, ALU.add), r=[t_, a], w=[a])
            ts("dve", T_[:], A[:], PI, -TWO_PI, ALU.is_gt, ALU.mult, [a], [t_])
            tt("dve", A[:], A[:], T_[:], ALU.add, [t_, a], [a])
            ts("dve", T_[:], A[:], -PI, TWO_PI, ALU.is_lt, ALU.mult, [a], [t_])
            tt("dve", A[:], A[:], T_[:], ALU.add, [t_, a], [a])
            ts("dve", A[:], A[:], PI, -PI, ALU.min, ALU.max, [a], [a])
            act(A[:], A[:], AF.Sin, [a], [a])

        def act(out, in_, func, r, w, scale=1.0):
            P.op("act", lambda e, o=out, i=in_, f=func, s=scale: e.activation(o, i, f, scale=s), r=r, w=w)

        def cast_w(scr, sname, idx, src, c0, ncols, col_off=0):
            nk = src.shape[0] // 128
            dst = scr[idx].rearrange("p (k c) -> p k c", c=256)[:, 0:nk, col_off:col_off + ncols]
            P.dma("pool", dst, src.rearrange("(k p) c -> p k c", p=128)[:, :, c0:c0 + ncols],
                  w=[(sname, idx, col_off)], bg=True)

        cast_jobs = []

        def cast_all(first):
            if first:
                for i in list(range(4)) + [13]:
                    cast_w(s_in, "s_in", i, w_in, 256 * i, 256)
                for kh in range(4):
                    for dup in range(2):
                        cast_w(s_k2, "s_k2", kh // 2, w_in, C_K + 64 * kh, 64, 128 * (kh % 2) + 64 * dup)
                return
            J = cast_jobs.append
            for i in list(range(8, 12)) + list(range(14, 18)):
                J((s_in, "s_in", i, w_in, 256 * i, 256, 0))
            for i in range(8):
                J((s_in, "s_in", 26 + i, w_in, 256 * (26 + i), 256, 0))
                J((s_ba, "s_ba", i, w_ba, 256 * i, 256, 0))
            for i in range(4, 8):
                J((s_in, "s_in", i, w_in, 256 * i, 256, 0))
            for i in range(8):
                J((s_glu, "s_glu", i, w_glu, 128 * i, 128, 0)); J((s_glu, "s_glu", i, w_glu, 1024 + 128 * i, 128, 128))
            for i in range(8):
                J((s_in, "s_in", 18 + i, w_in, 256 * (18 + i), 256, 0))
                J((s_bs, "s_bs", i, w_bs, 256 * i, 256, 0))
            for i in range(8):
                J((s_out, "s_out", i, w_out, 256 * i, 256, 0))

        def issue_casts(n):
            for _ in range(min(n, len(cast_jobs))):
                cast_w(*cast_jobs.pop(0))

        def load_c(scr, sname, idx, nk=16, cw=256, offs=(0, 128)):
            dst, key = nwb()
            P.dma("sp", dst[:, 0:nk * 256], scr[idx][:, 0:nk * 256], r=[(sname, idx, o_) for o_ in offs], w=[key])
            return dst[:].rearrange("p (k c) -> p k c", c=cw), key

        with ExitStack() as s0:
            sb0 = lambda n, s, d=F32: sb(n, s, d, s0)
            lamr = sb0("lamr_s", [128, 32]); lami = sb0("lami_s", [128, 32]); stp = sb0("stp", [128, 32])
            W1re = sb0("W1re", [128, 32 * 128], BF16)
            W1im = sb0("W1im", [128, 32 * 128], BF16)
            Wc = sb0("Wc", [128, 32 * 2 * 128], BF16)
            Tm = sb0("Tm", [128, 64 * 128], BF16)
            brs = sb0("brs", [128, 512]); bis = sb0("bis", [128, 512]); crs = sb0("crs", [128, 512]); cis = sb0("cis", [128, 512])
            dIs = sb0("dIs", [16, 1024]); sks = sb0("sks", [128, 8])
            for t_, d_ in ((lamr, lamr_d), (lami, lami_d), (stp, lstep_d), (brs, br_d), (bis, bi_d), (crs, cr_d),
                           (cis, ci_d), (dIs, dI_d), (sks, sk_d), (hneg, hneg_d), (identf, idf_d), (antif, anti_d)):
                P.dma("sp", t_[:], d_, w=[t_.name])
            copy("dve", identb[:], identf[:], [identf.name], [identb.name])
            VH = [sb0(f"VH{i}", [128, 256]) for i in range(4)]

            def bias_head(h):
                vh = VH[h % 4]
                for kt_ in range(2):
                    P.dma("sp", vh[:, 128 * kt_:128 * kt_ + 128],
                          bass.AP(line_d.tensor, 384 * h + 129 - 128 * kt_, [[1, 128], [1, 128]]), w=[vh.name])
                ps, pk = nps()
                P.group("pe", [lambda e, o=ps[:, 128 * k_:128 * k_ + 128], r_=vh[:, 128 * k_:128 * k_ + 128]:
                               e.matmul(o, antif[:], r_, start=True, stop=True) for k_ in range(2)], r=[vh.name, antif.name], w=[pk])
                copy("act", biasT[:, 256 * h:256 * h + 256], ps[:, 0:256], [pk], ["biasT"])
            P.op("dve", lambda e: e.memset(ones64[:], 1.0), w=["ones64"])
            P.op("dve", lambda e: e.memset(carry[:], 0.0), w=["carry"])
            act(esk[:], sks[:], AF.Exp, [sks.name], ["esk"])
            ts("dve", ar[:], stp[:], 0.125, None, ALU.mult, None, [stp.name], ["ar"])
            taylor_exp(stp[:], ar[:], 10, stp.name, "ar")
            for _ in range(3):
                tt("dve", stp[:], stp[:], stp[:], ALU.mult, [stp.name], [stp.name])
            tt("dve", ar[:], lamr[:], stp[:], ALU.mult, [lamr.name, stp.name], ["ar"])
            tt("dve", ai[:], lami[:], stp[:], ALU.mult, [lami.name, stp.name], ["ai"])
            MAR = sb0("MAR", [128, 9 * 32]); ANG = sb0("ANG", [128, 2 * 9 * 32]); TMPA = sb0("TMPA", [128, 576])
            KI = sb0("KI", [128, 576], I32)
            for m in range(9):
                ts("dve", MAR[:, 32 * m:32 * m + 32], ar[:], float(m), None, ALU.mult, None, ["ar"], ["MAR"])
                ts("dve", ANG[:, 32 * m:32 * m + 32], ai[:], float(m), None, ALU.mult, None, ["ai"], ["ANG"])
                ts("dve", ANG[:, 288 + 32 * m:288 + 32 * m + 32], ai[:], float(m), PI / 2, ALU.mult, ALU.add, ["ai"], ["ANG"])
            sin_reduced(ANG, TMPA, KI)
            MAG = sb0("MAG", [128, 9 * 32])
            taylor_exp(MAG[:], MAR[:], 8, "MAG", "MAR")
            PR = sb0("PR", [128, 288]); PIm = sb0("PIm", [128, 288])
            tt("dve", PR[:], MAG[:], ANG[:, 288:576], ALU.mult, ["MAG", "ANG"], ["PR"])
            tt("dve", PIm[:], MAG[:], ANG[:, 0:288], ALU.mult, ["MAG", "ANG"], ["PIm"])
            copy("dve", CA[:, 0:32], PR[:, 256:288], ["PR"], ["CA"])
            copy("dve", CA[:, 32:64], PR[:, 256:288], ["PR"], ["CA"])
            copy("dve", CBp[:], PIm[:, 256:288], ["PIm"], ["CBp"])
            ts("dve", CBn[:], PIm[:, 256:288], -1.0, None, ALU.mult, None, ["PIm"], ["CBn"])
            nr = sb0("nr", [128, 32]); den = sb0("den", [128, 32]); t1 = sb0("t1", [128, 32]); t2 = sb0("t2", [128, 32])
            kr = sb0("kr", [128, 32]); ki_ = sb0("ki_", [128, 32])
            ts("dve", nr[:], PR[:, 32:64], -1.0, None, ALU.add, None, ["PR"], ["nr"])
            tt("dve", den[:], lamr[:], lamr[:], ALU.mult, [lamr.name], ["den"])
            tt("dve", t1[:], lami[:], lami[:], ALU.mult, [lami.name], ["t1"])
            tt("dve", den[:], den[:], t1[:], ALU.add, ["t1"], ["den"])
            P.op("dve", lambda e: e.reciprocal(den[:], den[:]), r=["den"], w=["den"])
            tt("dve", t1[:], nr[:], lamr[:], ALU.mult, ["nr"], ["t1"])
            tt("dve", t2[:], PIm[:, 32:64], lami[:], ALU.mult, ["PIm"], ["t2"])
            tt("dve", t1[:], t1[:], t2[:], ALU.add, ["t2"], ["t1"])
            tt("dve", kr[:], t1[:], den[:], ALU.mult, ["t1", "den"], ["kr"])
            tt("dve", t1[:], PIm[:, 32:64], lamr[:], ALU.mult, ["PIm"], ["t1"])
            tt("dve", t2[:], nr[:], lami[:], ALU.mult, ["nr"], ["t2"])
            tt("dve", t1[:], t1[:], t2[:], ALU.subtract, ["t2"], ["t1"])
            tt("dve", ki_[:], t1[:], den[:], ALU.mult, ["t1", "den"], ["ki_"])
            OMR = sb0("OMR", [128, 256]); OMI = sb0("OMI", [128, 256]); T8 = sb0("T8", [128, 256])
            pr3 = PR[:, 0:256].rearrange("p (m q) -> p m q", q=32); pi3 = PIm[:, 0:256].rearrange("p (m q) -> p m q", q=32)
            krb = kr[:].unsqueeze(1).to_broadcast([128, 8, 32]); kib = ki_[:].unsqueeze(1).to_broadcast([128, 8, 32])
            omr3 = OMR[:].rearrange("p (m q) -> p m q", q=32); omi3 = OMI[:].rearrange("p (m q) -> p m q", q=32)
            t83 = T8[:].rearrange("p (m q) -> p m q", q=32)
            tt("dve", omr3, pr3, krb, ALU.mult, ["PR", "kr"], ["OMR"])
            tt("dve", t83, pi3, kib, ALU.mult, ["PIm", "ki_"], ["T8"])
            tt("dve", OMR[:], OMR[:], T8[:], ALU.subtract, ["T8"], ["OMR"])
            tt("dve", omi3, pi3, krb, ALU.mult, ["PIm", "kr"], ["OMI"])
            tt("dve", t83, pr3, kib, ALU.mult, ["PR", "ki_"], ["T8"])
            tt("dve", OMI[:], OMI[:], T8[:], ALU.add, ["T8"], ["OMI"])
            TB = sb0("TB", [128, 512]); A0r = sb0("A0r", [128, 512]); A0i = sb0("A0i", [128, 512])
            sA = ExitStack()
            AR = sb("AR", [128, 4096], F32, sA); AI = sb("AI", [128, 4096], F32, sA)
            ar4 = AR[:].rearrange("p (q j h) -> p q j h", j=8, h=16); ai4 = AI[:].rearrange("p (q j h) -> p q j h", j=8, h=16)
            br3 = brs[:].rearrange("p (q h) -> p q h", h=16); bi3 = bis[:].rearrange("p (q h) -> p q h", h=16)
            tb3 = TB[:].rearrange("p (q h) -> p q h", h=16)
            for j in range(8):
                m = 7 - j
                orb = OMR[:, 32 * m:32 * m + 32].unsqueeze(2).to_broadcast([128, 32, 16])
                oib = OMI[:, 32 * m:32 * m + 32].unsqueeze(2).to_broadcast([128, 32, 16])
                tt("dve", ar4[:, :, j, :], br3, orb, ALU.mult, [brs.name, "OMR"], ["AR"])
                tt("dve", tb3, bi3, oib, ALU.mult, [bis.name, "OMI"], ["TB"])
                tt("dve", ar4[:, :, j, :], ar4[:, :, j, :], tb3, ALU.subtract, ["TB"], ["AR"])
                tt("dve", ai4[:, :, j, :], bi3, orb, ALU.mult, [bis.name, "OMR"], ["AI"])
                tt("dve", tb3, br3, oib, ALU.mult, [brs.name, "OMI"], ["TB"])
                tt("dve", ai4[:, :, j, :], ai4[:, :, j, :], tb3, ALU.add, ["TB"], ["AI"])
            for src, dst, nm in ((AR, W1re, "W1re"), (AI, W1im, "W1im")):
                for q0 in range(0, 32, 4):
                    ps, pk = nps()
                    P.group("pe", [lambda e, o=ps[:, 128 * i:128 * i + 128], s_=src[:, 128 * (q0 + i):128 * (q0 + i) + 128]:
                                   e.transpose(o, s_, identf[:]) for i in range(4)], r=[src.name, identf.name], w=[pk])
                    copy(evac_eng(), dst[:, 128 * q0:128 * q0 + 512], ps[:, :], [pk], [nm])
            copy("dve", A0r[:].rearrange("p (q h) -> p q h", h=16), ar4[:, :, 7, :], ["AR"], ["A0r"])
            copy("dve", A0i[:].rearrange("p (q h) -> p q h", h=16), ai4[:, :, 7, :], ["AI"], ["A0i"])
            P.barrier()
            sA.close()
            ER = sb0("ER", [128, 32 * 9 * 16]); NEI = sb0("NEI", [128, 32 * 9 * 16])
            er4 = ER[:].rearrange("p (q k h) -> p q k h", k=9, h=16); ne4 = NEI[:].rearrange("p (q k h) -> p q k h", k=9, h=16)
            cr3 = crs[:].rearrange("p (q h) -> p q h", h=16); ci3 = cis[:].rearrange("p (q h) -> p q h", h=16)
            for k in range(9):
                prb = PR[:, 32 * k:32 * k + 32].unsqueeze(2).to_broadcast([128, 32, 16])
                pib = PIm[:, 32 * k:32 * k + 32].unsqueeze(2).to_broadcast([128, 32, 16])
                tt("dve", er4[:, :, k, :], cr3, prb, ALU.mult, [crs.name, "PR"], ["ER"])
                tt("dve", tb3, ci3, pib, ALU.mult, [cis.name, "PIm"], ["TB"])
                tt("dve", er4[:, :, k, :], er4[:, :, k, :], tb3, ALU.subtract, ["TB"], ["ER"])
                tt("dve", ne4[:, :, k, :], cr3, pib, ALU.mult, [crs.name, "PIm"], ["NEI"])
                tt("dve", tb3, ci3, prb, ALU.mult, [cis.name, "PR"], ["TB"])
                P.op("dve", lambda e, o=ne4[:, :, k, :]: e.scalar_tensor_tensor(o, o, -1.0, tb3, ALU.mult, ALU.subtract), r=["TB"], w=["NEI"])
            P.op("pool", lambda e: e.memset(Tm[:], 0.0), w=["Tm"])
            cast_all(True)
            cast_all(False)
            wc4 = Wc[:].rearrange("p (q r c) -> p q r c", r=2, c=128)
            for ri, src in ((0, ER), (1, NEI)):
                s4 = src[:].rearrange("p (q c) -> p q c", c=144)
                copy("dve", wc4[:, :, ri, :], s4[:, :, 16:144], [src.name], ["Wc"])
            WBF = [sb0(f"WBF{i}", [128, 2 * 2 * 128]) for i in range(4)]
            KTb = sb0("KTb", [16, 64 * 128], BF16)
            for wb in WBF:
                P.op("dve", lambda e, t_=wb: e.memset(t_[:], 0.0), w=[wb.name])
            dI3 = dIs[:].rearrange("p (g h) -> p g h", h=16)
            kt4 = KTb[:].rearrange("p (g k h) -> p g k h", k=8, h=16)
            for q in range(32):
                if q % 2 == 0:
                    bias_head(q // 2)
                wb = WBF[q % 4]
                w4 = wb[:].rearrange("p (r g c) -> p r g c", r=2, g=2)
                for g2 in range(2):
                    rows = slice(64 * g2, 64 * g2 + 64)
                    copy("act", w4[rows, 0, g2, :], ER[rows, 144 * q:144 * q + 128], ["ER"], [wb.name])
                    copy("act", w4[rows, 1, g2, :], NEI[rows, 144 * q:144 * q + 128], ["NEI"], [wb.name])
                ps, pk = nps()
                P.group("pe", [
                    lambda e, o=ps[0:16, 0:256], l=A0r[:, 16 * q:16 * q + 16], r_=wb[:, 0:256]: e.matmul(o, l, r_, start=True, stop=False),
                    lambda e, o=ps[0:16, 0:256], l=A0i[:, 16 * q:16 * q + 16], r_=wb[:, 256:512]: e.matmul(o, l, r_, start=False, stop=True),
                ], r=["A0r", "A0i", wb.name], w=[pk])
                copy("dve", KTb[:, 256 * q:256 * q + 256], ps[0:16, 0:256], [pk], ["KTb"])
                tt("dve", kt4[:, 2 * q:2 * q + 2, 0, :], ps[0:16, 0:256].rearrange("p (g k h) -> p g k h", g=2, h=16)[:, :, 0, :],
                   dI3[:, 2 * q:2 * q + 2, :], ALU.add, [pk, dIs.name], ["KTb"])
            tm3 = Tm[:].rearrange("p (g c) -> p g c", c=128)
            kt3 = KTb[:].rearrange("p (g c) -> p g c", c=128)
            for jp in range(8):
                P.dma("sp", tm3[16 * jp:16 * jp + 16, :, 16 * jp:128], kt3[:, :, 0:(8 - jp) * 16], r=["KTb", "Tm"], w=[("Tm", jp)])
            for i, (t_, nm_) in enumerate(((W1re, "W1re"), (W1im, "W1im"))):
                P.dma("sp", s_w1[i], t_[:], r=[nm_])
            for i in range(2):
                P.dma("sp", s_wc[i], Wc[:, 4096 * i:4096 * i + 4096], r=["Wc"])
                P.dma("sp", s_tm[i], Tm[:, 4096 * i:4096 * i + 4096], r=["Tm"] + [("Tm", jp) for jp in range(8)])
            P.barrier()
            for nm_, t_ in (("PR", PR), ("PIm", PIm), ("kr", kr), ("ki_", ki_), ("W1re", W1re), ("W1im", W1im), ("Wc", Wc), ("Tm", Tm),
                            ("biasT", biasT), ("esk", esk), ("CA", CA), ("CBn", CBn), ("ER", ER), ("NEI", NEI)):
                dump(nm_, t_, "none")
            dump("KTb", KTb, "none", parts=16)
            P.barrier()

        xT = sb("xT", [128, 16 * SPAN], BF16)
        gyT = sb("gyT", [128, 8 * SPAN], BF16)
        kT = sb("kT", [128, 4 * 640], BF16)
        vext = sb("vext", [128, 5 * 4 * 64], BF16)
        wbuf.extend(sb(f"wbuf{i}", [128, 16 * 256], BF16) for i in range(4))
        xb_t = [sb(f"xb{i}", [128, D], BF16) for i in range(2)]
        xb = [(t_[:, :], t_.name) for t_ in xb_t]
        xpre = {"n": 0}
        P.op("dve", lambda e: e.memset(kT[:], 0.0), w=["kT"])
        P.op("dve", lambda e: e.memset(vext[:], 0.0), w=["vext"])
        xT3 = xT[:].rearrange("p (k t) -> p k t", t=SPAN)
        gy3 = gyT[:].rearrange("p (c t) -> p c t", t=SPAN)
        kT3 = kT[:].rearrange("p (h t) -> p h t", t=640)
        vx4 = vext[:].rearrange("p (t h d) -> p t h d", h=4, d=64)
        bT4 = biasT[:].rearrange("p (h k q) -> p h k q", k=2, q=128)

        def build_xT(xsrc, row0, bufs, ev=None):
            for t4 in range(SPAN // 128):
                b_, bk_ = bufs[t4 % len(bufs)]
                if t4 >= xpre["n"]:
                    P.dma("pool", b_, xsrc[row0 + 128 * t4:row0 + 128 * t4 + 128, :], w=[bk_])
                for half in range(2):
                    ps, pk = npb()
                    P.group("pe", [lambda e, o=ps[:, 128 * i:128 * i + 128], s2=b_[:, 128 * (8 * half + i):128 * (8 * half + i) + 128]:
                                   e.transpose(o, s2, identb[:]) for i in range(8)], r=[bk_], w=[pk])
                    copy(ev or evac_eng(), xT3[:, 8 * half:8 * half + 8, 128 * t4:128 * t4 + 128],
                         ps[:, :].rearrange("p (k t) -> p k t", t=128), [pk], ["xT"])
            xpre["n"] = 0

        def prefetch_x(xsrc, row0, bufs, t_lo, t_hi):
            for t4 in range(t_lo, t_hi):
                P.dma("pool", bufs[t4][0], xsrc[row0 + 128 * t4:row0 + 128 * t4 + 128, :], w=[bufs[t4][1]])
            xpre["n"] = t_hi

        def ssm_state(sc, own, pfx=None):
            ubm = sb("ubm", [128, 8 * 1024], BF16, sc)
            U = sb("U", [128, 64 * NB], BF16, sc)
            X = sb("X", [128, 2 * 32 * NB], F32, sc)
            ub4 = ubm[:, 0:4096].rearrange("p (g j h) -> p g j h", j=4, h=16)
            U3 = U[:].rearrange("p (g b) -> p g b", b=NB)
            X4 = X[:].rearrange("p (r q b) -> p r q b", r=2, b=NB)
            xTj = xT[:].rearrange("p (k b j) -> p k j b", j=8, b=NB)
            for cb in range(4):
                wv, wk = load_c(s_in, "s_in", cb)
                for j4 in range(4):
                    ps, pk = nps()
                    fns = []
                    for kt in range(16):
                        for jh in range(2):
                            fns.append(lambda e, o=ps[64 * jh:64 * jh + 64, 0:256], l=xTj[:, kt, 4 * jh + j4, :], r_=wv[:, kt, :], kt=kt:
                                       e.matmul(o, l, r_, start=(kt == 0), stop=(kt == 15)))
                    P.group("pe", fns, r=["xT", wk], w=[pk])
                    copy("act" if (pfx is not None and cb == 0) else evac_eng(),
                         ub4[:, 16 * cb:16 * cb + 16, j4, :], ps[:, 0:256].rearrange("p (g h) -> p g h", h=16), [pk], ["ubm"])
                if own and cb in (0, 2) and late.get("pending"):
                    late["pending"].pop(0)()
            for g0 in range(0, 64, 16):
                ps, pk = npb()
                fns = []
                for i in range(16):
                    for jh in range(2):
                        rows = slice(64 * jh, 64 * jh + 64)
                        fns.append(lambda e, o=ps[rows, NB * i:NB * i + NB], s_=ubm[rows, 64 * (g0 + i):64 * (g0 + i) + 64],
                                   idn=identb[rows, 64 * jh:64 * jh + 64]: e.transpose(o, s_, idn))
                P.group("pe", fns, r=["ubm"], w=[pk])
                copy(evac_eng(), U[:, NB * g0:NB * g0 + NB * 16], ps[:, 0:NB * 16], [pk], ["U"])
            w1r, w1rk = load_c(s_w1, "s_w1", 0, 16, 128)
            w1i, w1ik = load_c(s_w1, "s_w1", 1, 16, 128)
            w13 = {0: w1r, 1: w1i}
            for q0 in range(0, 32, 4):
                ps, pk = nps()
                ps4 = ps[:, :].rearrange("p (q r b) -> p q r b", r=2, b=NB)
                fns = []
                for i in range(4):
                    q = q0 + i
                    for g2 in range(2):
                        for ri in range(2):
                            fns.append(lambda e, o=ps4[64 * g2:64 * g2 + 64, i, ri, :], l=w13[ri][:, q, 64 * g2:64 * g2 + 64],
                                       r_=U3[:, 2 * q + g2, :]: e.matmul(o, l, r_, start=True, stop=True))
                P.group("pe", fns, r=["U", w1rk, w1ik], w=[pk])
                copy(evac_eng(), X4[:, :, q0:q0 + 4, :], ps4.rearrange("p q r b -> p r q b"), [pk], ["X"])
            tA = sb("tA", [128, 64], F32, sc); tB = sb("tB", [128, 64], F32, sc)
            if pfx is not None:
                t1 = sb("pt1", [128, 32 * NB], F32, sc); t2 = sb("pt2", [128, 32 * NB], F32, sc)
                red = sb("pred", [128, 64], F32, sc)
                xr, xi = X[:, 0:32 * NB], X[:, 32 * NB:64 * NB]
                for ri_, (a_, b_, op_) in enumerate(((pfx["PWr"], pfx["PWi"], ALU.subtract), (pfx["PWi"], pfx["PWr"], ALU.add))):
                    tt("dve", t1[:], xr, a_[:], ALU.mult, ["X"], ["pt1"])
                    tt("dve", t2[:], xi, b_[:], ALU.mult, ["X"], ["pt2"])
                    tt("dve", t1[:], t1[:], t2[:], op_, ["pt1", "pt2"], ["pt1"])
                    P.op("dve", lambda e, o=red[:, 32 * ri_:32 * ri_ + 32], i_=t1[:].rearrange("p (q b) -> p q b", b=NB):
                         e.reduce_sum(o, i_, mybir.AxisListType.X), r=["pt1"], w=["pred"])
                c3 = carry[:].rearrange("p (r q) -> p r q", r=2)
                ta3 = tA[:].rearrange("p (r q) -> p r q", r=2); tb3_ = tB[:].rearrange("p (r q) -> p r q", r=2)
                tt("dve", ta3, c3, pfx["CA64"][:].rearrange("p (r q) -> p r q", r=2), ALU.mult, ["carry"], ["tA"])
                tt("dve", tb3_[:, 0, :], c3[:, 1, :], pfx["CB64n"][:], ALU.mult, ["carry"], ["tB"])
                tt("dve", tb3_[:, 1, :], c3[:, 0, :], pfx["CB64p"][:], ALU.mult, ["carry"], ["tB"])
                tt("dve", tA[:], tA[:], tB[:], ALU.add, ["tA", "tB"], ["tA"])
                tt("dve", carry[:], tA[:], red[:], ALU.add, ["tA", "pred"], ["carry"])
                return None
            Sh = None
            if own:
                Sh = sb("Sh", [128, 2 * 2 * 32 * NB], BF16, sc)
                Sh5 = Sh[:].rearrange("p (g r q b) -> p g r q b", g=2, r=2, b=NB)
                P.op("dve", lambda e, o=Sh5[64:128, 0]: e.memset(o, 0.0), w=["Sh"])
                P.op("dve", lambda e, o=Sh5[0:64, 1]: e.memset(o, 0.0), w=["Sh"])
                for g2 in range(2):
                    rows = slice(64 * g2, 64 * g2 + 64)
                    copy("pool", Sh5[rows, g2, :, :, 0], carry[rows, :].rearrange("p (r q) -> p r q", r=2), ["carry"], ["Sh"])
            c3 = carry[:].rearrange("p (r q) -> p r q", r=2)
            ca3 = CA[:].rearrange("p (r q) -> p r q", r=2)
            ta3 = tA[:].rearrange("p (r q) -> p r q", r=2); tb3_ = tB[:].rearrange("p (r q) -> p r q", r=2)
            for b in range(NB):
                prev = c3 if b == 0 else X4[:, :, :, b - 1]
                tt("pool", ta3, prev, ca3, ALU.mult, ["X", "carry"], ["tA"])
                tt("pool", tb3_[:, 0, :], prev[:, 1, :], CBn[:], ALU.mult, ["X", "carry"], ["tB"])
                tt("pool", tb3_[:, 1, :], prev[:, 0, :], CBp[:], ALU.mult, ["X", "carry"], ["tB"])
                tt("pool", ta3, ta3, tb3_, ALU.add, ["tB"], ["tA"])
                tt("pool", X4[:, :, :, b], X4[:, :, :, b], ta3, ALU.add, ["tA"], ["X"])
            copy("pool", c3, X4[:, :, :, NB - 1], ["X", "Sh"], ["carry"])
            return ubm, U3, (Sh, X4), (U, X)

        def ssm_out(ubm, U3, Sh):
            Sh, X4 = Sh
            Sh5 = Sh[:].rearrange("p (g r q b) -> p g r q b", g=2, r=2, b=NB)
            for g2 in range(2):
                rows = slice(64 * g2, 64 * g2 + 64)
                copy("dve" if g2 else "act", Sh5[rows, g2, :, :, 1:NB], X4[rows, :, :, 0:NB - 1], ["X"], ["Sh"])
            gb5 = ubm[:, 0:4096].rearrange("p (c j g h) -> p c g j h", c=8, j=8, g=4, h=16)
            for ct in range(8):
                if ct % 4 == 0:
                    tmv, tmk = load_c(s_tm, "s_tm", ct // 4, 16, 128)
                    wcv, wck = load_c(s_wc, "s_wc", ct // 4, 16, 128)
                ps, pk = nps()
                fns = []
                for g4 in range(4):
                    for step in range(3):
                        for gp in range(2):
                            g = 8 * ct + 4 * gp + g4; q, g2 = g // 2, g % 2
                            o = ps[64 * gp:64 * gp + 64, 128 * g4:128 * g4 + 128]
                            if step == 0:
                                fns.append(lambda e, o=o, l=U3[:, g, :], r_=tmv[:, g % 32, :]: e.matmul(o, l, r_, start=True, stop=False))
                            elif step == 1:
                                fns.append(lambda e, o=o, l=Sh5[:, g2, 0, q, :], r_=wcv[:, 2 * (q % 16), :]: e.matmul(o, l, r_, start=False, stop=False))
                            else:
                                fns.append(lambda e, o=o, l=Sh5[:, g2, 1, q, :], r_=wcv[:, 2 * (q % 16) + 1, :]: e.matmul(o, l, r_, start=False, stop=True))
                P.group("pe", fns, r=["U", "Sh", tmk, wck], w=[pk])
                act(gb5[:, ct, :, :, :], ps[:, :].rearrange("p (g j h) -> p g j h", j=8, h=16), AF.Gelu, [pk, "U"], ["ubm"])
            for ct in range(8):
                ps, pk = npb()
                fns = []
                for j in range(8):
                    for gp in range(2):
                        rows = slice(64 * gp, 64 * gp + 64)
                        fns.append(lambda e, o=ps[rows, NB * j:NB * j + NB], s_=ubm[rows, 512 * ct + 64 * j:512 * ct + 64 * j + 64],
                                   idn=identb[rows, 64 * gp:64 * gp + 64]: e.transpose(o, s_, idn))
                P.group("pe", fns, r=["ubm"], w=[pk])
                copy(evac_eng(), gy3[:, ct, :].rearrange("p (b j) -> p j b", j=8),
                     ps[:, 0:8 * NB].rearrange("p (j b) -> p j b", b=NB), [pk], ["gyT"])

        def proj_fm_unused(col0, ncols_tiles, consume):
            for c2 in range(0, ncols_tiles, 2):
                n = min(2, ncols_tiles - c2)
                wv, wk = load_c(s_in, "s_in", (col0 + 128 * c2) // 256)
                for i in range(n):
                    ps, pk = nps()
                    P.group("pe", [lambda e, o=ps[:, :], l=wv[:, kt, 128 * i:128 * i + 128], r_=xT3[:, kt, :], kt=kt:
                                   e.matmul(o, l, r_, start=(kt == 0), stop=(kt == 15)) for kt in range(16)], r=["xT", wk], w=[pk])
                    consume(c2 + i, ps, pk)

        def kv_proj(tok0, ntok, dst_tile0, kcol0):
            for kh in range(4):
                if kh % 2 == 0:
                    wv, wk = load_c(s_k2, "s_k2", kh // 2, offs=(0, 64, 128, 192))
                ps, pk = nps()
                P.group("pe", [lambda e, o=ps[:, 0:ntok], l=wv[:, kt, 128 * (kh % 2):128 * (kh % 2) + 128], r_=xT3[:, kt, tok0:tok0 + ntok], kt=kt:
                               e.matmul(o, l, r_, start=(kt == 0), stop=(kt == 15)) for kt in range(16)], r=["xT", wk], w=[pk])
                copy(evac_eng(), kT3[:, kh, kcol0:kcol0 + ntok], ps[:, 0:ntok], [pk], ["kT"])
            wv, wk = load_c(s_in, "s_in", 13)
            for t4 in range(ntok // 128):
                ps, pk = nps()
                P.group("pe", [lambda e, o=ps[:, 0:256], l=xT3[:, kt, tok0 + 128 * t4:tok0 + 128 * t4 + 128], r_=wv[:, kt, :], kt=kt:
                               e.matmul(o, l, r_, start=(kt == 0), stop=(kt == 15)) for kt in range(16)], r=["xT", wk], w=[pk])
                copy(evac_eng(), vx4[:, dst_tile0 + t4, :, :], ps[:, 0:256].rearrange("p (h d) -> p h d", d=64), [pk], ["vext"])

        late = {}

        def attn_stage(sp_i, sa):
            ha3 = late["ha3"]; haT = late["haT"]
            if True:
                qT = sb("qT", [128, 2 * SPAN], BF16, sa); qT4 = qT[:].rearrange("p (b h q) -> p b h q", h=2, q=128)
                pT = sb("pT", [128, 4 * 2 * 512], BF16, sa)
                pT5 = pT[:].rearrange("p (b k h q) -> p b k h q", k=2, h=4, q=128)
                lg = [sb(f"lg{i}", [128, 512], F32, sa) for i in range(2)]
                dn = sb("dn", [128, 512], F32, sa); zt2 = sb("zt", [128, 2 * 512], BF16, sa)
                kv_proj(0, SPAN, 1, 128)
                for kh in range(4):
                    wv, wk = load_c(s_in, "s_in", 8 + kh)
                    wza, wzak = load_c(s_in, "s_in", 14 + kh)
                    for hp in range(2):
                        ps, pk = nps()
                        P.group("pe", [lambda e, o=ps[:, :], l=wv[:, kt, 128 * hp:128 * hp + 128], r_=xT3[:, kt, :], kt=kt:
                                       e.matmul(o, l, r_, start=(kt == 0), stop=(kt == 15)) for kt in range(16)], r=["xT", wk], w=[pk])
                        act(qT4[:, :, hp, :], ps[:, :].rearrange("p (b q) -> p b q", q=128), AF.Identity, [pk], ["qT"], scale=0.125)
                    for blk in range(4):
                        for kt_ in range(2):
                            l_ = lg[(2 * blk + kt_) % 2]
                            pp, pks = nps2()
                            for half in range(2):
                                rows = slice(64 * half, 64 * half + 64)
                                P.group("pe", [lambda e, o=pp[:, 512 * half:512 * half + 256],
                                               l=kT3[rows, kh, 128 * (blk + kt_):128 * (blk + kt_) + 128],
                                               r_=qT[rows, 256 * blk:256 * blk + 256]: e.matmul(o, l, r_, start=True, stop=True)],
                                        r=["kT", "qT"], w=[pks[half]])
                            tt("dve", l_[:].rearrange("p (hf hp q) -> p hf hp q", hf=2, q=128),
                               pp.rearrange("p (b hp q) -> p b hp q", b=2, q=128)[:, :, 0:2, :],
                               bT4[:, 4 * kh:4 * kh + 4, kt_, :].rearrange("p (hp hf) q -> p hf hp q", hf=2),
                               ALU.add, [pks[0], pks[1], "biasT"], [l_.name])
                            if sp_i == 0 and blk == 0 and kt_ == 0:
                                P.op("act", lambda e, o=pT5[:, blk, kt_, :, :], i_=l_[:].rearrange("p (h q) -> p h q", q=128):
                                     e.activation(o, i_, AF.Exp, bias=hneg[:, 0:1]), r=[l_.name, "hneg_s"], w=["pT"])
                            else:
                                act(pT5[:, blk, kt_, :, :], l_[:].rearrange("p (h q) -> p h q", q=128), AF.Exp, [l_.name], ["pT"])
                    for hp in range(2):
                        ps, pk = nps()
                        P.group("pe", [lambda e, o=ps[:, :], l=wza[:, kt, 128 * hp:128 * hp + 128], r_=xT3[:, kt, :], kt=kt:
                                       e.matmul(o, l, r_, start=(kt == 0), stop=(kt == 15)) for kt in range(16)], r=["xT", wzak], w=[pk])
                        act(zt2[:, 512 * hp:512 * hp + 512], ps[:, :], AF.Silu, [pk], ["zt"])
                    for hp in range(2):
                        hpg = 2 * kh + hp
                        zt = zt2[:, 512 * hp:512 * hp + 512]
                        pv, pvk = nps(); rs, rsk = nps()
                        fns = []
                        for blk in range(4):
                            for kt_ in range(2):
                                for half in range(2):
                                    hq = 2 * half + hp
                                    fns.append(lambda e, o=pv[64 * half:64 * half + 64, 128 * blk:128 * blk + 128], l=vx4[:, blk + kt_, kh, :],
                                               r_=pT5[:, blk, kt_, hq, :], kt_=kt_: e.matmul(o, l, r_, start=(kt_ == 0), stop=(kt_ == 1)))
                            for kt_ in range(2):
                                for half in range(2):
                                    hq = 2 * half + hp
                                    fns.append(lambda e, o=rs[64 * half:64 * half + 64, 128 * blk:128 * blk + 128], r_=pT5[:, blk, kt_, hq, :], kt_=kt_:
                                               e.matmul(o, ones64[:], r_, start=(kt_ == 0), stop=(kt_ == 1)))
                        P.group("pe", fns, r=["pT", "vext", "ones64"], w=[pvk, rsk])
                        P.op("act", lambda e, r2=rs, h_=hpg: e.activation(dn[:], r2[:, :], AF.Ln, bias=esk[:, h_:h_ + 1]), r=[rsk, "esk"], w=["dn"])
                        P.op("act", lambda e: e.activation(dn[:], dn[:], AF.Exp, scale=-1.0), r=["dn"], w=["dn"])
                        tt("dve", dn[:], dn[:], pv[:, :], ALU.mult, [pvk], ["dn"])
                        tt("dve", ha3[:, hpg, :], dn[:], zt, ALU.mult, ["dn", "zt"], ["haT"])
                copy("dve", kT3[:, :, 0:128], kT3[:, :, 512:640], ["kT"], ["kT"])
                copy("dve", vx4[:, 0, :, :], vx4[:, 4, :, :], ["vext"], ["vext"])

        def merge_half(attn, sm):
            tg = "a" if attn else "s"
            ha3, hs3, mT3 = late["ha3"], late["hs3"], late["mT3"]
            g1 = sb("g1", [128, 512], F32, sm); m1 = late["ftmp"]
            h3, hn = (ha3, "haT") if attn else (hs3, "hsT")
            for d2 in range(8):
                wg3, wgk = load_c(s_in, "s_in", (26 if attn else 18) + d2)
                wb3, wbk = load_c(s_ba if attn else s_bs, "s_ba" if attn else "s_bs", d2, 8)
                for i in range(2):
                    dt = 2 * d2 + i
                    ps, pk = nps()
                    P.group("pe", [lambda e, o=ps[:, :], l=wg3[:, kt, 128 * i:128 * i + 128], r_=xT3[:, kt, :], kt=kt:
                                   e.matmul(o, l, r_, start=(kt == 0), stop=(kt == 15)) for kt in range(16)], r=["xT", wgk], w=[pk])
                    act(g1[:], ps[:, :], AF.Sigmoid, [pk], [g1.name])
                    ps, pk = nps()
                    P.group("pe", [lambda e, o=ps[:, :], l=wb3[:, ct, 128 * i:128 * i + 128], r_=h3[:, ct, :], ct=ct:
                                   e.matmul(o, l, r_, start=(ct == 0), stop=(ct == 7)) for ct in range(8)], r=[hn, wbk], w=[pk])
                    if attn:
                        tt("dve", mT3[:, dt, :], ps[:, :], g1[:], ALU.mult, [pk, g1.name], ["mT"])
                    else:
                        tt("dve", m1[:], ps[:, :], g1[:], ALU.mult, [pk, g1.name], ["ftmp"])
                        tt("dve", mT3[:, dt, :], mT3[:, dt, :], m1[:], ALU.add, ["ftmp"], ["mT"])

        def main_stage(sp_i, sc, ubm, U, X, Sh, pre_tail, defer):
            hs3, mT3, hsT, mT = late["hs3"], late["mT3"], late["hsT"], late["mT"]
            if True:
                sg = sc
                zs = sb("zs", [128, 512], BF16, sg); sgb = sb("sgb", [128, 512], F32, sg); ga_ = late["ftmp"]
                for et in range(8):
                    if et % 2 == 0:
                        wv2, wk2 = load_c(s_in, "s_in", 4 + et // 2)
                    ps, pk = nps()
                    P.group("pe", [lambda e, o=ps[:, :], l=wv2[:, kt, 128 * (et % 2):128 * (et % 2) + 128], r_=xT3[:, kt, :], kt=kt:
                                   e.matmul(o, l, r_, start=(kt == 0), stop=(kt == 15)) for kt in range(16)], r=["xT", wk2], w=[pk])
                    act(zs[:], ps[:, :], AF.Silu, [pk], ["zs"])
                    wg3, wgk = load_c(s_glu, "s_glu", et, 8)
                    pa, pak = nps(); pb, pbk = nps()
                    P.group("pe", [lambda e, o=pa[:, :], l=wg3[:, ct, 0:128], r_=gy3[:, ct, :], ct=ct:
                                   e.matmul(o, l, r_, start=(ct == 0), stop=(ct == 7)) for ct in range(8)], r=["gyT", wgk], w=[pak])
                    P.group("pe", [lambda e, o=pb[:, :], l=wg3[:, ct, 128:256], r_=gy3[:, ct, :], ct=ct:
                                   e.matmul(o, l, r_, start=(ct == 0), stop=(ct == 7)) for ct in range(8)], r=["gyT", wgk], w=[pbk])
                    act(sgb[:], pb[:, :], AF.Sigmoid, [pbk], ["sgb"])
                    tt("dve", ga_[:], pa[:, :], sgb[:], ALU.mult, [pak, "sgb"], ["ftmp"])
                    tt("dve", hs3[:, et, :], ga_[:], zs[:], ALU.mult, ["ftmp", "zs"], ["hsT"])
            merge_half(False, sc)
            if True:
                so = sc
                rrA = ubm[:].bitcast(F32).rearrange("p (t c) -> p t c", c=D)
                rrB = Sh[:].bitcast(F32).rearrange("p (t c) -> p t c", c=D)
                rts = [(rrA[:, 0, :], "ubm"), (rrA[:, 1, :], "ubm"), (rrB[:, 0, :], "Sh"), (rrB[:, 1, :], "Sh")]
                sq = U[:].bitcast(F32)
                gch = X[:, 0:D]; bch = X[:, D:2 * D]
                st = sb("lnst", [128, 8], F32, so)
                for t4 in range(4):
                    rt, rk = rts[t4]
                    P.dma("pool", rt, xo[SPAN * sp_i + 128 * t4:SPAN * sp_i + 128 * t4 + 128, :], w=[rk])
                P.dma("pool", gch, lng_d, w=["X"]); P.dma("pool", bch, lnb_d, w=["X"])

                def layer_norm(t4):
                    rt, rk = rts[t4]
                    P.op("dve", lambda e: e.reduce_sum(st[:, 0:1], rt, mybir.AxisListType.X), r=[rk], w=["lnst"])
                    tt("dve", sq, rt, rt, ALU.mult, [rk], ["U"])
                    P.op("dve", lambda e: e.reduce_sum(st[:, 1:2], sq, mybir.AxisListType.X), r=["U"], w=["lnst"])
                    ts("dve", st[:, 2:3], st[:, 0:1], 1.0 / D, None, ALU.mult, None, ["lnst"], ["lnst"])
                    tt("dve", st[:, 3:4], st[:, 2:3], st[:, 2:3], ALU.mult, ["lnst"], ["lnst"])
                    ts("dve", st[:, 4:5], st[:, 1:2], 1.0 / D, float(LN_EPS), ALU.mult, ALU.add, ["lnst"], ["lnst"])
                    tt("dve", st[:, 4:5], st[:, 4:5], st[:, 3:4], ALU.subtract, ["lnst"], ["lnst"])
                    P.op("act", lambda e: e.activation(st[:, 5:6], st[:, 4:5], AF.Sqrt), r=["lnst"], w=["lnst"])
                    P.op("dve", lambda e: e.reciprocal(st[:, 5:6], st[:, 5:6]), r=["lnst"], w=["lnst"])
                    P.op("dve", lambda e: e.scalar_tensor_tensor(st[:, 6:7], st[:, 2:3], -1.0, st[:, 5:6], ALU.mult, ALU.mult),
                         r=["lnst"], w=["lnst"])
                    P.op("act", lambda e: e.activation(rt, rt, AF.Identity, bias=st[:, 6:7], scale=st[:, 5:6]), r=["lnst"], w=[rk])
                    tt("dve", rt, rt, gch, ALU.mult, ["X"], [rk])
                    tt("dve", rt, rt, bch, ALU.add, ["X"], [rk])
                    P.dma("pool", y_out[SPAN * sp_i + 128 * t4:SPAN * sp_i + 128 * t4 + 128, :], rt, r=[rk])

                for th in range(2):
                    for cb in range(8):
                        wv, wk = load_c(s_out, "s_out", cb)
                        for t2 in range(2):
                            t4 = 2 * th + t2
                            rt, rk = rts[t4]
                            ps, pk = nps()
                            P.group("pe", [lambda e, o=ps[:, 0:256], l=mT3[:, dt, 128 * t4:128 * t4 + 128], r_=wv[:, dt, :], dt=dt:
                                           e.matmul(o, l, r_, start=(dt == 0), stop=(dt == 15)) for dt in range(16)], r=["mT", wk], w=[pk])
                            P.op("dve", lambda e, o=rt[:, 256 * cb:256 * cb + 256], p_=ps[:, 0:256]:
                                 e.scalar_tensor_tensor(o, o, float(ALPHA), p_, ALU.mult, ALU.add), r=[pk], w=[rk])
                        if th == 1 and cb == 1:
                            layer_norm(0)
                        if th == 1 and cb == 4:
                            layer_norm(1)
                pre_tail()
                if defer:
                    late["pending"] = [lambda: layer_norm(2), lambda: layer_norm(3)]
                else:
                    layer_norm(2)
                    layer_norm(3)

        n_pre = dbg.get("n_pre", NSPAN)
        n_own = dbg.get("n_own", NSPAN)
        spre = ExitStack()
        PWr = sb("PWr", [128, 32 * NB], F32, spre); PWi = sb("PWi", [128, 32 * NB], F32, spre)
        CA64 = sb("CA64", [128, 64], F32, spre); CB64n = sb("CB64n", [128, 32], F32, spre); CB64p = sb("CB64p", [128, 32], F32, spre)
        with ExitStack() as sw:
            NM = 65
            msc = sb("msc_s", [128, NM], F32, sw)
            ANGx = sb("ANGx", [128, 2 * NM * 32], F32, sw); TMPx = sb("TMPx", [128, 2 * NM * 32], F32, sw)
            KIx = sb("KIx", [128, 2 * NM * 32], I32, sw); MARx = sb("MARx", [128, NM * 32], F32, sw)
            P.dma("sp", msc[:], msc_d, w=["msc_s"])
            an4 = ANGx[:].rearrange("p (s m q) -> p s m q", s=2, q=32)
            mb = msc[:].unsqueeze(2).to_broadcast([128, NM, 32])
            tt("dve", an4[:, 0], mb, ai[:].unsqueeze(1).to_broadcast([128, NM, 32]), ALU.mult, ["msc_s", "ai"], ["ANGx"])
            ts("dve", an4[:, 1], an4[:, 0], PI / 2, None, ALU.add, None, ["ANGx"], ["ANGx"])
            sin_reduced(ANGx, TMPx, KIx)
            ma3 = MARx[:].rearrange("p (m q) -> p m q", q=32)
            tt("dve", ma3, mb, ar[:].unsqueeze(1).to_broadcast([128, NM, 32]), ALU.mult, ["msc_s", "ar"], ["MARx"])
            act(MARx[:], MARx[:], AF.Exp, ["MARx"], ["MARx"])
            tt("dve", PWr[:].rearrange("p (q m) -> p m q", m=NB), ma3[:, 0:NB, :], an4[:, 1, 0:NB, :], ALU.mult, ["MARx", "ANGx"], ["PWr"])
            tt("dve", PWi[:].rearrange("p (q m) -> p m q", m=NB), ma3[:, 0:NB, :], an4[:, 0, 0:NB, :], ALU.mult, ["MARx", "ANGx"], ["PWi"])
            tt("dve", CA64[:, 0:32], ma3[:, NB, :], an4[:, 1, NB, :], ALU.mult, ["MARx", "ANGx"], ["CA64"])
            copy("dve", CA64[:, 32:64], CA64[:, 0:32], ["CA64"], ["CA64"])
            tt("dve", CB64p[:], ma3[:, NB, :], an4[:, 0, NB, :], ALU.mult, ["MARx", "ANGx"], ["CB64p"])
            ts("dve", CB64n[:], CB64p[:], -1.0, None, ALU.mult, None, ["CB64p"], ["CB64n"])
            P.barrier()
        pfx = dict(PWr=PWr, PWi=PWi, CA64=CA64, CB64n=CB64n, CB64p=CB64p)
        xb4 = xb + [(t_[:, :], t_.name) for t_ in (sb(f"xb{i}", [128, D], BF16, pre_scope) for i in (2, 3))]
        n_cast = -(-len(cast_jobs) // max(1, n_pre))
        for sp_i in range(NSPAN - n_pre, NSPAN):
            sc = pre_scope
            build_xT(xp, SPAN * sp_i, xb4, ev="act")
            ssm_state(sc, own=False, pfx=pfx)
            if sp_i + 1 < NSPAN:
                prefetch_x(xp, SPAN * (sp_i + 1), xb4, 0, 4)
            elif n_own:
                prefetch_x(xo, 0, xb, 0, 2)
            issue_casts(n_cast)
            if sp_i == NSPAN - 1:
                kv_proj(SPAN - 128, 128, 0, 0)
        issue_casts(len(cast_jobs))
        P.barrier()
        pre_scope.close()
        memo.clear()
        spre.close()
        mT = sb("mT", [128, 16 * SPAN], BF16); haT = sb("haT", [128, 8 * SPAN], BF16); hsT = sb("hsT", [128, 8 * SPAN], BF16)
        late.update(mT=mT, haT=haT, hsT=hsT, mT3=mT[:].rearrange("p (c t) -> p c t", t=SPAN),
                    ha3=haT[:].rearrange("p (c t) -> p c t", t=SPAN), hs3=hsT[:].rearrange("p (c t) -> p c t", t=SPAN))
        xbuilt = {}
        for sp_i in range(n_own):
            if True:
                sc = own_scope
                if not xbuilt.get(sp_i):
                    build_xT(xo, SPAN * sp_i, xb)
                late["ftmp"] = sb("ftmp", [128, 512], F32, sc)
                ubm, U3, Sh, (U_, X_) = ssm_state(sc, own=True)
                if sp_i + 1 < n_own:
                    prefetch_x(xo, SPAN * (sp_i + 1), xb, 0, 2)
                attn_stage(sp_i, sc)
                pT_ = memo["pT"]
                xb_own = xb + [(pT_[:, 0:D], "pT"), (pT_[:, D:2 * D], "pT")]
                if sp_i + 1 < n_own:
                    prefetch_x(xo, SPAN * (sp_i + 1), xb_own, 2, 4)
                merge_half(True, sc)
                ssm_out(ubm, U3, Sh)

                def pre_tail(n=sp_i):
                    if n + 1 < n_own:
                        build_xT(xo, SPAN * (n + 1), xb_own)
                        xbuilt[n + 1] = True
                main_stage(sp_i, sc, ubm, U_, X_, Sh[0], pre_tail, sp_i + 1 < n_own)

        P.barrier()
        own_scope.close()
        with nc.Block() as block:
            @block.tensor
            def _(e):
                P.emit("pe", e)

            @block.scalar
            def _(e):
                P.emit("act", e)

            @block.vector
            def _(e):
                P.emit("dve", e)

            @block.gpsimd
            def _(e):
                P.emit("pool", e)

            @block.sync
            def _(e):
                P.emit("sp", e)
    return nc


def _t5_bucket(dist):
    max_exact = 16
    d = np.maximum(dist, 1).astype(np.float32)
    large = max_exact + (np.log(d / np.float32(max_exact)) / np.float32(math.log(128 / max_exact)) * np.float32(16)).astype(np.int32)
    large = np.minimum(large, 31)
    return np.where(dist < max_exact, dist, large)


def kernel(x, w_in, ssm_lambda_re, ssm_lambda_im, ssm_b_re, ssm_b_im, ssm_c_re, ssm_c_im, ssm_d, ssm_log_step,
           w_glu, attn_sinks, rel_bias_table, w_branch_ssm, w_branch_attn, w_out, ln_gain, ln_bias):
    f = lambda a: np.ascontiguousarray(np.asarray(a, dtype=np.float32))
    x = f(x)
    pl = lambda a: f(a.reshape(32, 2, 64).transpose(1, 2, 0).reshape(128, 32))
    plb = lambda a: f(a.reshape(32, 2, 64, 16).transpose(1, 2, 0, 3).reshape(128, 512))
    plc = lambda a: f(a.reshape(32, 2, 16, 64).transpose(1, 3, 0, 2).reshape(128, 512))
    ls = np.asarray(ssm_log_step[0], np.float32).reshape(32, 2)
    lstep = f(np.broadcast_to(ls.T[:, None, :], (2, 64, 32)).reshape(128, 32))
    dI = np.zeros((16, 64, 16), np.float32)
    dd = np.asarray(ssm_d[0], np.float32).reshape(64, 16)
    for h in range(16):
        dI[h, :, h] = dd[:, h]
    sk = np.zeros((128, 8), np.float32)
    sinks = np.asarray(attn_sinks[0], np.float32)
    for hp in range(8):
        sk[0:64, hp] = sinks[2 * hp]
        sk[64:128, hp] = sinks[2 * hp + 1]
    tab = np.asarray(rel_bias_table, np.float32)
    line = np.full((16, 384), NEG, np.float32)
    bk = _t5_bucket(np.arange(128))
    line[:, 128:256] = tab[bk, :].T
    common = {
        "w_in": f(w_in[0]), "w_glu": f(w_glu[0]), "w_bs": f(w_branch_ssm[0]), "w_ba": f(w_branch_attn[0]), "w_out": f(w_out[0]),
        "lamr": pl(np.asarray(ssm_lambda_re[0])), "lami": pl(np.asarray(ssm_lambda_im[0])), "lstep": lstep,
        "br": plb(np.asarray(ssm_b_re[0])), "bi": plb(np.asarray(ssm_b_im[0])),
        "cr": plc(np.asarray(ssm_c_re[0])), "ci": plc(np.asarray(ssm_c_im[0])),
        "dI": f(dI.reshape(16, 1024)), "sk": sk, "line": line,
        "lng": f(np.broadcast_to(np.asarray(ln_gain[0], np.float32)[None, :], (128, D))),
        "lnb": f(np.broadcast_to(np.asarray(ln_bias[0], np.float32)[None, :], (128, D))),
        "idf": np.eye(128, dtype=np.float32), "anti": f(np.eye(128, dtype=np.float32)[::-1]),
        "msc": f(np.broadcast_to(np.array([8.0 * (NB - 1 - b) for b in range(NB)] + [8.0 * NB], np.float32)[None, :], (128, 65))),
    }
    in_maps = []
    for c in range(NCORES):
        b, h = c // 2, c % 2
        m = dict(common)
        m["xo"] = f(x[b, h * TOK:(h + 1) * TOK])
        m["xp"] = f(x[b, 0:TOK]) if h == 1 else np.zeros((TOK, D), np.float32)
        m["hneg"] = np.full((128, 1), 0.0 if h == 1 else NEG, np.float32)
        in_maps.append(m)
    nc = build_program()
    res = run_bass_kernel_spmd(nc, in_maps, core_ids=list(range(NCORES)))
    out = np.zeros((4, 8192, D), np.float32)
    for c in range(NCORES):
        b, h = c // 2, c % 2
        out[b, h * TOK:(h + 1) * TOK] = np.asarray(res.results[c]["y"], np.float32)
    return out
```

```python
import math
import numpy as np
from contextlib import ExitStack
import concourse.bass as bass
import concourse.mybir as mybir
from concourse.bass_utils import run_bass_kernel_spmd

F32 = mybir.dt.float32
BF16 = mybir.dt.bfloat16
I32 = mybir.dt.int32
ALU = mybir.AluOpType
AF = mybir.ActivationFunctionType
PI = float(np.pi)
TWO_PI = float(2 * np.pi)

NCORES = 8
TOK = 4096
SPAN = 512
NB = SPAN // 8
NSPAN = TOK // SPAN
D = 2048
DIN = 8704
ALPHA = 2.0 ** 0.25
LN_EPS = 1e-5
NEG = -30000.0
C_U, C_ZS, C_Q, C_K, C_V, C_ZA, C_G = 0, 1024, 2048, 3072, 3328, 3584, 4608

ENGS = ["pe", "act", "dve", "pool", "sp"]
NDS = 12
NBG = 6


class Prog:
    def __init__(self, nc, st):
        self.nc = nc
        self.q = {e: [] for e in ENGS}
        self.sem = {e: st.enter_context(nc.semaphore("c_" + e)) for e in ENGS if e != "sp"}
        self.cnt = {e: 0 for e in ENGS}
        self.dsem = [st.enter_context(nc.semaphore(f"dq{i}")) for i in range(NDS + NBG)]
        self.dcnt = [0] * (NDS + NBG)
        self.dnext = 0
        self.bnext = 0
        self.seen = {e: {} for e in ENGS}
        self.lastw = {}
        self.readers = {}

    def _need(self, eng, tok, waits, raw=False):
        kind, src, val = tok
        if kind == "e" and src == eng and (eng == "pe" or not raw):
            return
        key = (kind, src)
        if self.seen[eng].get(key, 0) >= val:
            return
        self.seen[eng][key] = val
        waits.append(tok)

    def _deps(self, eng, r, w):
        waits = []
        for b in r:
            t = self.lastw.get(b)
            if t:
                self._need(eng, t, waits, raw=True)
        for b in w:
            t = self.lastw.get(b)
            if t:
                self._need(eng, t, waits)
            for t in self.readers.get(b, ()):
                self._need(eng, t, waits)
        return waits

    def _commit(self, tok, r, w):
        for b in r:
            self.readers.setdefault(b, []).append(tok)
        for b in w:
            self.lastw[b] = tok
            self.readers[b] = []

    def op(self, eng, fn, r=(), w=()):
        waits = self._deps(eng, r, w)
        self.cnt[eng] += 1
        tok = ("e", eng, self.cnt[eng])
        self.q[eng].append((waits, fn, tok))
        self._commit(tok, r, w)

    def group(self, eng, fns, r=(), w=(), rr=()):
        waits = self._deps(eng, r, w)
        self.cnt[eng] += 1
        tok = ("e", eng, self.cnt[eng])
        for i, fn in enumerate(fns):
            self.q[eng].append((waits if i == 0 else [], fn, tok if i == len(fns) - 1 else None))
        self._commit(tok, list(r) + list(rr), w)

    def dma(self, eng, out, in_, r=(), w=(), bg=False):
        waits = self._deps(eng, r, w)
        if bg:
            s = NDS + self.bnext
            self.bnext = (self.bnext + 1) % NBG
        else:
            s = self.dnext
            self.dnext = (self.dnext + 1) % NDS
        if self.dcnt[s]:
            self._need(eng, ("d", s, self.dcnt[s]), waits)
        self.dcnt[s] += 16
        tok = ("d", s, self.dcnt[s])
        self.q[eng].append((waits, lambda e, o=out, i=in_: e.dma_start(out=o, in_=i), tok))
        self._commit(tok, r, w)

    def barrier(self):
        for e in ENGS:
            waits = []
            for f in ENGS:
                if f != "sp" and self.cnt[f]:
                    self._need(e, ("e", f, self.cnt[f]), waits)
            for s in range(NDS):
                if self.dcnt[s]:
                    self._need(e, ("d", s, self.dcnt[s]), waits)
            if waits:
                self.q[e].append((waits, None, None))
        keep = {k: v for k, v in self.lastw.items() if isinstance(k, tuple) and v[0] == "d" and v[1] >= NDS}
        self.lastw.clear()
        self.lastw.update(keep)
        self.readers.clear()

    def emit(self, eng, e):
        for waits, fn, tok in self.q[eng]:
            for kind, src, val in waits:
                e.wait_ge(self.sem[src] if kind == "e" else self.dsem[src], val)
            if fn is None:
                continue
            ins = fn(e)
            if tok is not None:
                if tok[0] == "e":
                    ins.then_inc(self.sem[eng], 1)
                else:
                    ins.then_inc(self.dsem[tok[1]], 16)


def build_program(debug=None):
    dbg = debug or {}
    nc = bass.Bass("TRN2", target_bir_lowering=False)
    dr = lambda n, s, k="ExternalInput", d=F32: nc.dram_tensor(n, list(s), d, kind=k).ap()
    xo = dr("xo", [TOK, D])
    xp = dr("xp", [TOK, D])
    w_in = dr("w_in", [D, DIN])
    w_glu = dr("w_glu", [1024, 2048])
    w_bs = dr("w_bs", [1024, 2048])
    w_ba = dr("w_ba", [1024, 2048])
    w_out = dr("w_out", [D, D])
    lamr_d = dr("lamr", [128, 32]); lami_d = dr("lami", [128, 32]); lstep_d = dr("lstep", [128, 32])
    br_d = dr("br", [128, 512]); bi_d = dr("bi", [128, 512]); cr_d = dr("cr", [128, 512]); ci_d = dr("ci", [128, 512])
    dI_d = dr("dI", [16, 1024])
    sk_d = dr("sk", [128, 8])
    line_d = dr("line", [16, 384])
    hneg_d = dr("hneg", [128, 1])
    lng_d = dr("lng", [128, D]); lnb_d = dr("lnb", [128, D])
    idf_d = dr("idf", [128, 128]); anti_d = dr("anti", [128, 128])
    msc_d = dr("msc", [128, 65])
    y_out = dr("y", [TOK, D], k="ExternalOutput")
    s_in = dr("s_in", [34, 128, 4096], k="Internal", d=BF16)
    s_glu = dr("s_glu", [8, 128, 2048], k="Internal", d=BF16)
    s_bs = dr("s_bs", [8, 128, 2048], k="Internal", d=BF16)
    s_ba = dr("s_ba", [8, 128, 2048], k="Internal", d=BF16)
    s_out = dr("s_out", [8, 128, 4096], k="Internal", d=BF16)
    s_k2 = dr("s_k2", [2, 128, 4096], k="Internal", d=BF16)
    s_w1 = dr("s_w1", [2, 128, 4096], k="Internal", d=BF16)
    s_wc = dr("s_wc", [2, 128, 4096], k="Internal", d=BF16)
    s_tm = dr("s_tm", [2, 128, 4096], k="Internal", d=BF16)

    with ExitStack() as st:
        P = Prog(nc, st)
        used = {}
        dumped = {}

        def dump(name, t, key, parts=128):
            if name not in dbg.get("dumps", ()) or name in dumped:
                return
            shape = [parts, t.shape[-1]]
            dst = nc.dram_tensor("dbg_" + name, shape, t.dtype, kind="ExternalOutput").ap()
            P.dma("sp", dst, t[0:parts, :], r=[key])
            dumped[name] = True

        memo = {}
        own_scope = ExitStack()
        pre_scope = ExitStack()

        def sb(n, s, d=F32, c=st):
            if c is own_scope or c is pre_scope:
                if n in memo:
                    return memo[n]
            k = used.get(n, 0)
            used[n] = k + 1
            t = c.enter_context(nc.sbuf_tensor(n if k == 0 else f"{n}_{k}", list(s), d))
            if c is own_scope or c is pre_scope:
                memo[n] = t
            return t
        identb = sb("identb", [128, 128], BF16)
        identf = sb("identf", [128, 128])
        antif = sb("antif", [128, 128])
        ones64 = sb("ones64", [128, 64], BF16)
        biasT = sb("biasT", [128, 16 * 2 * 128])
        CA = sb("CA", [128, 64])
        CBn = sb("CBn", [128, 32]); CBp = sb("CBp", [128, 32])
        carry = sb("carry", [128, 64])
        esk = sb("esk", [128, 8])
        ar = sb("ar", [128, 32]); ai = sb("ai", [128, 32])
        hneg = sb("hneg_s", [128, 1])
        wstate = {"i": 0}
        wbuf = []
        psbig = st.enter_context(nc.psum_tensor("psbig", [128, 6 * 512], F32))
        psf = [psbig[:, 512 * i:512 * i + 512] for i in range(6)]
        psb = [st.enter_context(nc.psum_tensor(f"psb{i}", [128, 1024], BF16)) for i in range(2)]
        pstate = {"f": 0, "b": 0}

        def nps():
            i = pstate["f"]; pstate["f"] = (i + 1) % 6
            return psf[i], f"psf{i}"

        def nps2():
            i = pstate["f"]
            if i % 2:
                i = (i + 1) % 6
            pstate["f"] = (i + 2) % 6
            return psbig[:, 512 * i:512 * i + 1024], (f"psf{i}", f"psf{i + 1}")

        def npb():
            i = pstate["b"]; pstate["b"] = (i + 1) % 2
            return psb[i], f"psb{i}"

        def nwb():
            i = wstate["i"]; wstate["i"] = (i + 1) % 4
            return wbuf[i], f"wbuf{i}"

        evs = {"i": 0}

        def evac_eng():
            evs["i"] ^= 1
            return "dve" if evs["i"] else "act"

        def copy(eng, out, in_, r, w):
            if eng == "act":
                P.op("act", lambda e, o=out, i=in_: e.copy(o, i), r=r, w=w)
            else:
                P.op(eng, lambda e, o=out, i=in_: e.tensor_copy(o, i), r=r, w=w)

        def tt(eng, out, a, b, op, r, w):
            P.op(eng, lambda e, o=out, x=a, y=b, p=op: e.tensor_tensor(o, x, y, p), r=r, w=w)

        def ts(eng, out, a, s1, s2, op0, op1, r, w):
            if op1 is None:
                P.op(eng, lambda e, o=out, x=a: e.tensor_scalar(o, x, s1, None, op0), r=r, w=w)
            else:
                P.op(eng, lambda e, o=out, x=a: e.tensor_scalar(o, x, s1, s2, op0, op1), r=r, w=w)

        def taylor_exp(q, x, deg, keyq, keyx):
            ts("dve", q, x, 1.0 / deg, 1.0, ALU.mult, ALU.add, [keyx], [keyq])
            for k in range(deg - 1, 0, -1):
                tt("dve", q, q, x, ALU.mult, [keyq, keyx], [keyq])
                ts("dve", q, q, 1.0 / k, 1.0, ALU.mult, ALU.add, [keyq], [keyq])

        C1 = 6.28125
        C2 = TWO_PI - C1

        def sin_reduced(A, T_, K_):
            a, t_, k_ = A.name, T_.name, K_.name
            ts("dve", T_[:], A[:], 1.0 / TWO_PI, None, ALU.mult, None, [a], [t_])
            copy("dve", K_[:], T_[:], [t_], [k_])
            copy("dve", T_[:], K_[:], [k_], [t_])
            P.op("dve", lambda e: e.scalar_tensor_tensor(A[:], T_[:], -C1, A[:], ALU.mult, ALU.add), r=[t_, a], w=[a])
            P.op("dve", lambda e: e.scalar_tensor_tensor(A[:], T_[:], -C2, A[:], ALU.mult, ALU.add), r=[t_, a], w=[a])
            ts("dve", T_[:], A[:], PI, -TWO_PI, ALU.is_gt, ALU.mult, [a], [t_])
            tt("dve", A[:], A[:], T_[:], ALU.add, [t_, a], [a])
            ts("dve", T_[:], A[:], -PI, TWO_PI, ALU.is_lt, ALU.mult, [a], [t_])
            tt("dve", A[:], A[:], T_[:], ALU.add, [t_, a], [a])
            ts("dve", A[:], A[:], PI, -PI, ALU.min, ALU.max, [a], [a])
            act(A[:], A[:], AF.Sin, [a], [a])

        def act(out, in_, func, r, w, scale=1.0):
            P.op("act", lambda e, o=out, i=in_, f=func, s=scale: e.activation(o, i, f, scale=s), r=r, w=w)

        def cast_w(scr, sname, idx, src, c0, ncols, col_off=0):
            nk = src.shape[0] // 128
            dst = scr[idx].rearrange("p (k c) -> p k c", c=256)[:, 0:nk, col_off:col_off + ncols]
            P.dma("pool", dst, src.rearrange("(k p) c -> p k c", p=128)[:, :, c0:c0 + ncols],
                  w=[(sname, idx, col_off)], bg=True)

        cast_jobs = []

        def cast_all(first):
            if first:
                for i in list(range(4)) + [13]:
                    cast_w(s_in, "s_in", i, w_in, 256 * i, 256)
                for kh in range(4):
                    for dup in range(2):
                        cast_w(s_k2, "s_k2", kh // 2, w_in, C_K + 64 * kh, 64, 128 * (kh % 2) + 64 * dup)
                return
            J = cast_jobs.append
            for i in list(range(8, 12)) + list(range(14, 18)):
                J((s_in, "s_in", i, w_in, 256 * i, 256, 0))
            for i in range(8):
                J((s_in, "s_in", 26 + i, w_in, 256 * (26 + i), 256, 0))
                J((s_ba, "s_ba", i, w_ba, 256 * i, 256, 0))
            for i in range(4, 8):
                J((s_in, "s_in", i, w_in, 256 * i, 256, 0))
            for i in range(8):
                J((s_glu, "s_glu", i, w_glu, 128 * i, 128, 0)); J((s_glu, "s_glu", i, w_glu, 1024 + 128 * i, 128, 128))
            for i in range(8):
                J((s_in, "s_in", 18 + i, w_in, 256 * (18 + i), 256, 0))
                J((s_bs, "s_bs", i, w_bs, 256 * i, 256, 0))
            for i in range(8):
                J((s_out, "s_out", i, w_out, 256 * i, 256, 0))

        def issue_casts(n):
            for _ in range(min(n, len(cast_jobs))):
                cast_w(*cast_jobs.pop(0))

        def load_c(scr, sname, idx, nk=16, cw=256, offs=(0, 128)):
            dst, key = nwb()
            P.dma("sp", dst[:, 0:nk * 256], scr[idx][:, 0:nk * 256], r=[(sname, idx, o_) for o_ in offs], w=[key])
            return dst[:].rearrange("p (k c) -> p k c", c=cw), key

        with ExitStack() as s0:
            sb0 = lambda n, s, d=F32: sb(n, s, d, s0)
            lamr = sb0("lamr_s", [128, 32]); lami = sb0("lami_s", [128, 32]); stp = sb0("stp", [128, 32])
            W1re = sb0("W1re", [128, 32 * 128], BF16)
            W1im = sb0("W1im", [128, 32 * 128], BF16)
            Wc = sb0("Wc", [128, 32 * 2 * 128], BF16)
            Tm = sb0("Tm", [128, 64 * 128], BF16)
            brs = sb0("brs", [128, 512]); bis = sb0("bis", [128, 512]); crs = sb0("crs", [128, 512]); cis = sb0("cis", [128, 512])
            dIs = sb0("dIs", [16, 1024]); sks = sb0("sks", [128, 8])
            for t_, d_ in ((lamr, lamr_d), (lami, lami_d), (stp, lstep_d), (brs, br_d), (bis, bi_d), (crs, cr_d),
                           (cis, ci_d), (dIs, dI_d), (sks, sk_d), (hneg, hneg_d), (identf, idf_d), (antif, anti_d)):
                P.dma("sp", t_[:], d_, w=[t_.name])
            copy("dve", identb[:], identf[:], [identf.name], [identb.name])
            VH = [sb0(f"VH{i}", [128, 256]) for i in range(4)]

            def bias_head(h):
                vh = VH[h % 4]
                for kt_ in range(2):
                    P.dma("sp", vh[:, 128 * kt_:128 * kt_ + 128],
                          bass.AP(line_d.tensor, 384 * h + 129 - 128 * kt_, [[1, 128], [1, 128]]), w=[vh.name])
                ps, pk = nps()
                P.group("pe", [lambda e, o=ps[:, 128 * k_:128 * k_ + 128], r_=vh[:, 128 * k_:128 * k_ + 128]:
                               e.matmul(o, antif[:], r_, start=True, stop=True) for k_ in range(2)], r=[vh.name, antif.name], w=[pk])
                copy("act", biasT[:, 256 * h:256 * h + 256], ps[:, 0:256], [pk], ["biasT"])
            P.op("dve", lambda e: e.memset(ones64[:], 1.0), w=["ones64"])
            P.op("dve", lambda e: e.memset(carry[:], 0.0), w=["carry"])
            act(esk[:], sks[:], AF.Exp, [sks.name], ["esk"])
            ts("dve", ar[:], stp[:], 0.125, None, ALU.mult, None, [stp.name], ["ar"])
            taylor_exp(stp[:], ar[:], 10, stp.name, "ar")
            for _ in range(3):
                tt("dve", stp[:], stp[:], stp[:], ALU.mult, [stp.name], [stp.name])
            tt("dve", ar[:], lamr[:], stp[:], ALU.mult, [lamr.name, stp.name], ["ar"])
            tt("dve", ai[:], lami[:], stp[:], ALU.mult, [lami.name, stp.name], ["ai"])
            MAR = sb0("MAR", [128, 9 * 32]); ANG = sb0("ANG", [128, 2 * 9 * 32]); TMPA = sb0("TMPA", [128, 576])
            KI = sb0("KI", [128, 576], I32)
            for m in range(9):
                ts("dve", MAR[:, 32 * m:32 * m + 32], ar[:], float(m), None, ALU.mult, None, ["ar"], ["MAR"])
                ts("dve", ANG[:, 32 * m:32 * m + 32], ai[:], float(m), None, ALU.mult, None, ["ai"], ["ANG"])
                ts("dve", ANG[:, 288 + 32 * m:288 + 32 * m + 32], ai[:], float(m), PI / 2, ALU.mult, ALU.add, ["ai"], ["ANG"])
            sin_reduced(ANG, TMPA, KI)
            MAG = sb0("MAG", [128, 9 * 32])
            taylor_exp(MAG[:], MAR[:], 8, "MAG", "MAR")
            PR = sb0("PR", [128, 288]); PIm = sb0("PIm", [128, 288])
            tt("dve", PR[:], MAG[:], ANG[:, 288:576], ALU.mult, ["MAG", "ANG"], ["PR"])
            tt("dve", PIm[:], MAG[:], ANG[:, 0:288], ALU.mult, ["MAG", "ANG"], ["PIm"])
            copy("dve", CA[:, 0:32], PR[:, 256:288], ["PR"], ["CA"])
            copy("dve", CA[:, 32:64], PR[:, 256:288], ["PR"], ["CA"])
            copy("dve", CBp[:], PIm[:, 256:288], ["PIm"], ["CBp"])
            ts("dve", CBn[:], PIm[:, 256:288], -1.0, None, ALU.mult, None, ["PIm"], ["CBn"])
            nr = sb0("nr", [128, 32]); den = sb0("den", [128, 32]); t1 = sb0("t1", [128, 32]); t2 = sb0("t2", [128, 32])
            kr = sb0("kr", [128, 32]); ki_ = sb0("ki_", [128, 32])
            ts("dve", nr[:], PR[:, 32:64], -1.0, None, ALU.add, None, ["PR"], ["nr"])
            tt("dve", den[:], lamr[:], lamr[:], ALU.mult, [lamr.name], ["den"])
            tt("dve", t1[:], lami[:], lami[:], ALU.mult, [lami.name], ["t1"])
            tt("dve", den[:], den[:], t1[:], ALU.add, ["t1"], ["den"])
            P.op("dve", lambda e: e.reciprocal(den[:], den[:]), r=["den"], w=["den"])
            tt("dve", t1[:], nr[:], lamr[:], ALU.mult, ["nr"], ["t1"])
            tt("dve", t2[:], PIm[:, 32:64], lami[:], ALU.mult, ["PIm"], ["t2"])
            tt("dve", t1[:], t1[:], t2[:], ALU.add, ["t2"], ["t1"])
            tt("dve", kr[:], t1[:], den[:], ALU.mult, ["t1", "den"], ["kr"])
            tt("dve", t1[:], PIm[:, 32:64], lamr[:], ALU.mult, ["PIm"], ["t1"])
            tt("dve", t2[:], nr[:], lami[:], ALU.mult, ["nr"], ["t2"])
            tt("dve", t1[:], t1[:], t2[:], ALU.subtract, ["t2"], ["t1"])
            tt("dve", ki_[:], t1[:], den[:], ALU.mult, ["t1", "den"], ["ki_"])
            OMR = sb0("OMR", [128, 256]); OMI = sb0("OMI", [128, 256]); T8 = sb0("T8", [128, 256])
            pr3 = PR[:, 0:256].rearrange("p (m q) -> p m q", q=32); pi3 = PIm[:, 0:256].rearrange("p (m q) -> p m q", q=32)
            krb = kr[:].unsqueeze(1).to_broadcast([128, 8, 32]); kib = ki_[:].unsqueeze(1).to_broadcast([128, 8, 32])
            omr3 = OMR[:].rearrange("p (m q) -> p m q", q=32); omi3 = OMI[:].rearrange("p (m q) -> p m q", q=32)
            t83 = T8[:].rearrange("p (m q) -> p m q", q=32)
            tt("dve", omr3, pr3, krb, ALU.mult, ["PR", "kr"], ["OMR"])
            tt("dve", t83, pi3, kib, ALU.mult, ["PIm", "ki_"], ["T8"])
            tt("dve", OMR[:], OMR[:], T8[:], ALU.subtract, ["T8"], ["OMR"])
            tt("dve", omi3, pi3, krb, ALU.mult, ["PIm", "kr"], ["OMI"])
            tt("dve", t83, pr3, kib, ALU.mult, ["PR", "ki_"], ["T8"])
            tt("dve", OMI[:], OMI[:], T8[:], ALU.add, ["T8"], ["OMI"])
            TB = sb0("TB", [128, 512]); A0r = sb0("A0r", [128, 512]); A0i = sb0("A0i", [128, 512])
            sA = ExitStack()
            AR = sb("AR", [128, 4096], F32, sA); AI = sb("AI", [128, 4096], F32, sA)
            ar4 = AR[:].rearrange("p (q j h) -> p q j h", j=8, h=16); ai4 = AI[:].rearrange("p (q j h) -> p q j h", j=8, h=16)
            br3 = brs[:].rearrange("p (q h) -> p q h", h=16); bi3 = bis[:].rearrange("p (q h) -> p q h", h=16)
            tb3 = TB[:].rearrange("p (q h) -> p q h", h=16)
            for j in range(8):
                m = 7 - j
                orb = OMR[:, 32 * m:32 * m + 32].unsqueeze(2).to_broadcast([128, 32, 16])
                oib = OMI[:, 32 * m:32 * m + 32].unsqueeze(2).to_broadcast([128, 32, 16])
                tt("dve", ar4[:, :, j, :], br3, orb, ALU.mult, [brs.name, "OMR"], ["AR"])
                tt("dve", tb3, bi3, oib, ALU.mult, [bis.name, "OMI"], ["TB"])
                tt("dve", ar4[:, :, j, :], ar4[:, :, j, :], tb3, ALU.subtract, ["TB"], ["AR"])
                tt("dve", ai4[:, :, j, :], bi3, orb, ALU.mult, [bis.name, "OMR"], ["AI"])
                tt("dve", tb3, br3, oib, ALU.mult, [brs.name, "OMI"], ["TB"])
                tt("dve", ai4[:, :, j, :], ai4[:, :, j, :], tb3, ALU.add, ["TB"], ["AI"])
            for src, dst, nm in ((AR, W1re, "W1re"), (AI, W1im, "W1im")):
                for q0 in range(0, 32, 4):
                    ps, pk = nps()
                    P.group("pe", [lambda e, o=ps[:, 128 * i:128 * i + 128], s_=src[:, 128 * (q0 + i):128 * (q0 + i) + 128]:
                                   e.transpose(o, s_, identf[:]) for i in range(4)], r=[src.name, identf.name], w=[pk])
                    copy(evac_eng(), dst[:, 128 * q0:128 * q0 + 512], ps[:, :], [pk], [nm])
            copy("dve", A0r[:].rearrange("p (q h) -> p q h", h=16), ar4[:, :, 7, :], ["AR"], ["A0r"])
            copy("dve", A0i[:].rearrange("p (q h) -> p q h", h=16), ai4[:, :, 7, :], ["AI"], ["A0i"])
            P.barrier()
            sA.close()
            ER = sb0("ER", [128, 32 * 9 * 16]); NEI = sb0("NEI", [128, 32 * 9 * 16])
            er4 = ER[:].rearrange("p (q k h) -> p q k h", k=9, h=16); ne4 = NEI[:].rearrange("p (q k h) -> p q k h", k=9, h=16)
            cr3 = crs[:].rearrange("p (q h) -> p q h", h=16); ci3 = cis[:].rearrange("p (q h) -> p q h", h=16)
            for k in range(9):
                prb = PR[:, 32 * k:32 * k + 32].unsqueeze(2).to_broadcast([128, 32, 16])
                pib = PIm[:, 32 * k:32 * k + 32].unsqueeze(2).to_broadcast([128, 32, 16])
                tt("dve", er4[:, :, k, :], cr3, prb, ALU.mult, [crs.name, "PR"], ["ER"])
                tt("dve", tb3, ci3, pib, ALU.mult, [cis.name, "PIm"], ["TB"])
                tt("dve", er4[:, :, k, :], er4[:, :, k, :], tb3, ALU.subtract, ["TB"], ["ER"])
                tt("dve", ne4[:, :, k, :], cr3, pib, ALU.mult, [crs.name, "PIm"], ["NEI"])
                tt("dve", tb3, ci3, prb, ALU.mult, [cis.name, "PR"], ["TB"])
                P.op("dve", lambda e, o=ne4[:, :, k, :]: e.scalar_tensor_tensor(o, o, -1.0, tb3, ALU.mult, ALU.subtract), r=["TB"], w=["NEI"])
            P.op("pool", lambda e: e.memset(Tm[:], 0.0), w=["Tm"])
            cast_all(True)
            cast_all(False)
            wc4 = Wc[:].rearrange("p (q r c) -> p q r c", r=2, c=128)
            for ri, src in ((0, ER), (1, NEI)):
                s4 = src[:].rearrange("p (q c) -> p q c", c=144)
                copy("dve", wc4[:, :, ri, :], s4[:, :, 16:144], [src.name], ["Wc"])
            WBF = [sb0(f"WBF{i}", [128, 2 * 2 * 128]) for i in range(4)]
            KTb = sb0("KTb", [16, 64 * 128], BF16)
            for wb in WBF:
                P.op("dve", lambda e, t_=wb: e.memset(t_[:], 0.0), w=[wb.name])
            dI3 = dIs[:].rearrange("p (g h) -> p g h", h=16)
            kt4 = KTb[:].rearrange("p (g k h) -> p g k h", k=8, h=16)
            for q in range(32):
                if q % 2 == 0:
                    bias_head(q // 2)
                wb = WBF[q % 4]
                w4 = wb[:].rearrange("p (r g c) -> p r g c", r=2, g=2)
                for g2 in range(2):
                    rows = slice(64 * g2, 64 * g2 + 64)
                    copy("act", w4[rows, 0, g2, :], ER[rows, 144 * q:144 * q + 128], ["ER"], [wb.name])
                    copy("act", w4[rows, 1, g2, :], NEI[rows, 144 * q:144 * q + 128], ["NEI"], [wb.name])
                ps, pk = nps()
                P.group("pe", [
                    lambda e, o=ps[0:16, 0:256], l=A0r[:, 16 * q:16 * q + 16], r_=wb[:, 0:256]: e.matmul(o, l, r_, start=True, stop=False),
                    lambda e, o=ps[0:16, 0:256], l=A0i[:, 16 * q:16 * q + 16], r_=wb[:, 256:512]: e.matmul(o, l, r_, start=False, stop=True),
                ], r=["A0r", "A0i", wb.name], w=[pk])
                copy("dve", KTb[:, 256 * q:256 * q + 256], ps[0:16, 0:256], [pk], ["KTb"])
                tt("dve", kt4[:, 2 * q:2 * q + 2, 0, :], ps[0:16, 0:256].rearrange("p (g k h) -> p g k h", g=2, h=16)[:, :, 0, :],
                   dI3[:, 2 * q:2 * q + 2, :], ALU.add, [pk, dIs.name], ["KTb"])
            tm3 = Tm[:].rearrange("p (g c) -> p g c", c=128)
            kt3 = KTb[:].rearrange("p (g c) -> p g c", c=128)
            for jp in range(8):
                P.dma("sp", tm3[16 * jp:16 * jp + 16, :, 16 * jp:128], kt3[:, :, 0:(8 - jp) * 16], r=["KTb", "Tm"], w=[("Tm", jp)])
            for i, (t_, nm_) in enumerate(((W1re, "W1re"), (W1im, "W1im"))):
                P.dma("sp", s_w1[i], t_[:], r=[nm_])
            for i in range(2):
                P.dma("sp", s_wc[i], Wc[:, 4096 * i:4096 * i + 4096], r=["Wc"])
                P.dma("sp", s_tm[i], Tm[:, 4096 * i:4096 * i + 4096], r=["Tm"] + [("Tm", jp) for jp in range(8)])
            P.barrier()
            for nm_, t_ in (("PR", PR), ("PIm", PIm), ("kr", kr), ("ki_", ki_), ("W1re", W1re), ("W1im", W1im), ("Wc", Wc), ("Tm", Tm),
                            ("biasT", biasT), ("esk", esk), ("CA", CA), ("CBn", CBn), ("ER", ER), ("NEI", NEI)):
                dump(nm_, t_, "none")
            dump("KTb", KTb, "none", parts=16)
            P.barrier()

        xT = sb("xT", [128, 16 * SPAN], BF16)
        gyT = sb("gyT", [128, 8 * SPAN], BF16)
        kT = sb("kT", [128, 4 * 640], BF16)
        vext = sb("vext", [128, 5 * 4 * 64], BF16)
        wbuf.extend(sb(f"wbuf{i}", [128, 16 * 256], BF16) for i in range(4))
        xb_t = [sb(f"xb{i}", [128, D], BF16) for i in range(2)]
        xb = [(t_[:, :], t_.name) for t_ in xb_t]
        xpre = {"n": 0}
        P.op("dve", lambda e: e.memset(kT[:], 0.0), w=["kT"])
        P.op("dve", lambda e: e.memset(vext[:], 0.0), w=["vext"])
        xT3 = xT[:].rearrange("p (k t) -> p k t", t=SPAN)
        gy3 = gyT[:].rearrange("p (c t) -> p c t", t=SPAN)
        kT3 = kT[:].rearrange("p (h t) -> p h t", t=640)
        vx4 = vext[:].rearrange("p (t h d) -> p t h d", h=4, d=64)
        bT4 = biasT[:].rearrange("p (h k q) -> p h k q", k=2, q=128)

        def build_xT(xsrc, row0, bufs, ev=None):
            for t4 in range(SPAN // 128):
                b_, bk_ = bufs[t4 % len(bufs)]
                if t4 >= xpre["n"]:
                    P.dma("pool", b_, xsrc[row0 + 128 * t4:row0 + 128 * t4 + 128, :], w=[bk_])
                for half in range(2):
                    ps, pk = npb()
                    P.group("pe", [lambda e, o=ps[:, 128 * i:128 * i + 128], s2=b_[:, 128 * (8 * half + i):128 * (8 * half + i) + 128]:
                                   e.transpose(o, s2, identb[:]) for i in range(8)], r=[bk_], w=[pk])
                    copy(ev or evac_eng(), xT3[:, 8 * half:8 * half + 8, 128 * t4:128 * t4 + 128],
                         ps[:, :].rearrange("p (k t) -> p k t", t=128), [pk], ["xT"])
            xpre["n"] = 0

        def prefetch_x(xsrc, row0, bufs, t_lo, t_hi):
            for t4 in range(t_lo, t_hi):
                P.dma("pool", bufs[t4][0], xsrc[row0 + 128 * t4:row0 + 128 * t4 + 128, :], w=[bufs[t4][1]])
            xpre["n"] = t_hi

        def ssm_state(sc, own, pfx=None):
            ubm = sb("ubm", [128, 8 * 1024], BF16, sc)
            U = sb("U", [128, 64 * NB], BF16, sc)
            X = sb("X", [128, 2 * 32 * NB], F32, sc)
            ub4 = ubm[:, 0:4096].rearrange("p (g j h) -> p g j h", j=4, h=16)
            U3 = U[:].rearrange("p (g b) -> p g b", b=NB)
            X4 = X[:].rearrange("p (r q b) -> p r q b", r=2, b=NB)
            xTj = xT[:].rearrange("p (k b j) -> p k j b", j=8, b=NB)

            def u_transposes(g0):
                ps, pk = npb()
                fns = []
                for i in range(16):
                    for jh in range(2):
                        rows = slice(64 * jh, 64 * jh + 64)
                        fns.append(lambda e, o=ps[rows, NB * i:NB * i + NB], s_=ubm[rows, 64 * (g0 + i):64 * (g0 + i) + 64],
                                   idn=identb[rows, 64 * jh:64 * jh + 64]: e.transpose(o, s_, idn))
                P.group("pe", fns, r=[("u", g0 // 16)], rr=["ubm"], w=[pk])
                copy(evac_eng(), U[:, NB * g0:NB * g0 + NB * 16], ps[:, 0:NB * 16], [pk], ["U"])

            for cb in range(4):
                wv, wk = load_c(s_in, "s_in", cb)
                for j4 in range(4):
                    ps, pk = nps()
                    fns = []
                    for kt in range(16):
                        for jh in range(2):
                            fns.append(lambda e, o=ps[64 * jh:64 * jh + 64, 0:256], l=xTj[:, kt, 4 * jh + j4, :], r_=wv[:, kt, :], kt=kt:
                                       e.matmul(o, l, r_, start=(kt == 0), stop=(kt == 15)))
                    P.group("pe", fns, r=["xT", wk], w=[pk])
                    copy("act" if (pfx is not None and cb == 0) else evac_eng(),
                         ub4[:, 16 * cb:16 * cb + 16, j4, :], ps[:, 0:256].rearrange("p (g h) -> p g h", h=16), [pk], ["ubm", ("u", cb)])
                if own and cb in (0, 2) and late.get("pending"):
                    late["pending"].pop(0)()
                if cb == 2:
                    u_transposes(0); u_transposes(16)
                if cb == 3:
                    u_transposes(32)
            u_transposes(48)
            w1r, w1rk = load_c(s_w1, "s_w1", 0, 16, 128)
            w1i, w1ik = load_c(s_w1, "s_w1", 1, 16, 128)
            w13 = {0: w1r, 1: w1i}
            for q0 in range(0, 32, 4):
                ps, pk = nps()
                ps4 = ps[:, :].rearrange("p (q r b) -> p q r b", r=2, b=NB)
                fns = []
                for i in range(4):
                    q = q0 + i
                    for g2 in range(2):
                        for ri in range(2):
                            fns.append(lambda e, o=ps4[64 * g2:64 * g2 + 64, i, ri, :], l=w13[ri][:, q, 64 * g2:64 * g2 + 64],
                                       r_=U3[:, 2 * q + g2, :]: e.matmul(o, l, r_, start=True, stop=True))
                P.group("pe", fns, r=["U", w1rk, w1ik], w=[pk])
                copy(evac_eng(), X4[:, :, q0:q0 + 4, :], ps4.rearrange("p q r b -> p r q b"), [pk], ["X"])
            tA = sb("tA", [128, 64], F32, sc); tB = sb("tB", [128, 64], F32, sc)
            if pfx is not None:
                t1 = sb("pt1", [128, 32 * NB], F32, sc); t2 = sb("pt2", [128, 32 * NB], F32, sc)
                red = sb("pred", [128, 64], F32, sc)
                xr, xi = X[:, 0:32 * NB], X[:, 32 * NB:64 * NB]
                for ri_, (a_, b_, op_) in enumerate(((pfx["PWr"], pfx["PWi"], ALU.subtract), (pfx["PWi"], pfx["PWr"], ALU.add))):
                    tt("dve", t1[:], xr, a_[:], ALU.mult, ["X"], ["pt1"])
                    tt("dve", t2[:], xi, b_[:], ALU.mult, ["X"], ["pt2"])
                    tt("dve", t1[:], t1[:], t2[:], op_, ["pt1", "pt2"], ["pt1"])
                    P.op("dve", lambda e, o=red[:, 32 * ri_:32 * ri_ + 32], i_=t1[:].rearrange("p (q b) -> p q b", b=NB):
                         e.reduce_sum(o, i_, mybir.AxisListType.X), r=["pt1"], w=["pred"])
                c3 = carry[:].rearrange("p (r q) -> p r q", r=2)
                ta3 = tA[:].rearrange("p (r q) -> p r q", r=2); tb3_ = tB[:].rearrange("p (r q) -> p r q", r=2)
                tt("dve", ta3, c3, pfx["CA64"][:].rearrange("p (r q) -> p r q", r=2), ALU.mult, ["carry"], ["tA"])
                tt("dve", tb3_[:, 0, :], c3[:, 1, :], pfx["CB64n"][:], ALU.mult, ["carry"], ["tB"])
                tt("dve", tb3_[:, 1, :], c3[:, 0, :], pfx["CB64p"][:], ALU.mult, ["carry"], ["tB"])
                tt("dve", tA[:], tA[:], tB[:], ALU.add, ["tA", "tB"], ["tA"])
                tt("dve", carry[:], tA[:], red[:], ALU.add, ["tA", "pred"], ["carry"])
                return None
            Sh = None
            if own:
                Sh = sb("Sh", [128, 2 * 2 * 32 * NB], BF16, sc)
                Sh5 = Sh[:].rearrange("p (g r q b) -> p g r q b", g=2, r=2, b=NB)
                P.op("dve", lambda e, o=Sh5[64:128, 0]: e.memset(o, 0.0), w=["Sh"])
                P.op("dve", lambda e, o=Sh5[0:64, 1]: e.memset(o, 0.0), w=["Sh"])
                for g2 in range(2):
                    rows = slice(64 * g2, 64 * g2 + 64)
                    copy("pool", Sh5[rows, g2, :, :, 0], carry[rows, :].rearrange("p (r q) -> p r q", r=2), ["carry"], ["Sh"])
            c3 = carry[:].rearrange("p (r q) -> p r q", r=2)
            ca3 = CA[:].rearrange("p (r q) -> p r q", r=2)
            ta3 = tA[:].rearrange("p (r q) -> p r q", r=2); tb3_ = tB[:].rearrange("p (r q) -> p r q", r=2)
            for b in range(NB):
                prev = c3 if b == 0 else X4[:, :, :, b - 1]
                tt("pool", ta3, prev, ca3, ALU.mult, ["X", "carry"], ["tA"])
                tt("pool", tb3_[:, 0, :], prev[:, 1, :], CBn[:], ALU.mult, ["X", "carry"], ["tB"])
                tt("pool", tb3_[:, 1, :], prev[:, 0, :], CBp[:], ALU.mult, ["X", "carry"], ["tB"])
                tt("pool", ta3, ta3, tb3_, ALU.add, ["tB"], ["tA"])
                tt("pool", X4[:, :, :, b], X4[:, :, :, b], ta3, ALU.add, ["tA"], ["X"])
            copy("pool", c3, X4[:, :, :, NB - 1], ["X", "Sh"], ["carry"])
            return ubm, U3, (Sh, X4), (U, X)

        def ssm_out(ubm, U3, Sh):
            Sh, X4 = Sh
            Sh5 = Sh[:].rearrange("p (g r q b) -> p g r q b", g=2, r=2, b=NB)
            for g2 in range(2):
                rows = slice(64 * g2, 64 * g2 + 64)
                copy("dve" if g2 else "act", Sh5[rows, g2, :, :, 1:NB], X4[rows, :, :, 0:NB - 1], ["X"], ["Sh"])
            gb5 = ubm[:, 0:4096].rearrange("p (c j g h) -> p c g j h", c=8, j=8, g=4, h=16)
            for ct in range(8):
                if ct % 4 == 0:
                    tmv, tmk = load_c(s_tm, "s_tm", ct // 4, 16, 128)
                    wcv, wck = load_c(s_wc, "s_wc", ct // 4, 16, 128)
                ps, pk = nps()
                fns = []
                for g4 in range(4):
                    for step in range(3):
                        for gp in range(2):
                            g = 8 * ct + 4 * gp + g4; q, g2 = g // 2, g % 2
                            o = ps[64 * gp:64 * gp + 64, 128 * g4:128 * g4 + 128]
                            if step == 0:
                                fns.append(lambda e, o=o, l=U3[:, g, :], r_=tmv[:, g % 32, :]: e.matmul(o, l, r_, start=True, stop=False))
                            elif step == 1:
                                fns.append(lambda e, o=o, l=Sh5[:, g2, 0, q, :], r_=wcv[:, 2 * (q % 16), :]: e.matmul(o, l, r_, start=False, stop=False))
                            else:
                                fns.append(lambda e, o=o, l=Sh5[:, g2, 1, q, :], r_=wcv[:, 2 * (q % 16) + 1, :]: e.matmul(o, l, r_, start=False, stop=True))
                P.group("pe", fns, r=["U", "Sh", tmk, wck], w=[pk])
                act(gb5[:, ct, :, :, :], ps[:, :].rearrange("p (g j h) -> p g j h", j=8, h=16), AF.Gelu, [pk, "U"], ["ubm", ("gy", ct)])
                if ct >= 1:
                    gy_transposes(ct - 1, ubm)
            gy_transposes(7, ubm)

        def gy_transposes(ct, ubm):
            if True:
                ps, pk = npb()
                fns = []
                for j in range(8):
                    for gp in range(2):
                        rows = slice(64 * gp, 64 * gp + 64)
                        fns.append(lambda e, o=ps[rows, NB * j:NB * j + NB], s_=ubm[rows, 512 * ct + 64 * j:512 * ct + 64 * j + 64],
                                   idn=identb[rows, 64 * gp:64 * gp + 64]: e.transpose(o, s_, idn))
                P.group("pe", fns, r=[("gy", ct)], rr=["ubm"], w=[pk])
                copy(evac_eng(), gy3[:, ct, :].rearrange("p (b j) -> p j b", j=8),
                     ps[:, 0:8 * NB].rearrange("p (j b) -> p j b", b=NB), [pk], ["gyT"])

        def proj_fm_unused(col0, ncols_tiles, consume):
            for c2 in range(0, ncols_tiles, 2):
                n = min(2, ncols_tiles - c2)
                wv, wk = load_c(s_in, "s_in", (col0 + 128 * c2) // 256)
                for i in range(n):
                    ps, pk = nps()
                    P.group("pe", [lambda e, o=ps[:, :], l=wv[:, kt, 128 * i:128 * i + 128], r_=xT3[:, kt, :], kt=kt:
                                   e.matmul(o, l, r_, start=(kt == 0), stop=(kt == 15)) for kt in range(16)], r=["xT", wk], w=[pk])
                    consume(c2 + i, ps, pk)

        def kv_proj(tok0, ntok, dst_tile0, kcol0):
            for kh in range(4):
                if kh % 2 == 0:
                    wv, wk = load_c(s_k2, "s_k2", kh // 2, offs=(0, 64, 128, 192))
                ps, pk = nps()
                P.group("pe", [lambda e, o=ps[:, 0:ntok], l=wv[:, kt, 128 * (kh % 2):128 * (kh % 2) + 128], r_=xT3[:, kt, tok0:tok0 + ntok], kt=kt:
                               e.matmul(o, l, r_, start=(kt == 0), stop=(kt == 15)) for kt in range(16)], r=["xT", wk], w=[pk])
                copy(evac_eng(), kT3[:, kh, kcol0:kcol0 + ntok], ps[:, 0:ntok], [pk], ["kT"])
            wv, wk = load_c(s_in, "s_in", 13)
            for t4 in range(ntok // 128):
                ps, pk = nps()
                P.group("pe", [lambda e, o=ps[:, 0:256], l=xT3[:, kt, tok0 + 128 * t4:tok0 + 128 * t4 + 128], r_=wv[:, kt, :], kt=kt:
                               e.matmul(o, l, r_, start=(kt == 0), stop=(kt == 15)) for kt in range(16)], r=["xT", wk], w=[pk])
                copy(evac_eng(), vx4[:, dst_tile0 + t4, :, :], ps[:, 0:256].rearrange("p (h d) -> p h d", d=64), [pk], ["vext"])

        late = {}

        def attn_stage(sp_i, sa):
            ha3 = late["ha3"]; haT = late["haT"]
            if True:
                qT = sb("qT", [128, 2 * SPAN], BF16, sa); qT4 = qT[:].rearrange("p (b h q) -> p b h q", h=2, q=128)
                pT = sb("pT", [128, 4 * 2 * 512], BF16, sa)
                pT5 = pT[:].rearrange("p (b k h q) -> p b k h q", k=2, h=4, q=128)
                lg = [sb(f"lg{i}", [128, 512], F32, sa) for i in range(2)]
                dn = sb("dn", [128, 512], F32, sa); zt2 = sb("zt", [128, 2 * 512], BF16, sa)
                kv_proj(0, SPAN, 1, 128)
                for kh in range(4):
                    wv, wk = load_c(s_in, "s_in", 8 + kh)
                    wza, wzak = load_c(s_in, "s_in", 14 + kh)
                    for hp in range(2):
                        ps, pk = nps()
                        P.group("pe", [lambda e, o=ps[:, :], l=wv[:, kt, 128 * hp:128 * hp + 128], r_=xT3[:, kt, :], kt=kt:
                                       e.matmul(o, l, r_, start=(kt == 0), stop=(kt == 15)) for kt in range(16)], r=["xT", wk], w=[pk])
                        act(qT4[:, :, hp, :], ps[:, :].rearrange("p (b q) -> p b q", q=128), AF.Identity, [pk], ["qT"], scale=0.125)
                    for blk in range(4):
                        for kt_ in range(2):
                            l_ = lg[(2 * blk + kt_) % 2]
                            pp, pks = nps2()
                            for half in range(2):
                                rows = slice(64 * half, 64 * half + 64)
                                P.group("pe", [lambda e, o=pp[:, 512 * half:512 * half + 256],
                                               l=kT3[rows, kh, 128 * (blk + kt_):128 * (blk + kt_) + 128],
                                               r_=qT[rows, 256 * blk:256 * blk + 256]: e.matmul(o, l, r_, start=True, stop=True)],
                                        r=["kT", "qT"], w=[pks[half]])
                            tt("dve", l_[:].rearrange("p (hf hp q) -> p hf hp q", hf=2, q=128),
                               pp.rearrange("p (b hp q) -> p b hp q", b=2, q=128)[:, :, 0:2, :],
                               bT4[:, 4 * kh:4 * kh + 4, kt_, :].rearrange("p (hp hf) q -> p hf hp q", hf=2),
                               ALU.add, [pks[0], pks[1], "biasT"], [l_.name])
                            if sp_i == 0 and blk == 0 and kt_ == 0:
                                P.op("act", lambda e, o=pT5[:, blk, kt_, :, :], i_=l_[:].rearrange("p (h q) -> p h q", q=128):
                                     e.activation(o, i_, AF.Exp, bias=hneg[:, 0:1]), r=[l_.name, "hneg_s"], w=["pT"])
                            else:
                                act(pT5[:, blk, kt_, :, :], l_[:].rearrange("p (h q) -> p h q", q=128), AF.Exp, [l_.name], ["pT"])
                    for hp in range(2):
                        ps, pk = nps()
                        P.group("pe", [lambda e, o=ps[:, :], l=wza[:, kt, 128 * hp:128 * hp + 128], r_=xT3[:, kt, :], kt=kt:
                                       e.matmul(o, l, r_, start=(kt == 0), stop=(kt == 15)) for kt in range(16)], r=["xT", wzak], w=[pk])
                        act(zt2[:, 512 * hp:512 * hp + 512], ps[:, :], AF.Silu, [pk], ["zt"])
                    for hp in range(2):
                        hpg = 2 * kh + hp
                        zt = zt2[:, 512 * hp:512 * hp + 512]
                        pv, pvk = nps(); rs, rsk = nps()
                        fns = []
                        for blk in range(4):
                            for half in range(2):
                                hq = 2 * half + hp
                                for kt_ in range(2):
                                    fns.append(lambda e, o=pv[64 * half:64 * half + 64, 128 * blk:128 * blk + 128], l=vx4[:, blk + kt_, kh, :],
                                               r_=pT5[:, blk, kt_, hq, :], kt_=kt_: e.matmul(o, l, r_, start=(kt_ == 0), stop=(kt_ == 1)))
                                for kt_ in range(2):
                                    fns.append(lambda e, o=rs[64 * half:64 * half + 64, 128 * blk:128 * blk + 128], r_=pT5[:, blk, kt_, hq, :], kt_=kt_:
                                               e.matmul(o, ones64[:], r_, start=(kt_ == 0), stop=(kt_ == 1)))
                        P.group("pe", fns, r=["pT", "vext", "ones64"], w=[pvk, rsk])
                        P.op("act", lambda e, r2=rs, h_=hpg: e.activation(dn[:], r2[:, :], AF.Ln, bias=esk[:, h_:h_ + 1]), r=[rsk, "esk"], w=["dn"])
                        P.op("act", lambda e: e.activation(dn[:], dn[:], AF.Exp, scale=-1.0), r=["dn"], w=["dn"])
                        tt("dve", dn[:], dn[:], pv[:, :], ALU.mult, [pvk], ["dn"])
                        tt("dve", ha3[:, hpg, :], dn[:], zt, ALU.mult, ["dn", "zt"], ["haT"])
                copy("dve", kT3[:, :, 0:128], kT3[:, :, 512:640], ["kT"], ["kT"])
                copy("dve", vx4[:, 0, :, :], vx4[:, 4, :, :], ["vext"], ["vext"])

        def merge_half(attn, sm):
            tg = "a" if attn else "s"
            ha3, hs3, mT3 = late["ha3"], late["hs3"], late["mT3"]
            g1 = sb("g1", [128, 512], F32, sm); m1 = late["ftmp"]
            h3, hn = (ha3, "haT") if attn else (hs3, "hsT")
            for d2 in range(8):
                wg3, wgk = load_c(s_in, "s_in", (26 if attn else 18) + d2)
                wb3, wbk = load_c(s_ba if attn else s_bs, "s_ba" if attn else "s_bs", d2, 8)
                for i in range(2):
                    dt = 2 * d2 + i
                    ps, pk = nps()
                    P.group("pe", [lambda e, o=ps[:, :], l=wg3[:, kt, 128 * i:128 * i + 128], r_=xT3[:, kt, :], kt=kt:
                                   e.matmul(o, l, r_, start=(kt == 0), stop=(kt == 15)) for kt in range(16)], r=["xT", wgk], w=[pk])
                    act(g1[:], ps[:, :], AF.Sigmoid, [pk], [g1.name])
                    ps, pk = nps()
                    P.group("pe", [lambda e, o=ps[:, :], l=wb3[:, ct, 128 * i:128 * i + 128], r_=h3[:, ct, :], ct=ct:
                                   e.matmul(o, l, r_, start=(ct == 0), stop=(ct == 7)) for ct in range(8)], r=[hn, wbk], w=[pk])
                    if attn:
                        tt("dve", mT3[:, dt, :], ps[:, :], g1[:], ALU.mult, [pk, g1.name], ["mT"])
                    else:
                        tt("dve", m1[:], ps[:, :], g1[:], ALU.mult, [pk, g1.name], ["ftmp"])
                        tt("dve", mT3[:, dt, :], mT3[:, dt, :], m1[:], ALU.add, ["ftmp"], ["mT"])

        def main_stage(sp_i, sc, ubm, U, X, Sh, pre_tail, defer):
            hs3, mT3, hsT, mT = late["hs3"], late["mT3"], late["hsT"], late["mT"]
            if True:
                sg = sc
                zs = sb("zs", [128, 512], BF16, sg); sgb = sb("sgb", [128, 512], F32, sg); ga_ = late["ftmp"]
                for et in range(8):
                    if et % 2 == 0:
                        wv2, wk2 = load_c(s_in, "s_in", 4 + et // 2)
                    ps, pk = nps()
                    P.group("pe", [lambda e, o=ps[:, :], l=wv2[:, kt, 128 * (et % 2):128 * (et % 2) + 128], r_=xT3[:, kt, :], kt=kt:
                                   e.matmul(o, l, r_, start=(kt == 0), stop=(kt == 15)) for kt in range(16)], r=["xT", wk2], w=[pk])
                    act(zs[:], ps[:, :], AF.Silu, [pk], ["zs"])
                    wg3, wgk = load_c(s_glu, "s_glu", et, 8)
                    pa, pak = nps(); pb, pbk = nps()
                    P.group("pe", [lambda e, o=pa[:, :], l=wg3[:, ct, 0:128], r_=gy3[:, ct, :], ct=ct:
                                   e.matmul(o, l, r_, start=(ct == 0), stop=(ct == 7)) for ct in range(8)], r=["gyT", wgk], w=[pak])
                    P.group("pe", [lambda e, o=pb[:, :], l=wg3[:, ct, 128:256], r_=gy3[:, ct, :], ct=ct:
                                   e.matmul(o, l, r_, start=(ct == 0), stop=(ct == 7)) for ct in range(8)], r=["gyT", wgk], w=[pbk])
                    act(sgb[:], pb[:, :], AF.Sigmoid, [pbk], ["sgb"])
                    tt("dve", ga_[:], pa[:, :], sgb[:], ALU.mult, [pak, "sgb"], ["ftmp"])
                    tt("dve", hs3[:, et, :], ga_[:], zs[:], ALU.mult, ["ftmp", "zs"], ["hsT"])
            merge_half(False, sc)
            if True:
                so = sc
                rrA = ubm[:].bitcast(F32).rearrange("p (t c) -> p t c", c=D)
                rrB = Sh[:].bitcast(F32).rearrange("p (t c) -> p t c", c=D)
                rts = [(rrA[:, 0, :], "ubm"), (rrA[:, 1, :], "ubm"), (rrB[:, 0, :], "Sh"), (rrB[:, 1, :], "Sh")]
                sq = U[:].bitcast(F32)
                gch = X[:, 0:D]; bch = X[:, D:2 * D]
                st = sb("lnst", [128, 8], F32, so)
                for t4 in range(4):
                    rt, rk = rts[t4]
                    P.dma("pool", rt, xo[SPAN * sp_i + 128 * t4:SPAN * sp_i + 128 * t4 + 128, :], w=[rk])
                P.dma("pool", gch, lng_d, w=["X"]); P.dma("pool", bch, lnb_d, w=["X"])

                def layer_norm(t4):
                    rt, rk = rts[t4]
                    P.op("dve", lambda e: e.reduce_sum(st[:, 0:1], rt, mybir.AxisListType.X), r=[rk], w=["lnst"])
                    tt("dve", sq, rt, rt, ALU.mult, [rk], ["U"])
                    P.op("dve", lambda e: e.reduce_sum(st[:, 1:2], sq, mybir.AxisListType.X), r=["U"], w=["lnst"])
                    ts("dve", st[:, 2:3], st[:, 0:1], 1.0 / D, None, ALU.mult, None, ["lnst"], ["lnst"])
                    tt("dve", st[:, 3:4], st[:, 2:3], st[:, 2:3], ALU.mult, ["lnst"], ["lnst"])
                    ts("dve", st[:, 4:5], st[:, 1:2], 1.0 / D, float(LN_EPS), ALU.mult, ALU.add, ["lnst"], ["lnst"])
                    tt("dve", st[:, 4:5], st[:, 4:5], st[:, 3:4], ALU.subtract, ["lnst"], ["lnst"])
                    P.op("act", lambda e: e.activation(st[:, 5:6], st[:, 4:5], AF.Sqrt), r=["lnst"], w=["lnst"])
                    P.op("dve", lambda e: e.reciprocal(st[:, 5:6], st[:, 5:6]), r=["lnst"], w=["lnst"])
                    P.op("dve", lambda e: e.scalar_tensor_tensor(st[:, 6:7], st[:, 2:3], -1.0, st[:, 5:6], ALU.mult, ALU.mult),
                         r=["lnst"], w=["lnst"])
                    P.op("act", lambda e: e.activation(rt, rt, AF.Identity, bias=st[:, 6:7], scale=st[:, 5:6]), r=["lnst"], w=[rk])
                    tt("dve", rt, rt, gch, ALU.mult, ["X"], [rk])
                    tt("dve", rt, rt, bch, ALU.add, ["X"], [rk])
                    P.dma("pool", y_out[SPAN * sp_i + 128 * t4:SPAN * sp_i + 128 * t4 + 128, :], rt, r=[rk])

                for th in range(2):
                    for cb in range(8):
                        wv, wk = load_c(s_out, "s_out", cb)
                        for t2 in range(2):
                            t4 = 2 * th + t2
                            rt, rk = rts[t4]
                            ps, pk = nps()
                            P.group("pe", [lambda e, o=ps[:, 0:256], l=mT3[:, dt, 128 * t4:128 * t4 + 128], r_=wv[:, dt, :], dt=dt:
                                           e.matmul(o, l, r_, start=(dt == 0), stop=(dt == 15)) for dt in range(16)], r=["mT", wk], w=[pk])
                            P.op("dve", lambda e, o=rt[:, 256 * cb:256 * cb + 256], p_=ps[:, 0:256]:
                                 e.scalar_tensor_tensor(o, o, float(ALPHA), p_, ALU.mult, ALU.add), r=[pk], w=[rk])
                        if th == 1 and cb == 1:
                            layer_norm(0)
                        if th == 1 and cb == 4:
                            layer_norm(1)
                pre_tail()
                if defer:
                    late["pending"] = [lambda: layer_norm(2), lambda: layer_norm(3)]
                else:
                    layer_norm(2)
                    layer_norm(3)

        n_pre = dbg.get("n_pre", NSPAN)
        n_own = dbg.get("n_own", NSPAN)
        spre = ExitStack()
        PWr = sb("PWr", [128, 32 * NB], F32, spre); PWi = sb("PWi", [128, 32 * NB], F32, spre)
        CA64 = sb("CA64", [128, 64], F32, spre); CB64n = sb("CB64n", [128, 32], F32, spre); CB64p = sb("CB64p", [128, 32], F32, spre)
        with ExitStack() as sw:
            NM = 65
            msc = sb("msc_s", [128, NM], F32, sw)
            ANGx = sb("ANGx", [128, 2 * NM * 32], F32, sw); TMPx = sb("TMPx", [128, 2 * NM * 32], F32, sw)
            KIx = sb("KIx", [128, 2 * NM * 32], I32, sw); MARx = sb("MARx", [128, NM * 32], F32, sw)
            P.dma("sp", msc[:], msc_d, w=["msc_s"])
            an4 = ANGx[:].rearrange("p (s m q) -> p s m q", s=2, q=32)
            mb = msc[:].unsqueeze(2).to_broadcast([128, NM, 32])
            tt("dve", an4[:, 0], mb, ai[:].unsqueeze(1).to_broadcast([128, NM, 32]), ALU.mult, ["msc_s", "ai"], ["ANGx"])
            ts("dve", an4[:, 1], an4[:, 0], PI / 2, None, ALU.add, None, ["ANGx"], ["ANGx"])
            sin_reduced(ANGx, TMPx, KIx)
            ma3 = MARx[:].rearrange("p (m q) -> p m q", q=32)
            tt("dve", ma3, mb, ar[:].unsqueeze(1).to_broadcast([128, NM, 32]), ALU.mult, ["msc_s", "ar"], ["MARx"])
            act(MARx[:], MARx[:], AF.Exp, ["MARx"], ["MARx"])
            tt("dve", PWr[:].rearrange("p (q m) -> p m q", m=NB), ma3[:, 0:NB, :], an4[:, 1, 0:NB, :], ALU.mult, ["MARx", "ANGx"], ["PWr"])
            tt("dve", PWi[:].rearrange("p (q m) -> p m q", m=NB), ma3[:, 0:NB, :], an4[:, 0, 0:NB, :], ALU.mult, ["MARx", "ANGx"], ["PWi"])
            tt("dve", CA64[:, 0:32], ma3[:, NB, :], an4[:, 1, NB, :], ALU.mult, ["MARx", "ANGx"], ["CA64"])
            copy("dve", CA64[:, 32:64], CA64[:, 0:32], ["CA64"], ["CA64"])
            tt("dve", CB64p[:], ma3[:, NB, :], an4[:, 0, NB, :], ALU.mult, ["MARx", "ANGx"], ["CB64p"])
            ts("dve", CB64n[:], CB64p[:], -1.0, None, ALU.mult, None, ["CB64p"], ["CB64n"])
            P.barrier()
        pfx = dict(PWr=PWr, PWi=PWi, CA64=CA64, CB64n=CB64n, CB64p=CB64p)
        xb4 = xb + [(t_[:, :], t_.name) for t_ in (sb(f"xb{i}", [128, D], BF16, pre_scope) for i in (2, 3))]
        n_cast = -(-len(cast_jobs) // max(1, n_pre))
        for sp_i in range(NSPAN - n_pre, NSPAN):
            sc = pre_scope
            build_xT(xp, SPAN * sp_i, xb4, ev="act")
            ssm_state(sc, own=False, pfx=pfx)
            if sp_i + 1 < NSPAN:
                prefetch_x(xp, SPAN * (sp_i + 1), xb4, 0, 4)
            elif n_own:
                prefetch_x(xo, 0, xb, 0, 2)
            issue_casts(n_cast)
            if sp_i == NSPAN - 1:
                kv_proj(SPAN - 128, 128, 0, 0)
        issue_casts(len(cast_jobs))
        P.barrier()
        pre_scope.close()
        memo.clear()
        spre.close()
        mT = sb("mT", [128, 16 * SPAN], BF16); haT = sb("haT", [128, 8 * SPAN], BF16); hsT = sb("hsT", [128, 8 * SPAN], BF16)
        late.update(mT=mT, haT=haT, hsT=hsT, mT3=mT[:].rearrange("p (c t) -> p c t", t=SPAN),
                    ha3=haT[:].rearrange("p (c t) -> p c t", t=SPAN), hs3=hsT[:].rearrange("p (c t) -> p c t", t=SPAN))
        xbuilt = {}
        for sp_i in range(n_own):
            if True:
                sc = own_scope
                if not xbuilt.get(sp_i):
                    build_xT(xo, SPAN * sp_i, xb)
                late["ftmp"] = sb("ftmp", [128, 512], F32, sc)
                ubm, U3, Sh, (U_, X_) = ssm_state(sc, own=True)
                if sp_i + 1 < n_own:
                    prefetch_x(xo, SPAN * (sp_i + 1), xb, 0, 2)
                attn_stage(sp_i, sc)
                pT_ = memo["pT"]
                xb_own = xb + [(pT_[:, 0:D], "pT"), (pT_[:, D:2 * D], "pT")]
                if sp_i + 1 < n_own:
                    prefetch_x(xo, SPAN * (sp_i + 1), xb_own, 2, 4)
                merge_half(True, sc)
                ssm_out(ubm, U3, Sh)

                def pre_tail(n=sp_i):
                    if n + 1 < n_own:
                        build_xT(xo, SPAN * (n + 1), xb_own)
                        xbuilt[n + 1] = True
                main_stage(sp_i, sc, ubm, U_, X_, Sh[0], pre_tail, sp_i + 1 < n_own)

        P.barrier()
        own_scope.close()
        with nc.Block() as block:
            @block.tensor
            def _(e):
                P.emit("pe", e)

            @block.scalar
            def _(e):
                P.emit("act", e)

            @block.vector
            def _(e):
                P.emit("dve", e)

            @block.gpsimd
            def _(e):
                P.emit("pool", e)

            @block.sync
            def _(e):
                P.emit("sp", e)
    return nc


def _t5_bucket(dist):
    max_exact = 16
    d = np.maximum(dist, 1).astype(np.float32)
    large = max_exact + (np.log(d / np.float32(max_exact)) / np.float32(math.log(128 / max_exact)) * np.float32(16)).astype(np.int32)
    large = np.minimum(large, 31)
    return np.where(dist < max_exact, dist, large)


def kernel(x, w_in, ssm_lambda_re, ssm_lambda_im, ssm_b_re, ssm_b_im, ssm_c_re, ssm_c_im, ssm_d, ssm_log_step,
           w_glu, attn_sinks, rel_bias_table, w_branch_ssm, w_branch_attn, w_out, ln_gain, ln_bias):
    f = lambda a: np.ascontiguousarray(np.asarray(a, dtype=np.float32))
    x = f(x)
    pl = lambda a: f(a.reshape(32, 2, 64).transpose(1, 2, 0).reshape(128, 32))
    plb = lambda a: f(a.reshape(32, 2, 64, 16).transpose(1, 2, 0, 3).reshape(128, 512))
    plc = lambda a: f(a.reshape(32, 2, 16, 64).transpose(1, 3, 0, 2).reshape(128, 512))
    ls = np.asarray(ssm_log_step[0], np.float32).reshape(32, 2)
    lstep = f(np.broadcast_to(ls.T[:, None, :], (2, 64, 32)).reshape(128, 32))
    dI = np.zeros((16, 64, 16), np.float32)
    dd = np.asarray(ssm_d[0], np.float32).reshape(64, 16)
    for h in range(16):
        dI[h, :, h] = dd[:, h]
    sk = np.zeros((128, 8), np.float32)
    sinks = np.asarray(attn_sinks[0], np.float32)
    for hp in range(8):
        sk[0:64, hp] = sinks[2 * hp]
        sk[64:128, hp] = sinks[2 * hp + 1]
    tab = np.asarray(rel_bias_table, np.float32)
    line = np.full((16, 384), NEG, np.float32)
    bk = _t5_bucket(np.arange(128))
    line[:, 128:256] = tab[bk, :].T
    common = {
        "w_in": f(w_in[0]), "w_glu": f(w_glu[0]), "w_bs": f(w_branch_ssm[0]), "w_ba": f(w_branch_attn[0]), "w_out": f(w_out[0]),
        "lamr": pl(np.asarray(ssm_lambda_re[0])), "lami": pl(np.asarray(ssm_lambda_im[0])), "lstep": lstep,
        "br": plb(np.asarray(ssm_b_re[0])), "bi": plb(np.asarray(ssm_b_im[0])),
        "cr": plc(np.asarray(ssm_c_re[0])), "ci": plc(np.asarray(ssm_c_im[0])),
        "dI": f(dI.reshape(16, 1024)), "sk": sk, "line": line,
        "lng": f(np.broadcast_to(np.asarray(ln_gain[0], np.float32)[None, :], (128, D))),
        "lnb": f(np.broadcast_to(np.asarray(ln_bias[0], np.float32)[None, :], (128, D))),
        "idf": np.eye(128, dtype=np.float32), "anti": f(np.eye(128, dtype=np.float32)[::-1]),
        "msc": f(np.broadcast_to(np.array([8.0 * (NB - 1 - b) for b in range(NB)] + [8.0 * NB], np.float32)[None, :], (128, 65))),
    }
    in_maps = []
    for c in range(NCORES):
        b, h = c // 2, c % 2
        m = dict(common)
        m["xo"] = f(x[b, h * TOK:(h + 1) * TOK])
        m["xp"] = f(x[b, 0:TOK]) if h == 1 else np.zeros((TOK, D), np.float32)
        m["hneg"] = np.full((128, 1), 0.0 if h == 1 else NEG, np.float32)
        in_maps.append(m)
    nc = build_program()
    res = run_bass_kernel_spmd(nc, in_maps, core_ids=list(range(NCORES)))
    out = np.zeros((4, 8192, D), np.float32)
    for c in range(NCORES):
        b, h = c // 2, c % 2
        out[b, h * TOK:(h + 1) * TOK] = np.asarray(res.results[c]["y"], np.float32)
    return out
```

```python
import math
import numpy as np
from contextlib import ExitStack
import concourse.bass as bass
import concourse.mybir as mybir
from concourse.bass_utils import run_bass_kernel_spmd

F32 = mybir.dt.float32
BF16 = mybir.dt.bfloat16
I32 = mybir.dt.int32
ALU = mybir.AluOpType
AF = mybir.ActivationFunctionType
PI = float(np.pi)
TWO_PI = float(2 * np.pi)

NCORES = 8
TOK = 4096
SPAN = 512
NB = SPAN // 8
NSPAN = TOK // SPAN
D = 2048
DIN = 8704
ALPHA = 2.0 ** 0.25
LN_EPS = 1e-5
NEG = -30000.0
C_U, C_ZS, C_Q, C_K, C_V, C_ZA, C_G = 0, 1024, 2048, 3072, 3328, 3584, 4608

ENGS = ["pe", "act", "dve", "pool", "sp"]
NSP = 8
NPL = 6
NDS = NSP + NPL
NBG = 6


class Prog:
    def __init__(self, nc, st):
        self.nc = nc
        self.q = {e: [] for e in ENGS}
        self.sem = {e: st.enter_context(nc.semaphore("c_" + e)) for e in ENGS if e != "sp"}
        self.cnt = {e: 0 for e in ENGS}
        self.dsem = [st.enter_context(nc.semaphore(f"dq{i}")) for i in range(NDS + NBG)]
        self.dcnt = [0] * (NDS + NBG)
        self.dnext = {"sp": 0, "other": 0}
        self.bnext = 0
        self.seen = {e: {} for e in ENGS}
        self.lastw = {}
        self.readers = {}

    def _need(self, eng, tok, waits, raw=False):
        kind, src, val = tok
        if kind == "e" and src == eng and (eng == "pe" or not raw):
            return
        key = (kind, src)
        if self.seen[eng].get(key, 0) >= val:
            return
        self.seen[eng][key] = val
        waits.append(tok)

    def _deps(self, eng, r, w):
        waits = []
        for b in r:
            t = self.lastw.get(b)
            if t:
                self._need(eng, t, waits, raw=True)
        for b in w:
            t = self.lastw.get(b)
            if t:
                self._need(eng, t, waits)
            for t in self.readers.get(b, ()):
                self._need(eng, t, waits)
        return waits

    def _commit(self, tok, r, w):
        for b in r:
            self.readers.setdefault(b, []).append(tok)
        for b in w:
            self.lastw[b] = tok
            self.readers[b] = []

    def op(self, eng, fn, r=(), w=()):
        waits = self._deps(eng, r, w)
        self.cnt[eng] += 1
        tok = ("e", eng, self.cnt[eng])
        self.q[eng].append((waits, fn, tok))
        self._commit(tok, r, w)

    def group(self, eng, fns, r=(), w=(), rr=()):
        waits = self._deps(eng, r, w)
        self.cnt[eng] += 1
        tok = ("e", eng, self.cnt[eng])
        for i, fn in enumerate(fns):
            self.q[eng].append((waits if i == 0 else [], fn, tok if i == len(fns) - 1 else None))
        self._commit(tok, list(r) + list(rr), w)

    def dma(self, eng, out, in_, r=(), w=(), bg=False):
        waits = self._deps(eng, r, w)
        if bg:
            s = NDS + self.bnext
            self.bnext = (self.bnext + 1) % NBG
        elif eng == "sp":
            s = self.dnext["sp"]
            self.dnext["sp"] = (s + 1) % NSP
        else:
            s = NSP + self.dnext["other"]
            self.dnext["other"] = (self.dnext["other"] + 1) % NPL
        if self.dcnt[s]:
            self._need(eng, ("d", s, self.dcnt[s]), waits)
        self.dcnt[s] += 16
        tok = ("d", s, self.dcnt[s])
        self.q[eng].append((waits, lambda e, o=out, i=in_: e.dma_start(out=o, in_=i), tok))
        self._commit(tok, r, w)

    def barrier(self):
        for e in ENGS:
            waits = []
            for f in ENGS:
                if f != "sp" and self.cnt[f]:
                    self._need(e, ("e", f, self.cnt[f]), waits)
            for s in range(NDS):
                if self.dcnt[s]:
                    self._need(e, ("d", s, self.dcnt[s]), waits)
            if waits:
                self.q[e].append((waits, None, None))
        keep = {k: v for k, v in self.lastw.items() if isinstance(k, tuple) and v[0] == "d" and v[1] >= NDS}
        self.lastw.clear()
        self.lastw.update(keep)
        self.readers.clear()

    def emit(self, eng, e):
        for waits, fn, tok in self.q[eng]:
            for kind, src, val in waits:
                e.wait_ge(self.sem[src] if kind == "e" else self.dsem[src], val)
            if fn is None:
                continue
            ins = fn(e)
            if tok is not None:
                if tok[0] == "e":
                    ins.then_inc(self.sem[eng], 1)
                else:
                    ins.then_inc(self.dsem[tok[1]], 16)


def build_program(debug=None):
    dbg = debug or {}
    nc = bass.Bass("TRN2", target_bir_lowering=False)
    dr = lambda n, s, k="ExternalInput", d=F32: nc.dram_tensor(n, list(s), d, kind=k).ap()
    xo = dr("xo", [TOK, D])
    xp = dr("xp", [TOK, D])
    w_in = dr("w_in", [D, DIN])
    w_glu = dr("w_glu", [1024, 2048])
    w_bs = dr("w_bs", [1024, 2048])
    w_ba = dr("w_ba", [1024, 2048])
    w_out = dr("w_out", [D, D])
    lamr_d = dr("lamr", [128, 32]); lami_d = dr("lami", [128, 32]); lstep_d = dr("lstep", [128, 32])
    br_d = dr("br", [128, 512]); bi_d = dr("bi", [128, 512]); cr_d = dr("cr", [128, 512]); ci_d = dr("ci", [128, 512])
    dI_d = dr("dI", [16, 1024])
    sk_d = dr("sk", [128, 8])
    line_d = dr("line", [16, 384])
    hneg_d = dr("hneg", [128, 1])
    lng_d = dr("lng", [128, D]); lnb_d = dr("lnb", [128, D])
    idf_d = dr("idf", [128, 128]); anti_d = dr("anti", [128, 128])
    msc_d = dr("msc", [128, 65])
    y_out = dr("y", [TOK, D], k="ExternalOutput")
    s_in = dr("s_in", [34, 128, 4096], k="Internal", d=BF16)
    s_glu = dr("s_glu", [8, 128, 2048], k="Internal", d=BF16)
    s_bs = dr("s_bs", [8, 128, 2048], k="Internal", d=BF16)
    s_ba = dr("s_ba", [8, 128, 2048], k="Internal", d=BF16)
    s_out = dr("s_out", [8, 128, 4096], k="Internal", d=BF16)
    s_k2 = dr("s_k2", [2, 128, 4096], k="Internal", d=BF16)
    s_w1 = dr("s_w1", [2, 128, 4096], k="Internal", d=BF16)
    s_wc = dr("s_wc", [2, 128, 4096], k="Internal", d=BF16)
    s_tm = dr("s_tm", [2, 128, 4096], k="Internal", d=BF16)

    with ExitStack() as st:
        P = Prog(nc, st)
        used = {}
        dumped = {}

        def dump(name, t, key, parts=128):
            if name not in dbg.get("dumps", ()) or name in dumped:
                return
            shape = [parts, t.shape[-1]]
            dst = nc.dram_tensor("dbg_" + name, shape, t.dtype, kind="ExternalOutput").ap()
            P.dma("sp", dst, t[0:parts, :], r=[key])
            dumped[name] = True

        memo = {}
        own_scope = ExitStack()
        pre_scope = ExitStack()

        def sb(n, s, d=F32, c=st):
            if c is own_scope or c is pre_scope:
                if n in memo:
                    return memo[n]
            k = used.get(n, 0)
            used[n] = k + 1
            t = c.enter_context(nc.sbuf_tensor(n if k == 0 else f"{n}_{k}", list(s), d))
            if c is own_scope or c is pre_scope:
                memo[n] = t
            return t
        identb = sb("identb", [128, 128], BF16)
        identf = sb("identf", [128, 128])
        antif = sb("antif", [128, 128])
        ones64 = sb("ones64", [128, 64], BF16)
        biasT = sb("biasT", [128, 16 * 2 * 128])
        CA = sb("CA", [128, 64])
        CBn = sb("CBn", [128, 32]); CBp = sb("CBp", [128, 32])
        carry = sb("carry", [128, 64])
        esk = sb("esk", [128, 8])
        ar = sb("ar", [128, 32]); ai = sb("ai", [128, 32])
        hneg = sb("hneg_s", [128, 1])
        wstate = {"i": 0}
        wbuf = []
        psbig = st.enter_context(nc.psum_tensor("psbig", [128, 6 * 512], F32))
        psf = [psbig[:, 512 * i:512 * i + 512] for i in range(6)]
        psb = [st.enter_context(nc.psum_tensor(f"psb{i}", [128, 1024], BF16)) for i in range(2)]
        pstate = {"f": 0, "b": 0}

        def nps():
            i = pstate["f"]; pstate["f"] = (i + 1) % 6
            return psf[i], f"psf{i}"

        def nps2():
            i = pstate["f"]
            if i % 2:
                i = (i + 1) % 6
            pstate["f"] = (i + 2) % 6
            return psbig[:, 512 * i:512 * i + 1024], (f"psf{i}", f"psf{i + 1}")

        def npb():
            i = pstate["b"]; pstate["b"] = (i + 1) % 2
            return psb[i], f"psb{i}"

        def nwb():
            i = wstate["i"]; wstate["i"] = (i + 1) % 4
            return wbuf[i], f"wbuf{i}"

        evs = {"i": 0}

        def evac_eng():
            evs["i"] ^= 1
            return "dve" if evs["i"] else "act"

        def copy(eng, out, in_, r, w):
            if eng == "act":
                P.op("act", lambda e, o=out, i=in_: e.copy(o, i), r=r, w=w)
            else:
                P.op(eng, lambda e, o=out, i=in_: e.tensor_copy(o, i), r=r, w=w)

        def tt(eng, out, a, b, op, r, w):
            P.op(eng, lambda e, o=out, x=a, y=b, p=op: e.tensor_tensor(o, x, y, p), r=r, w=w)

        def ts(eng, out, a, s1, s2, op0, op1, r, w):
            if op1 is None:
                P.op(eng, lambda e, o=out, x=a: e.tensor_scalar(o, x, s1, None, op0), r=r, w=w)
            else:
                P.op(eng, lambda e, o=out, x=a: e.tensor_scalar(o, x, s1, s2, op0, op1), r=r, w=w)

        def taylor_exp(q, x, deg, keyq, keyx):
            ts("dve", q, x, 1.0 / deg, 1.0, ALU.mult, ALU.add, [keyx], [keyq])
            for k in range(deg - 1, 0, -1):
                tt("dve", q, q, x, ALU.mult, [keyq, keyx], [keyq])
                ts("dve", q, q, 1.0 / k, 1.0, ALU.mult, ALU.add, [keyq], [keyq])

        C1 = 6.28125
        C2 = TWO_PI - C1

        def sin_reduced(A, T_, K_):
            a, t_, k_ = A.name, T_.name, K_.name
            ts("dve", T_[:], A[:], 1.0 / TWO_PI, None, ALU.mult, None, [a], [t_])
            copy("dve", K_[:], T_[:], [t_], [k_])
            copy("dve", T_[:], K_[:], [k_], [t_])
            P.op("dve", lambda e: e.scalar_tensor_tensor(A[:], T_[:], -C1, A[:], ALU.mult, ALU.add), r=[t_, a], w=[a])
            P.op("dve", lambda e: e.scalar_tensor_tensor(A[:], T_[:], -C2, A[:], ALU.mult, ALU.add), r=[t_, a], w=[a])
            ts("dve", T_[:], A[:], PI, -TWO_PI, ALU.is_gt, ALU.mult, [a], [t_])
            tt("dve", A[:], A[:], T_[:], ALU.add, [t_, a], [a])
            ts("dve", T_[:], A[:], -PI, TWO_PI, ALU.is_lt, ALU.mult, [a], [t_])
            tt("dve", A[:], A[:], T_[:], ALU.add, [t_, a], [a])
            ts("dve", A[:], A[:], PI, -PI, ALU.min, ALU.max, [a], [a])
            act(A[:], A[:], AF.Sin, [a], [a])

        def act(out, in_, func, r, w, scale=1.0):
            P.op("act", lambda e, o=out, i=in_, f=func, s=scale: e.activation(o, i, f, scale=s), r=r, w=w)

        def cast_w(scr, sname, idx, src, c0, ncols, col_off=0):
            nk = src.shape[0] // 128
            dst = scr[idx].rearrange("p (k c) -> p k c", c=256)[:, 0:nk, col_off:col_off + ncols]
            P.dma("pool", dst, src.rearrange("(k p) c -> p k c", p=128)[:, :, c0:c0 + ncols],
                  w=[(sname, idx, col_off)], bg=True)

        cast_jobs = []

        def cast_all(first):
            if first:
                for i in list(range(4)) + [13]:
                    cast_w(s_in, "s_in", i, w_in, 256 * i, 256)
                for kh in range(4):
                    for dup in range(2):
                        cast_w(s_k2, "s_k2", kh // 2, w_in, C_K + 64 * kh, 64, 128 * (kh % 2) + 64 * dup)
                return
            J = cast_jobs.append
            for i in list(range(8, 12)) + list(range(14, 18)):
                J((s_in, "s_in", i, w_in, 256 * i, 256, 0))
            for i in range(8):
                J((s_in, "s_in", 26 + i, w_in, 256 * (26 + i), 256, 0))
                J((s_ba, "s_ba", i, w_ba, 256 * i, 256, 0))
            for i in range(4, 8):
                J((s_in, "s_in", i, w_in, 256 * i, 256, 0))
            for i in range(8):
                J((s_glu, "s_glu", i, w_glu, 128 * i, 128, 0)); J((s_glu, "s_glu", i, w_glu, 1024 + 128 * i, 128, 128))
            for i in range(8):
                J((s_in, "s_in", 18 + i, w_in, 256 * (18 + i), 256, 0))
                J((s_bs, "s_bs", i, w_bs, 256 * i, 256, 0))
            for i in range(8):
                J((s_out, "s_out", i, w_out, 256 * i, 256, 0))

        def issue_casts(n):
            for _ in range(min(n, len(cast_jobs))):
                cast_w(*cast_jobs.pop(0))

        def load_c(scr, sname, idx, nk=16, cw=256, offs=(0, 128)):
            dst, key = nwb()
            P.dma("sp", dst[:, 0:nk * 256], scr[idx][:, 0:nk * 256], r=[(sname, idx, o_) for o_ in offs], w=[key])
            return dst[:].rearrange("p (k c) -> p k c", c=cw), key

        with ExitStack() as s0:
            sb0 = lambda n, s, d=F32: sb(n, s, d, s0)
            lamr = sb0("lamr_s", [128, 32]); lami = sb0("lami_s", [128, 32]); stp = sb0("stp", [128, 32])
            W1re = sb0("W1re", [128, 32 * 128], BF16)
            W1im = sb0("W1im", [128, 32 * 128], BF16)
            Wc = sb0("Wc", [128, 32 * 2 * 128], BF16)
            Tm = sb0("Tm", [128, 64 * 128], BF16)
            brs = sb0("brs", [128, 512]); bis = sb0("bis", [128, 512]); crs = sb0("crs", [128, 512]); cis = sb0("cis", [128, 512])
            dIs = sb0("dIs", [16, 1024]); sks = sb0("sks", [128, 8])
            for t_, d_ in ((lamr, lamr_d), (lami, lami_d), (stp, lstep_d), (brs, br_d), (bis, bi_d), (crs, cr_d),
                           (cis, ci_d), (dIs, dI_d), (sks, sk_d), (hneg, hneg_d), (identf, idf_d), (antif, anti_d)):
                P.dma("sp", t_[:], d_, w=[t_.name])
            copy("dve", identb[:], identf[:], [identf.name], [identb.name])
            VH = [sb0(f"VH{i}", [128, 256]) for i in range(4)]

            def bias_head(h):
                vh = VH[h % 4]
                for kt_ in range(2):
                    P.dma("sp", vh[:, 128 * kt_:128 * kt_ + 128],
                          bass.AP(line_d.tensor, 384 * h + 129 - 128 * kt_, [[1, 128], [1, 128]]), w=[vh.name])
                ps, pk = nps()
                P.group("pe", [lambda e, o=ps[:, 128 * k_:128 * k_ + 128], r_=vh[:, 128 * k_:128 * k_ + 128]:
                               e.matmul(o, antif[:], r_, start=True, stop=True) for k_ in range(2)], r=[vh.name, antif.name], w=[pk])
                copy("act", biasT[:, 256 * h:256 * h + 256], ps[:, 0:256], [pk], ["biasT"])
            P.op("dve", lambda e: e.memset(ones64[:], 1.0), w=["ones64"])
            P.op("dve", lambda e: e.memset(carry[:], 0.0), w=["carry"])
            act(esk[:], sks[:], AF.Exp, [sks.name], ["esk"])
            ts("dve", ar[:], stp[:], 0.125, None, ALU.mult, None, [stp.name], ["ar"])
            taylor_exp(stp[:], ar[:], 10, stp.name, "ar")
            for _ in range(3):
                tt("dve", stp[:], stp[:], stp[:], ALU.mult, [stp.name], [stp.name])
            tt("dve", ar[:], lamr[:], stp[:], ALU.mult, [lamr.name, stp.name], ["ar"])
            tt("dve", ai[:], lami[:], stp[:], ALU.mult, [lami.name, stp.name], ["ai"])
            MAR = sb0("MAR", [128, 9 * 32]); ANG = sb0("ANG", [128, 2 * 9 * 32]); TMPA = sb0("TMPA", [128, 576])
            KI = sb0("KI", [128, 576], I32)
            for m in range(9):
                ts("dve", MAR[:, 32 * m:32 * m + 32], ar[:], float(m), None, ALU.mult, None, ["ar"], ["MAR"])
                ts("dve", ANG[:, 32 * m:32 * m + 32], ai[:], float(m), None, ALU.mult, None, ["ai"], ["ANG"])
                ts("dve", ANG[:, 288 + 32 * m:288 + 32 * m + 32], ai[:], float(m), PI / 2, ALU.mult, ALU.add, ["ai"], ["ANG"])
            sin_reduced(ANG, TMPA, KI)
            MAG = sb0("MAG", [128, 9 * 32])
            taylor_exp(MAG[:], MAR[:], 8, "MAG", "MAR")
            PR = sb0("PR", [128, 288]); PIm = sb0("PIm", [128, 288])
            tt("dve", PR[:], MAG[:], ANG[:, 288:576], ALU.mult, ["MAG", "ANG"], ["PR"])
            tt("dve", PIm[:], MAG[:], ANG[:, 0:288], ALU.mult, ["MAG", "ANG"], ["PIm"])
            copy("dve", CA[:, 0:32], PR[:, 256:288], ["PR"], ["CA"])
            copy("dve", CA[:, 32:64], PR[:, 256:288], ["PR"], ["CA"])
            copy("dve", CBp[:], PIm[:, 256:288], ["PIm"], ["CBp"])
            ts("dve", CBn[:], PIm[:, 256:288], -1.0, None, ALU.mult, None, ["PIm"], ["CBn"])
            nr = sb0("nr", [128, 32]); den = sb0("den", [128, 32]); t1 = sb0("t1", [128, 32]); t2 = sb0("t2", [128, 32])
            kr = sb0("kr", [128, 32]); ki_ = sb0("ki_", [128, 32])
            ts("dve", nr[:], PR[:, 32:64], -1.0, None, ALU.add, None, ["PR"], ["nr"])
            tt("dve", den[:], lamr[:], lamr[:], ALU.mult, [lamr.name], ["den"])
            tt("dve", t1[:], lami[:], lami[:], ALU.mult, [lami.name], ["t1"])
            tt("dve", den[:], den[:], t1[:], ALU.add, ["t1"], ["den"])
            P.op("dve", lambda e: e.reciprocal(den[:], den[:]), r=["den"], w=["den"])
            tt("dve", t1[:], nr[:], lamr[:], ALU.mult, ["nr"], ["t1"])
            tt("dve", t2[:], PIm[:, 32:64], lami[:], ALU.mult, ["PIm"], ["t2"])
            tt("dve", t1[:], t1[:], t2[:], ALU.add, ["t2"], ["t1"])
            tt("dve", kr[:], t1[:], den[:], ALU.mult, ["t1", "den"], ["kr"])
            tt("dve", t1[:], PIm[:, 32:64], lamr[:], ALU.mult, ["PIm"], ["t1"])
            tt("dve", t2[:], nr[:], lami[:], ALU.mult, ["nr"], ["t2"])
            tt("dve", t1[:], t1[:], t2[:], ALU.subtract, ["t2"], ["t1"])
            tt("dve", ki_[:], t1[:], den[:], ALU.mult, ["t1", "den"], ["ki_"])
            OMR = sb0("OMR", [128, 256]); OMI = sb0("OMI", [128, 256]); T8 = sb0("T8", [128, 256])
            pr3 = PR[:, 0:256].rearrange("p (m q) -> p m q", q=32); pi3 = PIm[:, 0:256].rearrange("p (m q) -> p m q", q=32)
            krb = kr[:].unsqueeze(1).to_broadcast([128, 8, 32]); kib = ki_[:].unsqueeze(1).to_broadcast([128, 8, 32])
            omr3 = OMR[:].rearrange("p (m q) -> p m q", q=32); omi3 = OMI[:].rearrange("p (m q) -> p m q", q=32)
            t83 = T8[:].rearrange("p (m q) -> p m q", q=32)
            tt("dve", omr3, pr3, krb, ALU.mult, ["PR", "kr"], ["OMR"])
            tt("dve", t83, pi3, kib, ALU.mult, ["PIm", "ki_"], ["T8"])
            tt("dve", OMR[:], OMR[:], T8[:], ALU.subtract, ["T8"], ["OMR"])
            tt("dve", omi3, pi3, krb, ALU.mult, ["PIm", "kr"], ["OMI"])
            tt("dve", t83, pr3, kib, ALU.mult, ["PR", "ki_"], ["T8"])
            tt("dve", OMI[:], OMI[:], T8[:], ALU.add, ["T8"], ["OMI"])
            TB = sb0("TB", [128, 512]); A0r = sb0("A0r", [128, 512]); A0i = sb0("A0i", [128, 512])
            sA = ExitStack()
            AR = sb("AR", [128, 4096], F32, sA); AI = sb("AI", [128, 4096], F32, sA)
            ar4 = AR[:].rearrange("p (q j h) -> p q j h", j=8, h=16); ai4 = AI[:].rearrange("p (q j h) -> p q j h", j=8, h=16)
            br3 = brs[:].rearrange("p (q h) -> p q h", h=16); bi3 = bis[:].rearrange("p (q h) -> p q h", h=16)
            tb3 = TB[:].rearrange("p (q h) -> p q h", h=16)
            for j in range(8):
                m = 7 - j
                orb = OMR[:, 32 * m:32 * m + 32].unsqueeze(2).to_broadcast([128, 32, 16])
                oib = OMI[:, 32 * m:32 * m + 32].unsqueeze(2).to_broadcast([128, 32, 16])
                tt("dve", ar4[:, :, j, :], br3, orb, ALU.mult, [brs.name, "OMR"], ["AR"])
                tt("dve", tb3, bi3, oib, ALU.mult, [bis.name, "OMI"], ["TB"])
                tt("dve", ar4[:, :, j, :], ar4[:, :, j, :], tb3, ALU.subtract, ["TB"], ["AR"])
                tt("dve", ai4[:, :, j, :], bi3, orb, ALU.mult, [bis.name, "OMR"], ["AI"])
                tt("dve", tb3, br3, oib, ALU.mult, [brs.name, "OMI"], ["TB"])
                tt("dve", ai4[:, :, j, :], ai4[:, :, j, :], tb3, ALU.add, ["TB"], ["AI"])
            for src, dst, nm in ((AR, W1re, "W1re"), (AI, W1im, "W1im")):
                for q0 in range(0, 32, 4):
                    ps, pk = nps()
                    P.group("pe", [lambda e, o=ps[:, 128 * i:128 * i + 128], s_=src[:, 128 * (q0 + i):128 * (q0 + i) + 128]:
                                   e.transpose(o, s_, identf[:]) for i in range(4)], r=[src.name, identf.name], w=[pk])
                    copy(evac_eng(), dst[:, 128 * q0:128 * q0 + 512], ps[:, :], [pk], [nm])
            copy("dve", A0r[:].rearrange("p (q h) -> p q h", h=16), ar4[:, :, 7, :], ["AR"], ["A0r"])
            copy("dve", A0i[:].rearrange("p (q h) -> p q h", h=16), ai4[:, :, 7, :], ["AI"], ["A0i"])
            P.barrier()
            sA.close()
            ER = sb0("ER", [128, 32 * 9 * 16]); NEI = sb0("NEI", [128, 32 * 9 * 16])
            er4 = ER[:].rearrange("p (q k h) -> p q k h", k=9, h=16); ne4 = NEI[:].rearrange("p (q k h) -> p q k h", k=9, h=16)
            cr3 = crs[:].rearrange("p (q h) -> p q h", h=16); ci3 = cis[:].rearrange("p (q h) -> p q h", h=16)
            for k in range(9):
                prb = PR[:, 32 * k:32 * k + 32].unsqueeze(2).to_broadcast([128, 32, 16])
                pib = PIm[:, 32 * k:32 * k + 32].unsqueeze(2).to_broadcast([128, 32, 16])
                tt("dve", er4[:, :, k, :], cr3, prb, ALU.mult, [crs.name, "PR"], ["ER"])
                tt("dve", tb3, ci3, pib, ALU.mult, [cis.name, "PIm"], ["TB"])
                tt("dve", er4[:, :, k, :], er4[:, :, k, :], tb3, ALU.subtract, ["TB"], ["ER"])
                tt("dve", ne4[:, :, k, :], cr3, pib, ALU.mult, [crs.name, "PIm"], ["NEI"])
                tt("dve", tb3, ci3, prb, ALU.mult, [cis.name, "PR"], ["TB"])
                P.op("dve", lambda e, o=ne4[:, :, k, :]: e.scalar_tensor_tensor(o, o, -1.0, tb3, ALU.mult, ALU.subtract), r=["TB"], w=["NEI"])
            P.op("pool", lambda e: e.memset(Tm[:], 0.0), w=["Tm"])
            cast_all(True)
            cast_all(False)
            wc4 = Wc[:].rearrange("p (q r c) -> p q r c", r=2, c=128)
            for ri, src in ((0, ER), (1, NEI)):
                s4 = src[:].rearrange("p (q c) -> p q c", c=144)
                copy("dve", wc4[:, :, ri, :], s4[:, :, 16:144], [src.name], ["Wc"])
            WBF = [sb0(f"WBF{i}", [128, 2 * 2 * 128]) for i in range(4)]
            KTb = sb0("KTb", [16, 64 * 128], BF16)
            for wb in WBF:
                P.op("dve", lambda e, t_=wb: e.memset(t_[:], 0.0), w=[wb.name])
            dI3 = dIs[:].rearrange("p (g h) -> p g h", h=16)
            kt4 = KTb[:].rearrange("p (g k h) -> p g k h", k=8, h=16)
            for q in range(32):
                if q % 2 == 0:
                    bias_head(q // 2)
                wb = WBF[q % 4]
                w4 = wb[:].rearrange("p (r g c) -> p r g c", r=2, g=2)
                for g2 in range(2):
                    rows = slice(64 * g2, 64 * g2 + 64)
                    copy("act", w4[rows, 0, g2, :], ER[rows, 144 * q:144 * q + 128], ["ER"], [wb.name])
                    copy("act", w4[rows, 1, g2, :], NEI[rows, 144 * q:144 * q + 128], ["NEI"], [wb.name])
                ps, pk = nps()
                P.group("pe", [
                    lambda e, o=ps[0:16, 0:256], l=A0r[:, 16 * q:16 * q + 16], r_=wb[:, 0:256]: e.matmul(o, l, r_, start=True, stop=False),
                    lambda e, o=ps[0:16, 0:256], l=A0i[:, 16 * q:16 * q + 16], r_=wb[:, 256:512]: e.matmul(o, l, r_, start=False, stop=True),
                ], r=["A0r", "A0i", wb.name], w=[pk])
                copy("dve", KTb[:, 256 * q:256 * q + 256], ps[0:16, 0:256], [pk], ["KTb"])
                tt("dve", kt4[:, 2 * q:2 * q + 2, 0, :], ps[0:16, 0:256].rearrange("p (g k h) -> p g k h", g=2, h=16)[:, :, 0, :],
                   dI3[:, 2 * q:2 * q + 2, :], ALU.add, [pk, dIs.name], ["KTb"])
            tm3 = Tm[:].rearrange("p (g c) -> p g c", c=128)
            kt3 = KTb[:].rearrange("p (g c) -> p g c", c=128)
            for jp in range(8):
                P.dma("sp", tm3[16 * jp:16 * jp + 16, :, 16 * jp:128], kt3[:, :, 0:(8 - jp) * 16], r=["KTb", "Tm"], w=[("Tm", jp)])
            for i, (t_, nm_) in enumerate(((W1re, "W1re"), (W1im, "W1im"))):
                P.dma("sp", s_w1[i], t_[:], r=[nm_])
            for i in range(2):
                P.dma("sp", s_wc[i], Wc[:, 4096 * i:4096 * i + 4096], r=["Wc"])
                P.dma("sp", s_tm[i], Tm[:, 4096 * i:4096 * i + 4096], r=["Tm"] + [("Tm", jp) for jp in range(8)])
            P.barrier()
            for nm_, t_ in (("PR", PR), ("PIm", PIm), ("kr", kr), ("ki_", ki_), ("W1re", W1re), ("W1im", W1im), ("Wc", Wc), ("Tm", Tm),
                            ("biasT", biasT), ("esk", esk), ("CA", CA), ("CBn", CBn), ("ER", ER), ("NEI", NEI)):
                dump(nm_, t_, "none")
            dump("KTb", KTb, "none", parts=16)
            P.barrier()

        xT = sb("xT", [128, 16 * SPAN], BF16)
        gyT = sb("gyT", [128, 8 * SPAN], BF16)
        kT = sb("kT", [128, 4 * 640], BF16)
        vext = sb("vext", [128, 5 * 4 * 64], BF16)
        wbuf.extend(sb(f"wbuf{i}", [128, 16 * 256], BF16) for i in range(4))
        xb_t = [sb(f"xb{i}", [128, D], BF16) for i in range(2)]
        xb = [(t_[:, :], t_.name) for t_ in xb_t]
        xpre = {"n": 0}
        P.op("dve", lambda e: e.memset(kT[:], 0.0), w=["kT"])
        P.op("dve", lambda e: e.memset(vext[:], 0.0), w=["vext"])
        xT3 = xT[:].rearrange("p (k t) -> p k t", t=SPAN)
        gy3 = gyT[:].rearrange("p (c t) -> p c t", t=SPAN)
        kT3 = kT[:].rearrange("p (h t) -> p h t", t=640)
        vx4 = vext[:].rearrange("p (t h d) -> p t h d", h=4, d=64)
        bT4 = biasT[:].rearrange("p (h k q) -> p h k q", k=2, q=128)

        def build_xT(xsrc, row0, bufs, ev=None):
            for t4 in range(SPAN // 128):
                b_, bk_ = bufs[t4 % len(bufs)]
                if t4 >= xpre["n"]:
                    P.dma("pool", b_, xsrc[row0 + 128 * t4:row0 + 128 * t4 + 128, :], w=[bk_])
                for half in range(2):
                    ps, pk = npb()
                    P.group("pe", [lambda e, o=ps[:, 128 * i:128 * i + 128], s2=b_[:, 128 * (8 * half + i):128 * (8 * half + i) + 128]:
                                   e.transpose(o, s2, identb[:]) for i in range(8)], r=[bk_], w=[pk])
                    copy(ev or evac_eng(), xT3[:, 8 * half:8 * half + 8, 128 * t4:128 * t4 + 128],
                         ps[:, :].rearrange("p (k t) -> p k t", t=128), [pk], ["xT"])
            xpre["n"] = 0

        def prefetch_x(xsrc, row0, bufs, t_lo, t_hi):
            for t4 in range(t_lo, t_hi):
                P.dma("pool", bufs[t4][0], xsrc[row0 + 128 * t4:row0 + 128 * t4 + 128, :], w=[bufs[t4][1]])
            xpre["n"] = t_hi

        def ssm_state(sc, own, pfx=None):
            ubm = sb("ubm", [128, 8 * 1024], BF16, sc)
            U = sb("U", [128, 64 * NB], BF16, sc)
            X = sb("X", [128, 2 * 32 * NB], F32, sc)
            ub4 = ubm[:, 0:4096].rearrange("p (g j h) -> p g j h", j=4, h=16)
            U3 = U[:].rearrange("p (g b) -> p g b", b=NB)
            X4 = X[:].rearrange("p (r q b) -> p r q b", r=2, b=NB)
            xTj = xT[:].rearrange("p (k b j) -> p k j b", j=8, b=NB)

            def u_transposes(g0):
                ps, pk = npb()
                fns = []
                for i in range(16):
                    for jh in range(2):
                        rows = slice(64 * jh, 64 * jh + 64)
                        fns.append(lambda e, o=ps[rows, NB * i:NB * i + NB], s_=ubm[rows, 64 * (g0 + i):64 * (g0 + i) + 64],
                                   idn=identb[rows, 64 * jh:64 * jh + 64]: e.transpose(o, s_, idn))
                P.group("pe", fns, r=[("u", g0 // 16)], rr=["ubm"], w=[pk])
                copy(evac_eng(), U[:, NB * g0:NB * g0 + NB * 16], ps[:, 0:NB * 16], [pk], ["U"])

            for cb in range(4):
                wv, wk = load_c(s_in, "s_in", cb)
                for j4 in range(4):
                    ps, pk = nps()
                    fns = []
                    for kt in range(16):
                        for jh in range(2):
                            fns.append(lambda e, o=ps[64 * jh:64 * jh + 64, 0:256], l=xTj[:, kt, 4 * jh + j4, :], r_=wv[:, kt, :], kt=kt:
                                       e.matmul(o, l, r_, start=(kt == 0), stop=(kt == 15)))
                    P.group("pe", fns, r=["xT", wk], w=[pk])
                    copy("act" if (pfx is not None and cb == 0) else evac_eng(),
                         ub4[:, 16 * cb:16 * cb + 16, j4, :], ps[:, 0:256].rearrange("p (g h) -> p g h", h=16), [pk], ["ubm", ("u", cb)])
                if own and cb in (0, 2) and late.get("pending"):
                    late["pending"].pop(0)()
                if cb == 2:
                    u_transposes(0); u_transposes(16)
                if cb == 3:
                    u_transposes(32)
            u_transposes(48)
            w1r, w1rk = load_c(s_w1, "s_w1", 0, 16, 128)
            w1i, w1ik = load_c(s_w1, "s_w1", 1, 16, 128)
            w13 = {0: w1r, 1: w1i}
            for q0 in range(0, 32, 4):
                ps, pk = nps()
                ps4 = ps[:, :].rearrange("p (q r b) -> p q r b", r=2, b=NB)
                fns = []
                for i in range(4):
                    q = q0 + i
                    for g2 in range(2):
                        for ri in range(2):
                            fns.append(lambda e, o=ps4[64 * g2:64 * g2 + 64, i, ri, :], l=w13[ri][:, q, 64 * g2:64 * g2 + 64],
                                       r_=U3[:, 2 * q + g2, :]: e.matmul(o, l, r_, start=True, stop=True))
                P.group("pe", fns, r=["U", w1rk, w1ik], w=[pk])
                copy(evac_eng(), X4[:, :, q0:q0 + 4, :], ps4.rearrange("p q r b -> p r q b"), [pk], ["X"])
            tA = sb("tA", [128, 64], F32, sc); tB = sb("tB", [128, 64], F32, sc)
            if pfx is not None:
                t1 = sb("pt1", [128, 32 * NB], F32, sc); t2 = sb("pt2", [128, 32 * NB], F32, sc)
                red = sb("pred", [128, 64], F32, sc)
                xr, xi = X[:, 0:32 * NB], X[:, 32 * NB:64 * NB]
                for ri_, (a_, b_, op_) in enumerate(((pfx["PWr"], pfx["PWi"], ALU.subtract), (pfx["PWi"], pfx["PWr"], ALU.add))):
                    tt("dve", t1[:], xr, a_[:], ALU.mult, ["X"], ["pt1"])
                    tt("dve", t2[:], xi, b_[:], ALU.mult, ["X"], ["pt2"])
                    tt("dve", t1[:], t1[:], t2[:], op_, ["pt1", "pt2"], ["pt1"])
                    P.op("dve", lambda e, o=red[:, 32 * ri_:32 * ri_ + 32], i_=t1[:].rearrange("p (q b) -> p q b", b=NB):
                         e.reduce_sum(o, i_, mybir.AxisListType.X), r=["pt1"], w=["pred"])
                c3 = carry[:].rearrange("p (r q) -> p r q", r=2)
                ta3 = tA[:].rearrange("p (r q) -> p r q", r=2); tb3_ = tB[:].rearrange("p (r q) -> p r q", r=2)
                tt("dve", ta3, c3, pfx["CA64"][:].rearrange("p (r q) -> p r q", r=2), ALU.mult, ["carry"], ["tA"])
                tt("dve", tb3_[:, 0, :], c3[:, 1, :], pfx["CB64n"][:], ALU.mult, ["carry"], ["tB"])
                tt("dve", tb3_[:, 1, :], c3[:, 0, :], pfx["CB64p"][:], ALU.mult, ["carry"], ["tB"])
                tt("dve", tA[:], tA[:], tB[:], ALU.add, ["tA", "tB"], ["tA"])
                tt("dve", carry[:], tA[:], red[:], ALU.add, ["tA", "pred"], ["carry"])
                return None
            Sh = None
            if own:
                Sh = sb("Sh", [128, 2 * 2 * 32 * NB], BF16, sc)
                Sh5 = Sh[:].rearrange("p (g r q b) -> p g r q b", g=2, r=2, b=NB)
                P.op("dve", lambda e, o=Sh5[64:128, 0]: e.memset(o, 0.0), w=["Sh"])
                P.op("dve", lambda e, o=Sh5[0:64, 1]: e.memset(o, 0.0), w=["Sh"])
                for g2 in range(2):
                    rows = slice(64 * g2, 64 * g2 + 64)
                    copy("pool", Sh5[rows, g2, :, :, 0], carry[rows, :].rearrange("p (r q) -> p r q", r=2), ["carry"], ["Sh"])
            c3 = carry[:].rearrange("p (r q) -> p r q", r=2)
            ca3 = CA[:].rearrange("p (r q) -> p r q", r=2)
            ta3 = tA[:].rearrange("p (r q) -> p r q", r=2); tb3_ = tB[:].rearrange("p (r q) -> p r q", r=2)
            for b in range(NB):
                prev = c3 if b == 0 else X4[:, :, :, b - 1]
                tt("pool", ta3, prev, ca3, ALU.mult, ["X", "carry"], ["tA"])
                tt("pool", tb3_[:, 0, :], prev[:, 1, :], CBn[:], ALU.mult, ["X", "carry"], ["tB"])
                tt("pool", tb3_[:, 1, :], prev[:, 0, :], CBp[:], ALU.mult, ["X", "carry"], ["tB"])
                tt("pool", ta3, ta3, tb3_, ALU.add, ["tB"], ["tA"])
                tt("pool", X4[:, :, :, b], X4[:, :, :, b], ta3, ALU.add, ["tA"], ["X"])
            copy("pool", c3, X4[:, :, :, NB - 1], ["X", "Sh"], ["carry"])
            return ubm, U3, (Sh, X4), (U, X)

        def ssm_out(ubm, U3, Sh):
            Sh, X4 = Sh
            Sh5 = Sh[:].rearrange("p (g r q b) -> p g r q b", g=2, r=2, b=NB)
            for g2 in range(2):
                rows = slice(64 * g2, 64 * g2 + 64)
                copy("dve" if g2 else "act", Sh5[rows, g2, :, :, 1:NB], X4[rows, :, :, 0:NB - 1], ["X"], ["Sh"])
            gb5 = ubm[:, 0:4096].rearrange("p (c j g h) -> p c g j h", c=8, j=8, g=4, h=16)
            for ct in range(8):
                if ct % 4 == 0:
                    tmv, tmk = load_c(s_tm, "s_tm", ct // 4, 16, 128)
                    wcv, wck = load_c(s_wc, "s_wc", ct // 4, 16, 128)
                ps, pk = nps()
                fns = []
                for g4 in range(4):
                    for step in range(3):
                        for gp in range(2):
                            g = 8 * ct + 4 * gp + g4; q, g2 = g // 2, g % 2
                            o = ps[64 * gp:64 * gp + 64, 128 * g4:128 * g4 + 128]
                            if step == 0:
                                fns.append(lambda e, o=o, l=U3[:, g, :], r_=tmv[:, g % 32, :]: e.matmul(o, l, r_, start=True, stop=False))
                            elif step == 1:
                                fns.append(lambda e, o=o, l=Sh5[:, g2, 0, q, :], r_=wcv[:, 2 * (q % 16), :]: e.matmul(o, l, r_, start=False, stop=False))
                            else:
                                fns.append(lambda e, o=o, l=Sh5[:, g2, 1, q, :], r_=wcv[:, 2 * (q % 16) + 1, :]: e.matmul(o, l, r_, start=False, stop=True))
                P.group("pe", fns, r=["U", "Sh", tmk, wck], w=[pk])
                act(gb5[:, ct, :, :, :], ps[:, :].rearrange("p (g j h) -> p g j h", j=8, h=16), AF.Gelu, [pk, "U"], ["ubm", ("gy", ct)])
                if ct >= 1:
                    gy_transposes(ct - 1, ubm)
            gy_transposes(7, ubm)

        def gy_transposes(ct, ubm):
            if True:
                ps, pk = npb()
                fns = []
                for j in range(8):
                    for gp in range(2):
                        rows = slice(64 * gp, 64 * gp + 64)
                        fns.append(lambda e, o=ps[rows, NB * j:NB * j + NB], s_=ubm[rows, 512 * ct + 64 * j:512 * ct + 64 * j + 64],
                                   idn=identb[rows, 64 * gp:64 * gp + 64]: e.transpose(o, s_, idn))
                P.group("pe", fns, r=[("gy", ct)], rr=["ubm"], w=[pk])
                copy(evac_eng(), gy3[:, ct, :].rearrange("p (b j) -> p j b", j=8),
                     ps[:, 0:8 * NB].rearrange("p (j b) -> p j b", b=NB), [pk], ["gyT"])

        def proj_fm_unused(col0, ncols_tiles, consume):
            for c2 in range(0, ncols_tiles, 2):
                n = min(2, ncols_tiles - c2)
                wv, wk = load_c(s_in, "s_in", (col0 + 128 * c2) // 256)
                for i in range(n):
                    ps, pk = nps()
                    P.group("pe", [lambda e, o=ps[:, :], l=wv[:, kt, 128 * i:128 * i + 128], r_=xT3[:, kt, :], kt=kt:
                                   e.matmul(o, l, r_, start=(kt == 0), stop=(kt == 15)) for kt in range(16)], r=["xT", wk], w=[pk])
                    consume(c2 + i, ps, pk)

        def kv_proj(tok0, ntok, dst_tile0, kcol0):
            for kh in range(4):
                if kh % 2 == 0:
                    wv, wk = load_c(s_k2, "s_k2", kh // 2, offs=(0, 64, 128, 192))
                ps, pk = nps()
                P.group("pe", [lambda e, o=ps[:, 0:ntok], l=wv[:, kt, 128 * (kh % 2):128 * (kh % 2) + 128], r_=xT3[:, kt, tok0:tok0 + ntok], kt=kt:
                               e.matmul(o, l, r_, start=(kt == 0), stop=(kt == 15)) for kt in range(16)], r=["xT", wk], w=[pk])
                copy(evac_eng(), kT3[:, kh, kcol0:kcol0 + ntok], ps[:, 0:ntok], [pk], ["kT"])
            wv, wk = load_c(s_in, "s_in", 13)
            for t4 in range(ntok // 128):
                ps, pk = nps()
                P.group("pe", [lambda e, o=ps[:, 0:256], l=xT3[:, kt, tok0 + 128 * t4:tok0 + 128 * t4 + 128], r_=wv[:, kt, :], kt=kt:
                               e.matmul(o, l, r_, start=(kt == 0), stop=(kt == 15)) for kt in range(16)], r=["xT", wk], w=[pk])
                copy(evac_eng(), vx4[:, dst_tile0 + t4, :, :], ps[:, 0:256].rearrange("p (h d) -> p h d", d=64), [pk], ["vext"])

        late = {}

        def attn_stage(sp_i, sa):
            ha3 = late["ha3"]; haT = late["haT"]
            if True:
                qT = sb("qT", [128, 2 * SPAN], BF16, sa); qT4 = qT[:].rearrange("p (b h q) -> p b h q", h=2, q=128)
                pT = sb("pT", [128, 4 * 2 * 512], BF16, sa)
                pT5 = pT[:].rearrange("p (b k h q) -> p b k h q", k=2, h=4, q=128)
                lg = [sb(f"lg{i}", [128, 512], F32, sa) for i in range(2)]
                dn = sb("dn", [128, 512], F32, sa); zt2 = sb("zt", [128, 2 * 512], BF16, sa)
                kv_proj(0, SPAN, 1, 128)
                for kh in range(4):
                    wv, wk = load_c(s_in, "s_in", 8 + kh)
                    wza, wzak = load_c(s_in, "s_in", 14 + kh)
                    for hp in range(2):
                        ps, pk = nps()
                        P.group("pe", [lambda e, o=ps[:, :], l=wv[:, kt, 128 * hp:128 * hp + 128], r_=xT3[:, kt, :], kt=kt:
                                       e.matmul(o, l, r_, start=(kt == 0), stop=(kt == 15)) for kt in range(16)], r=["xT", wk], w=[pk])
                        act(qT4[:, :, hp, :], ps[:, :].rearrange("p (b q) -> p b q", q=128), AF.Identity, [pk], ["qT"], scale=0.125)
                    for blk in range(4):
                        for kt_ in range(2):
                            l_ = lg[(2 * blk + kt_) % 2]
                            pp, pks = nps2()
                            for half in range(2):
                                rows = slice(64 * half, 64 * half + 64)
                                P.group("pe", [lambda e, o=pp[:, 512 * half:512 * half + 256],
                                               l=kT3[rows, kh, 128 * (blk + kt_):128 * (blk + kt_) + 128],
                                               r_=qT[rows, 256 * blk:256 * blk + 256]: e.matmul(o, l, r_, start=True, stop=True)],
                                        r=["kT", "qT"], w=[pks[half]])
                            tt("dve", l_[:].rearrange("p (hf hp q) -> p hf hp q", hf=2, q=128),
                               pp.rearrange("p (b hp q) -> p b hp q", b=2, q=128)[:, :, 0:2, :],
                               bT4[:, 4 * kh:4 * kh + 4, kt_, :].rearrange("p (hp hf) q -> p hf hp q", hf=2),
                               ALU.add, [pks[0], pks[1], "biasT"], [l_.name])
                            if sp_i == 0 and blk == 0 and kt_ == 0:
                                P.op("act", lambda e, o=pT5[:, blk, kt_, :, :], i_=l_[:].rearrange("p (h q) -> p h q", q=128):
                                     e.activation(o, i_, AF.Exp, bias=hneg[:, 0:1]), r=[l_.name, "hneg_s"], w=["pT"])
                            else:
                                act(pT5[:, blk, kt_, :, :], l_[:].rearrange("p (h q) -> p h q", q=128), AF.Exp, [l_.name], ["pT"])
                    for hp in range(2):
                        ps, pk = nps()
                        P.group("pe", [lambda e, o=ps[:, :], l=wza[:, kt, 128 * hp:128 * hp + 128], r_=xT3[:, kt, :], kt=kt:
                                       e.matmul(o, l, r_, start=(kt == 0), stop=(kt == 15)) for kt in range(16)], r=["xT", wzak], w=[pk])
                        act(zt2[:, 512 * hp:512 * hp + 512], ps[:, :], AF.Silu, [pk], ["zt"])
                    for hp in range(2):
                        hpg = 2 * kh + hp
                        zt = zt2[:, 512 * hp:512 * hp + 512]
                        pv, pvk = nps(); rs, rsk = nps()
                        fns = []
                        for blk in range(4):
                            for half in range(2):
                                hq = 2 * half + hp
                                for kt_ in range(2):
                                    fns.append(lambda e, o=pv[64 * half:64 * half + 64, 128 * blk:128 * blk + 128], l=vx4[:, blk + kt_, kh, :],
                                               r_=pT5[:, blk, kt_, hq, :], kt_=kt_: e.matmul(o, l, r_, start=(kt_ == 0), stop=(kt_ == 1)))
                                for kt_ in range(2):
                                    fns.append(lambda e, o=rs[64 * half:64 * half + 64, 128 * blk:128 * blk + 128], r_=pT5[:, blk, kt_, hq, :], kt_=kt_:
                                               e.matmul(o, ones64[:], r_, start=(kt_ == 0), stop=(kt_ == 1)))
                        P.group("pe", fns, r=["pT", "vext", "ones64"], w=[pvk, rsk])
                        P.op("act", lambda e, r2=rs, h_=hpg: e.activation(dn[:], r2[:, :], AF.Ln, bias=esk[:, h_:h_ + 1]), r=[rsk, "esk"], w=["dn"])
                        P.op("act", lambda e: e.activation(dn[:], dn[:], AF.Exp, scale=-1.0), r=["dn"], w=["dn"])
                        tt("dve", dn[:], dn[:], pv[:, :], ALU.mult, [pvk], ["dn"])
                        tt("dve", ha3[:, hpg, :], dn[:], zt, ALU.mult, ["dn", "zt"], ["haT"])
                copy("dve", kT3[:, :, 0:128], kT3[:, :, 512:640], ["kT"], ["kT"])
                copy("dve", vx4[:, 0, :, :], vx4[:, 4, :, :], ["vext"], ["vext"])

        def merge_half(attn, sm):
            tg = "a" if attn else "s"
            ha3, hs3, mT3 = late["ha3"], late["hs3"], late["mT3"]
            g1 = sb("g1", [128, 512], F32, sm); m1 = late["ftmp"]
            h3, hn = (ha3, "haT") if attn else (hs3, "hsT")
            for d2 in range(8):
                wg3, wgk = load_c(s_in, "s_in", (26 if attn else 18) + d2)
                wb3, wbk = load_c(s_ba if attn else s_bs, "s_ba" if attn else "s_bs", d2, 8)
                for i in range(2):
                    dt = 2 * d2 + i
                    ps, pk = nps()
                    P.group("pe", [lambda e, o=ps[:, :], l=wg3[:, kt, 128 * i:128 * i + 128], r_=xT3[:, kt, :], kt=kt:
                                   e.matmul(o, l, r_, start=(kt == 0), stop=(kt == 15)) for kt in range(16)], r=["xT", wgk], w=[pk])
                    act(g1[:], ps[:, :], AF.Sigmoid, [pk], [g1.name])
                    ps, pk = nps()
                    P.group("pe", [lambda e, o=ps[:, :], l=wb3[:, ct, 128 * i:128 * i + 128], r_=h3[:, ct, :], ct=ct:
                                   e.matmul(o, l, r_, start=(ct == 0), stop=(ct == 7)) for ct in range(8)], r=[hn, wbk], w=[pk])
                    if attn:
                        tt("dve", mT3[:, dt, :], ps[:, :], g1[:], ALU.mult, [pk, g1.name], ["mT"])
                    else:
                        tt("dve", m1[:], ps[:, :], g1[:], ALU.mult, [pk, g1.name], ["ftmp"])
                        tt("dve", mT3[:, dt, :], mT3[:, dt, :], m1[:], ALU.add, ["ftmp"], ["mT"])

        def main_stage(sp_i, sc, ubm, U, X, Sh, pre_tail, defer):
            hs3, mT3, hsT, mT = late["hs3"], late["mT3"], late["hsT"], late["mT"]
            if True:
                sg = sc
                zs = sb("zs", [128, 512], BF16, sg); sgb = sb("sgb", [128, 512], F32, sg); ga_ = late["ftmp"]
                for et in range(8):
                    if et % 2 == 0:
                        wv2, wk2 = load_c(s_in, "s_in", 4 + et // 2)
                    ps, pk = nps()
                    P.group("pe", [lambda e, o=ps[:, :], l=wv2[:, kt, 128 * (et % 2):128 * (et % 2) + 128], r_=xT3[:, kt, :], kt=kt:
                                   e.matmul(o, l, r_, start=(kt == 0), stop=(kt == 15)) for kt in range(16)], r=["xT", wk2], w=[pk])
                    act(zs[:], ps[:, :], AF.Silu, [pk], ["zs"])
                    wg3, wgk = load_c(s_glu, "s_glu", et, 8)
                    pa, pak = nps(); pb, pbk = nps()
                    P.group("pe", [lambda e, o=pa[:, :], l=wg3[:, ct, 0:128], r_=gy3[:, ct, :], ct=ct:
                                   e.matmul(o, l, r_, start=(ct == 0), stop=(ct == 7)) for ct in range(8)], r=["gyT", wgk], w=[pak])
                    P.group("pe", [lambda e, o=pb[:, :], l=wg3[:, ct, 128:256], r_=gy3[:, ct, :], ct=ct:
                                   e.matmul(o, l, r_, start=(ct == 0), stop=(ct == 7)) for ct in range(8)], r=["gyT", wgk], w=[pbk])
                    act(sgb[:], pb[:, :], AF.Sigmoid, [pbk], ["sgb"])
                    tt("dve", ga_[:], pa[:, :], sgb[:], ALU.mult, [pak, "sgb"], ["ftmp"])
                    tt("dve", hs3[:, et, :], ga_[:], zs[:], ALU.mult, ["ftmp", "zs"], ["hsT"])
            merge_half(False, sc)
            if True:
                so = sc
                rrA = ubm[:].bitcast(F32).rearrange("p (t c) -> p t c", c=D)
                rrB = Sh[:].bitcast(F32).rearrange("p (t c) -> p t c", c=D)
                rts = [(rrA[:, 0, :], "ubm"), (rrA[:, 1, :], "ubm"), (rrB[:, 0, :], "Sh"), (rrB[:, 1, :], "Sh")]
                sq = U[:].bitcast(F32)
                gch = X[:, 0:D]; bch = X[:, D:2 * D]
                st = sb("lnst", [128, 8], F32, so)
                for t4 in range(4):
                    rt, rk = rts[t4]
                    P.dma("pool", rt, xo[SPAN * sp_i + 128 * t4:SPAN * sp_i + 128 * t4 + 128, :], w=[rk])
                P.dma("pool", gch, lng_d, w=["X"]); P.dma("pool", bch, lnb_d, w=["X"])

                def layer_norm(t4):
                    rt, rk = rts[t4]
                    P.op("dve", lambda e: e.reduce_sum(st[:, 0:1], rt, mybir.AxisListType.X), r=[rk], w=["lnst"])
                    tt("dve", sq, rt, rt, ALU.mult, [rk], ["U"])
                    P.op("dve", lambda e: e.reduce_sum(st[:, 1:2], sq, mybir.AxisListType.X), r=["U"], w=["lnst"])
                    ts("dve", st[:, 2:3], st[:, 0:1], 1.0 / D, None, ALU.mult, None, ["lnst"], ["lnst"])
                    tt("dve", st[:, 3:4], st[:, 2:3], st[:, 2:3], ALU.mult, ["lnst"], ["lnst"])
                    ts("dve", st[:, 4:5], st[:, 1:2], 1.0 / D, float(LN_EPS), ALU.mult, ALU.add, ["lnst"], ["lnst"])
                    tt("dve", st[:, 4:5], st[:, 4:5], st[:, 3:4], ALU.subtract, ["lnst"], ["lnst"])
                    P.op("act", lambda e: e.activation(st[:, 5:6], st[:, 4:5], AF.Sqrt), r=["lnst"], w=["lnst"])
                    P.op("dve", lambda e: e.reciprocal(st[:, 5:6], st[:, 5:6]), r=["lnst"], w=["lnst"])
                    P.op("dve", lambda e: e.scalar_tensor_tensor(st[:, 6:7], st[:, 2:3], -1.0, st[:, 5:6], ALU.mult, ALU.mult),
                         r=["lnst"], w=["lnst"])
                    P.op("act", lambda e: e.activation(rt, rt, AF.Identity, bias=st[:, 6:7], scale=st[:, 5:6]), r=["lnst"], w=[rk])
                    tt("dve", rt, rt, gch, ALU.mult, ["X"], [rk])
                    tt("dve", rt, rt, bch, ALU.add, ["X"], [rk])
                    P.dma("pool", y_out[SPAN * sp_i + 128 * t4:SPAN * sp_i + 128 * t4 + 128, :], rt, r=[rk])

                for th in range(2):
                    for cb in range(8):
                        wv, wk = load_c(s_out, "s_out", cb)
                        for t2 in range(2):
                            t4 = 2 * th + t2
                            rt, rk = rts[t4]
                            ps, pk = nps()
                            P.group("pe", [lambda e, o=ps[:, 0:256], l=mT3[:, dt, 128 * t4:128 * t4 + 128], r_=wv[:, dt, :], dt=dt:
                                           e.matmul(o, l, r_, start=(dt == 0), stop=(dt == 15)) for dt in range(16)], r=["mT", wk], w=[pk])
                            P.op("dve", lambda e, o=rt[:, 256 * cb:256 * cb + 256], p_=ps[:, 0:256]:
                                 e.scalar_tensor_tensor(o, o, float(ALPHA), p_, ALU.mult, ALU.add), r=[pk], w=[rk])
                        if th == 1 and cb == 1:
                            layer_norm(0)
                        if th == 1 and cb == 4:
                            layer_norm(1)
                pre_tail()
                if defer:
                    late["pending"] = [lambda: layer_norm(2), lambda: layer_norm(3)]
                else:
                    layer_norm(2)
                    layer_norm(3)

        n_pre = dbg.get("n_pre", NSPAN)
        n_own = dbg.get("n_own", NSPAN)
        spre = ExitStack()
        PWr = sb("PWr", [128, 32 * NB], F32, spre); PWi = sb("PWi", [128, 32 * NB], F32, spre)
        CA64 = sb("CA64", [128, 64], F32, spre); CB64n = sb("CB64n", [128, 32], F32, spre); CB64p = sb("CB64p", [128, 32], F32, spre)
        with ExitStack() as sw:
            NM = 65
            msc = sb("msc_s", [128, NM], F32, sw)
            ANGx = sb("ANGx", [128, 2 * NM * 32], F32, sw); TMPx = sb("TMPx", [128, 2 * NM * 32], F32, sw)
            KIx = sb("KIx", [128, 2 * NM * 32], I32, sw); MARx = sb("MARx", [128, NM * 32], F32, sw)
            P.dma("sp", msc[:], msc_d, w=["msc_s"])
            an4 = ANGx[:].rearrange("p (s m q) -> p s m q", s=2, q=32)
            mb = msc[:].unsqueeze(2).to_broadcast([128, NM, 32])
            tt("dve", an4[:, 0], mb, ai[:].unsqueeze(1).to_broadcast([128, NM, 32]), ALU.mult, ["msc_s", "ai"], ["ANGx"])
            ts("dve", an4[:, 1], an4[:, 0], PI / 2, None, ALU.add, None, ["ANGx"], ["ANGx"])
            sin_reduced(ANGx, TMPx, KIx)
            ma3 = MARx[:].rearrange("p (m q) -> p m q", q=32)
            tt("dve", ma3, mb, ar[:].unsqueeze(1).to_broadcast([128, NM, 32]), ALU.mult, ["msc_s", "ar"], ["MARx"])
            act(MARx[:], MARx[:], AF.Exp, ["MARx"], ["MARx"])
            tt("dve", PWr[:].rearrange("p (q m) -> p m q", m=NB), ma3[:, 0:NB, :], an4[:, 1, 0:NB, :], ALU.mult, ["MARx", "ANGx"], ["PWr"])
            tt("dve", PWi[:].rearrange("p (q m) -> p m q", m=NB), ma3[:, 0:NB, :], an4[:, 0, 0:NB, :], ALU.mult, ["MARx", "ANGx"], ["PWi"])
            tt("dve", CA64[:, 0:32], ma3[:, NB, :], an4[:, 1, NB, :], ALU.mult, ["MARx", "ANGx"], ["CA64"])
            copy("dve", CA64[:, 32:64], CA64[:, 0:32], ["CA64"], ["CA64"])
            tt("dve", CB64p[:], ma3[:, NB, :], an4[:, 0, NB, :], ALU.mult, ["MARx", "ANGx"], ["CB64p"])
            ts("dve", CB64n[:], CB64p[:], -1.0, None, ALU.mult, None, ["CB64p"], ["CB64n"])
            P.barrier()
        pfx = dict(PWr=PWr, PWi=PWi, CA64=CA64, CB64n=CB64n, CB64p=CB64p)
        xb4 = xb + [(t_[:, :], t_.name) for t_ in (sb(f"xb{i}", [128, D], BF16, pre_scope) for i in (2, 3))]
        n_cast = -(-len(cast_jobs) // max(1, n_pre))
        for sp_i in range(NSPAN - n_pre, NSPAN):
            sc = pre_scope
            build_xT(xp, SPAN * sp_i, xb4, ev="act")
            ssm_state(sc, own=False, pfx=pfx)
            if sp_i + 1 < NSPAN:
                prefetch_x(xp, SPAN * (sp_i + 1), xb4, 0, 4)
            elif n_own:
                prefetch_x(xo, 0, xb, 0, 2)
            issue_casts(n_cast)
            if sp_i == NSPAN - 1:
                kv_proj(SPAN - 128, 128, 0, 0)
        issue_casts(len(cast_jobs))
        P.barrier()
        pre_scope.close()
        memo.clear()
        spre.close()
        mT = sb("mT", [128, 16 * SPAN], BF16); haT = sb("haT", [128, 8 * SPAN], BF16); hsT = sb("hsT", [128, 8 * SPAN], BF16)
        late.update(mT=mT, haT=haT, hsT=hsT, mT3=mT[:].rearrange("p (c t) -> p c t", t=SPAN),
                    ha3=haT[:].rearrange("p (c t) -> p c t", t=SPAN), hs3=hsT[:].rearrange("p (c t) -> p c t", t=SPAN))
        xbuilt = {}
        for sp_i in range(n_own):
            if True:
                sc = own_scope
                if not xbuilt.get(sp_i):
                    build_xT(xo, SPAN * sp_i, xb)
                late["ftmp"] = sb("ftmp", [128, 512], F32, sc)
                if sp_i + 1 < n_own:
                    prefetch_x(xo, SPAN * (sp_i + 1), xb, 0, 2)
                ubm, U3, Sh, (U_, X_) = ssm_state(sc, own=True)
                attn_stage(sp_i, sc)
                pT_ = memo["pT"]
                xb_own = xb + [(pT_[:, 0:D], "pT"), (pT_[:, D:2 * D], "pT")]
                if sp_i + 1 < n_own:
                    prefetch_x(xo, SPAN * (sp_i + 1), xb_own, 2, 4)
                merge_half(True, sc)
                ssm_out(ubm, U3, Sh)

                def pre_tail(n=sp_i):
                    if n + 1 < n_own:
                        build_xT(xo, SPAN * (n + 1), xb_own)
                        xbuilt[n + 1] = True
                main_stage(sp_i, sc, ubm, U_, X_, Sh[0], pre_tail, sp_i + 1 < n_own)

        P.barrier()
        own_scope.close()
        with nc.Block() as block:
            @block.tensor
            def _(e):
                P.emit("pe", e)

            @block.scalar
            def _(e):
                P.emit("act", e)

            @block.vector
            def _(e):
                P.emit("dve", e)

            @block.gpsimd
            def _(e):
                P.emit("pool", e)

            @block.sync
            def _(e):
                P.emit("sp", e)
    return nc


def _t5_bucket(dist):
    max_exact = 16
    d = np.maximum(dist, 1).astype(np.float32)
    large = max_exact + (np.log(d / np.float32(max_exact)) / np.float32(math.log(128 / max_exact)) * np.float32(16)).astype(np.int32)
    large = np.minimum(large, 31)
    return np.where(dist < max_exact, dist, large)


def kernel(x, w_in, ssm_lambda_re, ssm_lambda_im, ssm_b_re, ssm_b_im, ssm_c_re, ssm_c_im, ssm_d, ssm_log_step,
           w_glu, attn_sinks, rel_bias_table, w_branch_ssm, w_branch_attn, w_out, ln_gain, ln_bias):
    f = lambda a: np.ascontiguousarray(np.asarray(a, dtype=np.float32))
    x = f(x)
    pl = lambda a: f(a.reshape(32, 2, 64).transpose(1, 2, 0).reshape(128, 32))
    plb = lambda a: f(a.reshape(32, 2, 64, 16).transpose(1, 2, 0, 3).reshape(128, 512))
    plc = lambda a: f(a.reshape(32, 2, 16, 64).transpose(1, 3, 0, 2).reshape(128, 512))
    ls = np.asarray(ssm_log_step[0], np.float32).reshape(32, 2)
    lstep = f(np.broadcast_to(ls.T[:, None, :], (2, 64, 32)).reshape(128, 32))
    dI = np.zeros((16, 64, 16), np.float32)
    dd = np.asarray(ssm_d[0], np.float32).reshape(64, 16)
    for h in range(16):
        dI[h, :, h] = dd[:, h]
    sk = np.zeros((128, 8), np.float32)
    sinks = np.asarray(attn_sinks[0], np.float32)
    for hp in range(8):
        sk[0:64, hp] = sinks[2 * hp]
        sk[64:128, hp] = sinks[2 * hp + 1]
    tab = np.asarray(rel_bias_table, np.float32)
    line = np.full((16, 384), NEG, np.float32)
    bk = _t5_bucket(np.arange(128))
    line[:, 128:256] = tab[bk, :].T
    common = {
        "w_in": f(w_in[0]), "w_glu": f(w_glu[0]), "w_bs": f(w_branch_ssm[0]), "w_ba": f(w_branch_attn[0]), "w_out": f(w_out[0]),
        "lamr": pl(np.asarray(ssm_lambda_re[0])), "lami": pl(np.asarray(ssm_lambda_im[0])), "lstep": lstep,
        "br": plb(np.asarray(ssm_b_re[0])), "bi": plb(np.asarray(ssm_b_im[0])),
        "cr": plc(np.asarray(ssm_c_re[0])), "ci": plc(np.asarray(ssm_c_im[0])),
        "dI": f(dI.reshape(16, 1024)), "sk": sk, "line": line,
        "lng": f(np.broadcast_to(np.asarray(ln_gain[0], np.float32)[None, :], (128, D))),
        "lnb": f(np.broadcast_to(np.asarray(ln_bias[0], np.float32)[None, :], (128, D))),
        "idf": np.eye(128, dtype=np.float32), "anti": f(np.eye(128, dtype=np.float32)[::-1]),
        "msc": f(np.broadcast_to(np.array([8.0 * (NB - 1 - b) for b in range(NB)] + [8.0 * NB], np.float32)[None, :], (128, 65))),
    }
    in_maps = []
    for c in range(NCORES):
        b, h = c // 2, c % 2
        m = dict(common)
        m["xo"] = f(x[b, h * TOK:(h + 1) * TOK])
        m["xp"] = f(x[b, 0:TOK]) if h == 1 else np.zeros((TOK, D), np.float32)
        m["hneg"] = np.full((128, 1), 0.0 if h == 1 else NEG, np.float32)
        in_maps.append(m)
    nc = build_program()
    res = run_bass_kernel_spmd(nc, in_maps, core_ids=list(range(NCORES)))
    out = np.zeros((4, 8192, D), np.float32)
    for c in range(NCORES):
        b, h = c // 2, c % 2
        out[b, h * TOK:(h + 1) * TOK] = np.asarray(res.results[c]["y"], np.float32)
    return out
```

```python
import math
import numpy as np
from contextlib import ExitStack
import concourse.bass as bass
import concourse.mybir as mybir
from concourse.bass_utils import run_bass_kernel_spmd

F32 = mybir.dt.float32
BF16 = mybir.dt.bfloat16
I32 = mybir.dt.int32
ALU = mybir.AluOpType
AF = mybir.ActivationFunctionType
PI = float(np.pi)
TWO_PI = float(2 * np.pi)

NCORES = 8
TOK = 4096
SPAN = 512
NB = SPAN // 8
NSPAN = TOK // SPAN
D = 2048
DIN = 8704
ALPHA = 2.0 ** 0.25
LN_EPS = 1e-5
NEG = -30000.0
C_U, C_ZS, C_Q, C_K, C_V, C_ZA, C_G = 0, 1024, 2048, 3072, 3328, 3584, 4608

ENGS = ["pe", "act", "dve", "pool", "sp"]
NSP = 8
NPL = 6
NDS = NSP + NPL
NBG = 6


class Prog:
    def __init__(self, nc, st):
        self.nc = nc
        self.q = {e: [] for e in ENGS}
        self.sem = {e: st.enter_context(nc.semaphore("c_" + e)) for e in ENGS if e != "sp"}
        self.cnt = {e: 0 for e in ENGS}
        self.dsem = [st.enter_context(nc.semaphore(f"dq{i}")) for i in range(NDS + NBG)]
        self.dcnt = [0] * (NDS + NBG)
        self.dnext = {"sp": 0, "other": 0}
        self.bnext = 0
        self.seen = {e: {} for e in ENGS}
        self.lastw = {}
        self.readers = {}

    def _need(self, eng, tok, waits, raw=False):
        kind, src, val = tok
        if kind == "e" and src == eng and eng == "pe":
            return
        key = (kind, src)
        if self.seen[eng].get(key, 0) >= val:
            return
        self.seen[eng][key] = val
        waits.append(tok)

    def _deps(self, eng, r, w):
        waits = []
        for b in r:
            t = self.lastw.get(b)
            if t:
                self._need(eng, t, waits, raw=True)
        for b in w:
            t = self.lastw.get(b)
            if t:
                self._need(eng, t, waits)
            for t in self.readers.get(b, ()):
                self._need(eng, t, waits)
        return waits

    def _commit(self, tok, r, w):
        for b in r:
            self.readers.setdefault(b, []).append(tok)
        for b in w:
            self.lastw[b] = tok
            self.readers[b] = []

    def op(self, eng, fn, r=(), w=()):
        waits = self._deps(eng, r, w)
        self.cnt[eng] += 1
        tok = ("e", eng, self.cnt[eng])
        self.q[eng].append((waits, fn, tok))
        self._commit(tok, r, w)

    def group(self, eng, fns, r=(), w=(), rr=()):
        waits = self._deps(eng, r, w)
        self.cnt[eng] += 1
        tok = ("e", eng, self.cnt[eng])
        for i, fn in enumerate(fns):
            self.q[eng].append((waits if i == 0 else [], fn, tok if i == len(fns) - 1 else None))
        self._commit(tok, list(r) + list(rr), w)

    def dma(self, eng, out, in_, r=(), w=(), bg=False):
        waits = self._deps(eng, r, w)
        if bg:
            s = NDS + self.bnext
            self.bnext = (self.bnext + 1) % NBG
        elif eng == "sp":
            s = self.dnext["sp"]
            self.dnext["sp"] = (s + 1) % NSP
        else:
            s = NSP + self.dnext["other"]
            self.dnext["other"] = (self.dnext["other"] + 1) % NPL
        if self.dcnt[s]:
            self._need(eng, ("d", s, self.dcnt[s]), waits)
        self.dcnt[s] += 16
        tok = ("d", s, self.dcnt[s])
        self.q[eng].append((waits, lambda e, o=out, i=in_: e.dma_start(out=o, in_=i), tok))
        self._commit(tok, r, w)

    def barrier(self):
        for e in ENGS:
            waits = []
            for f in ENGS:
                if f != "sp" and self.cnt[f]:
                    self._need(e, ("e", f, self.cnt[f]), waits)
            for s in range(NDS):
                if self.dcnt[s]:
                    self._need(e, ("d", s, self.dcnt[s]), waits)
            if waits:
                self.q[e].append((waits, None, None))
        keep = {k: v for k, v in self.lastw.items() if isinstance(k, tuple) and v[0] == "d" and v[1] >= NDS}
        self.lastw.clear()
        self.lastw.update(keep)
        self.readers.clear()

    def emit(self, eng, e):
        for waits, fn, tok in self.q[eng]:
            for kind, src, val in waits:
                e.wait_ge(self.sem[src] if kind == "e" else self.dsem[src], val)
            if fn is None:
                continue
            ins = fn(e)
            if tok is not None:
                if tok[0] == "e":
                    ins.then_inc(self.sem[eng], 1)
                else:
                    ins.then_inc(self.dsem[tok[1]], 16)


def build_program(debug=None):
    dbg = debug or {}
    nc = bass.Bass("TRN2", target_bir_lowering=False)
    dr = lambda n, s, k="ExternalInput", d=F32: nc.dram_tensor(n, list(s), d, kind=k).ap()
    xo = dr("xo", [TOK, D])
    xp = dr("xp", [TOK, D])
    w_in = dr("w_in", [D, DIN])
    w_glu = dr("w_glu", [1024, 2048])
    w_bs = dr("w_bs", [1024, 2048])
    w_ba = dr("w_ba", [1024, 2048])
    w_out = dr("w_out", [D, D])
    lamr_d = dr("lamr", [128, 32]); lami_d = dr("lami", [128, 32]); lstep_d = dr("lstep", [128, 32])
    br_d = dr("br", [128, 512]); bi_d = dr("bi", [128, 512]); cr_d = dr("cr", [128, 512]); ci_d = dr("ci", [128, 512])
    dI_d = dr("dI", [16, 1024])
    sk_d = dr("sk", [128, 8])
    line_d = dr("line", [16, 384])
    hneg_d = dr("hneg", [128, 1])
    lng_d = dr("lng", [128, D]); lnb_d = dr("lnb", [128, D])
    idf_d = dr("idf", [128, 128]); anti_d = dr("anti", [128, 128])
    msc_d = dr("msc", [128, 65])
    y_out = dr("y", [TOK, D], k="ExternalOutput")
    s_in = dr("s_in", [34, 128, 4096], k="Internal", d=BF16)
    s_glu = dr("s_glu", [8, 128, 2048], k="Internal", d=BF16)
    s_bs = dr("s_bs", [8, 128, 2048], k="Internal", d=BF16)
    s_ba = dr("s_ba", [8, 128, 2048], k="Internal", d=BF16)
    s_out = dr("s_out", [8, 128, 4096], k="Internal", d=BF16)
    s_k2 = dr("s_k2", [2, 128, 4096], k="Internal", d=BF16)
    s_w1 = dr("s_w1", [2, 128, 4096], k="Internal", d=BF16)
    s_wc = dr("s_wc", [2, 128, 4096], k="Internal", d=BF16)
    s_tm = dr("s_tm", [2, 128, 4096], k="Internal", d=BF16)

    with ExitStack() as st:
        P = Prog(nc, st)
        used = {}
        dumped = {}

        def dump(name, t, key, parts=128):
            if name not in dbg.get("dumps", ()) or name in dumped:
                return
            shape = [parts, t.shape[-1]]
            dst = nc.dram_tensor("dbg_" + name, shape, t.dtype, kind="ExternalOutput").ap()
            P.dma("sp", dst, t[0:parts, :], r=[key])
            dumped[name] = True

        memo = {}
        own_scope = ExitStack()
        pre_scope = ExitStack()

        def sb(n, s, d=F32, c=st):
            if c is own_scope or c is pre_scope:
                if n in memo:
                    return memo[n]
            k = used.get(n, 0)
            used[n] = k + 1
            t = c.enter_context(nc.sbuf_tensor(n if k == 0 else f"{n}_{k}", list(s), d))
            if c is own_scope or c is pre_scope:
                memo[n] = t
            return t
        identb = sb("identb", [128, 128], BF16)
        identf = sb("identf", [128, 128])
        antif = sb("antif", [128, 128])
        ones64 = sb("ones64", [128, 64], BF16)
        biasT = sb("biasT", [128, 16 * 2 * 128])
        CA = sb("CA", [128, 64])
        CBn = sb("CBn", [128, 32]); CBp = sb("CBp", [128, 32])
        carry = sb("carry", [128, 64])
        esk = sb("esk", [128, 8])
        ar = sb("ar", [128, 32]); ai = sb("ai", [128, 32])
        hneg = sb("hneg_s", [128, 1])
        wstate = {"i": 0}
        wbuf = []
        psbig = st.enter_context(nc.psum_tensor("psbig", [128, 6 * 512], F32))
        psf = [psbig[:, 512 * i:512 * i + 512] for i in range(6)]
        psb = [st.enter_context(nc.psum_tensor(f"psb{i}", [128, 1024], BF16)) for i in range(2)]
        pstate = {"f": 0, "b": 0}

        def nps():
            i = pstate["f"]; pstate["f"] = (i + 1) % 6
            return psf[i], f"psf{i}"

        def nps2():
            i = pstate["f"]
            if i % 2:
                i = (i + 1) % 6
            pstate["f"] = (i + 2) % 6
            return psbig[:, 512 * i:512 * i + 1024], (f"psf{i}", f"psf{i + 1}")

        def npb():
            i = pstate["b"]; pstate["b"] = (i + 1) % 2
            return psb[i], f"psb{i}"

        def nwb():
            i = wstate["i"]; wstate["i"] = (i + 1) % 4
            return wbuf[i], f"wbuf{i}"

        evs = {"i": 0}

        def evac_eng():
            evs["i"] ^= 1
            return "dve" if evs["i"] else "act"

        def copy(eng, out, in_, r, w):
            if eng == "act":
                P.op("act", lambda e, o=out, i=in_: e.copy(o, i), r=r, w=w)
            else:
                P.op(eng, lambda e, o=out, i=in_: e.tensor_copy(o, i), r=r, w=w)

        def tt(eng, out, a, b, op, r, w):
            P.op(eng, lambda e, o=out, x=a, y=b, p=op: e.tensor_tensor(o, x, y, p), r=r, w=w)

        def ts(eng, out, a, s1, s2, op0, op1, r, w):
            if op1 is None:
                P.op(eng, lambda e, o=out, x=a: e.tensor_scalar(o, x, s1, None, op0), r=r, w=w)
            else:
                P.op(eng, lambda e, o=out, x=a: e.tensor_scalar(o, x, s1, s2, op0, op1), r=r, w=w)

        def taylor_exp(q, x, deg, keyq, keyx):
            ts("dve", q, x, 1.0 / deg, 1.0, ALU.mult, ALU.add, [keyx], [keyq])
            for k in range(deg - 1, 0, -1):
                tt("dve", q, q, x, ALU.mult, [keyq, keyx], [keyq])
                ts("dve", q, q, 1.0 / k, 1.0, ALU.mult, ALU.add, [keyq], [keyq])

        C1 = 6.28125
        C2 = TWO_PI - C1

        def sin_reduced(A, T_, K_):
            a, t_, k_ = A.name, T_.name, K_.name
            ts("dve", T_[:], A[:], 1.0 / TWO_PI, None, ALU.mult, None, [a], [t_])
            copy("dve", K_[:], T_[:], [t_], [k_])
            copy("dve", T_[:], K_[:], [k_], [t_])
            P.op("dve", lambda e: e.scalar_tensor_tensor(A[:], T_[:], -C1, A[:], ALU.mult, ALU.add), r=[t_, a], w=[a])
            P.op("dve", lambda e: e.scalar_tensor_tensor(A[:], T_[:], -C2, A[:], ALU.mult, ALU.add), r=[t_, a], w=[a])
            ts("dve", T_[:], A[:], PI, -TWO_PI, ALU.is_gt, ALU.mult, [a], [t_])
            tt("dve", A[:], A[:], T_[:], ALU.add, [t_, a], [a])
            ts("dve", T_[:], A[:], -PI, TWO_PI, ALU.is_lt, ALU.mult, [a], [t_])
            tt("dve", A[:], A[:], T_[:], ALU.add, [t_, a], [a])
            ts("dve", A[:], A[:], PI, -PI, ALU.min, ALU.max, [a], [a])
            act(A[:], A[:], AF.Sin, [a], [a])

        def act(out, in_, func, r, w, scale=1.0):
            P.op("act", lambda e, o=out, i=in_, f=func, s=scale: e.activation(o, i, f, scale=s), r=r, w=w)

        def cast_w(scr, sname, idx, src, c0, ncols, col_off=0):
            nk = src.shape[0] // 128
            dst = scr[idx].rearrange("p (k c) -> p k c", c=256)[:, 0:nk, col_off:col_off + ncols]
            P.dma("pool", dst, src.rearrange("(k p) c -> p k c", p=128)[:, :, c0:c0 + ncols],
                  w=[(sname, idx, col_off)], bg=True)

        cast_jobs = []

        def cast_all(first):
            if first:
                for i in list(range(4)) + [13]:
                    cast_w(s_in, "s_in", i, w_in, 256 * i, 256)
                for kh in range(4):
                    for dup in range(2):
                        cast_w(s_k2, "s_k2", kh // 2, w_in, C_K + 64 * kh, 64, 128 * (kh % 2) + 64 * dup)
                return
            J = cast_jobs.append
            for i in list(range(8, 12)) + list(range(14, 18)):
                J((s_in, "s_in", i, w_in, 256 * i, 256, 0))
            for i in range(8):
                J((s_in, "s_in", 26 + i, w_in, 256 * (26 + i), 256, 0))
                J((s_ba, "s_ba", i, w_ba, 256 * i, 256, 0))
            for i in range(4, 8):
                J((s_in, "s_in", i, w_in, 256 * i, 256, 0))
            for i in range(8):
                J((s_glu, "s_glu", i, w_glu, 128 * i, 128, 0)); J((s_glu, "s_glu", i, w_glu, 1024 + 128 * i, 128, 128))
            for i in range(8):
                J((s_in, "s_in", 18 + i, w_in, 256 * (18 + i), 256, 0))
                J((s_bs, "s_bs", i, w_bs, 256 * i, 256, 0))
            for i in range(8):
                J((s_out, "s_out", i, w_out, 256 * i, 256, 0))

        def issue_casts(n):
            for _ in range(min(n, len(cast_jobs))):
                cast_w(*cast_jobs.pop(0))

        def load_c(scr, sname, idx, nk=16, cw=256, offs=(0, 128)):
            dst, key = nwb()
            P.dma("sp", dst[:, 0:nk * 256], scr[idx][:, 0:nk * 256], r=[(sname, idx, o_) for o_ in offs], w=[key])
            return dst[:].rearrange("p (k c) -> p k c", c=cw), key

        with ExitStack() as s0:
            sb0 = lambda n, s, d=F32: sb(n, s, d, s0)
            lamr = sb0("lamr_s", [128, 32]); lami = sb0("lami_s", [128, 32]); stp = sb0("stp", [128, 32])
            W1re = sb0("W1re", [128, 32 * 128], BF16)
            W1im = sb0("W1im", [128, 32 * 128], BF16)
            Wc = sb0("Wc", [128, 32 * 2 * 128], BF16)
            Tm = sb0("Tm", [128, 64 * 128], BF16)
            brs = sb0("brs", [128, 512]); bis = sb0("bis", [128, 512]); crs = sb0("crs", [128, 512]); cis = sb0("cis", [128, 512])
            dIs = sb0("dIs", [16, 1024]); sks = sb0("sks", [128, 8])
            for t_, d_ in ((lamr, lamr_d), (lami, lami_d), (stp, lstep_d), (brs, br_d), (bis, bi_d), (crs, cr_d),
                           (cis, ci_d), (dIs, dI_d), (sks, sk_d), (hneg, hneg_d), (identf, idf_d), (antif, anti_d)):
                P.dma("sp", t_[:], d_, w=[t_.name])
            copy("dve", identb[:], identf[:], [identf.name], [identb.name])
            VH = [sb0(f"VH{i}", [128, 256]) for i in range(4)]

            def bias_head(h):
                vh = VH[h % 4]
                for kt_ in range(2):
                    P.dma("sp", vh[:, 128 * kt_:128 * kt_ + 128],
                          bass.AP(line_d.tensor, 384 * h + 129 - 128 * kt_, [[1, 128], [1, 128]]), w=[vh.name])
                ps, pk = nps()
                P.group("pe", [lambda e, o=ps[:, 128 * k_:128 * k_ + 128], r_=vh[:, 128 * k_:128 * k_ + 128]:
                               e.matmul(o, antif[:], r_, start=True, stop=True) for k_ in range(2)], r=[vh.name, antif.name], w=[pk])
                copy("act", biasT[:, 256 * h:256 * h + 256], ps[:, 0:256], [pk], ["biasT"])
            P.op("dve", lambda e: e.memset(ones64[:], 1.0), w=["ones64"])
            P.op("dve", lambda e: e.memset(carry[:], 0.0), w=["carry"])
            act(esk[:], sks[:], AF.Exp, [sks.name], ["esk"])
            ts("dve", ar[:], stp[:], 0.125, None, ALU.mult, None, [stp.name], ["ar"])
            taylor_exp(stp[:], ar[:], 10, stp.name, "ar")
            for _ in range(3):
                tt("dve", stp[:], stp[:], stp[:], ALU.mult, [stp.name], [stp.name])
            tt("dve", ar[:], lamr[:], stp[:], ALU.mult, [lamr.name, stp.name], ["ar"])
            tt("dve", ai[:], lami[:], stp[:], ALU.mult, [lami.name, stp.name], ["ai"])
            MAR = sb0("MAR", [128, 9 * 32]); ANG = sb0("ANG", [128, 2 * 9 * 32]); TMPA = sb0("TMPA", [128, 576])
            KI = sb0("KI", [128, 576], I32)
            for m in range(9):
                ts("dve", MAR[:, 32 * m:32 * m + 32], ar[:], float(m), None, ALU.mult, None, ["ar"], ["MAR"])
                ts("dve", ANG[:, 32 * m:32 * m + 32], ai[:], float(m), None, ALU.mult, None, ["ai"], ["ANG"])
                ts("dve", ANG[:, 288 + 32 * m:288 + 32 * m + 32], ai[:], float(m), PI / 2, ALU.mult, ALU.add, ["ai"], ["ANG"])
            sin_reduced(ANG, TMPA, KI)
            MAG = sb0("MAG", [128, 9 * 32])
            taylor_exp(MAG[:], MAR[:], 8, "MAG", "MAR")
            PR = sb0("PR", [128, 288]); PIm = sb0("PIm", [128, 288])
            tt("dve", PR[:], MAG[:], ANG[:, 288:576], ALU.mult, ["MAG", "ANG"], ["PR"])
            tt("dve", PIm[:], MAG[:], ANG[:, 0:288], ALU.mult, ["MAG", "ANG"], ["PIm"])
            copy("dve", CA[:, 0:32], PR[:, 256:288], ["PR"], ["CA"])
            copy("dve", CA[:, 32:64], PR[:, 256:288], ["PR"], ["CA"])
            copy("dve", CBp[:], PIm[:, 256:288], ["PIm"], ["CBp"])
            ts("dve", CBn[:], PIm[:, 256:288], -1.0, None, ALU.mult, None, ["PIm"], ["CBn"])
            nr = sb0("nr", [128, 32]); den = sb0("den", [128, 32]); t1 = sb0("t1", [128, 32]); t2 = sb0("t2", [128, 32])
            kr = sb0("kr", [128, 32]); ki_ = sb0("ki_", [128, 32])
            ts("dve", nr[:], PR[:, 32:64], -1.0, None, ALU.add, None, ["PR"], ["nr"])
            tt("dve", den[:], lamr[:], lamr[:], ALU.mult, [lamr.name], ["den"])
            tt("dve", t1[:], lami[:], lami[:], ALU.mult, [lami.name], ["t1"])
            tt("dve", den[:], den[:], t1[:], ALU.add, ["t1"], ["den"])
            P.op("dve", lambda e: e.reciprocal(den[:], den[:]), r=["den"], w=["den"])
            tt("dve", t1[:], nr[:], lamr[:], ALU.mult, ["nr"], ["t1"])
            tt("dve", t2[:], PIm[:, 32:64], lami[:], ALU.mult, ["PIm"], ["t2"])
            tt("dve", t1[:], t1[:], t2[:], ALU.add, ["t2"], ["t1"])
            tt("dve", kr[:], t1[:], den[:], ALU.mult, ["t1", "den"], ["kr"])
            tt("dve", t1[:], PIm[:, 32:64], lamr[:], ALU.mult, ["PIm"], ["t1"])
            tt("dve", t2[:], nr[:], lami[:], ALU.mult, ["nr"], ["t2"])
            tt("dve", t1[:], t1[:], t2[:], ALU.subtract, ["t2"], ["t1"])
            tt("dve", ki_[:], t1[:], den[:], ALU.mult, ["t1", "den"], ["ki_"])
            OMR = sb0("OMR", [128, 256]); OMI = sb0("OMI", [128, 256]); T8 = sb0("T8", [128, 256])
            pr3 = PR[:, 0:256].rearrange("p (m q) -> p m q", q=32); pi3 = PIm[:, 0:256].rearrange("p (m q) -> p m q", q=32)
            krb = kr[:].unsqueeze(1).to_broadcast([128, 8, 32]); kib = ki_[:].unsqueeze(1).to_broadcast([128, 8, 32])
            omr3 = OMR[:].rearrange("p (m q) -> p m q", q=32); omi3 = OMI[:].rearrange("p (m q) -> p m q", q=32)
            t83 = T8[:].rearrange("p (m q) -> p m q", q=32)
            tt("dve", omr3, pr3, krb, ALU.mult, ["PR", "kr"], ["OMR"])
            tt("dve", t83, pi3, kib, ALU.mult, ["PIm", "ki_"], ["T8"])
            tt("dve", OMR[:], OMR[:], T8[:], ALU.subtract, ["T8"], ["OMR"])
            tt("dve", omi3, pi3, krb, ALU.mult, ["PIm", "kr"], ["OMI"])
            tt("dve", t83, pr3, kib, ALU.mult, ["PR", "ki_"], ["T8"])
            tt("dve", OMI[:], OMI[:], T8[:], ALU.add, ["T8"], ["OMI"])
            TB = sb0("TB", [128, 512]); A0r = sb0("A0r", [128, 512]); A0i = sb0("A0i", [128, 512])
            sA = ExitStack()
            AR = sb("AR", [128, 4096], F32, sA); AI = sb("AI", [128, 4096], F32, sA)
            ar4 = AR[:].rearrange("p (q j h) -> p q j h", j=8, h=16); ai4 = AI[:].rearrange("p (q j h) -> p q j h", j=8, h=16)
            br3 = brs[:].rearrange("p (q h) -> p q h", h=16); bi3 = bis[:].rearrange("p (q h) -> p q h", h=16)
            tb3 = TB[:].rearrange("p (q h) -> p q h", h=16)
            for j in range(8):
                m = 7 - j
                orb = OMR[:, 32 * m:32 * m + 32].unsqueeze(2).to_broadcast([128, 32, 16])
                oib = OMI[:, 32 * m:32 * m + 32].unsqueeze(2).to_broadcast([128, 32, 16])
                tt("dve", ar4[:, :, j, :], br3, orb, ALU.mult, [brs.name, "OMR"], ["AR"])
                tt("dve", tb3, bi3, oib, ALU.mult, [bis.name, "OMI"], ["TB"])
                tt("dve", ar4[:, :, j, :], ar4[:, :, j, :], tb3, ALU.subtract, ["TB"], ["AR"])
                tt("dve", ai4[:, :, j, :], bi3, orb, ALU.mult, [bis.name, "OMR"], ["AI"])
                tt("dve", tb3, br3, oib, ALU.mult, [brs.name, "OMI"], ["TB"])
                tt("dve", ai4[:, :, j, :], ai4[:, :, j, :], tb3, ALU.add, ["TB"], ["AI"])
            for src, dst, nm in ((AR, W1re, "W1re"), (AI, W1im, "W1im")):
                for q0 in range(0, 32, 4):
                    ps, pk = nps()
                    P.group("pe", [lambda e, o=ps[:, 128 * i:128 * i + 128], s_=src[:, 128 * (q0 + i):128 * (q0 + i) + 128]:
                                   e.transpose(o, s_, identf[:]) for i in range(4)], r=[src.name, identf.name], w=[pk])
                    copy(evac_eng(), dst[:, 128 * q0:128 * q0 + 512], ps[:, :], [pk], [nm])
            copy("dve", A0r[:].rearrange("p (q h) -> p q h", h=16), ar4[:, :, 7, :], ["AR"], ["A0r"])
            copy("dve", A0i[:].rearrange("p (q h) -> p q h", h=16), ai4[:, :, 7, :], ["AI"], ["A0i"])
            P.barrier()
            sA.close()
            ER = sb0("ER", [128, 32 * 9 * 16]); NEI = sb0("NEI", [128, 32 * 9 * 16])
            er4 = ER[:].rearrange("p (q k h) -> p q k h", k=9, h=16); ne4 = NEI[:].rearrange("p (q k h) -> p q k h", k=9, h=16)
            cr3 = crs[:].rearrange("p (q h) -> p q h", h=16); ci3 = cis[:].rearrange("p (q h) -> p q h", h=16)
            for k in range(9):
                prb = PR[:, 32 * k:32 * k + 32].unsqueeze(2).to_broadcast([128, 32, 16])
                pib = PIm[:, 32 * k:32 * k + 32].unsqueeze(2).to_broadcast([128, 32, 16])
                tt("dve", er4[:, :, k, :], cr3, prb, ALU.mult, [crs.name, "PR"], ["ER"])
                tt("dve", tb3, ci3, pib, ALU.mult, [cis.name, "PIm"], ["TB"])
                tt("dve", er4[:, :, k, :], er4[:, :, k, :], tb3, ALU.subtract, ["TB"], ["ER"])
                tt("dve", ne4[:, :, k, :], cr3, pib, ALU.mult, [crs.name, "PIm"], ["NEI"])
                tt("dve", tb3, ci3, prb, ALU.mult, [cis.name, "PR"], ["TB"])
                P.op("dve", lambda e, o=ne4[:, :, k, :]: e.scalar_tensor_tensor(o, o, -1.0, tb3, ALU.mult, ALU.subtract), r=["TB"], w=["NEI"])
            P.op("pool", lambda e: e.memset(Tm[:], 0.0), w=["Tm"])
            cast_all(True)
            cast_all(False)
            wc4 = Wc[:].rearrange("p (q r c) -> p q r c", r=2, c=128)
            for ri, src in ((0, ER), (1, NEI)):
                s4 = src[:].rearrange("p (q c) -> p q c", c=144)
                copy("dve", wc4[:, :, ri, :], s4[:, :, 16:144], [src.name], ["Wc"])
            WBF = [sb0(f"WBF{i}", [128, 2 * 2 * 128]) for i in range(4)]
            KTb = sb0("KTb", [16, 64 * 128], BF16)
            for wb in WBF:
                P.op("dve", lambda e, t_=wb: e.memset(t_[:], 0.0), w=[wb.name])
            dI3 = dIs[:].rearrange("p (g h) -> p g h", h=16)
            kt4 = KTb[:].rearrange("p (g k h) -> p g k h", k=8, h=16)
            for q in range(32):
                if q % 2 == 0:
                    bias_head(q // 2)
                wb = WBF[q % 4]
                w4 = wb[:].rearrange("p (r g c) -> p r g c", r=2, g=2)
                for g2 in range(2):
                    rows = slice(64 * g2, 64 * g2 + 64)
                    copy("act", w4[rows, 0, g2, :], ER[rows, 144 * q:144 * q + 128], ["ER"], [wb.name])
                    copy("act", w4[rows, 1, g2, :], NEI[rows, 144 * q:144 * q + 128], ["NEI"], [wb.name])
                ps, pk = nps()
                P.group("pe", [
                    lambda e, o=ps[0:16, 0:256], l=A0r[:, 16 * q:16 * q + 16], r_=wb[:, 0:256]: e.matmul(o, l, r_, start=True, stop=False),
                    lambda e, o=ps[0:16, 0:256], l=A0i[:, 16 * q:16 * q + 16], r_=wb[:, 256:512]: e.matmul(o, l, r_, start=False, stop=True),
                ], r=["A0r", "A0i", wb.name], w=[pk])
                copy("dve", KTb[:, 256 * q:256 * q + 256], ps[0:16, 0:256], [pk], ["KTb"])
                tt("dve", kt4[:, 2 * q:2 * q + 2, 0, :], ps[0:16, 0:256].rearrange("p (g k h) -> p g k h", g=2, h=16)[:, :, 0, :],
                   dI3[:, 2 * q:2 * q + 2, :], ALU.add, [pk, dIs.name], ["KTb"])
            tm3 = Tm[:].rearrange("p (g c) -> p g c", c=128)
            kt3 = KTb[:].rearrange("p (g c) -> p g c", c=128)
            for jp in range(8):
                P.dma("sp", tm3[16 * jp:16 * jp + 16, :, 16 * jp:128], kt3[:, :, 0:(8 - jp) * 16], r=["KTb", "Tm"], w=[("Tm", jp)])
            for i, (t_, nm_) in enumerate(((W1re, "W1re"), (W1im, "W1im"))):
                P.dma("sp", s_w1[i], t_[:], r=[nm_])
            for i in range(2):
                P.dma("sp", s_wc[i], Wc[:, 4096 * i:4096 * i + 4096], r=["Wc"])
                P.dma("sp", s_tm[i], Tm[:, 4096 * i:4096 * i + 4096], r=["Tm"] + [("Tm", jp) for jp in range(8)])
            P.barrier()
            for nm_, t_ in (("PR", PR), ("PIm", PIm), ("kr", kr), ("ki_", ki_), ("W1re", W1re), ("W1im", W1im), ("Wc", Wc), ("Tm", Tm),
                            ("biasT", biasT), ("esk", esk), ("CA", CA), ("CBn", CBn), ("ER", ER), ("NEI", NEI)):
                dump(nm_, t_, "none")
            dump("KTb", KTb, "none", parts=16)
            P.barrier()

        xT = sb("xT", [128, 16 * SPAN], BF16)
        gyT = sb("gyT", [128, 8 * SPAN], BF16)
        kT = sb("kT", [128, 4 * 640], BF16)
        vext = sb("vext", [128, 5 * 4 * 64], BF16)
        wbuf.extend(sb(f"wbuf{i}", [128, 16 * 256], BF16) for i in range(4))
        xb_t = [sb(f"xb{i}", [128, D], BF16) for i in range(2)]
        xb = [(t_[:, :], t_.name) for t_ in xb_t]
        xpre = {"n": 0}
        P.op("dve", lambda e: e.memset(kT[:], 0.0), w=["kT"])
        P.op("dve", lambda e: e.memset(vext[:], 0.0), w=["vext"])
        xT3 = xT[:].rearrange("p (k t) -> p k t", t=SPAN)
        gy3 = gyT[:].rearrange("p (c t) -> p c t", t=SPAN)
        kT3 = kT[:].rearrange("p (h t) -> p h t", t=640)
        vx4 = vext[:].rearrange("p (t h d) -> p t h d", h=4, d=64)
        bT4 = biasT[:].rearrange("p (h k q) -> p h k q", k=2, q=128)

        def build_xT(xsrc, row0, bufs, ev=None):
            for t4 in range(SPAN // 128):
                b_, bk_ = bufs[t4 % len(bufs)]
                if t4 >= xpre["n"]:
                    P.dma("pool", b_, xsrc[row0 + 128 * t4:row0 + 128 * t4 + 128, :], w=[bk_])
                for half in range(2):
                    ps, pk = npb()
                    P.group("pe", [lambda e, o=ps[:, 128 * i:128 * i + 128], s2=b_[:, 128 * (8 * half + i):128 * (8 * half + i) + 128]:
                                   e.transpose(o, s2, identb[:]) for i in range(8)], r=[bk_], w=[pk])
                    copy(ev or evac_eng(), xT3[:, 8 * half:8 * half + 8, 128 * t4:128 * t4 + 128],
                         ps[:, :].rearrange("p (k t) -> p k t", t=128), [pk], ["xT"])
            xpre["n"] = 0

        def prefetch_x(xsrc, row0, bufs, t_lo, t_hi):
            for t4 in range(t_lo, t_hi):
                P.dma("pool", bufs[t4][0], xsrc[row0 + 128 * t4:row0 + 128 * t4 + 128, :], w=[bufs[t4][1]])
            xpre["n"] = t_hi

        def ssm_state(sc, own, pfx=None):
            ubm = sb("ubm", [128, 8 * 1024], BF16, sc)
            U = sb("U", [128, 64 * NB], BF16, sc)
            X = sb("X", [128, 2 * 32 * NB], F32, sc)
            ub4 = ubm[:, 0:4096].rearrange("p (g j h) -> p g j h", j=4, h=16)
            U3 = U[:].rearrange("p (g b) -> p g b", b=NB)
            X4 = X[:].rearrange("p (r q b) -> p r q b", r=2, b=NB)
            xTj = xT[:].rearrange("p (k b j) -> p k j b", j=8, b=NB)

            def u_transposes(g0):
                ps, pk = npb()
                fns = []
                for i in range(16):
                    for jh in range(2):
                        rows = slice(64 * jh, 64 * jh + 64)
                        fns.append(lambda e, o=ps[rows, NB * i:NB * i + NB], s_=ubm[rows, 64 * (g0 + i):64 * (g0 + i) + 64],
                                   idn=identb[rows, 64 * jh:64 * jh + 64]: e.transpose(o, s_, idn))
                P.group("pe", fns, r=[("u", g0 // 16)], rr=["ubm"], w=[pk])
                copy(evac_eng(), U[:, NB * g0:NB * g0 + NB * 16], ps[:, 0:NB * 16], [pk], ["U"])

            for cb in range(4):
                wv, wk = load_c(s_in, "s_in", cb)
                for j4 in range(4):
                    ps, pk = nps()
                    fns = []
                    for kt in range(16):
                        for jh in range(2):
                            fns.append(lambda e, o=ps[64 * jh:64 * jh + 64, 0:256], l=xTj[:, kt, 4 * jh + j4, :], r_=wv[:, kt, :], kt=kt:
                                       e.matmul(o, l, r_, start=(kt == 0), stop=(kt == 15)))
                    P.group("pe", fns, r=["xT", wk], w=[pk])
                    copy("act" if (pfx is not None and cb == 0) else evac_eng(),
                         ub4[:, 16 * cb:16 * cb + 16, j4, :], ps[:, 0:256].rearrange("p (g h) -> p g h", h=16), [pk], ["ubm", ("u", cb)])
                if own and cb in (0, 2) and late.get("pending"):
                    late["pending"].pop(0)()
                if cb == 2:
                    u_transposes(0); u_transposes(16)
                if cb == 3:
                    u_transposes(32)
            u_transposes(48)
            w1r, w1rk = load_c(s_w1, "s_w1", 0, 16, 128)
            w1i, w1ik = load_c(s_w1, "s_w1", 1, 16, 128)
            w13 = {0: w1r, 1: w1i}
            for q0 in range(0, 32, 4):
                ps, pk = nps()
                ps4 = ps[:, :].rearrange("p (q r b) -> p q r b", r=2, b=NB)
                fns = []
                for i in range(4):
                    q = q0 + i
                    for g2 in range(2):
                        for ri in range(2):
                            fns.append(lambda e, o=ps4[64 * g2:64 * g2 + 64, i, ri, :], l=w13[ri][:, q, 64 * g2:64 * g2 + 64],
                                       r_=U3[:, 2 * q + g2, :]: e.matmul(o, l, r_, start=True, stop=True))
                P.group("pe", fns, r=["U", w1rk, w1ik], w=[pk])
                copy(evac_eng(), X4[:, :, q0:q0 + 4, :], ps4.rearrange("p q r b -> p r q b"), [pk], ["X"])
            tA = sb("tA", [128, 64], F32, sc); tB = sb("tB", [128, 64], F32, sc)
            if pfx is not None:
                t1 = sb("pt1", [128, 32 * NB], F32, sc); t2 = sb("pt2", [128, 32 * NB], F32, sc)
                red = sb("pred", [128, 64], F32, sc)
                xr, xi = X[:, 0:32 * NB], X[:, 32 * NB:64 * NB]
                for ri_, (a_, b_, op_) in enumerate(((pfx["PWr"], pfx["PWi"], ALU.subtract), (pfx["PWi"], pfx["PWr"], ALU.add))):
                    tt("dve", t1[:], xr, a_[:], ALU.mult, ["X"], ["pt1"])
                    tt("dve", t2[:], xi, b_[:], ALU.mult, ["X"], ["pt2"])
                    tt("dve", t1[:], t1[:], t2[:], op_, ["pt1", "pt2"], ["pt1"])
                    P.op("dve", lambda e, o=red[:, 32 * ri_:32 * ri_ + 32], i_=t1[:].rearrange("p (q b) -> p q b", b=NB):
                         e.reduce_sum(o, i_, mybir.AxisListType.X), r=["pt1"], w=["pred"])
                c3 = carry[:].rearrange("p (r q) -> p r q", r=2)
                ta3 = tA[:].rearrange("p (r q) -> p r q", r=2); tb3_ = tB[:].rearrange("p (r q) -> p r q", r=2)
                tt("dve", ta3, c3, pfx["CA64"][:].rearrange("p (r q) -> p r q", r=2), ALU.mult, ["carry"], ["tA"])
                tt("dve", tb3_[:, 0, :], c3[:, 1, :], pfx["CB64n"][:], ALU.mult, ["carry"], ["tB"])
                tt("dve", tb3_[:, 1, :], c3[:, 0, :], pfx["CB64p"][:], ALU.mult, ["carry"], ["tB"])
                tt("dve", tA[:], tA[:], tB[:], ALU.add, ["tA", "tB"], ["tA"])
                tt("dve", carry[:], tA[:], red[:], ALU.add, ["tA", "pred"], ["carry"])
                return None
            Sh = None
            if own:
                Sh = sb("Sh", [128, 2 * 2 * 32 * NB], BF16, sc)
                Sh5 = Sh[:].rearrange("p (g r q b) -> p g r q b", g=2, r=2, b=NB)
                P.op("dve", lambda e, o=Sh5[64:128, 0]: e.memset(o, 0.0), w=["Sh"])
                P.op("dve", lambda e, o=Sh5[0:64, 1]: e.memset(o, 0.0), w=["Sh"])
                for g2 in range(2):
                    rows = slice(64 * g2, 64 * g2 + 64)
                    copy("pool", Sh5[rows, g2, :, :, 0], carry[rows, :].rearrange("p (r q) -> p r q", r=2), ["carry"], ["Sh"])
            c3 = carry[:].rearrange("p (r q) -> p r q", r=2)
            ca3 = CA[:].rearrange("p (r q) -> p r q", r=2)
            ta3 = tA[:].rearrange("p (r q) -> p r q", r=2); tb3_ = tB[:].rearrange("p (r q) -> p r q", r=2)
            for b in range(NB):
                prev = c3 if b == 0 else X4[:, :, :, b - 1]
                tt("pool", ta3, prev, ca3, ALU.mult, ["X", "carry"], ["tA"])
                tt("pool", tb3_[:, 0, :], prev[:, 1, :], CBn[:], ALU.mult, ["X", "carry"], ["tB"])
                tt("pool", tb3_[:, 1, :], prev[:, 0, :], CBp[:], ALU.mult, ["X", "carry"], ["tB"])
                tt("pool", ta3, ta3, tb3_, ALU.add, ["tB"], ["tA"])
                tt("pool", X4[:, :, :, b], X4[:, :, :, b], ta3, ALU.add, ["tA"], ["X"])
            copy("pool", c3, X4[:, :, :, NB - 1], ["X", "Sh"], ["carry"])
            return ubm, U3, (Sh, X4), (U, X)

        def ssm_out(ubm, U3, Sh):
            Sh, X4 = Sh
            Sh5 = Sh[:].rearrange("p (g r q b) -> p g r q b", g=2, r=2, b=NB)
            for g2 in range(2):
                rows = slice(64 * g2, 64 * g2 + 64)
                copy("dve" if g2 else "act", Sh5[rows, g2, :, :, 1:NB], X4[rows, :, :, 0:NB - 1], ["X"], ["Sh"])
            gb5 = ubm[:, 0:4096].rearrange("p (c j g h) -> p c g j h", c=8, j=8, g=4, h=16)
            for ct in range(8):
                if ct % 4 == 0:
                    tmv, tmk = load_c(s_tm, "s_tm", ct // 4, 16, 128)
                    wcv, wck = load_c(s_wc, "s_wc", ct // 4, 16, 128)
                ps, pk = nps()
                fns = []
                for g4 in range(4):
                    for step in range(3):
                        for gp in range(2):
                            g = 8 * ct + 4 * gp + g4; q, g2 = g // 2, g % 2
                            o = ps[64 * gp:64 * gp + 64, 128 * g4:128 * g4 + 128]
                            if step == 0:
                                fns.append(lambda e, o=o, l=U3[:, g, :], r_=tmv[:, g % 32, :]: e.matmul(o, l, r_, start=True, stop=False))
                            elif step == 1:
                                fns.append(lambda e, o=o, l=Sh5[:, g2, 0, q, :], r_=wcv[:, 2 * (q % 16), :]: e.matmul(o, l, r_, start=False, stop=False))
                            else:
                                fns.append(lambda e, o=o, l=Sh5[:, g2, 1, q, :], r_=wcv[:, 2 * (q % 16) + 1, :]: e.matmul(o, l, r_, start=False, stop=True))
                P.group("pe", fns, r=["U", "Sh", tmk, wck], w=[pk])
                act(gb5[:, ct, :, :, :], ps[:, :].rearrange("p (g j h) -> p g j h", j=8, h=16), AF.Gelu, [pk, "U"], ["ubm", ("gy", ct)])
                if ct >= 1:
                    gy_transposes(ct - 1, ubm)
            gy_transposes(7, ubm)

        def gy_transposes(ct, ubm):
            if True:
                ps, pk = npb()
                fns = []
                for j in range(8):
                    for gp in range(2):
                        rows = slice(64 * gp, 64 * gp + 64)
                        fns.append(lambda e, o=ps[rows, NB * j:NB * j + NB], s_=ubm[rows, 512 * ct + 64 * j:512 * ct + 64 * j + 64],
                                   idn=identb[rows, 64 * gp:64 * gp + 64]: e.transpose(o, s_, idn))
                P.group("pe", fns, r=[("gy", ct)], rr=["ubm"], w=[pk])
                copy(evac_eng(), gy3[:, ct, :].rearrange("p (b j) -> p j b", j=8),
                     ps[:, 0:8 * NB].rearrange("p (j b) -> p j b", b=NB), [pk], ["gyT"])

        def proj_fm_unused(col0, ncols_tiles, consume):
            for c2 in range(0, ncols_tiles, 2):
                n = min(2, ncols_tiles - c2)
                wv, wk = load_c(s_in, "s_in", (col0 + 128 * c2) // 256)
                for i in range(n):
                    ps, pk = nps()
                    P.group("pe", [lambda e, o=ps[:, :], l=wv[:, kt, 128 * i:128 * i + 128], r_=xT3[:, kt, :], kt=kt:
                                   e.matmul(o, l, r_, start=(kt == 0), stop=(kt == 15)) for kt in range(16)], r=["xT", wk], w=[pk])
                    consume(c2 + i, ps, pk)

        def kv_proj(tok0, ntok, dst_tile0, kcol0):
            for kh in range(4):
                if kh % 2 == 0:
                    wv, wk = load_c(s_k2, "s_k2", kh // 2, offs=(0, 64, 128, 192))
                ps, pk = nps()
                P.group("pe", [lambda e, o=ps[:, 0:ntok], l=wv[:, kt, 128 * (kh % 2):128 * (kh % 2) + 128], r_=xT3[:, kt, tok0:tok0 + ntok], kt=kt:
                               e.matmul(o, l, r_, start=(kt == 0), stop=(kt == 15)) for kt in range(16)], r=["xT", wk], w=[pk])
                copy(evac_eng(), kT3[:, kh, kcol0:kcol0 + ntok], ps[:, 0:ntok], [pk], ["kT"])
            wv, wk = load_c(s_in, "s_in", 13)
            for t4 in range(ntok // 128):
                ps, pk = nps()
                P.group("pe", [lambda e, o=ps[:, 0:256], l=xT3[:, kt, tok0 + 128 * t4:tok0 + 128 * t4 + 128], r_=wv[:, kt, :], kt=kt:
                               e.matmul(o, l, r_, start=(kt == 0), stop=(kt == 15)) for kt in range(16)], r=["xT", wk], w=[pk])
                copy(evac_eng(), vx4[:, dst_tile0 + t4, :, :], ps[:, 0:256].rearrange("p (h d) -> p h d", d=64), [pk], ["vext"])

        late = {}

        def attn_stage(sp_i, sa):
            ha3 = late["ha3"]; haT = late["haT"]
            if True:
                qT = sb("qT", [128, 2 * SPAN], BF16, sa); qT4 = qT[:].rearrange("p (b h q) -> p b h q", h=2, q=128)
                pT = sb("pT", [128, 4 * 2 * 512], BF16, sa)
                pT5 = pT[:].rearrange("p (b k h q) -> p b k h q", k=2, h=4, q=128)
                lg = [sb(f"lg{i}", [128, 512], F32, sa) for i in range(2)]
                dn = sb("dn", [128, 512], F32, sa); zt2 = sb("zt", [128, 2 * 512], BF16, sa)
                kv_proj(0, SPAN, 1, 128)
                for kh in range(4):
                    wv, wk = load_c(s_in, "s_in", 8 + kh)
                    wza, wzak = load_c(s_in, "s_in", 14 + kh)
                    for hp in range(2):
                        ps, pk = nps()
                        P.group("pe", [lambda e, o=ps[:, :], l=wv[:, kt, 128 * hp:128 * hp + 128], r_=xT3[:, kt, :], kt=kt:
                                       e.matmul(o, l, r_, start=(kt == 0), stop=(kt == 15)) for kt in range(16)], r=["xT", wk], w=[pk])
                        act(qT4[:, :, hp, :], ps[:, :].rearrange("p (b q) -> p b q", q=128), AF.Identity, [pk], ["qT"], scale=0.125)
                    for blk in range(4):
                        for kt_ in range(2):
                            l_ = lg[(2 * blk + kt_) % 2]
                            pp, pks = nps2()
                            for half in range(2):
                                rows = slice(64 * half, 64 * half + 64)
                                P.group("pe", [lambda e, o=pp[:, 512 * half:512 * half + 256],
                                               l=kT3[rows, kh, 128 * (blk + kt_):128 * (blk + kt_) + 128],
                                               r_=qT[rows, 256 * blk:256 * blk + 256]: e.matmul(o, l, r_, start=True, stop=True)],
                                        r=["kT", "qT"], w=[pks[half]])
                            tt("dve", l_[:].rearrange("p (hf hp q) -> p hf hp q", hf=2, q=128),
                               pp.rearrange("p (b hp q) -> p b hp q", b=2, q=128)[:, :, 0:2, :],
                               bT4[:, 4 * kh:4 * kh + 4, kt_, :].rearrange("p (hp hf) q -> p hf hp q", hf=2),
                               ALU.add, [pks[0], pks[1], "biasT"], [l_.name])
                            if sp_i == 0 and blk == 0 and kt_ == 0:
                                P.op("act", lambda e, o=pT5[:, blk, kt_, :, :], i_=l_[:].rearrange("p (h q) -> p h q", q=128):
                                     e.activation(o, i_, AF.Exp, bias=hneg[:, 0:1]), r=[l_.name, "hneg_s"], w=["pT"])
                            else:
                                act(pT5[:, blk, kt_, :, :], l_[:].rearrange("p (h q) -> p h q", q=128), AF.Exp, [l_.name], ["pT"])
                    for hp in range(2):
                        ps, pk = nps()
                        P.group("pe", [lambda e, o=ps[:, :], l=wza[:, kt, 128 * hp:128 * hp + 128], r_=xT3[:, kt, :], kt=kt:
                                       e.matmul(o, l, r_, start=(kt == 0), stop=(kt == 15)) for kt in range(16)], r=["xT", wzak], w=[pk])
                        act(zt2[:, 512 * hp:512 * hp + 512], ps[:, :], AF.Silu, [pk], ["zt"])
                    for hp in range(2):
                        hpg = 2 * kh + hp
                        zt = zt2[:, 512 * hp:512 * hp + 512]
                        pv, pvk = nps(); rs, rsk = nps()
                        fns = []
                        for blk in range(4):
                            for half in range(2):
                                hq = 2 * half + hp
                                for kt_ in range(2):
                                    fns.append(lambda e, o=pv[64 * half:64 * half + 64, 128 * blk:128 * blk + 128], l=vx4[:, blk + kt_, kh, :],
                                               r_=pT5[:, blk, kt_, hq, :], kt_=kt_: e.matmul(o, l, r_, start=(kt_ == 0), stop=(kt_ == 1)))
                                for kt_ in range(2):
                                    fns.append(lambda e, o=rs[64 * half:64 * half + 64, 128 * blk:128 * blk + 128], r_=pT5[:, blk, kt_, hq, :], kt_=kt_:
                                               e.matmul(o, ones64[:], r_, start=(kt_ == 0), stop=(kt_ == 1)))
                        P.group("pe", fns, r=["pT", "vext", "ones64"], w=[pvk, rsk])
                        P.op("act", lambda e, r2=rs, h_=hpg: e.activation(dn[:], r2[:, :], AF.Ln, bias=esk[:, h_:h_ + 1]), r=[rsk, "esk"], w=["dn"])
                        P.op("act", lambda e: e.activation(dn[:], dn[:], AF.Exp, scale=-1.0), r=["dn"], w=["dn"])
                        tt("dve", dn[:], dn[:], pv[:, :], ALU.mult, [pvk], ["dn"])
                        tt("dve", ha3[:, hpg, :], dn[:], zt, ALU.mult, ["dn", "zt"], ["haT"])
                copy("dve", kT3[:, :, 0:128], kT3[:, :, 512:640], ["kT"], ["kT"])
                copy("dve", vx4[:, 0, :, :], vx4[:, 4, :, :], ["vext"], ["vext"])

        def merge_half(attn, sm):
            tg = "a" if attn else "s"
            ha3, hs3, mT3 = late["ha3"], late["hs3"], late["mT3"]
            g1 = sb("g1", [128, 512], F32, sm); m1 = late["ftmp"]
            h3, hn = (ha3, "haT") if attn else (hs3, "hsT")
            for d2 in range(8):
                wg3, wgk = load_c(s_in, "s_in", (26 if attn else 18) + d2)
                wb3, wbk = load_c(s_ba if attn else s_bs, "s_ba" if attn else "s_bs", d2, 8)
                for i in range(2):
                    dt = 2 * d2 + i
                    ps, pk = nps()
                    P.group("pe", [lambda e, o=ps[:, :], l=wg3[:, kt, 128 * i:128 * i + 128], r_=xT3[:, kt, :], kt=kt:
                                   e.matmul(o, l, r_, start=(kt == 0), stop=(kt == 15)) for kt in range(16)], r=["xT", wgk], w=[pk])
                    act(g1[:], ps[:, :], AF.Sigmoid, [pk], [g1.name])
                    ps, pk = nps()
                    P.group("pe", [lambda e, o=ps[:, :], l=wb3[:, ct, 128 * i:128 * i + 128], r_=h3[:, ct, :], ct=ct:
                                   e.matmul(o, l, r_, start=(ct == 0), stop=(ct == 7)) for ct in range(8)], r=[hn, wbk], w=[pk])
                    if attn:
                        tt("dve", mT3[:, dt, :], ps[:, :], g1[:], ALU.mult, [pk, g1.name], ["mT"])
                    else:
                        tt("dve", m1[:], ps[:, :], g1[:], ALU.mult, [pk, g1.name], ["ftmp"])
                        tt("dve", mT3[:, dt, :], mT3[:, dt, :], m1[:], ALU.add, ["ftmp"], ["mT"])

        def main_stage(sp_i, sc, ubm, U, X, Sh, pre_tail, defer):
            hs3, mT3, hsT, mT = late["hs3"], late["mT3"], late["hsT"], late["mT"]
            if True:
                sg = sc
                zs = sb("zs", [128, 512], BF16, sg); sgb = sb("sgb", [128, 512], F32, sg); ga_ = late["ftmp"]
                for et in range(8):
                    if et % 2 == 0:
                        wv2, wk2 = load_c(s_in, "s_in", 4 + et // 2)
                    ps, pk = nps()
                    P.group("pe", [lambda e, o=ps[:, :], l=wv2[:, kt, 128 * (et % 2):128 * (et % 2) + 128], r_=xT3[:, kt, :], kt=kt:
                                   e.matmul(o, l, r_, start=(kt == 0), stop=(kt == 15)) for kt in range(16)], r=["xT", wk2], w=[pk])
                    act(zs[:], ps[:, :], AF.Silu, [pk], ["zs"])
                    wg3, wgk = load_c(s_glu, "s_glu", et, 8)
                    pa, pak = nps(); pb, pbk = nps()
                    P.group("pe", [lambda e, o=pa[:, :], l=wg3[:, ct, 0:128], r_=gy3[:, ct, :], ct=ct:
                                   e.matmul(o, l, r_, start=(ct == 0), stop=(ct == 7)) for ct in range(8)], r=["gyT", wgk], w=[pak])
                    P.group("pe", [lambda e, o=pb[:, :], l=wg3[:, ct, 128:256], r_=gy3[:, ct, :], ct=ct:
                                   e.matmul(o, l, r_, start=(ct == 0), stop=(ct == 7)) for ct in range(8)], r=["gyT", wgk], w=[pbk])
                    act(sgb[:], pb[:, :], AF.Sigmoid, [pbk], ["sgb"])
                    tt("dve", ga_[:], pa[:, :], sgb[:], ALU.mult, [pak, "sgb"], ["ftmp"])
                    tt("dve", hs3[:, et, :], ga_[:], zs[:], ALU.mult, ["ftmp", "zs"], ["hsT"])
            merge_half(False, sc)
            if True:
                so = sc
                rrA = ubm[:].bitcast(F32).rearrange("p (t c) -> p t c", c=D)
                rrB = Sh[:].bitcast(F32).rearrange("p (t c) -> p t c", c=D)
                rts = [(rrA[:, 0, :], "ubm"), (rrA[:, 1, :], "ubm"), (rrB[:, 0, :], "Sh"), (rrB[:, 1, :], "Sh")]
                sq = U[:].bitcast(F32)
                gch = X[:, 0:D]; bch = X[:, D:2 * D]
                st = sb("lnst", [128, 8], F32, so)
                for t4 in range(4):
                    rt, rk = rts[t4]
                    P.dma("pool", rt, xo[SPAN * sp_i + 128 * t4:SPAN * sp_i + 128 * t4 + 128, :], w=[rk])
                P.dma("pool", gch, lng_d, w=["X"]); P.dma("pool", bch, lnb_d, w=["X"])

                def layer_norm(t4):
                    rt, rk = rts[t4]
                    P.op("dve", lambda e: e.reduce_sum(st[:, 0:1], rt, mybir.AxisListType.X), r=[rk], w=["lnst"])
                    tt("dve", sq, rt, rt, ALU.mult, [rk], ["U"])
                    P.op("dve", lambda e: e.reduce_sum(st[:, 1:2], sq, mybir.AxisListType.X), r=["U"], w=["lnst"])
                    ts("dve", st[:, 2:3], st[:, 0:1], 1.0 / D, None, ALU.mult, None, ["lnst"], ["lnst"])
                    tt("dve", st[:, 3:4], st[:, 2:3], st[:, 2:3], ALU.mult, ["lnst"], ["lnst"])
                    ts("dve", st[:, 4:5], st[:, 1:2], 1.0 / D, float(LN_EPS), ALU.mult, ALU.add, ["lnst"], ["lnst"])
                    tt("dve", st[:, 4:5], st[:, 4:5], st[:, 3:4], ALU.subtract, ["lnst"], ["lnst"])
                    P.op("act", lambda e: e.activation(st[:, 5:6], st[:, 4:5], AF.Sqrt), r=["lnst"], w=["lnst"])
                    P.op("dve", lambda e: e.reciprocal(st[:, 5:6], st[:, 5:6]), r=["lnst"], w=["lnst"])
                    P.op("dve", lambda e: e.scalar_tensor_tensor(st[:, 6:7], st[:, 2:3], -1.0, st[:, 5:6], ALU.mult, ALU.mult),
                         r=["lnst"], w=["lnst"])
                    P.op("act", lambda e: e.activation(rt, rt, AF.Identity, bias=st[:, 6:7], scale=st[:, 5:6]), r=["lnst"], w=[rk])
                    tt("dve", rt, rt, gch, ALU.mult, ["X"], [rk])
                    tt("dve", rt, rt, bch, ALU.add, ["X"], [rk])
                    P.dma("pool", y_out[SPAN * sp_i + 128 * t4:SPAN * sp_i + 128 * t4 + 128, :], rt, r=[rk])

                for th in range(2):
                    for cb in range(8):
                        wv, wk = load_c(s_out, "s_out", cb)
                        for t2 in range(2):
                            t4 = 2 * th + t2
                            rt, rk = rts[t4]
                            ps, pk = nps()
                            P.group("pe", [lambda e, o=ps[:, 0:256], l=mT3[:, dt, 128 * t4:128 * t4 + 128], r_=wv[:, dt, :], dt=dt:
                                           e.matmul(o, l, r_, start=(dt == 0), stop=(dt == 15)) for dt in range(16)], r=["mT", wk], w=[pk])
                            P.op("dve", lambda e, o=rt[:, 256 * cb:256 * cb + 256], p_=ps[:, 0:256]:
                                 e.scalar_tensor_tensor(o, o, float(ALPHA), p_, ALU.mult, ALU.add), r=[pk], w=[rk])
                        if th == 1 and cb == 1:
                            layer_norm(0)
                        if th == 1 and cb == 4:
                            layer_norm(1)
                pre_tail()
                if defer:
                    late["pending"] = [lambda: layer_norm(2), lambda: layer_norm(3)]
                else:
                    layer_norm(2)
                    layer_norm(3)

        n_pre = dbg.get("n_pre", NSPAN)
        n_own = dbg.get("n_own", NSPAN)
        spre = ExitStack()
        PWr = sb("PWr", [128, 32 * NB], F32, spre); PWi = sb("PWi", [128, 32 * NB], F32, spre)
        CA64 = sb("CA64", [128, 64], F32, spre); CB64n = sb("CB64n", [128, 32], F32, spre); CB64p = sb("CB64p", [128, 32], F32, spre)
        with ExitStack() as sw:
            NM = 65
            msc = sb("msc_s", [128, NM], F32, sw)
            ANGx = sb("ANGx", [128, 2 * NM * 32], F32, sw); TMPx = sb("TMPx", [128, 2 * NM * 32], F32, sw)
            KIx = sb("KIx", [128, 2 * NM * 32], I32, sw); MARx = sb("MARx", [128, NM * 32], F32, sw)
            P.dma("sp", msc[:], msc_d, w=["msc_s"])
            an4 = ANGx[:].rearrange("p (s m q) -> p s m q", s=2, q=32)
            mb = msc[:].unsqueeze(2).to_broadcast([128, NM, 32])
            tt("dve", an4[:, 0], mb, ai[:].unsqueeze(1).to_broadcast([128, NM, 32]), ALU.mult, ["msc_s", "ai"], ["ANGx"])
            ts("dve", an4[:, 1], an4[:, 0], PI / 2, None, ALU.add, None, ["ANGx"], ["ANGx"])
            sin_reduced(ANGx, TMPx, KIx)
            ma3 = MARx[:].rearrange("p (m q) -> p m q", q=32)
            tt("dve", ma3, mb, ar[:].unsqueeze(1).to_broadcast([128, NM, 32]), ALU.mult, ["msc_s", "ar"], ["MARx"])
            act(MARx[:], MARx[:], AF.Exp, ["MARx"], ["MARx"])
            tt("dve", PWr[:].rearrange("p (q m) -> p m q", m=NB), ma3[:, 0:NB, :], an4[:, 1, 0:NB, :], ALU.mult, ["MARx", "ANGx"], ["PWr"])
            tt("dve", PWi[:].rearrange("p (q m) -> p m q", m=NB), ma3[:, 0:NB, :], an4[:, 0, 0:NB, :], ALU.mult, ["MARx", "ANGx"], ["PWi"])
            tt("dve", CA64[:, 0:32], ma3[:, NB, :], an4[:, 1, NB, :], ALU.mult, ["MARx", "ANGx"], ["CA64"])
            copy("dve", CA64[:, 32:64], CA64[:, 0:32], ["CA64"], ["CA64"])
            tt("dve", CB64p[:], ma3[:, NB, :], an4[:, 0, NB, :], ALU.mult, ["MARx", "ANGx"], ["CB64p"])
            ts("dve", CB64n[:], CB64p[:], -1.0, None, ALU.mult, None, ["CB64p"], ["CB64n"])
            P.barrier()
        pfx = dict(PWr=PWr, PWi=PWi, CA64=CA64, CB64n=CB64n, CB64p=CB64p)
        xb4 = xb + [(t_[:, :], t_.name) for t_ in (sb(f"xb{i}", [128, D], BF16, pre_scope) for i in (2, 3))]
        n_cast = -(-len(cast_jobs) // max(1, n_pre))
        for sp_i in range(NSPAN - n_pre, NSPAN):
            sc = pre_scope
            build_xT(xp, SPAN * sp_i, xb4, ev="act")
            ssm_state(sc, own=False, pfx=pfx)
            if sp_i + 1 < NSPAN:
                prefetch_x(xp, SPAN * (sp_i + 1), xb4, 0, 4)
            elif n_own:
                prefetch_x(xo, 0, xb, 0, 2)
            issue_casts(n_cast)
            if sp_i == NSPAN - 1:
                kv_proj(SPAN - 128, 128, 0, 0)
        issue_casts(len(cast_jobs))
        P.barrier()
        pre_scope.close()
        memo.clear()
        spre.close()
        mT = sb("mT", [128, 16 * SPAN], BF16); haT = sb("haT", [128, 8 * SPAN], BF16); hsT = sb("hsT", [128, 8 * SPAN], BF16)
        late.update(mT=mT, haT=haT, hsT=hsT, mT3=mT[:].rearrange("p (c t) -> p c t", t=SPAN),
                    ha3=haT[:].rearrange("p (c t) -> p c t", t=SPAN), hs3=hsT[:].rearrange("p (c t) -> p c t", t=SPAN))
        xbuilt = {}
        for sp_i in range(n_own):
            if True:
                sc = own_scope
                if not xbuilt.get(sp_i):
                    build_xT(xo, SPAN * sp_i, xb)
                late["ftmp"] = sb("ftmp", [128, 512], F32, sc)
                if sp_i + 1 < n_own:
                    prefetch_x(xo, SPAN * (sp_i + 1), xb, 0, 2)
                ubm, U3, Sh, (U_, X_) = ssm_state(sc, own=True)
                attn_stage(sp_i, sc)
                pT_ = memo["pT"]
                xb_own = xb + [(pT_[:, 0:D], "pT"), (pT_[:, D:2 * D], "pT")]
                if sp_i + 1 < n_own:
                    prefetch_x(xo, SPAN * (sp_i + 1), xb_own, 2, 4)
                merge_half(True, sc)
                ssm_out(ubm, U3, Sh)

                def pre_tail(n=sp_i):
                    if n + 1 < n_own:
                        build_xT(xo, SPAN * (n + 1), xb_own)
                        xbuilt[n + 1] = True
                main_stage(sp_i, sc, ubm, U_, X_, Sh[0], pre_tail, sp_i + 1 < n_own)

        P.barrier()
        own_scope.close()
        with nc.Block() as block:
            @block.tensor
            def _(e):
                P.emit("pe", e)

            @block.scalar
            def _(e):
                P.emit("act", e)

            @block.vector
            def _(e):
                P.emit("dve", e)

            @block.gpsimd
            def _(e):
                P.emit("pool", e)

            @block.sync
            def _(e):
                P.emit("sp", e)
    return nc


def _t5_bucket(dist):
    max_exact = 16
    d = np.maximum(dist, 1).astype(np.float32)
    large = max_exact + (np.log(d / np.float32(max_exact)) / np.float32(math.log(128 / max_exact)) * np.float32(16)).astype(np.int32)
    large = np.minimum(large, 31)
    return np.where(dist < max_exact, dist, large)


def kernel(x, w_in, ssm_lambda_re, ssm_lambda_im, ssm_b_re, ssm_b_im, ssm_c_re, ssm_c_im, ssm_d, ssm_log_step,
           w_glu, attn_sinks, rel_bias_table, w_branch_ssm, w_branch_attn, w_out, ln_gain, ln_bias):
    f = lambda a: np.ascontiguousarray(np.asarray(a, dtype=np.float32))
    x = f(x)
    pl = lambda a: f(a.reshape(32, 2, 64).transpose(1, 2, 0).reshape(128, 32))
    plb = lambda a: f(a.reshape(32, 2, 64, 16).transpose(1, 2, 0, 3).reshape(128, 512))
    plc = lambda a: f(a.reshape(32, 2, 16, 64).transpose(1, 3, 0, 2).reshape(128, 512))
    ls = np.asarray(ssm_log_step[0], np.float32).reshape(32, 2)
    lstep = f(np.broadcast_to(ls.T[:, None, :], (2, 64, 32)).reshape(128, 32))
    dI = np.zeros((16, 64, 16), np.float32)
    dd = np.asarray(ssm_d[0], np.float32).reshape(64, 16)
    for h in range(16):
        dI[h, :, h] = dd[:, h]
    sk = np.zeros((128, 8), np.float32)
    sinks = np.asarray(attn_sinks[0], np.float32)
    for hp in range(8):
        sk[0:64, hp] = sinks[2 * hp]
        sk[64:128, hp] = sinks[2 * hp + 1]
    tab = np.asarray(rel_bias_table, np.float32)
    line = np.full((16, 384), NEG, np.float32)
    bk = _t5_bucket(np.arange(128))
    line[:, 128:256] = tab[bk, :].T
    common = {
        "w_in": f(w_in[0]), "w_glu": f(w_glu[0]), "w_bs": f(w_branch_ssm[0]), "w_ba": f(w_branch_attn[0]), "w_out": f(w_out[0]),
        "lamr": pl(np.asarray(ssm_lambda_re[0])), "lami": pl(np.asarray(ssm_lambda_im[0])), "lstep": lstep,
        "br": plb(np.asarray(ssm_b_re[0])), "bi": plb(np.asarray(ssm_b_im[0])),
        "cr": plc(np.asarray(ssm_c_re[0])), "ci": plc(np.asarray(ssm_c_im[0])),
        "dI": f(dI.reshape(16, 1024)), "sk": sk, "line": line,
        "lng": f(np.broadcast_to(np.asarray(ln_gain[0], np.float32)[None, :], (128, D))),
        "lnb": f(np.broadcast_to(np.asarray(ln_bias[0], np.float32)[None, :], (128, D))),
        "idf": np.eye(128, dtype=np.float32), "anti": f(np.eye(128, dtype=np.float32)[::-1]),
        "msc": f(np.broadcast_to(np.array([8.0 * (NB - 1 - b) for b in range(NB)] + [8.0 * NB], np.float32)[None, :], (128, 65))),
    }
    in_maps = []
    for c in range(NCORES):
        b, h = c // 2, c % 2
        m = dict(common)
        m["xo"] = f(x[b, h * TOK:(h + 1) * TOK])
        m["xp"] = f(x[b, 0:TOK]) if h == 1 else np.zeros((TOK, D), np.float32)
        m["hneg"] = np.full((128, 1), 0.0 if h == 1 else NEG, np.float32)
        in_maps.append(m)
    nc = build_program()
    res = run_bass_kernel_spmd(nc, in_maps, core_ids=list(range(NCORES)))
    out = np.zeros((4, 8192, D), np.float32)
    for c in range(NCORES):
        b, h = c // 2, c % 2
        out[b, h * TOK:(h + 1) * TOK] = np.asarray(res.results[c]["y"], np.float32)
    return out
```
